# Optimizing a Trainium2 kernel written in Bass

```python
import jax, jax.numpy as jnp
from jax import lax
import numpy as np

D_MODEL = 1024
BATCH = 4
SEQ = 8192
DEPTH = 2

GRID_W = 64
CTX_LEN = 256
MIX_WIDTH = D_MODEL
N_MIXERS = 4
GROUP_WIDTH = MIX_WIDTH // N_MIXERS
HEAD_DIM = 64
RWKV_HEADS = GROUP_WIDTH // HEAD_DIM
DECAY_LORA = 64
ICLR_LORA = 64
GATE_LORA = 128
READ_COLS = GATE_LORA + GROUP_WIDTH
STATE_COLS = 2 * GROUP_WIDTH + 2 * DECAY_LORA + 2 * ICLR_LORA
RWKV_COLS = READ_COLS + STATE_COLS
POOL_WINDOWS = (2, 4, 8, 16)
POOL_GROUP_DIM = GROUP_WIDTH // len(POOL_WINDOWS)
FOURIER_GROUPS = 4
FOURIER_GROUP_DIM = GROUP_WIDTH // FOURIER_GROUPS
CONV_WIDTH = 31
N_IN = RWKV_COLS + GROUP_WIDTH + GROUP_WIDTH + 2 * GROUP_WIDTH
D_FF = -(-8 * D_MODEL // (3 * 256)) * 256
RMS_EPS = 1e-6
GN_EPS = 64e-5
LN_EPS = 1e-5
POS_BASE = 10000.0

kernel_name = 'hybrid_rwkv7_pool_fnet_conformer_dit'


def _rmsnorm(x, g):
    xf = x.astype(jnp.float32)
    y = xf * lax.rsqrt(jnp.mean(xf * xf, axis=-1, keepdims=True) + RMS_EPS)
    return (y * g.astype(jnp.float32)).astype(x.dtype)


def _modulate(x, g, shift, scale):
    return _rmsnorm(x, g) * (1 + scale) + shift


def _pos_embed_2d(rows, dim):
    quarter = dim // 4
    omega = 1.0 / (POS_BASE ** (jnp.arange(quarter, dtype=jnp.float32) / quarter))
    row = jnp.repeat(jnp.arange(rows, dtype=jnp.float32), GRID_W)
    col = jnp.tile(jnp.arange(GRID_W, dtype=jnp.float32), rows)

    def enc(p):
        ang = p[:, None] * omega[None, :]
        return jnp.concatenate([jnp.sin(ang), jnp.cos(ang)], axis=-1)

    return jnp.concatenate([enc(row), enc(col)], axis=-1)


def _token_shift(u, mu_prev, mu_next):
    pad = jnp.zeros_like(u[:, :1])
    prev = jnp.concatenate([pad, u[:, :-1]], axis=1)
    nxt = jnp.concatenate([u[:, 1:], pad], axis=1)
    return u + mu_prev * (prev - u) + mu_next * (nxt - u)


def _heads(z):
    return z.reshape(z.shape[:-1] + (RWKV_HEADS, HEAD_DIM))


def _rwkv_inputs(s, w0, w2, a0, a2, k_k, k_a):
    G = GROUP_WIDTH
    B, T, _ = s.shape
    k, v = s[..., :G], s[..., G:2 * G]
    wl = s[..., 2 * G:2 * G + 2 * DECAY_LORA].reshape(B, T, 2, DECAY_LORA)
    al = s[..., 2 * G + 2 * DECAY_LORA:].reshape(B, T, 2, ICLR_LORA)
    w = -jax.nn.softplus(-(w0 + jnp.einsum('btdr,drc->btdc', jnp.tanh(wl), w2))) - 0.5
    decay = jnp.exp(-jnp.exp(w))
    a = jax.nn.sigmoid(a0 + jnp.einsum('btdr,drc->btdc', al, a2))
    kk = _heads(k * k_k)
    kk = kk / jnp.maximum(jnp.sqrt(jnp.sum(kk * kk, axis=-1, keepdims=True)), 1e-12)
    k_dir = k[:, :, None, :] * (1.0 + (a - 1.0) * k_a)
    return _heads(decay), kk, _heads(a), _heads(k_dir), _heads(v)


def _wkv_scan(S0, decay, kk, a, k, v, r, reverse):
    with_out = r is not None

    def step(S, inp):
        w_t, kk_t, a_t, k_t, v_t = inp[:5]
        sa = jnp.einsum('bhvk,bhk->bhv', S, -kk_t)
        S = (S * w_t[:, :, None, :] + sa[..., None] * (kk_t * a_t)[:, :, None, :]
             + v_t[..., None] * k_t[:, :, None, :])
        y = jnp.einsum('bhvk,bhk->bhv', S, inp[5]) if with_out else None
        return S, y

    seq = (decay, kk, a, k, v) + ((r,) if with_out else ())
    S, ys = lax.scan(step, S0, tuple(jnp.swapaxes(z, 0, 1) for z in seq), reverse=reverse)
    return S, (jnp.swapaxes(ys, 0, 1) if with_out else None)


def _rwkv_readout(y, r, k_dirs, v, gl, g2, r_k, gn_w, gn_b):
    B, T = y.shape[:2]
    mu = jnp.mean(y, axis=-1, keepdims=True)
    var = jnp.mean(jnp.square(y - mu), axis=-1, keepdims=True)
    yn = ((y - mu) * lax.rsqrt(var + GN_EPS)).reshape(B, T, GROUP_WIDTH) * gn_w + gn_b
    coef = jnp.sum(jnp.sum(r[:, :, None] * k_dirs * r_k, axis=-1), axis=2)
    bonus = (coef[..., None] * v).reshape(B, T, GROUP_WIDTH)
    g = jax.nn.sigmoid(gl) @ g2
    return (yn + bonus) * g


def _rwkv_mixer(ux, uc, ctx_out, mu_prev, mu_next, w0, w2, a0, a2, g2, k_k, k_a, r_k, gn_w, gn_b):
    dtype = ux.dtype
    sx = _token_shift(ux.astype(jnp.float32), mu_prev, mu_next)
    wc = uc.shape[-1]
    sc = _token_shift(uc.astype(jnp.float32), mu_prev[-wc:], mu_next[-wc:])
    lw = (w0, w2, a0, a2, k_k, k_a)
    dx, kkx, ax, kx, vx = _rwkv_inputs(sx[..., READ_COLS:], *lw)
    dc, kkc, ac, kc, vc = _rwkv_inputs(sc[..., wc - STATE_COLS:], *lw)
    rx = _heads(sx[..., GATE_LORA:READ_COLS])
    rc = _heads(sc[..., GATE_LORA:READ_COLS]) if ctx_out else None
    S0 = jnp.zeros((ux.shape[0], RWKV_HEADS, HEAD_DIM, HEAD_DIM), jnp.float32)
    y_fwd_x = y_bwd_x = y_fwd_c = y_bwd_c = None
    for d, rev in enumerate((False, True)):
        S_ctx, yc = _wkv_scan(S0, dc[:, :, d], kkc, ac[:, :, d], kc[:, :, d], vc, rc, rev)
        _, yx = _wkv_scan(S_ctx, dx[:, :, d], kkx, ax[:, :, d], kx[:, :, d], vx, rx, rev)
        if d == 0:
            y_fwd_x, y_fwd_c = yx, yc
        else:
            y_bwd_x, y_bwd_c = yx, yc
    ro = (g2, r_k, gn_w, gn_b)
    out_x = _rwkv_readout(y_fwd_x + y_bwd_x, rx, kx, vx, sx[..., :GATE_LORA], *ro).astype(dtype)
    if not ctx_out:
        return out_x, None
    out_c = _rwkv_readout(y_fwd_c + y_bwd_c, rc, kc, vc, sc[..., :GATE_LORA], *ro).astype(dtype)
    return out_x, out_c


def _pool_mixer(u, pool_w, pool_scale):
    B, T, C = u.shape
    uf = u.astype(jnp.float32)
    cz = jnp.concatenate([jnp.zeros_like(uf[:, :1]), jnp.cumsum(uf, axis=1)], axis=1)
    t = jnp.arange(T)
    diffs = []
    for gi, w in enumerate(POOL_WINDOWS):
        sl = slice(gi * POOL_GROUP_DIM, (gi + 1) * POOL_GROUP_DIM)
        cg = cz[..., sl]
        hw = w // 2
        lo0, hi0 = jnp.clip(t - hw, 0, T), jnp.clip(t + hw, 0, T)
        lo1, hi1 = jnp.clip(t - hw + 1, 0, T), jnp.clip(t + hw + 1, 0, T)
        total = (cg[:, hi0] - cg[:, lo0]) + (cg[:, hi1] - cg[:, lo1])
        count = ((hi0 - lo0) + (hi1 - lo1)).astype(jnp.float32)
        diffs.append(total / count[None, :, None] - uf[..., sl])
    dmat = jnp.stack(diffs, axis=2)
    y = jnp.einsum('btgc,gcd->btgd', dmat, pool_w).reshape(B, T, C)
    return (y * pool_scale).astype(u.dtype)


def _fourier_mixer(u, fourier_w):
    B, T, C = u.shape
    z = u.astype(jnp.float32).reshape(B, T, FOURIER_GROUPS, FOURIER_GROUP_DIM)
    f = jnp.fft.fftn(z, axes=(1, 3), norm='ortho').real.reshape(B, T, C)
    return f.astype(u.dtype) @ fourier_w


def _conv_module(u, dw_w, dw_b, ln_g, ln_b, pw):
    a, b = jnp.split(u, 2, axis=-1)
    h = a * jax.nn.sigmoid(b)
    C = h.shape[-1]
    h = lax.conv_general_dilated(h, dw_w[:, None, :], window_strides=(1,), padding='SAME',
                                 dimension_numbers=('NWC', 'WIO', 'NWC'),
                                 feature_group_count=C) + dw_b
    hf = h.astype(jnp.float32)
    mu = jnp.mean(hf, axis=-1, keepdims=True)
    var = jnp.mean(jnp.square(hf - mu), axis=-1, keepdims=True)
    hn = (hf - mu) * lax.rsqrt(var + LN_EPS) * ln_g + ln_b
    return jax.nn.silu(hn).astype(u.dtype) @ pw


def _mix_and_project(u, y_rwkv, pool_w, pool_scale, fourier_w, dw_w, dw_b, ln_g, ln_b, pw, w_out):
    o, G = RWKV_COLS, GROUP_WIDTH
    y_pool = _pool_mixer(u[..., o:o + G], pool_w, pool_scale)
    y_four = _fourier_mixer(u[..., o + G:o + 2 * G], fourier_w)
    y_conv = _conv_module(u[..., o + 2 * G:o + 4 * G], dw_w, dw_b, ln_g, ln_b, pw)
    return jnp.concatenate([y_rwkv, y_pool, y_four, y_conv], axis=-1) @ w_out


def _swiglu(h, w_in, w_out):
    gate, up = jnp.split(h @ w_in, 2, axis=-1)
    return (jax.nn.silu(gate) * up) @ w_out


def setup_inputs(seed: int = 0) -> dict:
    key = jax.random.key(seed)
    ks = iter(jax.random.split(key, 64))
    L, D, G = DEPTH, D_MODEL, GROUP_WIDTH

    def nrm(shape, scale):
        return jax.random.normal(next(ks), shape, jnp.float32) * scale

    def uni(shape, lo, hi):
        return jax.random.uniform(next(ks), shape, jnp.float32, lo, hi)

    return {
        'x': nrm((BATCH, SEQ, D), 1.0),
        'c': nrm((BATCH, D), 1.0),
        'ctx': nrm((BATCH, CTX_LEN, D), 1.0),
        'c_ctx': nrm((D,), 1.0),
        'w_mod': nrm((L, D, 6 * D), 0.5 * D ** -0.5),
        'b_mod': nrm((L, 6 * D), 0.02),
        'norm1_g': 1.0 + nrm((L, D), 0.05),
        'norm2_g': 1.0 + nrm((L, D), 0.05),
        'w_in': nrm((L, D, N_IN), D ** -0.5),
        'w_out': nrm((L, MIX_WIDTH, D), MIX_WIDTH ** -0.5),
        'rwkv_mu_prev': uni((L, RWKV_COLS), 0.0, 0.5),
        'rwkv_mu_next': uni((L, RWKV_COLS), 0.0, 0.5),
        'rwkv_w0': uni((L, 2, G), -6.0, -1.0),
        'rwkv_w2': nrm((L, 2, DECAY_LORA, G), 0.1),
        'rwkv_a0': nrm((L, 2, G), 0.1),
        'rwkv_a2': nrm((L, 2, ICLR_LORA, G), 0.1),
        'rwkv_g2': nrm((L, GATE_LORA, G), GATE_LORA ** -0.5),
        'rwkv_k_k': 0.85 + nrm((L, G), 0.05),
        'rwkv_k_a': 1.0 + nrm((L, G), 0.05),
        'rwkv_r_k': nrm((L, RWKV_HEADS, HEAD_DIM), 0.1),
        'rwkv_gn_w': 1.0 + nrm((L, G), 0.05),
        'rwkv_gn_b': nrm((L, G), 0.02),
        'pool_w': nrm((L, len(POOL_WINDOWS), POOL_GROUP_DIM, POOL_GROUP_DIM), POOL_GROUP_DIM ** -0.5),
        'pool_scale': 1.0 + nrm((L, G), 0.1),
        'fourier_w': nrm((L, G, G), G ** -0.5),
        'conv_dw_w': nrm((L, CONV_WIDTH, G), CONV_WIDTH ** -0.5),
        'conv_dw_b': nrm((L, G), 0.02),
        'conv_ln_g': 1.0 + nrm((L, G), 0.05),
        'conv_ln_b': nrm((L, G), 0.02),
        'conv_pw': nrm((L, G, G), G ** -0.5),
        'ffn_w_in': nrm((L, D, 2 * D_FF), D ** -0.5),
        'ffn_w_out': nrm((L, D_FF, D), D_FF ** -0.5),
        'final_norm_g': 1.0 + nrm((D,), 0.05),
    }


def reference(x, c, ctx, c_ctx, w_mod, b_mod, norm1_g, norm2_g, w_in, w_out,
              rwkv_mu_prev, rwkv_mu_next, rwkv_w0, rwkv_w2, rwkv_a0, rwkv_a2, rwkv_g2,
              rwkv_k_k, rwkv_k_a, rwkv_r_k, rwkv_gn_w, rwkv_gn_b,
              pool_w, pool_scale, fourier_w,
              conv_dw_w, conv_dw_b, conv_ln_g, conv_ln_b, conv_pw,
              ffn_w_in, ffn_w_out, final_norm_g):
    B, T, D = x.shape
    rows = T // GRID_W
    h = x + _pos_embed_2d(rows, D).astype(x.dtype)[None]
    hc = ctx
    for l in range(DEPTH):
        last = l == DEPTH - 1
        mod_x = (jax.nn.silu(c) @ w_mod[l] + b_mod[l])[:, None, :]
        sh1, sc1, gt1, sh2, sc2, gt2 = jnp.split(mod_x, 6, axis=-1)
        n_ctx_mod = 2 if last else 6
        mod_c = jax.nn.silu(c_ctx) @ w_mod[l, :, :n_ctx_mod * D] + b_mod[l, :n_ctx_mod * D]
        mc = jnp.split(mod_c, n_ctx_mod)
        ux = _modulate(h, norm1_g[l], sh1, sc1) @ w_in[l]
        ac = _modulate(hc, norm1_g[l], mc[0], mc[1])
        uc = ac @ (w_in[l, :, READ_COLS:RWKV_COLS] if last else w_in[l])
        y_rwkv_x, y_rwkv_c = _rwkv_mixer(
            ux[..., :RWKV_COLS], uc[..., :RWKV_COLS], not last,
            rwkv_mu_prev[l], rwkv_mu_next[l], rwkv_w0[l], rwkv_w2[l], rwkv_a0[l], rwkv_a2[l],
            rwkv_g2[l], rwkv_k_k[l], rwkv_k_a[l], rwkv_r_k[l], rwkv_gn_w[l], rwkv_gn_b[l])
        other = (pool_w[l], pool_scale[l], fourier_w[l], conv_dw_w[l], conv_dw_b[l],
                 conv_ln_g[l], conv_ln_b[l], conv_pw[l], w_out[l])
        h = h + gt1 * _mix_and_project(ux, y_rwkv_x, *other)
        h = h + gt2 * _swiglu(_modulate(h, norm2_g[l], sh2, sc2), ffn_w_in[l], ffn_w_out[l])
        if not last:
            hc = hc + mc[2] * _mix_and_project(uc, y_rwkv_c, *other)
            hc = hc + mc[5] * _swiglu(_modulate(hc, norm2_g[l], mc[3], mc[4]), ffn_w_in[l], ffn_w_out[l])
    return _rmsnorm(h, final_norm_g)
```

```python
import numpy as np
import ml_dtypes
from contextlib import ExitStack
import concourse.bass as bass
import concourse.mybir as mybir
from concourse.bass_utils import run_bass_kernel_spmd

F32 = mybir.dt.float32
BF16 = mybir.dt.bfloat16
AF = mybir.ActivationFunctionType
ALU = mybir.AluOpType
AX = mybir.AxisListType

D = 1024
NIN = 2176
DFF = 2816
GW = 256
RW = 1152
N_DMA_SEMS = 12
NQ = {"sp": N_DMA_SEMS}


class Buf:
    __slots__ = ("w", "r", "x")

    def __init__(self):
        self.w = {}
        self.r = {}
        self.x = False


class Acc:
    __slots__ = ("b",)

    def __init__(self, b):
        self.b = b


class Sched:
    def __init__(self, nc, comp, dma):
        self.nc = nc
        self.streams = {k: [] for k in ("pe", "act", "dve", "pool", "sp")}
        self.sems = dict(comp)
        self.cnt = {k: 0 for k in comp}
        self.seen = {k: {} for k in self.streams}
        self.dma_sems = dma
        self.dma_cnt = {q: 0 for q in dma}

    def _need(self, eng, deps):
        st = self.streams[eng]
        seen = self.seen[eng]
        for key, val in deps.items():
            if eng == "pe" and key == "pe":
                continue
            if seen.get(key, 0) >= val:
                continue
            seen[key] = val
            st.append(("wait", key, val))

    def _sem_of(self, key):
        if isinstance(key, str):
            return self.sems[key]
        return self.dma_sems[key[0]][key[1]]

    @staticmethod
    def _collect(reads, writes, eng=None):
        deps = {}
        for b in reads:
            for k, v in b.w.items():
                if deps.get(k, 0) < v:
                    deps[k] = v
            if b.x:
                for k, v in b.r.items():
                    if k != eng and deps.get(k, 0) < v:
                        deps[k] = v
        for b in writes:
            if isinstance(b, Acc):
                for k, v in b.b.r.items():
                    if deps.get(k, 0) < v:
                        deps[k] = v
                continue
            for k, v in b.w.items():
                if deps.get(k, 0) < v:
                    deps[k] = v
            for k, v in b.r.items():
                if deps.get(k, 0) < v:
                    deps[k] = v
        return deps

    @staticmethod
    def _record(key, val, reads, writes):
        for b in reads:
            if b.r.get(key, 0) < val:
                b.r[key] = val
        for b in writes:
            if isinstance(b, Acc):
                if b.b.w.get(key, 0) < val:
                    b.b.w[key] = val
                continue
            b.w = {key: val}
            b.r = {}

    def op(self, eng, fn, reads=(), writes=()):
        deps = self._collect(reads, writes, eng)
        self._need(eng, deps)
        self.cnt[eng] += 1
        val = self.cnt[eng]
        self.streams[eng].append(("op", fn, eng))
        self._record(eng, val, reads, writes)

    def dma(self, out, in_, reads=(), writes=(), q="sp"):
        deps = self._collect(reads, writes)
        i = self.dma_cnt[q]
        self.dma_cnt[q] += 1
        slot = i % N_DMA_SEMS
        rnd = i // N_DMA_SEMS
        key = (q, slot)
        if rnd > 0:
            deps[key] = max(deps.get(key, 0), 16 * rnd)
        self._need(q, deps)
        self.streams[q].append(("dma", out, in_, key))
        self._record(key, 16 * (rnd + 1), reads, writes)

    def barrier(self):
        deps = {k: v for k, v in self.cnt.items() if v > 0}
        for q, n in self.dma_cnt.items():
            for slot in range(min(n, N_DMA_SEMS)):
                rounds = (n - 1 - slot) // N_DMA_SEMS + 1
                deps[(q, slot)] = 16 * rounds
        for eng in self.streams:
            d = dict(deps)
            self._need(eng, d)

    def emit(self):
        nc = self.nc
        streams = self.streams
        sem_of = self._sem_of
        sems = self.sems

        def run(name, e):
            for it in streams[name]:
                if it[0] == "wait":
                    e.wait_ge(sem_of(it[1]), it[2])
                elif it[0] == "op":
                    it[1](e).then_inc(sems[it[2]], 1)
                else:
                    e.dma_start(out=it[1], in_=it[2]).then_inc(sem_of(it[3]), 16)

        with nc.Block() as block:
            @block.tensor
            def _(e):
                run("pe", e)

            @block.scalar
            def _(e):
                run("act", e)

            @block.vector
            def _(e):
                run("dve", e)

            @block.gpsimd
            def _(e):
                run("pool", e)

            @block.sync
            def _(e):
                run("sp", e)


class Tl:
    __slots__ = ("t", "b")

    def __init__(self, t):
        self.t = t
        self.b = Buf()


class Ctx:
    def __init__(self, nc, S):
        self.nc = nc
        self.S = S
        self.uid = 0
        self.rr = 0

    def sb(self, es, shape, dt, name="t"):
        self.uid += 1
        return Tl(es.enter_context(self.nc.sbuf_tensor(f"{name}_{self.uid}", list(shape), dt)))

    def pool_of(self, es, n, shape, dt, name="p"):
        return Rot([self.sb(es, shape, dt, name) for _ in range(n)])

    def mm(self, out, lhsT, rhs, start, stop, reads, writes):
        self.S.op("pe", lambda e: e.matmul(out, lhsT=lhsT, rhs=rhs, start=start, stop=stop), reads, writes)

    def tr(self, out, in_, ident, reads, writes):
        self.S.op("pe", lambda e: e.transpose(out=out, in_=in_, identity=ident), reads, writes)

    def act(self, out, in_, func, reads, writes, scale=1.0, bias=0.0, accum=None):
        if accum is None:
            self.S.op("act", lambda e: e.activation(out=out, in_=in_, func=func, scale=scale, bias=bias), reads, writes)
        else:
            self.S.op("act", lambda e: e.activation(out=out, in_=in_, func=func, scale=scale, bias=bias,
                                                    accum_out=accum), reads, writes)

    def tt(self, eng, out, in0, in1, op, reads, writes):
        self.S.op(eng, lambda e: e.tensor_tensor(out=out, in0=in0, in1=in1, op=op), reads, writes)

    def ts(self, eng, out, in0, s1, s2, op0, op1, reads, writes):
        if s2 is None:
            self.S.op(eng, lambda e: e.tensor_scalar(out=out, in0=in0, scalar1=s1, scalar2=None, op0=op0), reads, writes)
        else:
            self.S.op(eng, lambda e: e.tensor_scalar(out=out, in0=in0, scalar1=s1, scalar2=s2, op0=op0, op1=op1),
                      reads, writes)

    def stt(self, eng, out, in0, sc, in1, op0, op1, reads, writes):
        self.S.op(eng, lambda e: e.scalar_tensor_tensor(out=out, in0=in0, scalar=sc, in1=in1, op0=op0, op1=op1),
                  reads, writes)

    def cp(self, eng, out, in_, reads, writes):
        if eng == "act":
            self.S.op("act", lambda e: e.activation(out=out, in_=in_, func=AF.Copy), reads, writes)
        else:
            self.S.op(eng, lambda e: e.tensor_copy(out=out, in_=in_), reads, writes)

    def memset(self, eng, ap, val, writes):
        self.S.op(eng, lambda e: e.memset(ap, val), (), writes)

    def recip(self, out, in_, reads, writes):
        self.S.op("dve", lambda e: e.reciprocal(out=out, in_=in_), reads, writes)

    def dma(self, out, in_, reads, writes, q="sp"):
        self.S.dma(out, in_, reads, writes, q=q)

    def alt(self, engs=("act", "dve")):
        self.rr += 1
        return engs[self.rr % len(engs)]

    def evac(self, out, in_, reads, writes, engs=("act", "dve")):
        self.cp(self.alt(engs), out, in_, reads, writes)


def interleave(*gens):
    gens = [g for g in gens if g is not None]
    while gens:
        for g in list(gens):
            try:
                next(g)
            except StopIteration:
                gens.remove(g)


class Rot:
    def __init__(self, items):
        self.items = items
        self.i = 0

    def get(self):
        it = self.items[self.i % len(self.items)]
        self.i += 1
        return it


def _bf(a):
    return np.ascontiguousarray(a.astype(ml_dtypes.bfloat16))


def _pool_mats():
    Tt = 512
    t = np.arange(Tt)
    out = np.zeros((4, 5, 128, 128), np.float64)
    for gi, w in enumerate((2, 4, 8, 16)):
        hw = w // 2
        lo0, hi0 = np.clip(t - hw, 0, Tt), np.clip(t + hw, 0, Tt)
        lo1, hi1 = np.clip(t - hw + 1, 0, Tt), np.clip(t + hw + 1, 0, Tt)
        cnt = (hi0 - lo0) + (hi1 - lo1)
        M = np.zeros((Tt, Tt))
        for tt in range(Tt):
            M[tt, lo0[tt]:hi0[tt]] += 1.0
            M[tt, lo1[tt]:hi1[tt]] += 1.0
            M[tt] /= cnt[tt]
            M[tt, tt] -= 1.0
        blk = lambda ti, si: M[ti * 128:(ti + 1) * 128, si * 128:(si + 1) * 128].T
        out[gi, 0] = blk(1, 0)
        out[gi, 1] = blk(1, 1)
        out[gi, 2] = blk(1, 2)
        out[gi, 3] = blk(0, 0)
        out[gi, 4] = blk(3, 3)
    return out


def _fourier_consts(T):
    R = T // 64
    c = np.arange(64)
    ang = 2 * np.pi * np.outer(c, c) / 64.0
    nrm = 1.0 / np.sqrt(64.0 * T)
    cc3 = np.zeros((128, 384))
    for g in range(2):
        sl = slice(g * 64, (g + 1) * 64)
        cc3[sl, 0 + g * 64:0 + (g + 1) * 64] = np.cos(ang)
        cc3[sl, 128 + g * 64:128 + (g + 1) * 64] = -np.sin(ang)
        cc3[sl, 256 + g * 64:256 + (g + 1) * 64] = -np.cos(ang)
    t1 = np.arange(R)
    angR = 2 * np.pi * np.outer(t1, t1) / R
    wr = np.zeros((128, 2, R))
    wr[:R, 0] = np.cos(angR)
    wr[:R, 1] = np.sin(angR)
    t2 = np.arange(64)
    k1 = np.arange(R)
    k2 = np.arange(64)
    th = 2 * np.pi * (t2[:, None, None] * k1[None, :, None] / T + t2[:, None, None] * k2[None, None, :] / 64.0)
    gm = np.concatenate([np.cos(th), np.sin(th)], axis=0) * nrm
    return _bf(cc3), _bf(wr), _bf(gm)


def _pos_tables(T):
    quarter = D // 4
    omega = 1.0 / (10000.0 ** (np.arange(quarter, dtype=np.float32) / np.float32(quarter)))
    omega = omega.astype(np.float32)

    def enc(p):
        ang = (p[:, None].astype(np.float32) * omega[None, :]).astype(np.float32)
        return np.concatenate([np.sin(ang), np.cos(ang)], axis=-1).astype(np.float32)

    rows = T // 64
    return enc(np.arange(rows, dtype=np.float32)), enc(np.arange(64, dtype=np.float32))


def _misc_consts():
    i = np.arange(128)
    su = (i[:, None] < i[None, :]).astype(np.float32)
    iu = (i[:, None] <= i[None, :]).astype(np.float32)
    sl = su.T.copy()
    il = iu.T.copy()
    ident = np.eye(128, dtype=np.float32)
    ones = np.ones((128, 128), np.float32)
    blk = np.zeros((128, 128), np.float32)
    blk[:64, :64] = 1.0
    blk[64:, 64:] = 1.0
    mats = [ident, ones, blk, su, iu, sl, il]
    mats.append(((i[:, None] // 8) == (i[None, :] // 8)).astype(np.float32))
    for b in (8, 16, 32, 64):
        same2 = (i[:, None] // (2 * b)) == (i[None, :] // (2 * b))
        diffb = (i[:, None] // b) != (i[None, :] // b)
        mats.append((same2 & diffb).astype(np.float32))
    return np.stack(mats, axis=0)


class Prog:
    pass


def build_program(T, TC, L, dbg=()):
    nc = bass.Bass("TRN2", target_bir_lowering=False)
    P = Prog()
    P.T, P.TC, P.L, P.TT = T, TC, L, T + TC
    P.dbg = dbg
    TT = P.TT
    R = T // 64
    RC = TC // 64

    def din(name, shape, dt=F32):
        return nc.dram_tensor(name, list(shape), dt, kind="ExternalInput").ap()

    def dscr(name, shape, dt=F32):
        kind = "ExternalOutput" if name in dbg else ("ExternalInput" if name + "_in" in dbg else "Internal")
        return nc.dram_tensor(name, list(shape), dt, kind=kind).ap()

    I = {}
    I["x"] = din("x", [T, D]); I["ctx"] = din("ctx", [TC, D]); I["cvec"] = din("cvec", [2, D])
    I["w_mod"] = din("w_mod", [L, D, 6 * D]); I["b_mod"] = din("b_mod", [L, 6 * D])
    I["norm1_g"] = din("norm1_g", [L, D]); I["norm2_g"] = din("norm2_g", [L, D])
    I["w_in"] = din("w_in", [L, D, NIN]); I["w_out"] = din("w_out", [L, D, D])
    I["mu_prev"] = din("mu_prev", [L, RW]); I["mu_next"] = din("mu_next", [L, RW])
    I["w0"] = din("w0", [L, 512]); I["w2"] = din("w2", [L, 128, 256])
    I["a0"] = din("a0", [L, 512]); I["a2"] = din("a2", [L, 128, 256])
    I["g2"] = din("g2", [L, 128, 256])
    for nm in ("k_k", "k_a", "r_k", "gn_w", "gn_b", "pool_scale", "dw_b", "ln_g", "ln_b"):
        I[nm] = din(nm, [L, 256])
    I["pool_w"] = din("pool_w", [L, 4, 64, 64]); I["fourier_w"] = din("fourier_w", [L, 256, 256])
    I["dw_w"] = din("dw_w", [L, 31, 256]); I["conv_pw"] = din("conv_pw", [L, 256, 256])
    I["ffn_w_in"] = din("ffn_w_in", [L, D, 2 * DFF]); I["ffn_w_out"] = din("ffn_w_out", [L, DFF, D])
    I["final_g"] = din("final_g", [D])
    I["misc"] = din("misc", [12, 128, 128]); I["poolm"] = din("poolm", [4, 5, 128, 128], BF16)
    I["cc3"] = din("cc3", [128, 384], BF16)
    I["wr_x"] = din("wr_x", [128, 2, R], BF16); I["gm_x"] = din("gm_x", [128, R, 64], BF16)
    I["wr_c"] = din("wr_c", [128, 2, RC], BF16); I["gm_c"] = din("gm_c", [128, RC, 64], BF16)
    I["pos_row"] = din("pos_row", [R, 512]); I["pos_col"] = din("pos_col", [64, 512])
    I["sel"] = din("sel", [2, 2, 128])
    out = nc.dram_tensor("out", [T, D], F32, kind="ExternalOutput").ap()
    P.I, P.out = I, out

    Dr = {}
    Dr["hx"] = dscr("hx", [T, D]); Dr["hc"] = dscr("hc", [TC, D])
    Dr["uR"] = dscr("uR", [RW, TT]); Dr["uM"] = dscr("uM", [1024, TT], BF16)
    Dr["yT"] = dscr("yT", [1024, TT], BF16); Dr["xn2"] = dscr("xn2", [1024, TT], BF16)
    Dr["sq"] = dscr("sq", [13, 256, TT]); Dr["yd"] = dscr("yd", [2, TT, 256])
    Dr["gtb"] = dscr("gtb", [4, 128, 1024])
    if "scan1" in dbg:
        Dr["dumpf"] = nc.dram_tensor("dumpf", [3, 128, 256], F32, kind="ExternalOutput").ap()
        Dr["dumpb"] = nc.dram_tensor("dumpb", [5, 128, 256], BF16, kind="ExternalOutput").ap()
    P.Dr = Dr

    with ExitStack() as es0:
        comp = {k: es0.enter_context(nc.semaphore(f"s_{k}")) for k in ("pe", "act", "dve", "pool")}
        dma = {"sp": [es0.enter_context(nc.semaphore(f"d_sp{i}")) for i in range(N_DMA_SEMS)],
               "act": [es0.enter_context(nc.semaphore(f"d_act{i}")) for i in range(N_DMA_SEMS)]}
        S = Sched(nc, comp, dma)
        C = Ctx(nc, S)
        P.C = C
        P.PB = [Tl(es0.enter_context(nc.psum_tensor(f"pb{i}", [128, 512], F32))) for i in range(6)]
        P.PT = [Tl(es0.enter_context(nc.psum_tensor(f"pt{i}", [128, 1024], BF16))) for i in range(2)]
        for t_ in P.PB + P.PT:
            t_.b.x = True
        P.misc = C.sb(es0, [128, 12, 128], F32, "misc")
        C.dma(P.misc.t[:], I["misc"].rearrange("k p n -> p k n"), [], [P.misc.b])
        P.identb = C.sb(es0, [128, 128], BF16, "identb")
        C.cp("dve", P.identb.t[:], P.misc.t[:, 0, :], [P.misc.b], [P.identb.b])
        P.maskb = C.sb(es0, [128, 4, 128], BF16, "maskb")
        C.cp("dve", P.maskb.t[:], P.misc.t[:, 3:7, :], [P.misc.b], [P.maskb.b])
        P.onesb = C.sb(es0, [128, 128], BF16, "onesb")
        C.cp("dve", P.onesb.t[:], P.misc.t[:, 1, :], [P.misc.b], [P.onesb.b])

        for l in range(L):
            with ExitStack() as esl:
                last = (l == L - 1)
                import os as _os
                stop = _os.environ.get("KSTOP", "")
                LP = phase_params(P, esl, l)
                S.barrier()
                if stop == "params":
                    break
                phase_A(P, LP, l)
                S.barrier()
                if stop == "A":
                    break
                rwkv_prep(P, LP, l)
                if stop == "prep":
                    break
                rwkv_scan_q(P, LP, l)
                if stop == "scan0":
                    break
                rwkv_readout(P, LP, l)
                if stop == "rwkv":
                    break
                phase_pool(P, LP, l)
                if stop == "pool":
                    break
                phase_fourier(P, LP, l)
                if stop == "fourier":
                    break
                phase_conv(P, LP, l)
                if stop == "conv":
                    break
                phase_C1(P, LP, l, last)
                if stop == "C1":
                    break
                phase_C2(P, LP, l, last)
        S.barrier()
        S.emit()
    return nc


def phase_params(P, esl, l):
    C, I, S = P.C, P.I, P.C.S
    LP = Prog()
    LP.pp1 = C.sb(esl, [128, 128], F32, "pp1")
    LP.pp2 = C.sb(esl, [128, 64], F32, "pp2")
    LP.modT = C.sb(esl, [128, 48, 2], F32, "modT")
    LP.gm1 = C.sb(esl, [128, 8, 2], F32, "gm1")
    LP.gm2 = C.sb(esl, [128, 8, 2], F32, "gm2")
    identf = P.misc.t[:, 0, :]
    with ExitStack() as es:
        rs1 = C.sb(es, [128, 128], F32, "rs1")
        rs2 = C.sb(es, [128, 128], F32, "rs2")
        rs3 = C.sb(es, [128, 128], F32, "rs3")
        for r in (rs1, rs2, rs3):
            C.memset("pool", r.t[:], 0.0, [r.b])
        row = [0]

        def put(dst, vec_ap, n):
            C.dma(dst.t[row[0]:row[0] + n, :], vec_ap.rearrange("(n p) -> n p", p=128), [], [dst.b])
            row[0] += n

        put(rs1, I["b_mod"][l], 48)
        put(rs1, I["norm1_g"][l], 8); put(rs1, I["norm2_g"][l], 8)
        put(rs1, I["mu_prev"][l], 9); put(rs1, I["mu_next"][l], 9)
        put(rs1, I["w0"][l], 4); put(rs1, I["a0"][l], 4)
        for nm in ("k_k", "k_a", "r_k", "gn_w", "gn_b", "pool_scale", "dw_b", "ln_g", "ln_b"):
            put(rs1, I[nm][l], 2)
        put(rs1, I["final_g"], 8)
        assert row[0] == 116
        C.dma(rs2.t[0:62, :], I["dw_w"][l].rearrange("j (n p) -> (j n) p", p=128), [], [rs2.b])
        C.dma(rs3.t[0:16, :], I["cvec"].rearrange("m (n p) -> (m n) p", p=128), [], [rs3.b])
        pb = P.PB[0]
        C.tr(pb.t[:, 0:128], rs1.t[:], identf, [rs1.b, P.misc.b], [pb.b])
        C.cp("dve", LP.pp1.t[:], pb.t[:, 0:128], [pb.b], [LP.pp1.b])
        pb = P.PB[1]
        C.tr(pb.t[:, 0:128], rs2.t[:], identf, [rs2.b, P.misc.b], [pb.b])
        C.cp("dve", LP.pp2.t[:], pb.t[:, 0:64], [pb.b], [LP.pp2.b])
        pb = P.PB[2]
        C.tr(pb.t[:, 0:128], rs3.t[:], identf, [rs3.b, P.misc.b], [pb.b])
        scT = C.sb(es, [128, 16], F32, "scT")
        C.act(scT.t[:], pb.t[:, 0:16], AF.Silu, [pb.b], [scT.b])
        scv = scT.t[:].rearrange("p (m dc) -> p dc m", m=2)
        sel = C.sb(es, [2, 2, 128], F32, "sel")
        C.dma(sel.t[:], I["sel"], [], [sel.b])
        wmp = C.pool_of(es, 4, [128, 8, 512], F32, "wm")
        grow = C.sb(es, [2, 512], F32, "grow")
        gts = C.pool_of(es, 2, [128, 512], F32, "gts")
        bmb = C.sb(es, [128, 512], F32, "bmb")
        pfm = P.PB[3]
        for cg in range(12):
            wm = wmp.get()
            C.dma(wm.t[:], I["w_mod"][l][:, cg * 512:(cg + 1) * 512].rearrange("(dc p) n -> p dc n", p=128),
                  [], [wm.b], q="sp" if cg % 2 == 0 else "act")
            for j in range(4):
                nb = cg * 4 + j
                for dc in range(8):
                    C.mm(pfm.t[:, nb * 2:nb * 2 + 2], wm.t[:, dc, j * 128:(j + 1) * 128], scv[:, dc, :],
                         dc == 0, dc == 7, [wm.b, scT.b], [pfm.b])
            if cg in (4, 5, 10, 11):
                g = 0 if cg < 6 else 1
                half = cg % 2
                prow = P.PB[4]
                for dc in range(8):
                    C.mm(prow.t[0:2, :], scv[:, dc, :], wm.t[:, dc, :], dc == 0, dc == 7, [wm.b, scT.b], [prow.b])
                C.cp("dve", grow.t[:], prow.t[0:2, :], [prow.b], [grow.b])
                C.dma(bmb.t[:], I["b_mod"][l][cg * 512:(cg + 1) * 512].partition_broadcast(128), [], [bmb.b])
                for m in range(2):
                    pbc = P.PB[5]
                    C.mm(pbc.t[:], sel.t[:, m, :], grow.t[:], True, True, [sel.b, grow.b], [pbc.b])
                    gt = gts.get()
                    C.tt("dve", gt.t[:], pbc.t[:], bmb.t[:], ALU.add, [pbc.b, bmb.b], [gt.b])
                    C.dma(P.Dr["gtb"][m * 2 + g, :, half * 512:(half + 1) * 512], gt.t[:], [gt.b], [])
        pv = pfm.t[:, 0:96].rearrange("p (nb m) -> p nb m", m=2)
        for m in range(2):
            C.tt("dve", LP.modT.t[:, :, m], pv[:, :, m], LP.pp1.t[:, 0:48], ALU.add, [pfm.b, LP.pp1.b], [LP.modT.b])
        for m in range(2):
            C.stt("dve", LP.gm1.t[:, :, m], LP.modT.t[:, 8:16, m], 1.0, LP.pp1.t[:, 48:56], ALU.add, ALU.mult,
                  [LP.modT.b, LP.pp1.b], [LP.gm1.b])
            C.stt("dve", LP.gm2.t[:, :, m], LP.modT.t[:, 32:40, m], 1.0, LP.pp1.t[:, 56:64], ALU.add, ALU.mult,
                  [LP.modT.b, LP.pp1.b], [LP.gm2.b])
        S.barrier()
    return LP


C_MUP, C_MUN, C_W0, C_A0 = 64, 73, 82, 86
C_KK, C_KA, C_RK, C_GNW, C_GNB, C_PSC, C_DWB, C_LNG, C_LNB, C_FG = 90, 92, 94, 96, 98, 100, 102, 104, 106, 108


def load_cast(C, es, dst, src_ap, ncols, stg_pool, engs=("pool", "dve", "act")):
    nk = src_ap.shape[0] // 128
    CH = 1024
    for k in range(nk):
        for c0 in range(0, ncols, CH):
            cw = min(CH, ncols - c0)
            st = stg_pool.get()
            C.dq = getattr(C, "dq", 0) + 1
            C.dma(st.t[:, 0:cw], src_ap[k * 128:(k + 1) * 128, c0:c0 + cw], [], [st.b], q="sp" if C.dq % 2 else "act")
            C.cp(C.alt(engs), dst.t[:, k, c0:c0 + cw], st.t[:, 0:cw], [st.b], [Acc(dst.b)])


def norm_block_g(P, C, hts, m, gm, sh_col0, LP, xnT, ss_p, rs_p, xs_p):
    n = len(hts)
    sss = [ss_p.get() for _ in range(n)]; rss = [rs_p.get() for _ in range(n)]; xss = [xs_p.get() for _ in range(n)]
    for j in range(n):
        C.act(xss[j].t[:], hts[j].t[:], AF.Square, [hts[j].b], [xss[j].b, sss[j].b], accum=sss[j].t[:])
    yield
    for j in range(n):
        C.act(rss[j].t[:], sss[j].t[:], AF.Sqrt, [sss[j].b], [rss[j].b], scale=1.0 / D, bias=1e-6)
    for j in range(n):
        C.recip(rss[j].t[:], rss[j].t[:], [rss[j].b], [rss[j].b])
    yield
    for j in range(n):
        C.act(xss[j].t[:], hts[j].t[:], AF.Copy, [hts[j].b, rss[j].b], [xss[j].b], scale=rss[j].t[:])
    yield
    for j in range(n):
        pt = P.PT[j % 2]
        yield
        for dc in range(8):
            C.tr(pt.t[:, dc * 128:(dc + 1) * 128], xss[j].t[:, dc * 128:(dc + 1) * 128], P.identb.t[:],
                 [xss[j].b, P.identb.b], [pt.b])
        for dc in range(8):
            o = xnT.t[:, dc, j * 128:(j + 1) * 128]
            i_ = pt.t[:, dc * 128:(dc + 1) * 128]
            if j % 2 == 0:
                C.act(o, i_, AF.Identity, [pt.b, gm.b, LP.modT.b], [Acc(xnT.b)], scale=gm.t[:, dc, m:m + 1],
                      bias=LP.modT.t[:, sh_col0 + dc, m:m + 1])
            else:
                C.ts("dve", o, i_, gm.t[:, dc, m:m + 1], LP.modT.t[:, sh_col0 + dc, m:m + 1], ALU.mult, ALU.add,
                     [pt.b, gm.b, LP.modT.b], [Acc(xnT.b)])


def norm_block(*a):
    for _ in norm_block_g(*a):
        pass


def streams(P):
    return [("c", P.TC, 0, 1), ("x", P.T, P.TC, 0)]


def phase_A(P, LP, l):
    C, I, Dr = P.C, P.I, P.Dr
    with ExitStack() as es:
        winb = C.sb(es, [128, 8, NIN], BF16, "winb")
        stg = C.pool_of(es, 3, [128, 1024], F32, "stg")
        load_cast(C, es, winb, I["w_in"][l], NIN, stg)
        htp = C.pool_of(es, 8, [128, D], F32, "ht")
        ssp = C.pool_of(es, 8, [128, 1], F32, "ss")
        rsp = C.pool_of(es, 8, [128, 1], F32, "rs")
        xsp = C.pool_of(es, 8, [128, D], BF16, "xs")
        xnp = C.pool_of(es, 2, [128, 8, 512], BF16, "xnT")
        osf = C.pool_of(es, 4, [128, 512], F32, "osf")
        osb = C.pool_of(es, 4, [128, 512], BF16, "osb")
        posc = C.sb(es, [128, 512], F32, "posc")
        posr = C.pool_of(es, 4, [128, 512], F32, "posr")
        if l == 0:
            for hh in range(2):
                C.dma(posc.t[hh * 64:(hh + 1) * 64, :], I["pos_col"], [], [posc.b])
        st = {"pbi": 0}
        blocks = []
        for (sn, Ts, c0, m) in streams(P):
            NB = min(512, Ts)
            for blk in range(Ts // NB):
                blocks.append((sn, Ts, c0, m, NB, blk))

        def do_norm(bd, out_box):
            sn, Ts, c0, m, NB, blk = bd
            src = (I["x"] if sn == "x" else I["ctx"]) if l == 0 else (Dr["hx"] if sn == "x" else Dr["hc"])
            xnT = xnp.get()
            hts = []
            for j in range(NB // 128):
                t0 = blk * NB + j * 128
                ht = htp.get()
                C.dma(ht.t[:], src[t0:t0 + 128, :], [], [ht.b])
                if l == 0 and sn == "x":
                    pr = posr.get()
                    for hh in range(2):
                        rr = t0 // 64 + hh
                        C.dma(pr.t[hh * 64:(hh + 1) * 64, :], I["pos_row"][rr, :].partition_broadcast(64), [], [pr.b])
                    C.tt("pool", ht.t[:, 0:512], ht.t[:, 0:512], pr.t[:], ALU.add, [pr.b], [ht.b])
                    C.tt("pool", ht.t[:, 512:1024], ht.t[:, 512:1024], posc.t[:], ALU.add, [posc.b], [ht.b])
                    C.dma(Dr["hx"][t0:t0 + 128, :], ht.t[:], [ht.b], [])
                hts.append(ht)
            out_box.append(xnT)
            yield
            yield from norm_block_g(P, C, hts, m, LP.gm1, 0, LP, xnT, ssp, rsp, xsp)

        def do_mm(bd, xnT):
            sn, Ts, c0, m, NB, blk = bd
            for nb in range(17):
                pb = P.PB[st["pbi"] % 6]; st["pbi"] += 1
                for dc in range(8):
                    C.mm(pb.t[:, 0:NB], winb.t[:, dc, nb * 128:(nb + 1) * 128], xnT.t[:, dc, 0:NB],
                         dc == 0, dc == 7, [winb.b, xnT.b], [pb.b])
                cs = slice(c0 + blk * NB, c0 + blk * NB + NB)
                if nb < 9:
                    o = osf.get()
                    C.evac(o.t[:, 0:NB], pb.t[:, 0:NB], [pb.b], [o.b])
                    C.dma(Dr["uR"][nb * 128:(nb + 1) * 128, cs], o.t[:, 0:NB], [o.b], [])
                else:
                    o = osb.get()
                    C.evac(o.t[:, 0:NB], pb.t[:, 0:NB], [pb.b], [o.b])
                    C.dma(Dr["uM"][(nb - 9) * 128:(nb - 8) * 128, cs], o.t[:, 0:NB], [o.b], [])
                yield

        box = []
        interleave(do_norm(blocks[0], box))
        cur = box[0]
        for i, bd in enumerate(blocks):
            box = []
            interleave(do_mm(bd, cur), do_norm(blocks[i + 1], box) if i + 1 < len(blocks) else None)
            cur = box[0] if box else None
        P.C.S.barrier()


def phase_rwkv(P, LP, l):
    rwkv_prep(P, LP, l)
    if "stop_prep" in P.dbg:
        return
    rwkv_scan_q(P, LP, l)
    rwkv_readout(P, LP, l)


def rwkv_prep(P, LP, l):
    C, I, Dr, S = P.C, P.I, P.Dr, P.C.S
    pp = LP.pp1
    blkf = P.misc.t[:, 2, :]
    onesf = P.misc.t[:, 1, :]
    sq = Dr["sq"]
    with ExitStack() as es:
        stg = C.pool_of(es, 2, [128, 1024], F32, "stg")
        w2b = C.sb(es, [128, 1, 256], BF16, "w2b"); load_cast(C, es, w2b, I["w2"][l], 256, stg)
        a2b = C.sb(es, [128, 1, 256], BF16, "a2b"); load_cast(C, es, a2b, I["a2"][l], 256, stg)
        g2b = C.sb(es, [128, 1, 256], BF16, "g2b"); load_cast(C, es, g2b, I["g2"][l], 256, stg)
        c0t = C.sb(es, [128, 9], F32, "c0t")
        C.tt("dve", c0t.t[:], pp.t[:, C_MUP:C_MUP + 9], pp.t[:, C_MUN:C_MUN + 9], ALU.add, [pp.b], [c0t.b])
        C.ts("dve", c0t.t[:], c0t.t[:], -1.0, 1.0, ALU.mult, ALU.add, [], [c0t.b])
        omka = C.sb(es, [128, 2], F32, "omka")
        C.ts("dve", omka.t[:], pp.t[:, C_KA:C_KA + 2], -1.0, 1.0, ALU.mult, ALU.add, [pp.b], [omka.b])
        dgs = C.sb(es, [128, 27, 128], F32, "dgs")
        identf_ = P.misc.t[:, 0, :]
        for n in range(9):
            for k_, col in enumerate((pp.t[:, C_MUP + n:C_MUP + n + 1], c0t.t[:, n:n + 1], pp.t[:, C_MUN + n:C_MUN + n + 1])):
                C.act(dgs.t[:, n * 3 + k_, :], identf_, AF.Copy, [P.misc.b, pp.b, c0t.b], [Acc(dgs.b)], scale=col)
        uhp = C.pool_of(es, 3, [128, 9, 514], F32, "uh")
        sxp = C.pool_of(es, 2, [128, 9, 512], F32, "sx")
        wk = C.pool_of(es, 34, [128, 512], F32, "wk")
        wkb = C.pool_of(es, 6, [128, 512], BF16, "wkb")
        pbs = {"i": 0}
        blocks = []
        for (sn, Ts, c0, m) in streams(P):
            NB = min(512, Ts)
            for blk in range(Ts // NB):
                blocks.append((sn, Ts, c0, m, NB, blk))

        def load_uh(bd):
            sn, Ts, c0, m, NB, blk = bd
            t0 = blk * NB
            uh = uhp.get()
            lo = 0 if blk == 0 else -1
            hi = 0 if blk == Ts // NB - 1 else 1
            if lo == 0:
                C.memset("pool", uh.t[:, :, 0:1], 0.0, [uh.b])
            if hi == 0:
                C.memset("pool", uh.t[:, :, NB + 1:NB + 2], 0.0, [uh.b])
            C.dma(uh.t[:, :, 1 + lo:NB + 1 + hi],
                  Dr["uR"][:, c0 + t0 + lo:c0 + t0 + NB + hi].rearrange("(n p) t -> p n t", p=128), [], [uh.b])
            return uh

        def tshift(bd, uh, sx):
            NB_ = bd[4]
            for n in range(9):
                pb = P.PB[pbs["i"] % 6]; pbs["i"] += 1
                for k_ in range(3):
                    C.mm(pb.t[:, 0:NB_], dgs.t[:, n * 3 + k_, :], uh.t[:, n, k_:k_ + NB_], k_ == 0, k_ == 2,
                         [dgs.b, uh.b], [pb.b])
                C.cp("act", sx.t[:, n, 0:NB_], pb.t[:, 0:NB_], [pb.b], [Acc(sx.b)])
                yield

        uh0 = load_uh(blocks[0])
        nxt_uh = load_uh(blocks[1]) if len(blocks) > 1 else None
        nxt_sx = sxp.get()
        interleave(tshift(blocks[0], uh0, nxt_sx))
        for bi, bd in enumerate(blocks):
            if True:
                sn, Ts, c0, m, NB, blk = bd
                t0 = blk * NB
                cs = slice(c0 + t0, c0 + t0 + NB)
                sx = nxt_sx
                if bi + 1 < len(blocks):
                    uh_n = nxt_uh
                    nxt_uh = load_uh(blocks[bi + 2]) if bi + 2 < len(blocks) else None
                    nxt_sx = sxp.get()
                    ts_gen = tshift(blocks[bi + 1], uh_n, nxt_sx)
                else:
                    ts_gen = None

                def store(qi, ct, tl):
                    C.dma(sq[qi, ct * 128:(ct + 1) * 128, cs], tl.t[:, 0:NB], [tl.b], [])

                tw = wkb.get(); alb = wkb.get(); sgl = wkb.get()
                C.act(tw.t[:, 0:NB], sx.t[:, 7, 0:NB], AF.Tanh, [sx.b], [tw.b])
                C.cp("pool", alb.t[:, 0:NB], sx.t[:, 8, 0:NB], [sx.b], [alb.b])
                C.act(sgl.t[:, 0:NB], sx.t[:, 0, 0:NB], AF.Sigmoid, [sx.b], [sgl.b])
                def ct_body(ct, sx=sx, tw=tw, alb=alb, sgl=sgl, cs=cs, NB=NB, store=store):
                    for qi, n in ((1, 1 + ct), (2, 5 + ct)):
                        C.dma(sq[qi, ct * 128:(ct + 1) * 128, cs], sx.t[:, n, 0:NB], [sx.b], [])
                    pb = P.PB[pbs["i"] % 6]; pbs["i"] += 1
                    C.mm(pb.t[:, 0:NB], g2b.t[:, 0, ct * 128:(ct + 1) * 128], sgl.t[:, 0:NB], True, True, [g2b.b, sgl.b], [pb.b])
                    gt_ = wk.get()
                    C.cp("act", gt_.t[:, 0:NB], pb.t[:, 0:NB], [pb.b], [gt_.b])
                    store(12, ct, gt_)
                    yield
                    kc = wk.get(); sk = wk.get(); rn = wk.get(); kk = wk.get()
                    C.act(kc.t[:, 0:NB], sx.t[:, 3 + ct, 0:NB], AF.Copy, [sx.b, pp.b], [kc.b], scale=pp.t[:, C_KK + ct:C_KK + ct + 1])
                    C.tt("pool", sk.t[:, 0:NB], kc.t[:, 0:NB], kc.t[:, 0:NB], ALU.mult, [kc.b], [sk.b])
                    pb = P.PB[pbs["i"] % 6]; pbs["i"] += 1
                    C.mm(pb.t[:, 0:NB], blkf, sk.t[:, 0:NB], True, True, [P.misc.b, sk.b], [pb.b])
                    C.act(rn.t[:, 0:NB], pb.t[:, 0:NB], AF.Sqrt, [pb.b], [rn.b])
                    C.ts("dve", rn.t[:, 0:NB], rn.t[:, 0:NB], 1e-12, None, ALU.max, None, [], [rn.b])
                    C.recip(rn.t[:, 0:NB], rn.t[:, 0:NB], [], [rn.b])
                    C.tt("dve", kk.t[:, 0:NB], kc.t[:, 0:NB], rn.t[:, 0:NB], ALU.mult, [kc.b, rn.b], [kk.b])
                    store(0, ct, kk)
                    yield
                    kds = []
                    for d in range(2):
                        ps = slice(64 * d, 64 * d + 64)
                        pw = P.PB[pbs["i"] % 6]; pbs["i"] += 1
                        C.mm(pw.t[:, 0:NB], w2b.t[ps, 0, ct * 128:(ct + 1) * 128], tw.t[ps, 0:NB], True, True, [w2b.b, tw.b], [pw.b])
                        pa = P.PB[pbs["i"] % 6]; pbs["i"] += 1
                        C.mm(pa.t[:, 0:NB], a2b.t[ps, 0, ct * 128:(ct + 1) * 128], alb.t[ps, 0:NB], True, True, [a2b.b, alb.b], [pa.b])
                        lw = wk.get(); ad = wk.get()
                        C.act(lw.t[:, 0:NB], pw.t[:, 0:NB], AF.Sigmoid, [pw.b, pp.b], [lw.b],
                              bias=pp.t[:, C_W0 + d * 2 + ct:C_W0 + d * 2 + ct + 1])
                        C.ts("dve", lw.t[:, 0:NB], lw.t[:, 0:NB], -0.6065306597126334, None, ALU.mult, None, [], [lw.b])
                        C.act(ad.t[:, 0:NB], pa.t[:, 0:NB], AF.Sigmoid, [pa.b, pp.b], [ad.b],
                              bias=pp.t[:, C_A0 + d * 2 + ct:C_A0 + d * 2 + ct + 1])
                        tmp = wk.get(); kd = wk.get(); al = wk.get()
                        C.ts("dve", tmp.t[:, 0:NB], ad.t[:, 0:NB], pp.t[:, C_KA + ct:C_KA + ct + 1], omka.t[:, ct:ct + 1],
                             ALU.mult, ALU.add, [ad.b, pp.b, omka.b], [tmp.b])
                        C.tt("pool", kd.t[:, 0:NB], sx.t[:, 3 + ct, 0:NB], tmp.t[:, 0:NB], ALU.mult, [sx.b, tmp.b], [kd.b])
                        C.stt("dve", al.t[:, 0:NB], ad.t[:, 0:NB], -1.0, kk.t[:, 0:NB], ALU.mult, ALU.mult, [ad.b, kk.b], [al.b])
                        store(3 + d, ct, kd); store(5 + d, ct, al)
                        yield
                        kds.append(kd)
                        cI = wk.get(); cX = wk.get()
                        for ch in range(NB // 128):
                            sl_ = slice(ch * 128, (ch + 1) * 128)
                            if d == 0:
                                S.op("dve", (lambda o, a, b: (lambda e: e.tensor_tensor_scan(out=o, data0=a, data1=b, initial=0.0,
                                     op0=ALU.mult, op1=ALU.add)))(cI.t[:, sl_], onesf, lw.t[:, sl_]), [lw.b, P.misc.b], [cI.b])
                                C.tt("pool", cX.t[:, sl_], cI.t[:, sl_], lw.t[:, sl_], ALU.subtract, [cI.b, lw.b], [cX.b])
                            else:
                                S.op("dve", (lambda o, a, b: (lambda e: e.tensor_tensor_scan(out=o, data0=a, data1=b, initial=0.0,
                                     op0=ALU.mult, op1=ALU.add)))(cI.t[:, sl_], onesf, lw.t[:, sl_]), [lw.b, P.misc.b], [cI.b])
                                C.ts("dve", cX.t[:, sl_], cI.t[:, sl_], cI.t[:, ch * 128 + 127:ch * 128 + 128], -1.0,
                                     ALU.subtract, ALU.mult, [cI.b], [cX.b])
                                C.tt("dve", cI.t[:, sl_], cX.t[:, sl_], lw.t[:, sl_], ALU.add, [cX.b, lw.b], [cI.b])
                        store(7 + 2 * d, ct, cI); store(8 + 2 * d, ct, cX)
                        yield
                    t1 = wk.get(); t3 = wk.get(); bo = wk.get()
                    C.tt("pool", t1.t[:, 0:NB], kds[0].t[:, 0:NB], kds[1].t[:, 0:NB], ALU.add, [kds[0].b, kds[1].b], [t1.b])
                    C.tt("pool", t1.t[:, 0:NB], t1.t[:, 0:NB], sx.t[:, 1 + ct, 0:NB], ALU.mult, [sx.b], [t1.b])
                    C.act(t3.t[:, 0:NB], t1.t[:, 0:NB], AF.Copy, [t1.b, pp.b], [t3.b], scale=pp.t[:, C_RK + ct:C_RK + ct + 1])
                    pb = P.PB[pbs["i"] % 6]; pbs["i"] += 1
                    C.mm(pb.t[:, 0:NB], blkf, t3.t[:, 0:NB], True, True, [P.misc.b, t3.b], [pb.b])
                    C.tt("dve", bo.t[:, 0:NB], pb.t[:, 0:NB], sx.t[:, 5 + ct, 0:NB], ALU.mult, [pb.b, sx.b], [bo.b])
                    store(11, ct, bo)
                    yield
                interleave(ct_body(0), ct_body(1), ts_gen)
        S.barrier()


import os as _os0
INV_BF16 = _os0.environ.get("INV_BF16", "1") == "1"


def rwkv_scan(P, LP, l):
    C, I, Dr, S = P.C, P.I, P.Dr, P.C.S
    sq = Dr["sq"]
    NC = P.TT // 128
    NCc = P.TC // 128
    orders = [list(range(NC)), list(range(NCc - 1, -1, -1)) + list(range(NC - 1, NCc - 1, -1))]
    gmasks = [(P.misc.t[:, 3:5, :]).rearrange("p a b -> p (a b)"), (P.misc.t[:, 5:7, :]).rearrange("p a b -> p (a b)")]
    nmasks = [P.misc.t[:, 5, :], P.misc.t[:, 3, :]]
    last_cols = [127, 0]
    identf = P.misc.t[:, 0, :]
    IDT = BF16 if INV_BF16 else F32
    ident_i = P.identb.t[:] if INV_BF16 else identf
    ident_b = P.identb.b if INV_BF16 else P.misc.b
    QIs = [(0, 1, 2, 3 + d, 5 + d, 7 + 2 * d, 8 + 2 * d) for d in range(2)]
    mk = lambda k: P.misc.t[:, 7 + k, :]
    with ExitStack() as es:
        Hs = [[C.sb(es, [128, 64], F32, "H") for _ in range(2)] for _ in range(2)]
        Hbs = [[C.sb(es, [128, 64], BF16, "Hb") for _ in range(2)] for _ in range(2)]
        for d in range(2):
            for ct in range(2):
                C.memset("pool", Hs[d][ct].t[:], 0.0, [Hs[d][ct].b])
                C.memset("pool", Hbs[d][ct].t[:], 0.0, [Hbs[d][ct].b])
        qinp = C.pool_of(es, 4, [128, 7, 2, 128], F32, "qin")
        ep = C.pool_of(es, 24, [128, 128], F32, "e")
        BRp = C.pool_of(es, 12, [128, 256], BF16, "BR")
        b16 = C.pool_of(es, 72, [128, 128], BF16, "b16")
        m256 = C.pool_of(es, 48, [128, 256], BF16, "m256")
        f128 = C.pool_of(es, 192 if INV_BF16 else 110, [128, 128], IDT, "f128")
        z64 = C.pool_of(es, 32, [128, 64], BF16, "z64")
        z64f = C.pool_of(es, 24, [128, 64], IDT, "z64f")
        ychp = C.pool_of(es, 4, [128, 256], F32, "ych")
        st = {"pbi": 0, "pti": 0}

        def bank():
            pb_ = P.PB[st["pbi"] % 6]; st["pbi"] += 1
            return pb_

        def tbank():
            if INV_BF16:
                pb_ = P.PT[st["pti"] % 2]; st["pti"] += 1
                return pb_
            return bank()

        def mmf(lhsT, rhs):
            pb_ = bank()
            C.mm(pb_.t[:, 0:128], lhsT.t[:], rhs.t[:], True, True, [lhsT.b, rhs.b], [pb_.b])
            return pb_

        def evc(pb_):
            o_ = f128.get()
            C.cp(C.alt(("act", "act", "dve")), o_.t[:], pb_.t[:, 0:128], [pb_.b], [o_.b])
            return o_

        def evadd(pb_, addt):
            o_ = f128.get()
            C.tt("dve", o_.t[:], pb_.t[:, 0:128], addt.t[:], ALU.add, [pb_.b, addt.b], [o_.b])
            return o_

        def masked(src, k):
            o_ = f128.get()
            C.tt("pool", o_.t[:], src.t[:], mk(k), ALU.mult, [src.b, P.misc.b], [o_.b])
            return o_

        def xpose(X):
            pb_ = tbank()
            C.tr(pb_.t[:, 0:128], X.t[:], ident_i, [X.b, ident_b], [pb_.b])
            return evc(pb_)

        for it_ in range(NC):
          heads = []
          ychs = []
          toksl = []
          for d in range(2):
            ci = orders[d][it_]
            gmask, nmask, last_col = gmasks[d], nmasks[d], last_cols[d]
            H, Hb = Hs[d], Hbs[d]
            tok = slice(ci * 128, (ci + 1) * 128)
            toksl.append(tok)
            qin = qinp.get()
            for k_, qi in enumerate(QIs[d]):
                C.dma(qin.t[:, k_, :, :], sq[qi, :, tok].rearrange("(ct p) t -> p ct t", p=128), [], [qin.b])
            ych = ychp.get()
            ychs.append(ych)
            cts = []
            for ct in range(2):
                q = (lambda ct_: (lambda k_: qin.t[:, k_, ct_, :]))(ct)
                eX = ep.get(); eI = ep.get(); eN = ep.get()
                C.act(eX.t[:], q(6), AF.Exp, [qin.b], [eX.b])
                C.act(eI.t[:], q(5), AF.Exp, [qin.b], [eI.b])
                C.act(eN.t[:], q(5), AF.Exp, [qin.b], [eN.b], scale=-1.0)
                BR = BRp.get(); AT = b16.get(); KT = b16.get(); AH = b16.get(); KH = b16.get(); vb = b16.get()
                C.tt("dve", BR.t[:, 0:128], q(0), eX.t[:], ALU.mult, [qin.b, eX.b], [BR.b])
                C.tt("pool", BR.t[:, 128:256], q(1), eI.t[:], ALU.mult, [qin.b, eI.b], [BR.b])
                C.tt("dve", AT.t[:], q(4), eN.t[:], ALU.mult, [qin.b, eN.b], [AT.b])
                C.tt("pool", KT.t[:], q(3), eN.t[:], ALU.mult, [qin.b, eN.b], [KT.b])
                pc = eI.t[:, last_col:last_col + 1]
                C.act(AH.t[:], AT.t[:], AF.Copy, [AT.b, eI.b], [AH.b], scale=pc)
                C.act(KH.t[:], KT.t[:], AF.Copy, [KT.b, eI.b], [KH.b], scale=pc)
                C.cp("pool", vb.t[:], q(2), [qin.b], [vb.b])
                pt = P.PT[st["pti"] % 2]; st["pti"] += 1
                for k_, src in enumerate((AH, KH, vb)):
                    C.tr(pt.t[:, k_ * 128:(k_ + 1) * 128], src.t[:], P.identb.t[:], [src.b, P.identb.b], [pt.b])
                toks = []
                for k_ in range(3):
                    tk = b16.get()
                    C.cp("act" if ct == 0 else "dve", tk.t[:], pt.t[:, k_ * 128:(k_ + 1) * 128], [pt.b], [tk.b])
                    toks.append(tk)
                cts.append(dict(BR=BR, AT=AT, KT=KT, eI=eI, AHt=toks[0], KHt=toks[1], Vt=toks[2]))
            for ct in range(2):
                for hh in range(2):
                    hd = dict(cts[ct]); hd["ct"] = ct; hd["ps"] = slice(64 * hh, 64 * hh + 64); hd["h4"] = ct * 2 + hh
                    hd.update(gmask=gmask, nmask=nmask, last_col=last_col, H=H, Hb=Hb, ych=ych)
                    heads.append(hd)
          if True:
            for hd in heads:
                ps, BR, AT, KT = hd["ps"], hd["BR"], hd["AT"], hd["KT"]
                gmask, nmask = hd["gmask"], hd["nmask"]
                pg1 = bank()
                C.mm(pg1.t[:, 0:256], KT.t[ps, :], BR.t[ps, :], True, True, [KT.b, BR.b], [pg1.b])
                M1 = m256.get()
                C.tt("dve", M1.t[:], pg1.t[:, 0:256], gmask, ALU.mult, [pg1.b, P.misc.b], [M1.b])
                pg2 = bank()
                C.mm(pg2.t[:, 0:256], AT.t[ps, :], BR.t[ps, :], True, True, [AT.b, BR.b], [pg2.b])
                M2 = m256.get()
                C.tt("dve", M2.t[:], pg2.t[:, 0:256], gmask, ALU.mult, [pg2.b, P.misc.b], [M2.b])
                Ntf = f128.get()
                C.tt("dve", Ntf.t[:], pg2.t[:, 0:128], gmask[:, 0:128], ALU.mult, [pg2.b, P.misc.b], [Ntf.b])
                pg3 = bank()
                C.mm(pg3.t[:, 0:128], BR.t[ps, 0:128], AT.t[ps, :], True, True, [AT.b, BR.b], [pg3.b])
                Nf = f128.get()
                C.tt("dve", Nf.t[:], pg3.t[:, 0:128], nmask, ALU.mult, [pg3.b, P.misc.b], [Nf.b])
                hd.update(M1=M1, M2=M2, Nf=Nf, Ntf=Ntf)
            for hd in heads:
                hd["Nd"] = masked(hd["Nf"], 0); hd["Ndt"] = masked(hd["Ntf"], 0)
            for hd in heads:
                hd["p1"] = mmf(hd["Ndt"], hd["Nd"]); hd["p2"] = mmf(hd["Nd"], hd["Ndt"])
                hd["Nd2"] = evc(hd["p1"]); hd["Ndt2"] = evc(hd["p2"])
            for hd in heads:
                hd["p2"] = mmf(hd["Nd2"], hd["Ndt2"])
                hd["Ndt4"] = evc(hd["p2"])
                P1 = f128.get()
                C.tt("pool", P1.t[:], hd["Nd"].t[:], ident_i, ALU.add, [hd["Nd"].b, ident_b], [P1.b])
                hd["P1"] = P1
            for hd in heads:
                hd["P2"] = evadd(mmf(hd["Ndt2"], hd["P1"]), hd["P1"])
            for hd in heads:
                hd["X"] = evadd(mmf(hd["Ndt4"], hd["P2"]), hd["P2"])
            for k in (1, 2, 3):
                for hd in heads:
                    hd["Xt"] = xpose(hd["X"])
                    hd["Noff"] = masked(hd["Ntf"], k)
                for hd in heads:
                    hd["W"] = evc(mmf(hd["Noff"], hd["X"]))
                for hd in heads:
                    hd["X"] = evadd(mmf(hd["Xt"], hd["W"]), hd["X"])
            for hd in heads:
                hd["Xt"] = xpose(hd["X"])
                hd["Noff"] = masked(hd["Nf"], 4)
            for hd in heads:
                hd["W"] = evc(mmf(hd["Noff"], hd["Xt"]))
            for hd in heads:
                hd["Tt"] = evadd(mmf(hd["X"], hd["W"]), hd["Xt"])
            for hd in heads:
                ps, ct = hd["ps"], hd["ct"]
                Hb = hd["Hb"]
                vh = hd["Vt"].t[:, ps]
                pz = bank()
                C.mm(pz.t[:, 0:64], hd["BR"].t[ps, 0:128], Hb[ct].t[ps, :], True, False, [hd["BR"].b, Hb[ct].b], [pz.b])
                C.mm(pz.t[:, 0:64], hd["M1"].t[:, 0:128], vh, False, True, [hd["M1"].b, hd["Vt"].b], [pz.b])
                Zs = z64f.get()
                C.cp("act", Zs.t[:], pz.t[:, 0:64], [pz.b], [Zs.b])
                hd["Zs"] = Zs
            for hd in heads:
                pu = bank()
                C.mm(pu.t[:, 0:64], hd["Tt"].t[:], hd["Zs"].t[:], True, True, [hd["Tt"].b, hd["Zs"].b], [pu.b])
                Us = z64.get()
                C.cp("dve", Us.t[:], pu.t[:, 0:64], [pu.b], [Us.b])
                hd["Us"] = Us
            for hd in heads:
                ps, ct, h4 = hd["ps"], hd["ct"], hd["h4"]
                H, Hb, ych, last_col = hd["H"], hd["Hb"], hd["ych"], hd["last_col"]
                vh = hd["Vt"].t[:, ps]
                py = bank()
                C.mm(py.t[:, 0:64], hd["BR"].t[ps, 128:256], Hb[ct].t[ps, :], True, False, [hd["BR"].b, Hb[ct].b], [py.b])
                C.mm(py.t[:, 0:64], hd["M2"].t[:, 128:256], hd["Us"].t[:], False, False, [hd["M2"].b, hd["Us"].b], [py.b])
                C.mm(py.t[:, 0:64], hd["M1"].t[:, 128:256], vh, False, True, [hd["M1"].b, hd["Vt"].b], [py.b])
                C.cp("act", ych.t[:, h4 * 64:(h4 + 1) * 64], py.t[:, 0:64], [py.b], [ych.b])
                ph = bank()
                C.mm(ph.t[:, 0:64], hd["AHt"].t[:], hd["Us"].t[:], True, False, [hd["AHt"].b, hd["Us"].b], [ph.b])
                C.mm(ph.t[:, 0:64], hd["KHt"].t[:], vh, False, True, [hd["KHt"].b, hd["Vt"].b], [ph.b])
                C.stt("dve", H[ct].t[ps, :], H[ct].t[ps, :], hd["eI"].t[ps, last_col:last_col + 1], ph.t[ps, 0:64],
                      ALU.mult, ALU.add, [hd["eI"].b, ph.b], [H[ct].b])
            for d in range(2):
                for ct in range(2):
                    C.cp("act" if d == 0 else "dve", Hbs[d][ct].t[:], Hs[d][ct].t[:], [Hs[d][ct].b], [Hbs[d][ct].b])
                C.dma(Dr["yd"][d, toksl[d], :], ychs[d].t[:], [ychs[d].b], [])
        S.barrier()


def rwkv_scan_q(P, LP, l):
    C, I, Dr, S = P.C, P.I, P.Dr, P.C.S
    sq = Dr["sq"]
    NC = P.TT // 128
    NCc = P.TC // 128
    orders = [list(range(NC)), list(range(NCc - 1, -1, -1)) + list(range(NC - 1, NCc - 1, -1))]
    last_cols = [127, 0]
    QIs = [(0, 1, 2, 3 + d, 5 + d, 7 + 2 * d, 8 + 2 * d) for d in range(2)]
    with ExitStack() as es:
        gm2 = [C.sb(es, [128, 2, 256], F32, "gm2") for _ in range(2)]
        nm4 = [C.sb(es, [128, 4, 128], F32, "nm4") for _ in range(2)]
        mk4 = [C.sb(es, [128, 4, 128], F32, "mk4") for _ in range(5)]
        id4 = C.sb(es, [128, 4, 128], BF16, "id4")
        for d in range(2):
            src = P.misc.t[:, 3:5, :] if d == 0 else P.misc.t[:, 5:7, :]
            for r in range(2):
                C.cp("pool", gm2[d].t[:, r, :].rearrange("p (a b) -> p a b", a=2), src, [P.misc.b], [gm2[d].b])
            for r in range(4):
                C.cp("pool", nm4[d].t[:, r, :], P.misc.t[:, 5 if d == 0 else 3, :], [P.misc.b], [nm4[d].b])
        for k in range(5):
            for r in range(4):
                C.cp("pool", mk4[k].t[:, r, :], P.misc.t[:, 7 + k, :], [P.misc.b], [mk4[k].b])
        for r in range(4):
            C.cp("pool", id4.t[:, r, :], P.misc.t[:, 0, :], [P.misc.b], [id4.b])
        Hs = [[C.sb(es, [128, 64], F32, "H") for _ in range(2)] for _ in range(2)]
        Hbs = [[C.sb(es, [128, 64], BF16, "Hb") for _ in range(2)] for _ in range(2)]
        for d in range(2):
            for ct in range(2):
                C.memset("pool", Hs[d][ct].t[:], 0.0, [Hs[d][ct].b])
                C.memset("pool", Hbs[d][ct].t[:], 0.0, [Hbs[d][ct].b])
        qinp = C.pool_of(es, 4, [128, 7, 2, 128], F32, "qin")
        eIp = C.pool_of(es, 12, [128, 128], F32, "eI")
        ep = C.pool_of(es, 8, [128, 128], F32, "e")
        BRp = C.pool_of(es, 12, [128, 256], BF16, "BR")
        b16 = C.pool_of(es, 56, [128, 128], BF16, "b16")
        m4 = C.pool_of(es, 12, [128, 4, 256], BF16, "m4")
        q4l = C.pool_of(es, 24, [128, 4, 128], BF16, "q4l")
        q4 = C.pool_of(es, 40, [128, 4, 128], BF16, "q4")
        z4 = C.pool_of(es, 12, [128, 4, 64], BF16, "z4")
        ychp = C.pool_of(es, 4, [128, 256], F32, "ych")
        st = {"pbi": 0, "pti": 0}

        def bank():
            pb_ = P.PB[st["pbi"] % 6]; st["pbi"] += 1
            return pb_

        def tbank():
            pb_ = P.PT[st["pti"] % 2]; st["pti"] += 1
            return pb_

        def mm4(A, B, bsl=None):
            pb_ = bank()
            for h in range(4):
                C.mm(pb_.t[:, h * 128:(h + 1) * 128], A.t[:, h, :] if bsl is None else A.t[:, h, bsl], B.t[:, h, :],
                     True, True, [A.b, B.b], [pb_.b])
            return pb_

        def ev4(pb_, eng):
            o_ = q4.get()
            C.cp(C.alt(("act", "act", "dve")), o_.t[:].rearrange("p a b -> p (a b)"), pb_.t[:, 0:512], [pb_.b], [o_.b])
            return o_

        def evadd4(pb_, addt):
            o_ = q4.get()
            C.tt("dve", o_.t[:].rearrange("p a b -> p (a b)"), pb_.t[:, 0:512], addt.t[:].rearrange("p a b -> p (a b)"),
                 ALU.add, [pb_.b, addt.b], [o_.b])
            return o_

        def masked4(src_ap, src_b, k):
            o_ = q4.get()
            C.tt("pool", o_.t[:], src_ap, mk4[k].t[:], ALU.mult, [src_b, mk4[k].b], [o_.b])
            return o_

        def xpose4(X, eng):
            pb_ = tbank()
            for h in range(4):
                C.tr(pb_.t[:, h * 128:(h + 1) * 128], X.t[:, h, :], P.identb.t[:], [X.b, P.identb.b], [pb_.b])
            o_ = q4.get()
            C.cp(eng, o_.t[:].rearrange("p a b -> p (a b)"), pb_.t[:, 0:512], [pb_.b], [o_.b])
            return o_

        import os as _os
        k2 = _os.environ.get("KSTOP2", "")
        if k2 == "q0":
            S.barrier(); return
        pref = {}

        def load_qin(it_, d):
            tok_ = slice(orders[d][it_] * 128, (orders[d][it_] + 1) * 128)
            qin_ = qinp.get()
            for k_, qi in enumerate(QIs[d]):
                C.dma(qin_.t[:, k_, :, :], sq[qi, :, tok_].rearrange("(ct p) t -> p ct t", p=128), [], [qin_.b])
            return qin_

        for it0 in range(0, NC, 2):
          its = [i_ for i_ in (it0, it0 + 1) if i_ < NC]
          quads = []
          for it_ in its:
            for d in range(2):
                ci = orders[d][it_]
                last_col = last_cols[d]
                tok = slice(ci * 128, (ci + 1) * 128)
                qin = pref.pop((it_, d)) if (it_, d) in pref else load_qin(it_, d)
                cts = []
                for ct in range(2):
                    q = (lambda ct_, qin_: (lambda k_: qin_.t[:, k_, ct_, :]))(ct, qin)
                    eX = ep.get(); eI = eIp.get(); eN = ep.get()
                    C.act(eX.t[:], q(6), AF.Exp, [qin.b], [eX.b])
                    C.act(eI.t[:], q(5), AF.Exp, [qin.b], [eI.b])
                    C.act(eN.t[:], q(5), AF.Exp, [qin.b], [eN.b], scale=-1.0)
                    BR = BRp.get(); AT = b16.get(); KT = b16.get(); AH = b16.get(); KH = b16.get(); vb = b16.get()
                    C.tt("dve", BR.t[:, 0:128], q(0), eX.t[:], ALU.mult, [qin.b, eX.b], [BR.b])
                    C.tt("pool", BR.t[:, 128:256], q(1), eI.t[:], ALU.mult, [qin.b, eI.b], [BR.b])
                    C.tt("dve", AT.t[:], q(4), eN.t[:], ALU.mult, [qin.b, eN.b], [AT.b])
                    C.tt("pool", KT.t[:], q(3), eN.t[:], ALU.mult, [qin.b, eN.b], [KT.b])
                    pc = eI.t[:, last_col:last_col + 1]
                    C.act(AH.t[:], AT.t[:], AF.Copy, [AT.b, eI.b], [AH.b], scale=pc)
                    C.act(KH.t[:], KT.t[:], AF.Copy, [KT.b, eI.b], [KH.b], scale=pc)
                    C.cp("pool", vb.t[:], q(2), [qin.b], [vb.b])
                    pt = tbank()
                    for k_, src in enumerate((AH, KH, vb)):
                        C.tr(pt.t[:, k_ * 128:(k_ + 1) * 128], src.t[:], P.identb.t[:], [src.b, P.identb.b], [pt.b])
                    tk = q4l.get()
                    C.cp("act" if d == 0 else "dve", tk.t[:, 0:3, :].rearrange("p a b -> p (a b)"), pt.t[:, 0:384], [pt.b], [tk.b])
                    cts.append(dict(BR=BR, AT=AT, KT=KT, eI=eI, tk=tk))
                quads.append(dict(d=d, it=it_, cts=cts, tok=tok, last_col=last_col, ev="act" if d == 0 else "dve"))
          if True:
            if k2 == "q1":
                S.barrier(); return
            for Q in quads:
                d = Q["d"]
                M1 = m4.get(); M2 = m4.get()
                for (lk, Mq) in (("KT", M1), ("AT", M2)):
                    for hh in range(2):
                        ps = slice(64 * hh, 64 * hh + 64)
                        pg = bank()
                        for ct in range(2):
                            cd = Q["cts"][ct]
                            C.mm(pg.t[:, ct * 256:(ct + 1) * 256], cd[lk].t[ps, :], cd["BR"].t[ps, :], True, True,
                                 [cd[lk].b, cd["BR"].b], [pg.b])
                        C.tt("dve", Mq.t[:, 2 * hh:2 * hh + 2, :], pg.t[:, 0:512].rearrange("p (a b) -> p a b", a=2), gm2[d].t[:],
                             ALU.mult, [pg.b, gm2[d].b], [Mq.b])
                Nq = q4l.get()
                for hh in range(2):
                    ps = slice(64 * hh, 64 * hh + 64)
                    pg = bank()
                    for ct in range(2):
                        cd = Q["cts"][ct]
                        C.mm(pg.t[:, ct * 128:(ct + 1) * 128], cd["BR"].t[ps, 0:128], cd["AT"].t[ps, :], True, True,
                             [cd["AT"].b, cd["BR"].b], [pg.b])
                    C.tt("dve", Nq.t[:, 2 * hh:2 * hh + 2, :], pg.t[:, 0:256].rearrange("p (a b) -> p a b", a=2), nm4[d].t[:, 0:2, :],
                         ALU.mult, [pg.b, nm4[d].b], [Nq.b])
                Q.update(M1=M1, M2=M2, Nq=Nq)
            if k2 == "q2":
                S.barrier(); return
            for Q in quads:
                Q["Nd"] = masked4(Q["Nq"].t[:], Q["Nq"].b, 0)
                Q["Ndt"] = masked4(Q["M2"].t[:, :, 0:128], Q["M2"].b, 0)
            for Q in quads:
                Q["Nd2"] = ev4(mm4(Q["Ndt"], Q["Nd"]), Q["ev"])
                Q["Ndt2"] = ev4(mm4(Q["Nd"], Q["Ndt"]), "act" if Q["ev"] == "dve" else "dve")
            for Q in quads:
                Q["Ndt4"] = ev4(mm4(Q["Nd2"], Q["Ndt2"]), Q["ev"])
                P1 = q4.get()
                C.tt("pool", P1.t[:], Q["Nd"].t[:], id4.t[:], ALU.add, [Q["Nd"].b, id4.b], [P1.b])
                Q["P1"] = P1
            for Q in quads:
                Q["P2"] = evadd4(mm4(Q["Ndt2"], Q["P1"]), Q["P1"])
            for Q in quads:
                Q["X"] = evadd4(mm4(Q["Ndt4"], Q["P2"]), Q["P2"])
            if k2 == "q3":
                S.barrier(); return
            for k in (1, 2, 3):
                for Q in quads:
                    Q["Xt"] = xpose4(Q["X"], Q["ev"])
                    Q["Noff"] = masked4(Q["M2"].t[:, :, 0:128], Q["M2"].b, k)
                for Q in quads:
                    Q["W"] = ev4(mm4(Q["Noff"], Q["X"]), Q["ev"])
                for Q in quads:
                    Q["X"] = evadd4(mm4(Q["Xt"], Q["W"]), Q["X"])
            for Q in quads:
                Q["Xt"] = xpose4(Q["X"], Q["ev"])
                Q["Noff"] = masked4(Q["Nq"].t[:], Q["Nq"].b, 4)
            for Q in quads:
                Q["W"] = ev4(mm4(Q["Noff"], Q["Xt"]), Q["ev"])
            for Q in quads:
                pb_ = mm4(Q["X"], Q["W"])
                Tt = q4l.get()
                C.tt("dve", Tt.t[:].rearrange("p a b -> p (a b)"), pb_.t[:, 0:512], Q["Xt"].t[:].rearrange("p a b -> p (a b)"),
                     ALU.add, [pb_.b, Q["Xt"].b], [Tt.b])
                Q["Tt"] = Tt
            if k2 == "q4":
                S.barrier(); return
            for itn in (it0 + 2, it0 + 3):
                if itn < NC:
                    for d in range(2):
                        pref[(itn, d)] = load_qin(itn, d)
            allquads = quads
          for it_ in its:
            quads = [Q for Q in allquads if Q["it"] == it_]
            for Q in quads:
                d = Q["d"]
                Zs = z4.get()
                for hh in range(2):
                    ps = slice(64 * hh, 64 * hh + 64)
                    pz = bank()
                    for ct in range(2):
                        s_ = hh * 2 + ct
                        cd = Q["cts"][ct]
                        o = pz.t[:, ct * 64:(ct + 1) * 64]
                        C.mm(o, cd["BR"].t[ps, 0:128], Hbs[d][ct].t[ps, :], True, False, [cd["BR"].b, Hbs[d][ct].b], [pz.b])
                        C.mm(o, Q["M1"].t[:, s_, 0:128], cd["tk"].t[:, 2, ps], False, True, [Q["M1"].b, cd["tk"].b], [pz.b])
                    C.cp(Q["ev"], Zs.t[:, 2 * hh:2 * hh + 2, :].rearrange("p a b -> p (a b)"), pz.t[:, 0:128], [pz.b], [Zs.b])
                Q["Zs"] = Zs
            for Q in quads:
                pu = bank()
                for s_ in range(4):
                    C.mm(pu.t[:, s_ * 64:(s_ + 1) * 64], Q["Tt"].t[:, s_, :], Q["Zs"].t[:, s_, :], True, True,
                         [Q["Tt"].b, Q["Zs"].b], [pu.b])
                Us = z4.get()
                C.cp(Q["ev"], Us.t[:].rearrange("p a b -> p (a b)"), pu.t[:, 0:256], [pu.b], [Us.b])
                Q["Us"] = Us
            for Q in quads:
                d = Q["d"]
                ych = ychp.get()
                yv = ych.t[:].rearrange("p (ct hh v) -> p hh ct v", ct=2, hh=2)
                for hh in range(2):
                    ps = slice(64 * hh, 64 * hh + 64)
                    py = bank()
                    for ct in range(2):
                        s_ = hh * 2 + ct
                        cd = Q["cts"][ct]
                        o = py.t[:, ct * 64:(ct + 1) * 64]
                        C.mm(o, cd["BR"].t[ps, 128:256], Hbs[d][ct].t[ps, :], True, False, [cd["BR"].b, Hbs[d][ct].b], [py.b])
                        C.mm(o, Q["M2"].t[:, s_, 128:256], Q["Us"].t[:, s_, :], False, False, [Q["M2"].b, Q["Us"].b], [py.b])
                        C.mm(o, Q["M1"].t[:, s_, 128:256], cd["tk"].t[:, 2, ps], False, True, [Q["M1"].b, cd["tk"].b], [py.b])
                    C.cp("act", yv[:, hh], py.t[:, 0:128].rearrange("p (a b) -> p a b", a=2), [py.b], [ych.b])
                C.dma(Dr["yd"][d, Q["tok"], :], ych.t[:], [ych.b], [])
                ph = bank()
                for s_ in range(4):
                    hh, ct = s_ // 2, s_ % 2
                    cd = Q["cts"][ct]; ps = slice(64 * hh, 64 * hh + 64)
                    o = ph.t[:, s_ * 64:(s_ + 1) * 64]
                    C.mm(o, cd["tk"].t[:, 0, :], Q["Us"].t[:, s_, :], True, False, [cd["tk"].b, Q["Us"].b], [ph.b])
                    C.mm(o, cd["tk"].t[:, 1, :], cd["tk"].t[:, 2, ps], False, True, [cd["tk"].b], [ph.b])
                for s_ in range(4):
                    hh, ct = s_ // 2, s_ % 2
                    cd = Q["cts"][ct]; ps = slice(64 * hh, 64 * hh + 64)
                    Hh = Hs[d][ct]
                    C.stt("dve", Hh.t[ps, :], Hh.t[ps, :], cd["eI"].t[ps, Q["last_col"]:Q["last_col"] + 1],
                          ph.t[ps, s_ * 64:(s_ + 1) * 64], ALU.mult, ALU.add, [cd["eI"].b, ph.b], [Hh.b])
                for ct in range(2):
                    C.cp("act" if d == 0 else "dve", Hbs[d][ct].t[:], Hs[d][ct].t[:], [Hs[d][ct].b], [Hbs[d][ct].b])
        S.barrier()


def rwkv_readout(P, LP, l):
    C, I, Dr, S = P.C, P.I, P.Dr, P.C.S
    pp = LP.pp1
    sq = Dr["sq"]
    NC = P.TT // 128
    GR = 4
    with ExitStack() as es:
        yp = C.pool_of(es, 4 * GR, [128, 256], F32, "yr")
        sqv = C.pool_of(es, 2 * GR, [128, 256], F32, "ysq")
        st = C.pool_of(es, 8 * GR, [128, 4], F32, "st")
        ynp = C.pool_of(es, 2 * GR, [128, 256], BF16, "yn")
        bgp = C.pool_of(es, 2 * GR, [128, 2, 2, 128], F32, "bg")
        op_ = C.pool_of(es, 4 * GR, [128, 128], F32, "o")
        obp = C.pool_of(es, 4 * GR, [128, 128], BF16, "ob")
        red = lambda o, i_: (lambda e: e.tensor_reduce(out=o, in_=i_, axis=AX.X, op=ALU.add))
        def load_group(c0_):
            G_ = []
            for ci in range(c0_, min(NC, c0_ + GR)):
                tok = slice(ci * 128, (ci + 1) * 128)
                g = dict(tok=tok, y0=yp.get(), y1=yp.get(), bg=bgp.get())
                C.dma(g["y0"].t[:], Dr["yd"][0, tok, :], [], [g["y0"].b])
                C.dma(g["y1"].t[:], Dr["yd"][1, tok, :], [], [g["y1"].b])
                for k_, qi in enumerate((11, 12)):
                    C.dma(g["bg"].t[:, k_, :, :], sq[qi, :, tok].rearrange("(ct p) t -> p ct t", p=128), [], [g["bg"].b])
                G_.append(g)
            return G_

        nxtG = load_group(0)
        for c0_ in range(0, NC, GR):
            G = nxtG
            nxtG = load_group(c0_ + GR) if c0_ + GR < NC else None
            for g in G:
                C.tt("dve", g["y0"].t[:], g["y0"].t[:], g["y1"].t[:], ALU.add, [g["y1"].b], [g["y0"].b])
            for g in G:
                g["ysq"] = sqv.get()
                C.tt("pool", g["ysq"].t[:], g["y0"].t[:], g["y0"].t[:], ALU.mult, [g["y0"].b], [g["ysq"].b])
                g["s1"] = st.get(); g["s2"] = st.get(); g["mu"] = st.get(); g["var"] = st.get()
                S.op("dve", red(g["s1"].t[:], g["y0"].t[:].rearrange("p (h j) -> p h j", j=64)), [g["y0"].b], [g["s1"].b])
            for g in G:
                S.op("dve", red(g["s2"].t[:], g["ysq"].t[:].rearrange("p (h j) -> p h j", j=64)), [g["ysq"].b], [g["s2"].b])
                C.ts("dve", g["mu"].t[:], g["s1"].t[:], 1.0 / 64, None, ALU.mult, None, [g["s1"].b], [g["mu"].b])
            for g in G:
                C.tt("dve", g["var"].t[:], g["mu"].t[:], g["mu"].t[:], ALU.mult, [g["mu"].b], [g["var"].b])
            for g in G:
                C.stt("dve", g["var"].t[:], g["s2"].t[:], 1.0 / 64, g["var"].t[:], ALU.mult, ALU.subtract, [g["s2"].b], [g["var"].b])
            for g in G:
                C.act(g["var"].t[:], g["var"].t[:], AF.Sqrt, [], [g["var"].b], bias=64e-5)
            for g in G:
                C.recip(g["var"].t[:], g["var"].t[:], [], [g["var"].b])
            for g in G:
                g["yn"] = ynp.get()
                for h4 in range(4):
                    hs = slice(h4 * 64, (h4 + 1) * 64)
                    C.ts("dve", g["yn"].t[:, hs], g["y0"].t[:, hs], g["mu"].t[:, h4:h4 + 1], g["var"].t[:, h4:h4 + 1],
                         ALU.subtract, ALU.mult, [g["y0"].b, g["mu"].b, g["var"].b], [g["yn"].b])
            for gi, g in enumerate(G):
                pt = P.PT[gi % 2]
                for ct in range(2):
                    C.tr(pt.t[:, ct * 128:(ct + 1) * 128], g["yn"].t[:, ct * 128:(ct + 1) * 128], P.identb.t[:],
                         [g["yn"].b, P.identb.b], [pt.b])
                g["o"] = []
                for ct in range(2):
                    o = op_.get()
                    C.act(o.t[:], pt.t[:, ct * 128:(ct + 1) * 128], AF.Identity, [pt.b, pp.b], [o.b],
                          scale=pp.t[:, C_GNW + ct:C_GNW + ct + 1], bias=pp.t[:, C_GNB + ct:C_GNB + ct + 1])
                    g["o"].append(o)
            for g in G:
                for ct in range(2):
                    o = g["o"][ct]; ob = obp.get()
                    C.tt("dve", o.t[:], o.t[:], g["bg"].t[:, 0, ct, :], ALU.add, [g["bg"].b], [o.b])
                    C.tt("pool", ob.t[:], o.t[:], g["bg"].t[:, 1, ct, :], ALU.mult, [o.b, g["bg"].b], [ob.b])
                    C.dma(Dr["yT"][ct * 128:(ct + 1) * 128, g["tok"]], ob.t[:], [ob.b], [])
        S.barrier()


def phase_pool(P, LP, l):
    C, I, Dr = P.C, P.I, P.Dr
    with ExitStack() as es:
        pm = C.sb(es, [128, 20, 128], BF16, "poolm")
        C.dma(pm.t[:], I["poolm"].rearrange("g v s t -> s (g v) t"), [], [pm.b])
        pwf = C.sb(es, [128, 2, 128], F32, "pwf")
        C.memset("pool", pwf.t[:], 0.0, [pwf.b])
        for g in range(4):
            ct, gl = g // 2, g % 2
            C.dma(pwf.t[gl * 64:(gl + 1) * 64, ct, gl * 64:(gl + 1) * 64], I["pool_w"][l, g], [], [pwf.b])
        pwb = C.sb(es, [128, 2, 128], BF16, "pwb")
        C.cp("dve", pwb.t[:], pwf.t[:], [pwf.b], [pwb.b])
        uT = [C.sb(es, [128, P.T], BF16, "uTp") for _ in range(2)]
        z = C.sb(es, [128, P.T // 128, 256], BF16, "z")
        yop = C.pool_of(es, 3, [128, 512], BF16, "yo")
        pbi = 0
        for (sn, Ts, c0, m) in streams(P):
            nT = Ts // 128
            for ct in range(2):
                C.dma(uT[ct].t[:, 0:Ts], Dr["uM"][ct * 128:(ct + 1) * 128, c0:c0 + Ts], [], [uT[ct].b])
            for ti in range(nT):
                pb = P.PB[pbi % 6]; pbi += 1
                for ct in range(2):
                    C.mm(pb.t[:, ct * 128:(ct + 1) * 128], uT[ct].t[:, ti * 128:(ti + 1) * 128], pwb.t[:, ct, :],
                         True, True, [uT[ct].b, pwb.b], [pb.b])
                C.evac(z.t[:, ti, :], pb.t[:, 0:256], [pb.b], [Acc(z.b)])
            NB = min(512, Ts)
            for ct in range(2):
                for blk in range(Ts // NB):
                    yo = yop.get()
                    for tj in range(NB // 128):
                        ti = blk * (NB // 128) + tj
                        for gl in range(2):
                            g = 2 * ct + gl
                            pb = P.PB[pbi % 6]; pbi += 1
                            sis = [si for si in (ti - 1, ti, ti + 1) if 0 <= si < nT]
                            for n_, si in enumerate(sis):
                                if si == ti - 1:
                                    v = 0
                                elif si == ti + 1:
                                    v = 2
                                else:
                                    v = 3 if ti == 0 else (4 if ti == nT - 1 else 1)
                                C.mm(pb.t[:, 0:128], z.t[:, si, ct * 128:(ct + 1) * 128], pm.t[:, g * 5 + v, :],
                                     n_ == 0, n_ == len(sis) - 1, [z.b, pm.b], [pb.b])
                            ps = slice(gl * 64, (gl + 1) * 64)
                            C.act(yo.t[ps, tj * 128:(tj + 1) * 128], pb.t[ps, 0:128], AF.Copy, [pb.b, LP.pp1.b], [yo.b],
                                  scale=LP.pp1.t[ps, C_PSC + ct:C_PSC + ct + 1])
                    C.dma(Dr["yT"][256 + ct * 128:256 + (ct + 1) * 128, c0 + blk * NB:c0 + (blk + 1) * NB],
                          yo.t[:, 0:NB], [yo.b], [])
        P.C.S.barrier()


def phase_fourier(P, LP, l):
    C, I, Dr = P.C, P.I, P.Dr
    with ExitStack() as es:
        cc3 = C.sb(es, [128, 384], BF16, "cc3")
        C.dma(cc3.t[:], I["cc3"], [], [cc3.b])
        stg = C.pool_of(es, 2, [128, 1024], F32, "stg")
        fwb = C.sb(es, [128, 2, 256], BF16, "fwb")
        load_cast(C, es, fwb, I["fourier_w"][l], 256, stg)
        Rmax = P.T // 64
        uTp = C.pool_of(es, 2, [128, max(P.T, 2048)], BF16, "uTf")
        Z = C.sb(es, [128, 128, 3, 64], BF16, "Z")
        Pq = C.sb(es, [128, Rmax, 128], BF16, "Pq")
        gm = C.sb(es, [128, Rmax, 64], BF16, "gmf")
        wr = C.sb(es, [128, 2, Rmax], BF16, "wrf")
        fT = [C.sb(es, [128, P.T], BF16, "fT") for _ in range(2)]
        yop = C.pool_of(es, 3, [128, 512], BF16, "yo")
        pbi = 0
        items = [(sn, Ts, c0, ct) for (sn, Ts, c0, m) in streams(P) for ct in range(2)]

        def load_uT(itm):
            sn_, Ts_, c0_, ct_ = itm
            R_ = Ts_ // 64
            Rp_ = max(R_, 32)
            u_ = uTp.get()
            if Rp_ > R_:
                C.memset("pool", u_.t[:, Ts_:Rp_ * 64], 0.0, [u_.b])
            C.dma(u_.t[:, 0:Ts_], Dr["uM"][256 + ct_ * 128:256 + (ct_ + 1) * 128, c0_:c0_ + Ts_], [], [u_.b])
            return u_

        nxt_u = load_uT(items[0])
        ii = 0
        for (sn, Ts, c0, m) in streams(P):
            R = Ts // 64
            C.dma(gm.t[:, 0:R, :], I["gm_x" if sn == "x" else "gm_c"], [], [gm.b])
            C.dma(wr.t[:, :, 0:R], I["wr_x" if sn == "x" else "wr_c"], [], [wr.b])
            for ct in range(2):
                Rp = max(R, 32)
                uT = nxt_u
                ii += 1
                nxt_u = load_uT(items[ii]) if ii < len(items) else None
                uv = uT.t[:, 0:Rp * 64].rearrange("p (t1 t2) -> p t2 t1", t2=64)
                for t2 in range(64):
                    pb = P.PB[pbi % 6]; pbi += 1
                    C.mm(pb.t[0:Rp, 0:384], uv[:, t2, :], cc3.t[:], True, True, [uT.b, cc3.b], [pb.b])
                    C.evac(Z.t[0:Rp, :, :, t2].rearrange("p q r -> p r q"), pb.t[0:Rp, 0:384].rearrange("p (r q) -> p r q", r=3),
                           [pb.b], [Acc(Z.b)])
                QB = min(128, 512 // R)
                for q0 in range(0, 128, QB):
                    pb = P.PB[pbi % 6]; pbi += 1
                    for qi in range(QB):
                        q = q0 + qi
                        o = pb.t[:, qi * R:(qi + 1) * R]
                        C.mm(o, Z.t[0:Rp, q, 0:2, :].rearrange("p r t -> p (r t)"), wr.t[0:Rp, 0, 0:R], True, False, [Z.b, wr.b], [pb.b])
                        C.mm(o, Z.t[0:Rp, q, 1:3, :].rearrange("p r t -> p (r t)"), wr.t[0:Rp, 1, 0:R], False, True, [Z.b, wr.b], [pb.b])
                    C.evac(Pq.t[:, 0:R, q0:q0 + QB], pb.t[:, 0:QB * R].rearrange("p (q k) -> p k q", k=R),
                           [pb.b], [Acc(Pq.b)])
                KB = min(8, R)
                fv = fT[ct].t[:, 0:Ts].rearrange("p (k2 k1) -> p k2 k1", k1=R)
                for k0 in range(0, R, KB):
                    pb = P.PB[pbi % 6]; pbi += 1
                    for ki in range(KB):
                        C.mm(pb.t[:, ki * 64:(ki + 1) * 64], Pq.t[:, k0 + ki, :], gm.t[:, k0 + ki, :], True, True,
                             [Pq.b, gm.b], [pb.b])
                    C.evac(fv[:, :, k0:k0 + KB], pb.t[:, 0:KB * 64].rearrange("p (k1 k2) -> p k2 k1", k2=64),
                           [pb.b], [Acc(fT[ct].b)])
            NB = min(512, Ts)
            for blk in range(Ts // NB):
                for nt in range(2):
                    pb = P.PB[pbi % 6]; pbi += 1
                    for ct in range(2):
                        C.mm(pb.t[:, 0:NB], fwb.t[:, ct, nt * 128:(nt + 1) * 128], fT[ct].t[:, blk * NB:(blk + 1) * NB],
                             ct == 0, ct == 1, [fwb.b, fT[ct].b], [pb.b])
                    yo = yop.get()
                    C.evac(yo.t[:, 0:NB], pb.t[:, 0:NB], [pb.b], [yo.b])
                    C.dma(Dr["yT"][512 + nt * 128:512 + (nt + 1) * 128, c0 + blk * NB:c0 + (blk + 1) * NB],
                          yo.t[:, 0:NB], [yo.b], [])
        P.C.S.barrier()


def phase_conv(P, LP, l):
    C, I, Dr = P.C, P.I, P.Dr
    onesf = P.misc.t[:, 1, :]
    identf = P.misc.t[:, 0, :]
    with ExitStack() as es:
        stg = C.pool_of(es, 2, [128, 1024], F32, "stg")
        pwb = C.sb(es, [128, 2, 256], BF16, "cpw")
        load_cast(C, es, pwb, I["conv_pw"][l], 256, stg)
        dg = C.sb(es, [128, 62, 128], BF16, "dg")
        for jj in range(62):
            C.ts("dve" if jj % 2 else "act", dg.t[:, jj, :], identf, LP.pp2.t[:, jj:jj + 1], None, ALU.mult, None,
                 [P.misc.b, LP.pp2.b], [dg.b]) if jj % 2 else \
                C.act(dg.t[:, jj, :], identf, AF.Copy, [P.misc.b, LP.pp2.b], [dg.b], scale=LP.pp2.t[:, jj:jj + 1])
        hT = [C.sb(es, [128, P.T + 30], BF16, "hT") for _ in range(2)]
        abp = C.pool_of(es, 4, [128, 512], BF16, "ab")
        sgp = C.pool_of(es, 2, [128, 512], F32, "sgc")
        co = [C.pool_of(es, 2, [128, 512], F32, "co") for _ in range(2)]
        sqp = [C.pool_of(es, 2, [128, 512], F32, "sqc") for _ in range(2)]
        mean = C.sb(es, [128, 512], F32, "mean")
        var = C.sb(es, [128, 512], F32, "var")
        dp = C.pool_of(es, 2, [128, 512], F32, "dcv")
        sT = [C.pool_of(es, 2, [128, 512], BF16, "sTc") for _ in range(2)]
        yop = C.pool_of(es, 3, [128, 512], BF16, "yo")
        pbi = 0
        for (sn, Ts, c0, m) in streams(P):
            NB = min(512, Ts)
            for ct in range(2):
                C.memset("pool", hT[ct].t[:, 0:15], 0.0, [hT[ct].b])
                C.memset("pool", hT[ct].t[:, 15 + Ts:30 + Ts], 0.0, [hT[ct].b])
            for blk in range(Ts // NB):
                cs = slice(c0 + blk * NB, c0 + (blk + 1) * NB)
                for ct in range(2):
                    a = abp.get(); b = abp.get()
                    C.dma(a.t[:, 0:NB], Dr["uM"][512 + ct * 128:512 + (ct + 1) * 128, cs], [], [a.b])
                    C.dma(b.t[:, 0:NB], Dr["uM"][768 + ct * 128:768 + (ct + 1) * 128, cs], [], [b.b])
                    sg = sgp.get()
                    C.act(sg.t[:, 0:NB], b.t[:, 0:NB], AF.Sigmoid, [b.b], [sg.b])
                    C.tt("dve", hT[ct].t[:, 15 + blk * NB:15 + (blk + 1) * NB], a.t[:, 0:NB], sg.t[:, 0:NB], ALU.mult,
                         [a.b, sg.b], [hT[ct].b])
            for blk in range(Ts // NB):
                cot, sqt = [], []
                for ct in range(2):
                    pb = P.PB[pbi % 6]; pbi += 1
                    for j in range(31):
                        C.mm(pb.t[:, 0:NB], dg.t[:, j * 2 + ct, :], hT[ct].t[:, blk * NB + j:blk * NB + j + NB],
                             j == 0, j == 30, [dg.b, hT[ct].b], [pb.b])
                    c_ = co[ct].get(); q_ = sqp[ct].get()
                    C.act(c_.t[:, 0:NB], pb.t[:, 0:NB], AF.Identity, [pb.b, LP.pp1.b], [c_.b],
                          bias=LP.pp1.t[:, C_DWB + ct:C_DWB + ct + 1])
                    C.tt("pool", q_.t[:, 0:NB], c_.t[:, 0:NB], c_.t[:, 0:NB], ALU.mult, [c_.b], [q_.b])
                    cot.append(c_); sqt.append(q_)
                pm_ = P.PB[pbi % 6]; pbi += 1
                pq_ = P.PB[pbi % 6]; pbi += 1
                for ct in range(2):
                    C.mm(pm_.t[:, 0:NB], onesf, cot[ct].t[:, 0:NB], ct == 0, ct == 1, [P.misc.b, cot[ct].b], [pm_.b])
                for ct in range(2):
                    C.mm(pq_.t[:, 0:NB], onesf, sqt[ct].t[:, 0:NB], ct == 0, ct == 1, [P.misc.b, sqt[ct].b], [pq_.b])
                C.act(mean.t[:, 0:NB], pm_.t[:, 0:NB], AF.Copy, [pm_.b], [mean.b], scale=1.0 / 256)
                C.tt("dve", var.t[:, 0:NB], mean.t[:, 0:NB], mean.t[:, 0:NB], ALU.mult, [mean.b], [var.b])
                C.stt("dve", var.t[:, 0:NB], pq_.t[:, 0:NB], 1.0 / 256, var.t[:, 0:NB], ALU.mult, ALU.subtract,
                      [pq_.b], [var.b])
                C.act(var.t[:, 0:NB], var.t[:, 0:NB], AF.Sqrt, [], [var.b], bias=1e-5)
                C.recip(var.t[:, 0:NB], var.t[:, 0:NB], [], [var.b])
                sts = []
                for ct in range(2):
                    d_ = dp.get()
                    C.tt("dve", d_.t[:, 0:NB], cot[ct].t[:, 0:NB], mean.t[:, 0:NB], ALU.subtract, [cot[ct].b, mean.b], [d_.b])
                    C.tt("pool", d_.t[:, 0:NB], d_.t[:, 0:NB], var.t[:, 0:NB], ALU.mult, [var.b], [d_.b])
                    s_ = sT[ct].get()
                    C.act(s_.t[:, 0:NB], d_.t[:, 0:NB], AF.Silu, [d_.b, LP.pp1.b], [s_.b],
                          scale=LP.pp1.t[:, C_LNG + ct:C_LNG + ct + 1], bias=LP.pp1.t[:, C_LNB + ct:C_LNB + ct + 1])
                    sts.append(s_)
                for nt in range(2):
                    pb = P.PB[pbi % 6]; pbi += 1
                    for ct in range(2):
                        C.mm(pb.t[:, 0:NB], pwb.t[:, ct, nt * 128:(nt + 1) * 128], sts[ct].t[:, 0:NB], ct == 0, ct == 1,
                             [pwb.b, sts[ct].b], [pb.b])
                    yo = yop.get()
                    C.evac(yo.t[:, 0:NB], pb.t[:, 0:NB], [pb.b], [yo.b])
                    C.dma(Dr["yT"][768 + nt * 128:768 + (nt + 1) * 128, c0 + blk * NB:c0 + (blk + 1) * NB],
                          yo.t[:, 0:NB], [yo.b], [])
        P.C.S.barrier()


def phase_C1(P, LP, l, last=False):
    C, I, Dr = P.C, P.I, P.Dr
    with ExitStack() as es:
        woutb = C.sb(es, [128, 8, D], BF16, "woutb")
        stg = C.pool_of(es, 3, [128, 1024], F32, "stg")
        load_cast(C, es, woutb, I["w_out"][l], D, stg)
        htp = C.pool_of(es, 9, [128, D], F32, "ht")
        ssp = C.pool_of(es, 8, [128, 1], F32, "ss")
        rsp = C.pool_of(es, 8, [128, 1], F32, "rs")
        xsp = C.pool_of(es, 8, [128, D], BF16, "xs")
        xnp = C.pool_of(es, 2, [128, 8, 512], BF16, "xnT")
        ytp = C.pool_of(es, 2, [128, 8, 512], BF16, "ytb")
        tmp = C.pool_of(es, 4, [128, 512], F32, "tmp")
        gt = C.sb(es, [128, D], F32, "gt")
        pbi = 0
        blocks = []
        for (sn, Ts, c0, m) in streams(P):
            if last and sn == "c":
                continue
            NB = min(512, Ts)
            for blk in range(Ts // NB):
                blocks.append((sn, Ts, c0, m, NB, blk))

        def load_blk(bd):
            sn, Ts, c0, m, NB, blk = bd
            hsrc = Dr["hx"] if sn == "x" else (I["ctx"] if l == 0 else Dr["hc"])
            cs = slice(c0 + blk * NB, c0 + blk * NB + NB)
            yt = ytp.get()
            C.dma(yt.t[:, :, 0:NB], Dr["yT"][:, cs].rearrange("(kc p) t -> p kc t", p=128), [], [yt.b])
            hts = []
            for j in range(NB // 128):
                t0 = blk * NB + j * 128
                ht = htp.get()
                C.dma(ht.t[:], hsrc[t0:t0 + 128, :], [], [ht.b])
                hts.append(ht)
            return yt, hts

        cur_m = None
        nxt = load_blk(blocks[0])
        for bi, bd in enumerate(blocks):
            sn, Ts, c0, m, NB, blk = bd
            if m != cur_m:
                C.dma(gt.t[:], Dr["gtb"][m * 2 + 0], [], [gt.b])
                cur_m = m
            hdst = Dr["hx"] if sn == "x" else Dr["hc"]
            cs = slice(c0 + blk * NB, c0 + blk * NB + NB)
            yt, hts = nxt
            nxt = load_blk(blocks[bi + 1]) if bi + 1 < len(blocks) else None
            xnT = xnp.get()
            for j in range(NB // 128):
                ht = hts[j]
                t0 = blk * NB + j * 128
                for nh in range(2):
                    pb = P.PB[pbi % 6]; pbi += 1
                    for kc in range(8):
                        C.mm(pb.t[:], yt.t[:, kc, j * 128:(j + 1) * 128], woutb.t[:, kc, nh * 512:(nh + 1) * 512],
                             kc == 0, kc == 7, [yt.b, woutb.b], [pb.b])
                    tm = tmp.get()
                    C.tt("dve", tm.t[:], pb.t[:], gt.t[:, nh * 512:(nh + 1) * 512], ALU.mult, [pb.b, gt.b], [tm.b])
                    C.tt("pool", ht.t[:, nh * 512:(nh + 1) * 512], ht.t[:, nh * 512:(nh + 1) * 512], tm.t[:],
                         ALU.add, [tm.b], [ht.b])
                C.dma(hdst[t0:t0 + 128, :], ht.t[:], [ht.b], [])
            norm_block(P, C, hts, m, LP.gm2, 24, LP, xnT, ssp, rsp, xsp)
            C.dma(Dr["xn2"][:, cs].rearrange("(kc p) t -> p kc t", p=128), xnT.t[:, :, 0:NB], [xnT.b], [])
        P.C.S.barrier()


def phase_C2(P, LP, l, last):
    C, I, Dr = P.C, P.I, P.Dr
    NB = 512
    with ExitStack() as es:
        w1b = C.sb(es, [128, 8, 2 * DFF], BF16, "w1b")
        w2b = C.sb(es, [128, 22, D], BF16, "w2b")
        with ExitStack() as es_s:
            stg = C.pool_of(es_s, 3, [128, 1024], F32, "stg")
            load_cast(C, es_s, w1b, I["ffn_w_in"][l], 2 * DFF, stg)
            load_cast(C, es_s, w2b, I["ffn_w_out"][l], D, stg)
            P.C.S.barrier()
        htp = C.pool_of(es, 2, [128, D], F32, "ht")
        xnp = C.pool_of(es, 2, [128, 8, 512], BF16, "xn2")
        aT = C.sb(es, [128, 22, 512], BF16, "aT")
        sgp = C.pool_of(es, 2, [128, 512], F32, "sg")
        tmp = C.pool_of(es, 2, [128, 512], F32, "tmp")
        gt = C.sb(es, [128, D], F32, "gt")
        ssp = C.pool_of(es, 2, [128, 1], F32, "ss")
        rsp = C.pool_of(es, 2, [128, 1], F32, "rs")
        if last:
            fgb = C.sb(es, [128, D], F32, "fgb")
            C.dma(fgb.t[:], I["final_g"].partition_broadcast(128), [], [fgb.b])
        pbi = 0
        for (sn, Ts, c0, m) in streams(P):
            if last and sn == "c":
                continue
            C.dma(gt.t[:], Dr["gtb"][m * 2 + 1], [], [gt.b])
            hbuf = Dr["hx"] if sn == "x" else Dr["hc"]
            NB = min(512, Ts)
            def load_xn(blk_):
                cs_ = slice(c0 + blk_ * NB, c0 + blk_ * NB + NB)
                xn_ = xnp.get()
                C.dma(xn_.t[:, :, 0:NB], Dr["xn2"][:, cs_].rearrange("(kc p) t -> p kc t", p=128), [], [xn_.b])
                return xn_

            nxt_xn = load_xn(0)
            for blk in range(Ts // NB):
                cs = slice(c0 + blk * NB, c0 + blk * NB + NB)
                xn = nxt_xn
                for fb in range(22):
                    pg = P.PB[pbi % 6]; pbi += 1
                    pu = P.PB[pbi % 6]; pbi += 1
                    for dc in range(8):
                        C.mm(pg.t[:, 0:NB], w1b.t[:, dc, fb * 128:(fb + 1) * 128], xn.t[:, dc, 0:NB], dc == 0, dc == 7,
                             [w1b.b, xn.b], [pg.b])
                    for dc in range(8):
                        C.mm(pu.t[:, 0:NB], w1b.t[:, dc, DFF + fb * 128:DFF + (fb + 1) * 128], xn.t[:, dc, 0:NB],
                             dc == 0, dc == 7, [w1b.b, xn.b], [pu.b])
                    sg = sgp.get()
                    C.act(sg.t[:, 0:NB], pg.t[:, 0:NB], AF.Silu, [pg.b], [sg.b])
                    C.tt("dve", aT.t[:, fb, 0:NB], sg.t[:, 0:NB], pu.t[:, 0:NB], ALU.mult, [sg.b, pu.b], [aT.b])
                nxt_xn = load_xn(blk + 1) if blk + 1 < Ts // NB else None
                for j in range(NB // 128):
                    t0 = blk * NB + j * 128
                    ht = htp.get()
                    C.dma(ht.t[:], hbuf[t0:t0 + 128, :], [], [ht.b])
                    for nh in range(2):
                        pb = P.PB[pbi % 6]; pbi += 1
                        for fb in range(22):
                            C.mm(pb.t[:], aT.t[:, fb, j * 128:(j + 1) * 128], w2b.t[:, fb, nh * 512:(nh + 1) * 512],
                                 fb == 0, fb == 21, [aT.b, w2b.b], [pb.b])
                        tm = tmp.get()
                        C.tt("dve", tm.t[:], pb.t[:], gt.t[:, nh * 512:(nh + 1) * 512], ALU.mult, [pb.b, gt.b], [tm.b])
                        C.tt("pool", ht.t[:, nh * 512:(nh + 1) * 512], ht.t[:, nh * 512:(nh + 1) * 512], tm.t[:],
                             ALU.add, [tm.b], [ht.b])
                    if last:
                        ss = ssp.get(); rs = rsp.get()
                        tm = tmp.get(); tm2 = tmp.get()
                        C.act(tm.t[:], ht.t[:, 0:512], AF.Square, [ht.b], [tm.b, ss.b], accum=ss.t[:])
                        C.act(tm.t[:], ht.t[:, 512:1024], AF.Square, [ht.b], [tm.b, rs.b], accum=rs.t[:])
                        C.tt("dve", ss.t[:], ss.t[:], rs.t[:], ALU.add, [rs.b], [ss.b])
                        C.act(rs.t[:], ss.t[:], AF.Sqrt, [ss.b], [rs.b], scale=1.0 / D, bias=1e-6)
                        C.recip(rs.t[:], rs.t[:], [rs.b], [rs.b])
                        C.stt("dve", ht.t[:], ht.t[:], rs.t[:], fgb.t[:], ALU.mult, ALU.mult, [rs.b, fgb.b], [ht.b])
                        C.dma(P.out[t0:t0 + 128, :], ht.t[:], [ht.b], [])
                    else:
                        C.dma(hbuf[t0:t0 + 128, :], ht.t[:], [ht.b], [])
        P.C.S.barrier()


def make_in_map(inp, b, T, TC, L):
    f = lambda a: np.ascontiguousarray(np.asarray(a, dtype=np.float32))
    m = {}
    m["x"] = f(inp["x"][b]); m["ctx"] = f(inp["ctx"][b])
    m["cvec"] = f(np.stack([np.asarray(inp["c"][b]), np.asarray(inp["c_ctx"])], axis=0))
    m["w_mod"] = f(inp["w_mod"]); m["b_mod"] = f(inp["b_mod"])
    m["norm1_g"] = f(inp["norm1_g"]); m["norm2_g"] = f(inp["norm2_g"])
    m["w_in"] = f(inp["w_in"]); m["w_out"] = f(inp["w_out"])
    m["mu_prev"] = f(inp["rwkv_mu_prev"]); m["mu_next"] = f(inp["rwkv_mu_next"])
    m["w0"] = f(np.asarray(inp["rwkv_w0"]).reshape(L, 512)); m["w2"] = f(np.asarray(inp["rwkv_w2"]).reshape(L, 128, 256))
    m["a0"] = f(np.asarray(inp["rwkv_a0"]).reshape(L, 512)); m["a2"] = f(np.asarray(inp["rwkv_a2"]).reshape(L, 128, 256))
    m["g2"] = f(inp["rwkv_g2"])
    m["k_k"] = f(inp["rwkv_k_k"]); m["k_a"] = f(inp["rwkv_k_a"])
    m["r_k"] = f(np.asarray(inp["rwkv_r_k"]).reshape(L, 256))
    m["gn_w"] = f(inp["rwkv_gn_w"]); m["gn_b"] = f(inp["rwkv_gn_b"])
    m["pool_scale"] = f(inp["pool_scale"]); m["dw_b"] = f(inp["conv_dw_b"])
    m["ln_g"] = f(inp["conv_ln_g"]); m["ln_b"] = f(inp["conv_ln_b"])
    m["pool_w"] = f(inp["pool_w"]); m["fourier_w"] = f(inp["fourier_w"])
    m["dw_w"] = f(inp["conv_dw_w"]); m["conv_pw"] = f(inp["conv_pw"])
    m["ffn_w_in"] = f(inp["ffn_w_in"]); m["ffn_w_out"] = f(inp["ffn_w_out"])
    m["final_g"] = f(inp["final_norm_g"])
    return m


_CONST_CACHE = {}


def const_map(T, TC):
    key = (T, TC)
    if key not in _CONST_CACHE:
        c = {}
        c["misc"] = _misc_consts()
        c["poolm"] = _bf(_pool_mats())
        cc3, wr_x, gm_x = _fourier_consts(T)
        _, wr_c, gm_c = _fourier_consts(TC)
        c["cc3"] = cc3; c["wr_x"] = wr_x; c["gm_x"] = gm_x; c["wr_c"] = wr_c; c["gm_c"] = gm_c
        pr, pc = _pos_tables(T)
        c["pos_row"] = pr; c["pos_col"] = pc
        sel = np.zeros((2, 2, 128), np.float32)
        sel[0, 0, :] = 1.0
        sel[1, 1, :] = 1.0
        c["sel"] = sel
        _CONST_CACHE[key] = c
    return _CONST_CACHE[key]


def kernel(**inp):
    x = np.asarray(inp["x"])
    B, T, _ = x.shape
    TC = np.asarray(inp["ctx"]).shape[1]
    L = np.asarray(inp["w_in"]).shape[0]
    nc = build_program(T, TC, L)
    cm = const_map(T, TC)
    in_maps = []
    for b in range(B):
        m = make_in_map(inp, b, T, TC, L)
        m.update(cm)
        in_maps.append(m)
    res = run_bass_kernel_spmd(nc, in_maps, core_ids=list(range(B)))
    return np.stack([np.asarray(r["out"], dtype=np.float32) for r in res.results], axis=0)
```

```python
import numpy as np
import ml_dtypes
from contextlib import ExitStack
import concourse.bass as bass
import concourse.mybir as mybir
from concourse.bass_utils import run_bass_kernel_spmd

F32 = mybir.dt.float32
BF16 = mybir.dt.bfloat16
AF = mybir.ActivationFunctionType
ALU = mybir.AluOpType
AX = mybir.AxisListType

D = 1024
NIN = 2176
DFF = 2816
GW = 256
RW = 1152
N_DMA_SEMS = 12
NQ = {"sp": N_DMA_SEMS}


class Buf:
    __slots__ = ("w", "r", "x")

    def __init__(self):
        self.w = {}
        self.r = {}
        self.x = False


class Acc:
    __slots__ = ("b",)

    def __init__(self, b):
        self.b = b


class Sched:
    def __init__(self, nc, comp, dma):
        self.nc = nc
        self.streams = {k: [] for k in ("pe", "act", "dve", "pool", "sp")}
        self.sems = dict(comp)
        self.cnt = {k: 0 for k in comp}
        self.seen = {k: {} for k in self.streams}
        self.dma_sems = dma
        self.dma_cnt = {q: 0 for q in dma}

    def _need(self, eng, deps):
        st = self.streams[eng]
        seen = self.seen[eng]
        for key, val in deps.items():
            if eng == "pe" and key == "pe":
                continue
            if seen.get(key, 0) >= val:
                continue
            seen[key] = val
            st.append(("wait", key, val))

    def _sem_of(self, key):
        if isinstance(key, str):
            return self.sems[key]
        return self.dma_sems[key[0]][key[1]]

    @staticmethod
    def _collect(reads, writes, eng=None):
        deps = {}
        for b in reads:
            for k, v in b.w.items():
                if deps.get(k, 0) < v:
                    deps[k] = v
            if b.x:
                for k, v in b.r.items():
                    if k != eng and deps.get(k, 0) < v:
                        deps[k] = v
        for b in writes:
            if isinstance(b, Acc):
                for k, v in b.b.r.items():
                    if deps.get(k, 0) < v:
                        deps[k] = v
                continue
            for k, v in b.w.items():
                if deps.get(k, 0) < v:
                    deps[k] = v
            for k, v in b.r.items():
                if deps.get(k, 0) < v:
                    deps[k] = v
        return deps

    @staticmethod
    def _record(key, val, reads, writes):
        for b in reads:
            if b.r.get(key, 0) < val:
                b.r[key] = val
        for b in writes:
            if isinstance(b, Acc):
                if b.b.w.get(key, 0) < val:
                    b.b.w[key] = val
                continue
            b.w = {key: val}
            b.r = {}

    def op(self, eng, fn, reads=(), writes=()):
        deps = self._collect(reads, writes, eng)
        self._need(eng, deps)
        self.cnt[eng] += 1
        val = self.cnt[eng]
        self.streams[eng].append(("op", fn, eng))
        self._record(eng, val, reads, writes)

    def dma(self, out, in_, reads=(), writes=(), q="sp"):
        deps = self._collect(reads, writes)
        i = self.dma_cnt[q]
        self.dma_cnt[q] += 1
        slot = i % N_DMA_SEMS
        rnd = i // N_DMA_SEMS
        key = (q, slot)
        if rnd > 0:
            deps[key] = max(deps.get(key, 0), 16 * rnd)
        self._need(q, deps)
        self.streams[q].append(("dma", out, in_, key))
        self._record(key, 16 * (rnd + 1), reads, writes)

    def barrier(self):
        deps = {k: v for k, v in self.cnt.items() if v > 0}
        for q, n in self.dma_cnt.items():
            for slot in range(min(n, N_DMA_SEMS)):
                rounds = (n - 1 - slot) // N_DMA_SEMS + 1
                deps[(q, slot)] = 16 * rounds
        for eng in self.streams:
            d = dict(deps)
            self._need(eng, d)

    def emit(self):
        nc = self.nc
        streams = self.streams
        sem_of = self._sem_of
        sems = self.sems

        def run(name, e):
            for it in streams[name]:
                if it[0] == "wait":
                    e.wait_ge(sem_of(it[1]), it[2])
                elif it[0] == "op":
                    it[1](e).then_inc(sems[it[2]], 1)
                else:
                    e.dma_start(out=it[1], in_=it[2]).then_inc(sem_of(it[3]), 16)

        with nc.Block() as block:
            @block.tensor
            def _(e):
                run("pe", e)

            @block.scalar
            def _(e):
                run("act", e)

            @block.vector
            def _(e):
                run("dve", e)

            @block.gpsimd
            def _(e):
                run("pool", e)

            @block.sync
            def _(e):
                run("sp", e)


class Tl:
    __slots__ = ("t", "b")

    def __init__(self, t):
        self.t = t
        self.b = Buf()


class Ctx:
    def __init__(self, nc, S):
        self.nc = nc
        self.S = S
        self.uid = 0
        self.rr = 0

    def sb(self, es, shape, dt, name="t"):
        self.uid += 1
        return Tl(es.enter_context(self.nc.sbuf_tensor(f"{name}_{self.uid}", list(shape), dt)))

    def pool_of(self, es, n, shape, dt, name="p"):
        return Rot([self.sb(es, shape, dt, name) for _ in range(n)])

    def mm(self, out, lhsT, rhs, start, stop, reads, writes):
        self.S.op("pe", lambda e: e.matmul(out, lhsT=lhsT, rhs=rhs, start=start, stop=stop), reads, writes)

    def tr(self, out, in_, ident, reads, writes):
        self.S.op("pe", lambda e: e.transpose(out=out, in_=in_, identity=ident), reads, writes)

    def act(self, out, in_, func, reads, writes, scale=1.0, bias=0.0, accum=None):
        if accum is None:
            self.S.op("act", lambda e: e.activation(out=out, in_=in_, func=func, scale=scale, bias=bias), reads, writes)
        else:
            self.S.op("act", lambda e: e.activation(out=out, in_=in_, func=func, scale=scale, bias=bias,
                                                    accum_out=accum), reads, writes)

    def tt(self, eng, out, in0, in1, op, reads, writes):
        self.S.op(eng, lambda e: e.tensor_tensor(out=out, in0=in0, in1=in1, op=op), reads, writes)

    def ts(self, eng, out, in0, s1, s2, op0, op1, reads, writes):
        if s2 is None:
            self.S.op(eng, lambda e: e.tensor_scalar(out=out, in0=in0, scalar1=s1, scalar2=None, op0=op0), reads, writes)
        else:
            self.S.op(eng, lambda e: e.tensor_scalar(out=out, in0=in0, scalar1=s1, scalar2=s2, op0=op0, op1=op1),
                      reads, writes)

    def stt(self, eng, out, in0, sc, in1, op0, op1, reads, writes):
        self.S.op(eng, lambda e: e.scalar_tensor_tensor(out=out, in0=in0, scalar=sc, in1=in1, op0=op0, op1=op1),
                  reads, writes)

    def cp(self, eng, out, in_, reads, writes):
        if eng == "act":
            self.S.op("act", lambda e: e.activation(out=out, in_=in_, func=AF.Copy), reads, writes)
        else:
            self.S.op(eng, lambda e: e.tensor_copy(out=out, in_=in_), reads, writes)

    def memset(self, eng, ap, val, writes):
        self.S.op(eng, lambda e: e.memset(ap, val), (), writes)

    def recip(self, out, in_, reads, writes):
        self.S.op("dve", lambda e: e.reciprocal(out=out, in_=in_), reads, writes)

    def dma(self, out, in_, reads, writes, q="sp"):
        self.S.dma(out, in_, reads, writes, q=q)

    def alt(self, engs=("act", "dve")):
        self.rr += 1
        return engs[self.rr % len(engs)]

    def evac(self, out, in_, reads, writes, engs=("act", "dve")):
        self.cp(self.alt(engs), out, in_, reads, writes)


def interleave(*gens):
    gens = [g for g in gens if g is not None]
    while gens:
        for g in list(gens):
            try:
                next(g)
            except StopIteration:
                gens.remove(g)


class Rot:
    def __init__(self, items):
        self.items = items
        self.i = 0

    def get(self):
        it = self.items[self.i % len(self.items)]
        self.i += 1
        return it


def _bf(a):
    return np.ascontiguousarray(a.astype(ml_dtypes.bfloat16))


def _pool_mats():
    Tt = 512
    t = np.arange(Tt)
    out = np.zeros((4, 5, 128, 128), np.float64)
    for gi, w in enumerate((2, 4, 8, 16)):
        hw = w // 2
        lo0, hi0 = np.clip(t - hw, 0, Tt), np.clip(t + hw, 0, Tt)
        lo1, hi1 = np.clip(t - hw + 1, 0, Tt), np.clip(t + hw + 1, 0, Tt)
        cnt = (hi0 - lo0) + (hi1 - lo1)
        M = np.zeros((Tt, Tt))
        for tt in range(Tt):
            M[tt, lo0[tt]:hi0[tt]] += 1.0
            M[tt, lo1[tt]:hi1[tt]] += 1.0
            M[tt] /= cnt[tt]
            M[tt, tt] -= 1.0
        blk = lambda ti, si: M[ti * 128:(ti + 1) * 128, si * 128:(si + 1) * 128].T
        out[gi, 0] = blk(1, 0)
        out[gi, 1] = blk(1, 1)
        out[gi, 2] = blk(1, 2)
        out[gi, 3] = blk(0, 0)
        out[gi, 4] = blk(3, 3)
    return out


def _fourier_consts(T):
    R = T // 64
    c = np.arange(64)
    ang = 2 * np.pi * np.outer(c, c) / 64.0
    nrm = 1.0 / np.sqrt(64.0 * T)
    cc3 = np.zeros((128, 384))
    for g in range(2):
        sl = slice(g * 64, (g + 1) * 64)
        cc3[sl, 0 + g * 64:0 + (g + 1) * 64] = np.cos(ang)
        cc3[sl, 128 + g * 64:128 + (g + 1) * 64] = -np.sin(ang)
        cc3[sl, 256 + g * 64:256 + (g + 1) * 64] = -np.cos(ang)
    t1 = np.arange(R)
    angR = 2 * np.pi * np.outer(t1, t1) / R
    wr = np.zeros((128, 2, R))
    wr[:R, 0] = np.cos(angR)
    wr[:R, 1] = np.sin(angR)
    t2 = np.arange(64)
    k1 = np.arange(R)
    k2 = np.arange(64)
    th = 2 * np.pi * (t2[:, None, None] * k1[None, :, None] / T + t2[:, None, None] * k2[None, None, :] / 64.0)
    gm = np.concatenate([np.cos(th), np.sin(th)], axis=0) * nrm
    return _bf(cc3), _bf(wr), _bf(gm)


def _pos_tables(T):
    quarter = D // 4
    omega = 1.0 / (10000.0 ** (np.arange(quarter, dtype=np.float32) / np.float32(quarter)))
    omega = omega.astype(np.float32)

    def enc(p):
        ang = (p[:, None].astype(np.float32) * omega[None, :]).astype(np.float32)
        return np.concatenate([np.sin(ang), np.cos(ang)], axis=-1).astype(np.float32)

    rows = T // 64
    return enc(np.arange(rows, dtype=np.float32)), enc(np.arange(64, dtype=np.float32))


def _misc_consts():
    i = np.arange(128)
    su = (i[:, None] < i[None, :]).astype(np.float32)
    iu = (i[:, None] <= i[None, :]).astype(np.float32)
    sl = su.T.copy()
    il = iu.T.copy()
    ident = np.eye(128, dtype=np.float32)
    ones = np.ones((128, 128), np.float32)
    blk = np.zeros((128, 128), np.float32)
    blk[:64, :64] = 1.0
    blk[64:, 64:] = 1.0
    mats = [ident, ones, blk, su, iu, sl, il]
    mats.append(((i[:, None] // 8) == (i[None, :] // 8)).astype(np.float32))
    for b in (8, 16, 32, 64):
        same2 = (i[:, None] // (2 * b)) == (i[None, :] // (2 * b))
        diffb = (i[:, None] // b) != (i[None, :] // b)
        mats.append((same2 & diffb).astype(np.float32))
    return np.stack(mats, axis=0)


class Prog:
    pass


def build_program(T, TC, L, dbg=()):
    nc = bass.Bass("TRN2", target_bir_lowering=False)
    P = Prog()
    P.T, P.TC, P.L, P.TT = T, TC, L, T + TC
    P.dbg = dbg
    TT = P.TT
    R = T // 64
    RC = TC // 64

    def din(name, shape, dt=F32):
        return nc.dram_tensor(name, list(shape), dt, kind="ExternalInput").ap()

    def dscr(name, shape, dt=F32):
        kind = "ExternalOutput" if name in dbg else ("ExternalInput" if name + "_in" in dbg else "Internal")
        return nc.dram_tensor(name, list(shape), dt, kind=kind).ap()

    I = {}
    I["x"] = din("x", [T, D]); I["ctx"] = din("ctx", [TC, D]); I["cvec"] = din("cvec", [2, D])
    I["w_mod"] = din("w_mod", [L, D, 6 * D]); I["b_mod"] = din("b_mod", [L, 6 * D])
    I["norm1_g"] = din("norm1_g", [L, D]); I["norm2_g"] = din("norm2_g", [L, D])
    I["w_in"] = din("w_in", [L, D, NIN]); I["w_out"] = din("w_out", [L, D, D])
    I["mu_prev"] = din("mu_prev", [L, RW]); I["mu_next"] = din("mu_next", [L, RW])
    I["w0"] = din("w0", [L, 512]); I["w2"] = din("w2", [L, 128, 256])
    I["a0"] = din("a0", [L, 512]); I["a2"] = din("a2", [L, 128, 256])
    I["g2"] = din("g2", [L, 128, 256])
    for nm in ("k_k", "k_a", "r_k", "gn_w", "gn_b", "pool_scale", "dw_b", "ln_g", "ln_b"):
        I[nm] = din(nm, [L, 256])
    I["pool_w"] = din("pool_w", [L, 4, 64, 64]); I["fourier_w"] = din("fourier_w", [L, 256, 256])
    I["dw_w"] = din("dw_w", [L, 31, 256]); I["conv_pw"] = din("conv_pw", [L, 256, 256])
    I["ffn_w_in"] = din("ffn_w_in", [L, D, 2 * DFF]); I["ffn_w_out"] = din("ffn_w_out", [L, DFF, D])
    I["final_g"] = din("final_g", [D])
    I["misc"] = din("misc", [12, 128, 128]); I["poolm"] = din("poolm", [4, 5, 128, 128], BF16)
    I["cc3"] = din("cc3", [128, 384], BF16)
    I["wr_x"] = din("wr_x", [128, 2, R], BF16); I["gm_x"] = din("gm_x", [128, R, 64], BF16)
    I["wr_c"] = din("wr_c", [128, 2, RC], BF16); I["gm_c"] = din("gm_c", [128, RC, 64], BF16)
    I["pos_row"] = din("pos_row", [R, 512]); I["pos_col"] = din("pos_col", [64, 512])
    I["sel"] = din("sel", [2, 2, 128])
    out = nc.dram_tensor("out", [T, D], F32, kind="ExternalOutput").ap()
    P.I, P.out = I, out

    Dr = {}
    Dr["hx"] = dscr("hx", [T, D]); Dr["hc"] = dscr("hc", [TC, D])
    Dr["uR"] = dscr("uR", [RW, TT]); Dr["uM"] = dscr("uM", [1024, TT], BF16)
    Dr["yT"] = dscr("yT", [1024, TT], BF16); Dr["xn2"] = dscr("xn2", [1024, TT], BF16)
    Dr["sq"] = dscr("sq", [13, 256, TT]); Dr["yd"] = dscr("yd", [2, TT, 256])
    Dr["gtb"] = dscr("gtb", [4, 128, 1024])
    if "scan1" in dbg:
        Dr["dumpf"] = nc.dram_tensor("dumpf", [3, 128, 256], F32, kind="ExternalOutput").ap()
        Dr["dumpb"] = nc.dram_tensor("dumpb", [5, 128, 256], BF16, kind="ExternalOutput").ap()
    P.Dr = Dr

    with ExitStack() as es0:
        comp = {k: es0.enter_context(nc.semaphore(f"s_{k}")) for k in ("pe", "act", "dve", "pool")}
        dma = {"sp": [es0.enter_context(nc.semaphore(f"d_sp{i}")) for i in range(N_DMA_SEMS)],
               "act": [es0.enter_context(nc.semaphore(f"d_act{i}")) for i in range(N_DMA_SEMS)]}
        S = Sched(nc, comp, dma)
        C = Ctx(nc, S)
        P.C = C
        P.PB = [Tl(es0.enter_context(nc.psum_tensor(f"pb{i}", [128, 512], F32))) for i in range(6)]
        P.PT = [Tl(es0.enter_context(nc.psum_tensor(f"pt{i}", [128, 1024], BF16))) for i in range(2)]
        for t_ in P.PB + P.PT:
            t_.b.x = True
        P.misc = C.sb(es0, [128, 12, 128], F32, "misc")
        C.dma(P.misc.t[:], I["misc"].rearrange("k p n -> p k n"), [], [P.misc.b])
        P.identb = C.sb(es0, [128, 128], BF16, "identb")
        C.cp("dve", P.identb.t[:], P.misc.t[:, 0, :], [P.misc.b], [P.identb.b])
        P.maskb = C.sb(es0, [128, 4, 128], BF16, "maskb")
        C.cp("dve", P.maskb.t[:], P.misc.t[:, 3:7, :], [P.misc.b], [P.maskb.b])
        P.onesb = C.sb(es0, [128, 128], BF16, "onesb")
        C.cp("dve", P.onesb.t[:], P.misc.t[:, 1, :], [P.misc.b], [P.onesb.b])

        for l in range(L):
            with ExitStack() as esl:
                last = (l == L - 1)
                import os as _os
                stop = _os.environ.get("KSTOP", "")
                LP = phase_params(P, esl, l)
                S.barrier()
                if stop == "params":
                    break
                phase_A(P, LP, l)
                S.barrier()
                if stop == "A":
                    break
                rwkv_prep(P, LP, l)
                if stop == "prep":
                    break
                rwkv_scan_q(P, LP, l)
                if stop == "scan0":
                    break
                rwkv_readout(P, LP, l)
                if stop == "rwkv":
                    break
                phase_pool(P, LP, l)
                if stop == "pool":
                    break
                phase_fourier(P, LP, l)
                if stop == "fourier":
                    break
                phase_conv(P, LP, l)
                if stop == "conv":
                    break
                phase_C1(P, LP, l, last)
                if stop == "C1":
                    break
                phase_C2(P, LP, l, last)
        S.barrier()
        S.emit()
    return nc


def phase_params(P, esl, l):
    C, I, S = P.C, P.I, P.C.S
    LP = Prog()
    LP.pp1 = C.sb(esl, [128, 128], F32, "pp1")
    LP.pp2 = C.sb(esl, [128, 64], F32, "pp2")
    LP.modT = C.sb(esl, [128, 48, 2], F32, "modT")
    LP.gm1 = C.sb(esl, [128, 8, 2], F32, "gm1")
    LP.gm2 = C.sb(esl, [128, 8, 2], F32, "gm2")
    identf = P.misc.t[:, 0, :]
    with ExitStack() as es:
        rs1 = C.sb(es, [128, 128], F32, "rs1")
        rs2 = C.sb(es, [128, 128], F32, "rs2")
        rs3 = C.sb(es, [128, 128], F32, "rs3")
        for r in (rs1, rs2, rs3):
            C.memset("pool", r.t[:], 0.0, [r.b])
        row = [0]

        def put(dst, vec_ap, n):
            C.dma(dst.t[row[0]:row[0] + n, :], vec_ap.rearrange("(n p) -> n p", p=128), [], [dst.b])
            row[0] += n

        put(rs1, I["b_mod"][l], 48)
        put(rs1, I["norm1_g"][l], 8); put(rs1, I["norm2_g"][l], 8)
        put(rs1, I["mu_prev"][l], 9); put(rs1, I["mu_next"][l], 9)
        put(rs1, I["w0"][l], 4); put(rs1, I["a0"][l], 4)
        for nm in ("k_k", "k_a", "r_k", "gn_w", "gn_b", "pool_scale", "dw_b", "ln_g", "ln_b"):
            put(rs1, I[nm][l], 2)
        put(rs1, I["final_g"], 8)
        assert row[0] == 116
        C.dma(rs2.t[0:62, :], I["dw_w"][l].rearrange("j (n p) -> (j n) p", p=128), [], [rs2.b])
        C.dma(rs3.t[0:16, :], I["cvec"].rearrange("m (n p) -> (m n) p", p=128), [], [rs3.b])
        pb = P.PB[0]
        C.tr(pb.t[:, 0:128], rs1.t[:], identf, [rs1.b, P.misc.b], [pb.b])
        C.cp("dve", LP.pp1.t[:], pb.t[:, 0:128], [pb.b], [LP.pp1.b])
        pb = P.PB[1]
        C.tr(pb.t[:, 0:128], rs2.t[:], identf, [rs2.b, P.misc.b], [pb.b])
        C.cp("dve", LP.pp2.t[:], pb.t[:, 0:64], [pb.b], [LP.pp2.b])
        pb = P.PB[2]
        C.tr(pb.t[:, 0:128], rs3.t[:], identf, [rs3.b, P.misc.b], [pb.b])
        scT = C.sb(es, [128, 16], F32, "scT")
        C.act(scT.t[:], pb.t[:, 0:16], AF.Silu, [pb.b], [scT.b])
        scv = scT.t[:].rearrange("p (m dc) -> p dc m", m=2)
        sel = C.sb(es, [2, 2, 128], F32, "sel")
        C.dma(sel.t[:], I["sel"], [], [sel.b])
        wmp = C.pool_of(es, 4, [128, 8, 512], F32, "wm")
        grow = C.sb(es, [2, 512], F32, "grow")
        gts = C.pool_of(es, 2, [128, 512], F32, "gts")
        bmb = C.sb(es, [128, 512], F32, "bmb")
        pfm = P.PB[3]
        for cg in range(12):
            wm = wmp.get()
            C.dma(wm.t[:], I["w_mod"][l][:, cg * 512:(cg + 1) * 512].rearrange("(dc p) n -> p dc n", p=128),
                  [], [wm.b], q="sp" if cg % 2 == 0 else "act")
            for j in range(4):
                nb = cg * 4 + j
                for dc in range(8):
                    C.mm(pfm.t[:, nb * 2:nb * 2 + 2], wm.t[:, dc, j * 128:(j + 1) * 128], scv[:, dc, :],
                         dc == 0, dc == 7, [wm.b, scT.b], [pfm.b])
            if cg in (4, 5, 10, 11):
                g = 0 if cg < 6 else 1
                half = cg % 2
                prow = P.PB[4]
                for dc in range(8):
                    C.mm(prow.t[0:2, :], scv[:, dc, :], wm.t[:, dc, :], dc == 0, dc == 7, [wm.b, scT.b], [prow.b])
                C.cp("dve", grow.t[:], prow.t[0:2, :], [prow.b], [grow.b])
                C.dma(bmb.t[:], I["b_mod"][l][cg * 512:(cg + 1) * 512].partition_broadcast(128), [], [bmb.b])
                for m in range(2):
                    pbc = P.PB[5]
                    C.mm(pbc.t[:], sel.t[:, m, :], grow.t[:], True, True, [sel.b, grow.b], [pbc.b])
                    gt = gts.get()
                    C.tt("dve", gt.t[:], pbc.t[:], bmb.t[:], ALU.add, [pbc.b, bmb.b], [gt.b])
                    C.dma(P.Dr["gtb"][m * 2 + g, :, half * 512:(half + 1) * 512], gt.t[:], [gt.b], [])
        pv = pfm.t[:, 0:96].rearrange("p (nb m) -> p nb m", m=2)
        for m in range(2):
            C.tt("dve", LP.modT.t[:, :, m], pv[:, :, m], LP.pp1.t[:, 0:48], ALU.add, [pfm.b, LP.pp1.b], [LP.modT.b])
        for m in range(2):
            C.stt("dve", LP.gm1.t[:, :, m], LP.modT.t[:, 8:16, m], 1.0, LP.pp1.t[:, 48:56], ALU.add, ALU.mult,
                  [LP.modT.b, LP.pp1.b], [LP.gm1.b])
            C.stt("dve", LP.gm2.t[:, :, m], LP.modT.t[:, 32:40, m], 1.0, LP.pp1.t[:, 56:64], ALU.add, ALU.mult,
                  [LP.modT.b, LP.pp1.b], [LP.gm2.b])
        S.barrier()
    return LP


C_MUP, C_MUN, C_W0, C_A0 = 64, 73, 82, 86
C_KK, C_KA, C_RK, C_GNW, C_GNB, C_PSC, C_DWB, C_LNG, C_LNB, C_FG = 90, 92, 94, 96, 98, 100, 102, 104, 106, 108


def load_cast(C, es, dst, src_ap, ncols, stg_pool, engs=("pool", "dve", "act")):
    nk = src_ap.shape[0] // 128
    CH = 1024
    for k in range(nk):
        for c0 in range(0, ncols, CH):
            cw = min(CH, ncols - c0)
            st = stg_pool.get()
            C.dq = getattr(C, "dq", 0) + 1
            C.dma(st.t[:, 0:cw], src_ap[k * 128:(k + 1) * 128, c0:c0 + cw], [], [st.b], q="sp" if C.dq % 2 else "act")
            C.cp(C.alt(engs), dst.t[:, k, c0:c0 + cw], st.t[:, 0:cw], [st.b], [Acc(dst.b)])


def norm_block_g(P, C, hts, m, gm, sh_col0, LP, xnT, ss_p, rs_p, xs_p):
    n = len(hts)
    sss = [ss_p.get() for _ in range(n)]; rss = [rs_p.get() for _ in range(n)]; xss = [xs_p.get() for _ in range(n)]
    for j in range(n):
        C.act(xss[j].t[:], hts[j].t[:], AF.Square, [hts[j].b], [xss[j].b, sss[j].b], accum=sss[j].t[:])
    yield
    for j in range(n):
        C.act(rss[j].t[:], sss[j].t[:], AF.Sqrt, [sss[j].b], [rss[j].b], scale=1.0 / D, bias=1e-6)
    for j in range(n):
        C.recip(rss[j].t[:], rss[j].t[:], [rss[j].b], [rss[j].b])
    yield
    for j in range(n):
        C.act(xss[j].t[:], hts[j].t[:], AF.Copy, [hts[j].b, rss[j].b], [xss[j].b], scale=rss[j].t[:])
    yield
    for j in range(n):
        pt = P.PT[j % 2]
        yield
        for dc in range(8):
            C.tr(pt.t[:, dc * 128:(dc + 1) * 128], xss[j].t[:, dc * 128:(dc + 1) * 128], P.identb.t[:],
                 [xss[j].b, P.identb.b], [pt.b])
        for dc in range(8):
            o = xnT.t[:, dc, j * 128:(j + 1) * 128]
            i_ = pt.t[:, dc * 128:(dc + 1) * 128]
            if j % 2 == 0:
                C.act(o, i_, AF.Identity, [pt.b, gm.b, LP.modT.b], [Acc(xnT.b)], scale=gm.t[:, dc, m:m + 1],
                      bias=LP.modT.t[:, sh_col0 + dc, m:m + 1])
            else:
                C.ts("dve", o, i_, gm.t[:, dc, m:m + 1], LP.modT.t[:, sh_col0 + dc, m:m + 1], ALU.mult, ALU.add,
                     [pt.b, gm.b, LP.modT.b], [Acc(xnT.b)])


def norm_block(*a):
    for _ in norm_block_g(*a):
        pass


def streams(P):
    return [("c", P.TC, 0, 1), ("x", P.T, P.TC, 0)]


def phase_A(P, LP, l):
    C, I, Dr = P.C, P.I, P.Dr
    with ExitStack() as es:
        winb = C.sb(es, [128, 8, NIN], BF16, "winb")
        stg = C.pool_of(es, 3, [128, 1024], F32, "stg")
        load_cast(C, es, winb, I["w_in"][l], NIN, stg)
        htp = C.pool_of(es, 8, [128, D], F32, "ht")
        ssp = C.pool_of(es, 8, [128, 1], F32, "ss")
        rsp = C.pool_of(es, 8, [128, 1], F32, "rs")
        xsp = C.pool_of(es, 8, [128, D], BF16, "xs")
        xnp = C.pool_of(es, 2, [128, 8, 512], BF16, "xnT")
        osf = C.pool_of(es, 4, [128, 512], F32, "osf")
        osb = C.pool_of(es, 4, [128, 512], BF16, "osb")
        posc = C.sb(es, [128, 512], F32, "posc")
        posr = C.pool_of(es, 4, [128, 512], F32, "posr")
        if l == 0:
            for hh in range(2):
                C.dma(posc.t[hh * 64:(hh + 1) * 64, :], I["pos_col"], [], [posc.b])
        st = {"pbi": 0}
        blocks = []
        for (sn, Ts, c0, m) in streams(P):
            NB = min(512, Ts)
            for blk in range(Ts // NB):
                blocks.append((sn, Ts, c0, m, NB, blk))

        def do_norm(bd, out_box):
            sn, Ts, c0, m, NB, blk = bd
            src = (I["x"] if sn == "x" else I["ctx"]) if l == 0 else (Dr["hx"] if sn == "x" else Dr["hc"])
            xnT = xnp.get()
            hts = []
            for j in range(NB // 128):
                t0 = blk * NB + j * 128
                ht = htp.get()
                C.dma(ht.t[:], src[t0:t0 + 128, :], [], [ht.b])
                if l == 0 and sn == "x":
                    pr = posr.get()
                    for hh in range(2):
                        rr = t0 // 64 + hh
                        C.dma(pr.t[hh * 64:(hh + 1) * 64, :], I["pos_row"][rr, :].partition_broadcast(64), [], [pr.b])
                    C.tt("pool", ht.t[:, 0:512], ht.t[:, 0:512], pr.t[:], ALU.add, [pr.b], [ht.b])
                    C.tt("pool", ht.t[:, 512:1024], ht.t[:, 512:1024], posc.t[:], ALU.add, [posc.b], [ht.b])
                    C.dma(Dr["hx"][t0:t0 + 128, :], ht.t[:], [ht.b], [])
                hts.append(ht)
            out_box.append(xnT)
            yield
            yield from norm_block_g(P, C, hts, m, LP.gm1, 0, LP, xnT, ssp, rsp, xsp)

        def do_mm(bd, xnT):
            sn, Ts, c0, m, NB, blk = bd
            for nb in range(17):
                pb = P.PB[st["pbi"] % 6]; st["pbi"] += 1
                for dc in range(8):
                    C.mm(pb.t[:, 0:NB], winb.t[:, dc, nb * 128:(nb + 1) * 128], xnT.t[:, dc, 0:NB],
                         dc == 0, dc == 7, [winb.b, xnT.b], [pb.b])
                cs = slice(c0 + blk * NB, c0 + blk * NB + NB)
                if nb < 9:
                    o = osf.get()
                    C.evac(o.t[:, 0:NB], pb.t[:, 0:NB], [pb.b], [o.b])
                    C.dma(Dr["uR"][nb * 128:(nb + 1) * 128, cs], o.t[:, 0:NB], [o.b], [])
                else:
                    o = osb.get()
                    C.evac(o.t[:, 0:NB], pb.t[:, 0:NB], [pb.b], [o.b])
                    C.dma(Dr["uM"][(nb - 9) * 128:(nb - 8) * 128, cs], o.t[:, 0:NB], [o.b], [])
                yield

        box = []
        interleave(do_norm(blocks[0], box))
        cur = box[0]
        for i, bd in enumerate(blocks):
            box = []
            interleave(do_mm(bd, cur), do_norm(blocks[i + 1], box) if i + 1 < len(blocks) else None)
            cur = box[0] if box else None
        P.C.S.barrier()


def phase_rwkv(P, LP, l):
    rwkv_prep(P, LP, l)
    if "stop_prep" in P.dbg:
        return
    rwkv_scan_q(P, LP, l)
    rwkv_readout(P, LP, l)


def rwkv_prep(P, LP, l):
    C, I, Dr, S = P.C, P.I, P.Dr, P.C.S
    pp = LP.pp1
    blkf = P.misc.t[:, 2, :]
    onesf = P.misc.t[:, 1, :]
    sq = Dr["sq"]
    with ExitStack() as es:
        stg = C.pool_of(es, 2, [128, 1024], F32, "stg")
        w2b = C.sb(es, [128, 1, 256], BF16, "w2b"); load_cast(C, es, w2b, I["w2"][l], 256, stg)
        a2b = C.sb(es, [128, 1, 256], BF16, "a2b"); load_cast(C, es, a2b, I["a2"][l], 256, stg)
        g2b = C.sb(es, [128, 1, 256], BF16, "g2b"); load_cast(C, es, g2b, I["g2"][l], 256, stg)
        c0t = C.sb(es, [128, 9], F32, "c0t")
        C.tt("dve", c0t.t[:], pp.t[:, C_MUP:C_MUP + 9], pp.t[:, C_MUN:C_MUN + 9], ALU.add, [pp.b], [c0t.b])
        C.ts("dve", c0t.t[:], c0t.t[:], -1.0, 1.0, ALU.mult, ALU.add, [], [c0t.b])
        omka = C.sb(es, [128, 2], F32, "omka")
        C.ts("dve", omka.t[:], pp.t[:, C_KA:C_KA + 2], -1.0, 1.0, ALU.mult, ALU.add, [pp.b], [omka.b])
        dgs = C.sb(es, [128, 27, 128], F32, "dgs")
        identf_ = P.misc.t[:, 0, :]
        for n in range(9):
            for k_, col in enumerate((pp.t[:, C_MUP + n:C_MUP + n + 1], c0t.t[:, n:n + 1], pp.t[:, C_MUN + n:C_MUN + n + 1])):
                C.act(dgs.t[:, n * 3 + k_, :], identf_, AF.Copy, [P.misc.b, pp.b, c0t.b], [Acc(dgs.b)], scale=col)
        uhp = C.pool_of(es, 3, [128, 9, 514], F32, "uh")
        sxp = C.pool_of(es, 2, [128, 9, 512], F32, "sx")
        wk = C.pool_of(es, 34, [128, 512], F32, "wk")
        wkb = C.pool_of(es, 6, [128, 512], BF16, "wkb")
        pbs = {"i": 0}
        blocks = []
        for (sn, Ts, c0, m) in streams(P):
            NB = min(512, Ts)
            for blk in range(Ts // NB):
                blocks.append((sn, Ts, c0, m, NB, blk))

        def load_uh(bd):
            sn, Ts, c0, m, NB, blk = bd
            t0 = blk * NB
            uh = uhp.get()
            lo = 0 if blk == 0 else -1
            hi = 0 if blk == Ts // NB - 1 else 1
            if lo == 0:
                C.memset("pool", uh.t[:, :, 0:1], 0.0, [uh.b])
            if hi == 0:
                C.memset("pool", uh.t[:, :, NB + 1:NB + 2], 0.0, [uh.b])
            C.dma(uh.t[:, :, 1 + lo:NB + 1 + hi],
                  Dr["uR"][:, c0 + t0 + lo:c0 + t0 + NB + hi].rearrange("(n p) t -> p n t", p=128), [], [uh.b])
            return uh

        def tshift(bd, uh, sx):
            NB_ = bd[4]
            for n in range(9):
                pb = P.PB[pbs["i"] % 6]; pbs["i"] += 1
                for k_ in range(3):
                    C.mm(pb.t[:, 0:NB_], dgs.t[:, n * 3 + k_, :], uh.t[:, n, k_:k_ + NB_], k_ == 0, k_ == 2,
                         [dgs.b, uh.b], [pb.b])
                C.cp("act", sx.t[:, n, 0:NB_], pb.t[:, 0:NB_], [pb.b], [Acc(sx.b)])
                yield

        uh0 = load_uh(blocks[0])
        nxt_uh = load_uh(blocks[1]) if len(blocks) > 1 else None
        nxt_sx = sxp.get()
        interleave(tshift(blocks[0], uh0, nxt_sx))
        for bi, bd in enumerate(blocks):
            if True:
                sn, Ts, c0, m, NB, blk = bd
                t0 = blk * NB
                cs = slice(c0 + t0, c0 + t0 + NB)
                sx = nxt_sx
                if bi + 1 < len(blocks):
                    uh_n = nxt_uh
                    nxt_uh = load_uh(blocks[bi + 2]) if bi + 2 < len(blocks) else None
                    nxt_sx = sxp.get()
                    ts_gen = tshift(blocks[bi + 1], uh_n, nxt_sx)
                else:
                    ts_gen = None

                def store(qi, ct, tl):
                    C.dma(sq[qi, ct * 128:(ct + 1) * 128, cs], tl.t[:, 0:NB], [tl.b], [])

                tw = wkb.get(); alb = wkb.get(); sgl = wkb.get()
                C.act(tw.t[:, 0:NB], sx.t[:, 7, 0:NB], AF.Tanh, [sx.b], [tw.b])
                C.cp("pool", alb.t[:, 0:NB], sx.t[:, 8, 0:NB], [sx.b], [alb.b])
                C.act(sgl.t[:, 0:NB], sx.t[:, 0, 0:NB], AF.Sigmoid, [sx.b], [sgl.b])
                def ct_body(ct, sx=sx, tw=tw, alb=alb, sgl=sgl, cs=cs, NB=NB, store=store):
                    for qi, n in ((1, 1 + ct), (2, 5 + ct)):
                        C.dma(sq[qi, ct * 128:(ct + 1) * 128, cs], sx.t[:, n, 0:NB], [sx.b], [])
                    pb = P.PB[pbs["i"] % 6]; pbs["i"] += 1
                    C.mm(pb.t[:, 0:NB], g2b.t[:, 0, ct * 128:(ct + 1) * 128], sgl.t[:, 0:NB], True, True, [g2b.b, sgl.b], [pb.b])
                    gt_ = wk.get()
                    C.cp("act", gt_.t[:, 0:NB], pb.t[:, 0:NB], [pb.b], [gt_.b])
                    store(12, ct, gt_)
                    yield
                    kc = wk.get(); sk = wk.get(); rn = wk.get(); kk = wk.get()
                    C.act(kc.t[:, 0:NB], sx.t[:, 3 + ct, 0:NB], AF.Copy, [sx.b, pp.b], [kc.b], scale=pp.t[:, C_KK + ct:C_KK + ct + 1])
                    C.tt("pool", sk.t[:, 0:NB], kc.t[:, 0:NB], kc.t[:, 0:NB], ALU.mult, [kc.b], [sk.b])
                    pb = P.PB[pbs["i"] % 6]; pbs["i"] += 1
                    C.mm(pb.t[:, 0:NB], blkf, sk.t[:, 0:NB], True, True, [P.misc.b, sk.b], [pb.b])
                    C.act(rn.t[:, 0:NB], pb.t[:, 0:NB], AF.Sqrt, [pb.b], [rn.b])
                    C.ts("dve", rn.t[:, 0:NB], rn.t[:, 0:NB], 1e-12, None, ALU.max, None, [], [rn.b])
                    C.recip(rn.t[:, 0:NB], rn.t[:, 0:NB], [], [rn.b])
                    C.tt("dve", kk.t[:, 0:NB], kc.t[:, 0:NB], rn.t[:, 0:NB], ALU.mult, [kc.b, rn.b], [kk.b])
                    store(0, ct, kk)
                    yield
                    kds = []
                    for d in range(2):
                        ps = slice(64 * d, 64 * d + 64)
                        pw = P.PB[pbs["i"] % 6]; pbs["i"] += 1
                        C.mm(pw.t[:, 0:NB], w2b.t[ps, 0, ct * 128:(ct + 1) * 128], tw.t[ps, 0:NB], True, True, [w2b.b, tw.b], [pw.b])
                        pa = P.PB[pbs["i"] % 6]; pbs["i"] += 1
                        C.mm(pa.t[:, 0:NB], a2b.t[ps, 0, ct * 128:(ct + 1) * 128], alb.t[ps, 0:NB], True, True, [a2b.b, alb.b], [pa.b])
                        lw = wk.get(); ad = wk.get()
                        C.act(lw.t[:, 0:NB], pw.t[:, 0:NB], AF.Sigmoid, [pw.b, pp.b], [lw.b],
                              bias=pp.t[:, C_W0 + d * 2 + ct:C_W0 + d * 2 + ct + 1])
                        C.ts("dve", lw.t[:, 0:NB], lw.t[:, 0:NB], -0.6065306597126334, None, ALU.mult, None, [], [lw.b])
                        C.act(ad.t[:, 0:NB], pa.t[:, 0:NB], AF.Sigmoid, [pa.b, pp.b], [ad.b],
                              bias=pp.t[:, C_A0 + d * 2 + ct:C_A0 + d * 2 + ct + 1])
                        tmp = wk.get(); kd = wk.get(); al = wk.get()
                        C.ts("dve", tmp.t[:, 0:NB], ad.t[:, 0:NB], pp.t[:, C_KA + ct:C_KA + ct + 1], omka.t[:, ct:ct + 1],
                             ALU.mult, ALU.add, [ad.b, pp.b, omka.b], [tmp.b])
                        C.tt("pool", kd.t[:, 0:NB], sx.t[:, 3 + ct, 0:NB], tmp.t[:, 0:NB], ALU.mult, [sx.b, tmp.b], [kd.b])
                        C.stt("dve", al.t[:, 0:NB], ad.t[:, 0:NB], -1.0, kk.t[:, 0:NB], ALU.mult, ALU.mult, [ad.b, kk.b], [al.b])
                        store(3 + d, ct, kd); store(5 + d, ct, al)
                        yield
                        kds.append(kd)
                        cI = wk.get(); cX = wk.get()
                        for ch in range(NB // 128):
                            sl_ = slice(ch * 128, (ch + 1) * 128)
                            if d == 0:
                                S.op("dve", (lambda o, a, b: (lambda e: e.tensor_tensor_scan(out=o, data0=a, data1=b, initial=0.0,
                                     op0=ALU.mult, op1=ALU.add)))(cI.t[:, sl_], onesf, lw.t[:, sl_]), [lw.b, P.misc.b], [cI.b])
                                C.tt("pool", cX.t[:, sl_], cI.t[:, sl_], lw.t[:, sl_], ALU.subtract, [cI.b, lw.b], [cX.b])
                            else:
                                S.op("dve", (lambda o, a, b: (lambda e: e.tensor_tensor_scan(out=o, data0=a, data1=b, initial=0.0,
                                     op0=ALU.mult, op1=ALU.add)))(cI.t[:, sl_], onesf, lw.t[:, sl_]), [lw.b, P.misc.b], [cI.b])
                                C.ts("dve", cX.t[:, sl_], cI.t[:, sl_], cI.t[:, ch * 128 + 127:ch * 128 + 128], -1.0,
                                     ALU.subtract, ALU.mult, [cI.b], [cX.b])
                                C.tt("dve", cI.t[:, sl_], cX.t[:, sl_], lw.t[:, sl_], ALU.add, [cX.b, lw.b], [cI.b])
                        store(7 + 2 * d, ct, cI); store(8 + 2 * d, ct, cX)
                        yield
                    t1 = wk.get(); t3 = wk.get(); bo = wk.get()
                    C.tt("pool", t1.t[:, 0:NB], kds[0].t[:, 0:NB], kds[1].t[:, 0:NB], ALU.add, [kds[0].b, kds[1].b], [t1.b])
                    C.tt("pool", t1.t[:, 0:NB], t1.t[:, 0:NB], sx.t[:, 1 + ct, 0:NB], ALU.mult, [sx.b], [t1.b])
                    C.act(t3.t[:, 0:NB], t1.t[:, 0:NB], AF.Copy, [t1.b, pp.b], [t3.b], scale=pp.t[:, C_RK + ct:C_RK + ct + 1])
                    pb = P.PB[pbs["i"] % 6]; pbs["i"] += 1
                    C.mm(pb.t[:, 0:NB], blkf, t3.t[:, 0:NB], True, True, [P.misc.b, t3.b], [pb.b])
                    C.tt("dve", bo.t[:, 0:NB], pb.t[:, 0:NB], sx.t[:, 5 + ct, 0:NB], ALU.mult, [pb.b, sx.b], [bo.b])
                    store(11, ct, bo)
                    yield
                interleave(ct_body(0), ct_body(1), ts_gen)
        S.barrier()


import os as _os0
INV_BF16 = _os0.environ.get("INV_BF16", "1") == "1"


def rwkv_scan(P, LP, l):
    C, I, Dr, S = P.C, P.I, P.Dr, P.C.S
    sq = Dr["sq"]
    NC = P.TT // 128
    NCc = P.TC // 128
    orders = [list(range(NC)), list(range(NCc - 1, -1, -1)) + list(range(NC - 1, NCc - 1, -1))]
    gmasks = [(P.misc.t[:, 3:5, :]).rearrange("p a b -> p (a b)"), (P.misc.t[:, 5:7, :]).rearrange("p a b -> p (a b)")]
    nmasks = [P.misc.t[:, 5, :], P.misc.t[:, 3, :]]
    last_cols = [127, 0]
    identf = P.misc.t[:, 0, :]
    IDT = BF16 if INV_BF16 else F32
    ident_i = P.identb.t[:] if INV_BF16 else identf
    ident_b = P.identb.b if INV_BF16 else P.misc.b
    QIs = [(0, 1, 2, 3 + d, 5 + d, 7 + 2 * d, 8 + 2 * d) for d in range(2)]
    mk = lambda k: P.misc.t[:, 7 + k, :]
    with ExitStack() as es:
        Hs = [[C.sb(es, [128, 64], F32, "H") for _ in range(2)] for _ in range(2)]
        Hbs = [[C.sb(es, [128, 64], BF16, "Hb") for _ in range(2)] for _ in range(2)]
        for d in range(2):
            for ct in range(2):
                C.memset("pool", Hs[d][ct].t[:], 0.0, [Hs[d][ct].b])
                C.memset("pool", Hbs[d][ct].t[:], 0.0, [Hbs[d][ct].b])
        qinp = C.pool_of(es, 4, [128, 7, 2, 128], F32, "qin")
        ep = C.pool_of(es, 24, [128, 128], F32, "e")
        BRp = C.pool_of(es, 12, [128, 256], BF16, "BR")
        b16 = C.pool_of(es, 72, [128, 128], BF16, "b16")
        m256 = C.pool_of(es, 48, [128, 256], BF16, "m256")
        f128 = C.pool_of(es, 192 if INV_BF16 else 110, [128, 128], IDT, "f128")
        z64 = C.pool_of(es, 32, [128, 64], BF16, "z64")
        z64f = C.pool_of(es, 24, [128, 64], IDT, "z64f")
        ychp = C.pool_of(es, 4, [128, 256], F32, "ych")
        st = {"pbi": 0, "pti": 0}

        def bank():
            pb_ = P.PB[st["pbi"] % 6]; st["pbi"] += 1
            return pb_

        def tbank():
            if INV_BF16:
                pb_ = P.PT[st["pti"] % 2]; st["pti"] += 1
                return pb_
            return bank()

        def mmf(lhsT, rhs):
            pb_ = bank()
            C.mm(pb_.t[:, 0:128], lhsT.t[:], rhs.t[:], True, True, [lhsT.b, rhs.b], [pb_.b])
            return pb_

        def evc(pb_):
            o_ = f128.get()
            C.cp(C.alt(("act", "act", "dve")), o_.t[:], pb_.t[:, 0:128], [pb_.b], [o_.b])
            return o_

        def evadd(pb_, addt):
            o_ = f128.get()
            C.tt("dve", o_.t[:], pb_.t[:, 0:128], addt.t[:], ALU.add, [pb_.b, addt.b], [o_.b])
            return o_

        def masked(src, k):
            o_ = f128.get()
            C.tt("pool", o_.t[:], src.t[:], mk(k), ALU.mult, [src.b, P.misc.b], [o_.b])
            return o_

        def xpose(X):
            pb_ = tbank()
            C.tr(pb_.t[:, 0:128], X.t[:], ident_i, [X.b, ident_b], [pb_.b])
            return evc(pb_)

        for it_ in range(NC):
          heads = []
          ychs = []
          toksl = []
          for d in range(2):
            ci = orders[d][it_]
            gmask, nmask, last_col = gmasks[d], nmasks[d], last_cols[d]
            H, Hb = Hs[d], Hbs[d]
            tok = slice(ci * 128, (ci + 1) * 128)
            toksl.append(tok)
            qin = qinp.get()
            for k_, qi in enumerate(QIs[d]):
                C.dma(qin.t[:, k_, :, :], sq[qi, :, tok].rearrange("(ct p) t -> p ct t", p=128), [], [qin.b])
            ych = ychp.get()
            ychs.append(ych)
            cts = []
            for ct in range(2):
                q = (lambda ct_: (lambda k_: qin.t[:, k_, ct_, :]))(ct)
                eX = ep.get(); eI = ep.get(); eN = ep.get()
                C.act(eX.t[:], q(6), AF.Exp, [qin.b], [eX.b])
                C.act(eI.t[:], q(5), AF.Exp, [qin.b], [eI.b])
                C.act(eN.t[:], q(5), AF.Exp, [qin.b], [eN.b], scale=-1.0)
                BR = BRp.get(); AT = b16.get(); KT = b16.get(); AH = b16.get(); KH = b16.get(); vb = b16.get()
                C.tt("dve", BR.t[:, 0:128], q(0), eX.t[:], ALU.mult, [qin.b, eX.b], [BR.b])
                C.tt("pool", BR.t[:, 128:256], q(1), eI.t[:], ALU.mult, [qin.b, eI.b], [BR.b])
                C.tt("dve", AT.t[:], q(4), eN.t[:], ALU.mult, [qin.b, eN.b], [AT.b])
                C.tt("pool", KT.t[:], q(3), eN.t[:], ALU.mult, [qin.b, eN.b], [KT.b])
                pc = eI.t[:, last_col:last_col + 1]
                C.act(AH.t[:], AT.t[:], AF.Copy, [AT.b, eI.b], [AH.b], scale=pc)
                C.act(KH.t[:], KT.t[:], AF.Copy, [KT.b, eI.b], [KH.b], scale=pc)
                C.cp("pool", vb.t[:], q(2), [qin.b], [vb.b])
                pt = P.PT[st["pti"] % 2]; st["pti"] += 1
                for k_, src in enumerate((AH, KH, vb)):
                    C.tr(pt.t[:, k_ * 128:(k_ + 1) * 128], src.t[:], P.identb.t[:], [src.b, P.identb.b], [pt.b])
                toks = []
                for k_ in range(3):
                    tk = b16.get()
                    C.cp("act" if ct == 0 else "dve", tk.t[:], pt.t[:, k_ * 128:(k_ + 1) * 128], [pt.b], [tk.b])
                    toks.append(tk)
                cts.append(dict(BR=BR, AT=AT, KT=KT, eI=eI, AHt=toks[0], KHt=toks[1], Vt=toks[2]))
            for ct in range(2):
                for hh in range(2):
                    hd = dict(cts[ct]); hd["ct"] = ct; hd["ps"] = slice(64 * hh, 64 * hh + 64); hd["h4"] = ct * 2 + hh
                    hd.update(gmask=gmask, nmask=nmask, last_col=last_col, H=H, Hb=Hb, ych=ych)
                    heads.append(hd)
          if True:
            for hd in heads:
                ps, BR, AT, KT = hd["ps"], hd["BR"], hd["AT"], hd["KT"]
                gmask, nmask = hd["gmask"], hd["nmask"]
                pg1 = bank()
                C.mm(pg1.t[:, 0:256], KT.t[ps, :], BR.t[ps, :], True, True, [KT.b, BR.b], [pg1.b])
                M1 = m256.get()
                C.tt("dve", M1.t[:], pg1.t[:, 0:256], gmask, ALU.mult, [pg1.b, P.misc.b], [M1.b])
                pg2 = bank()
                C.mm(pg2.t[:, 0:256], AT.t[ps, :], BR.t[ps, :], True, True, [AT.b, BR.b], [pg2.b])
                M2 = m256.get()
                C.tt("dve", M2.t[:], pg2.t[:, 0:256], gmask, ALU.mult, [pg2.b, P.misc.b], [M2.b])
                Ntf = f128.get()
                C.tt("dve", Ntf.t[:], pg2.t[:, 0:128], gmask[:, 0:128], ALU.mult, [pg2.b, P.misc.b], [Ntf.b])
                pg3 = bank()
                C.mm(pg3.t[:, 0:128], BR.t[ps, 0:128], AT.t[ps, :], True, True, [AT.b, BR.b], [pg3.b])
                Nf = f128.get()
                C.tt("dve", Nf.t[:], pg3.t[:, 0:128], nmask, ALU.mult, [pg3.b, P.misc.b], [Nf.b])
                hd.update(M1=M1, M2=M2, Nf=Nf, Ntf=Ntf)
            for hd in heads:
                hd["Nd"] = masked(hd["Nf"], 0); hd["Ndt"] = masked(hd["Ntf"], 0)
            for hd in heads:
                hd["p1"] = mmf(hd["Ndt"], hd["Nd"]); hd["p2"] = mmf(hd["Nd"], hd["Ndt"])
                hd["Nd2"] = evc(hd["p1"]); hd["Ndt2"] = evc(hd["p2"])
            for hd in heads:
                hd["p2"] = mmf(hd["Nd2"], hd["Ndt2"])
                hd["Ndt4"] = evc(hd["p2"])
                P1 = f128.get()
                C.tt("pool", P1.t[:], hd["Nd"].t[:], ident_i, ALU.add, [hd["Nd"].b, ident_b], [P1.b])
                hd["P1"] = P1
            for hd in heads:
                hd["P2"] = evadd(mmf(hd["Ndt2"], hd["P1"]), hd["P1"])
            for hd in heads:
                hd["X"] = evadd(mmf(hd["Ndt4"], hd["P2"]), hd["P2"])
            for k in (1, 2, 3):
                for hd in heads:
                    hd["Xt"] = xpose(hd["X"])
                    hd["Noff"] = masked(hd["Ntf"], k)
                for hd in heads:
                    hd["W"] = evc(mmf(hd["Noff"], hd["X"]))
                for hd in heads:
                    hd["X"] = evadd(mmf(hd["Xt"], hd["W"]), hd["X"])
            for hd in heads:
                hd["Xt"] = xpose(hd["X"])
                hd["Noff"] = masked(hd["Nf"], 4)
            for hd in heads:
                hd["W"] = evc(mmf(hd["Noff"], hd["Xt"]))
            for hd in heads:
                hd["Tt"] = evadd(mmf(hd["X"], hd["W"]), hd["Xt"])
            for hd in heads:
                ps, ct = hd["ps"], hd["ct"]
                Hb = hd["Hb"]
                vh = hd["Vt"].t[:, ps]
                pz = bank()
                C.mm(pz.t[:, 0:64], hd["BR"].t[ps, 0:128], Hb[ct].t[ps, :], True, False, [hd["BR"].b, Hb[ct].b], [pz.b])
                C.mm(pz.t[:, 0:64], hd["M1"].t[:, 0:128], vh, False, True, [hd["M1"].b, hd["Vt"].b], [pz.b])
                Zs = z64f.get()
                C.cp("act", Zs.t[:], pz.t[:, 0:64], [pz.b], [Zs.b])
                hd["Zs"] = Zs
            for hd in heads:
                pu = bank()
                C.mm(pu.t[:, 0:64], hd["Tt"].t[:], hd["Zs"].t[:], True, True, [hd["Tt"].b, hd["Zs"].b], [pu.b])
                Us = z64.get()
                C.cp("dve", Us.t[:], pu.t[:, 0:64], [pu.b], [Us.b])
                hd["Us"] = Us
            for hd in heads:
                ps, ct, h4 = hd["ps"], hd["ct"], hd["h4"]
                H, Hb, ych, last_col = hd["H"], hd["Hb"], hd["ych"], hd["last_col"]
                vh = hd["Vt"].t[:, ps]
                py = bank()
                C.mm(py.t[:, 0:64], hd["BR"].t[ps, 128:256], Hb[ct].t[ps, :], True, False, [hd["BR"].b, Hb[ct].b], [py.b])
                C.mm(py.t[:, 0:64], hd["M2"].t[:, 128:256], hd["Us"].t[:], False, False, [hd["M2"].b, hd["Us"].b], [py.b])
                C.mm(py.t[:, 0:64], hd["M1"].t[:, 128:256], vh, False, True, [hd["M1"].b, hd["Vt"].b], [py.b])
                C.cp("act", ych.t[:, h4 * 64:(h4 + 1) * 64], py.t[:, 0:64], [py.b], [ych.b])
                ph = bank()
                C.mm(ph.t[:, 0:64], hd["AHt"].t[:], hd["Us"].t[:], True, False, [hd["AHt"].b, hd["Us"].b], [ph.b])
                C.mm(ph.t[:, 0:64], hd["KHt"].t[:], vh, False, True, [hd["KHt"].b, hd["Vt"].b], [ph.b])
                C.stt("dve", H[ct].t[ps, :], H[ct].t[ps, :], hd["eI"].t[ps, last_col:last_col + 1], ph.t[ps, 0:64],
                      ALU.mult, ALU.add, [hd["eI"].b, ph.b], [H[ct].b])
            for d in range(2):
                for ct in range(2):
                    C.cp("act" if d == 0 else "dve", Hbs[d][ct].t[:], Hs[d][ct].t[:], [Hs[d][ct].b], [Hbs[d][ct].b])
                C.dma(Dr["yd"][d, toksl[d], :], ychs[d].t[:], [ychs[d].b], [])
        S.barrier()


def rwkv_scan_q(P, LP, l):
    C, I, Dr, S = P.C, P.I, P.Dr, P.C.S
    sq = Dr["sq"]
    NC = P.TT // 128
    NCc = P.TC // 128
    orders = [list(range(NC)), list(range(NCc - 1, -1, -1)) + list(range(NC - 1, NCc - 1, -1))]
    last_cols = [127, 0]
    QIs = [(0, 1, 2, 3 + d, 5 + d, 7 + 2 * d, 8 + 2 * d) for d in range(2)]
    with ExitStack() as es:
        gm2 = [C.sb(es, [128, 2, 256], F32, "gm2") for _ in range(2)]
        nm4 = [C.sb(es, [128, 4, 128], F32, "nm4") for _ in range(2)]
        mk4 = [C.sb(es, [128, 4, 128], F32, "mk4") for _ in range(5)]
        id4 = C.sb(es, [128, 4, 128], BF16, "id4")
        for d in range(2):
            src = P.misc.t[:, 3:5, :] if d == 0 else P.misc.t[:, 5:7, :]
            for r in range(2):
                C.cp("pool", gm2[d].t[:, r, :].rearrange("p (a b) -> p a b", a=2), src, [P.misc.b], [gm2[d].b])
            for r in range(4):
                C.cp("pool", nm4[d].t[:, r, :], P.misc.t[:, 5 if d == 0 else 3, :], [P.misc.b], [nm4[d].b])
        for k in range(5):
            for r in range(4):
                C.cp("pool", mk4[k].t[:, r, :], P.misc.t[:, 7 + k, :], [P.misc.b], [mk4[k].b])
        for r in range(4):
            C.cp("pool", id4.t[:, r, :], P.misc.t[:, 0, :], [P.misc.b], [id4.b])
        Hs = [[C.sb(es, [128, 64], F32, "H") for _ in range(2)] for _ in range(2)]
        Hbs = [[C.sb(es, [128, 64], BF16, "Hb") for _ in range(2)] for _ in range(2)]
        for d in range(2):
            for ct in range(2):
                C.memset("pool", Hs[d][ct].t[:], 0.0, [Hs[d][ct].b])
                C.memset("pool", Hbs[d][ct].t[:], 0.0, [Hbs[d][ct].b])
        qinp = C.pool_of(es, 4, [128, 7, 2, 128], F32, "qin")
        eIp = C.pool_of(es, 18, [128, 128], F32, "eI")
        ep = C.pool_of(es, 8, [128, 128], F32, "e")
        BRp = C.pool_of(es, 18, [128, 256], BF16, "BR")
        b16 = C.pool_of(es, 48, [128, 128], BF16, "b16")
        m4 = C.pool_of(es, 18, [128, 4, 256], BF16, "m4")
        q4l = C.pool_of(es, 34, [128, 4, 128], BF16, "q4l")
        q4 = C.pool_of(es, 30, [128, 4, 128], BF16, "q4")
        z4 = C.pool_of(es, 12, [128, 4, 64], BF16, "z4")
        ychp = C.pool_of(es, 4, [128, 256], F32, "ych")
        st = {"pbi": 0, "pti": 0}

        def bank():
            pb_ = P.PB[st["pbi"] % 6]; st["pbi"] += 1
            return pb_

        def tbank():
            pb_ = P.PT[st["pti"] % 2]; st["pti"] += 1
            return pb_

        def mm4(A, B, bsl=None):
            pb_ = bank()
            for h in range(4):
                C.mm(pb_.t[:, h * 128:(h + 1) * 128], A.t[:, h, :] if bsl is None else A.t[:, h, bsl], B.t[:, h, :],
                     True, True, [A.b, B.b], [pb_.b])
            return pb_

        def ev4(pb_, eng):
            o_ = q4.get()
            C.cp(C.alt(("act", "act", "dve")), o_.t[:].rearrange("p a b -> p (a b)"), pb_.t[:, 0:512], [pb_.b], [o_.b])
            return o_

        def evadd4(pb_, addt):
            o_ = q4.get()
            C.tt("dve", o_.t[:].rearrange("p a b -> p (a b)"), pb_.t[:, 0:512], addt.t[:].rearrange("p a b -> p (a b)"),
                 ALU.add, [pb_.b, addt.b], [o_.b])
            return o_

        def masked4(src_ap, src_b, k):
            o_ = q4.get()
            C.tt("pool", o_.t[:], src_ap, mk4[k].t[:], ALU.mult, [src_b, mk4[k].b], [o_.b])
            return o_

        def xpose4(X, eng):
            pb_ = tbank()
            for h in range(4):
                C.tr(pb_.t[:, h * 128:(h + 1) * 128], X.t[:, h, :], P.identb.t[:], [X.b, P.identb.b], [pb_.b])
            o_ = q4.get()
            C.cp(eng, o_.t[:].rearrange("p a b -> p (a b)"), pb_.t[:, 0:512], [pb_.b], [o_.b])
            return o_

        pref = {}

        def load_qin(it_, d):
            tok_ = slice(orders[d][it_] * 128, (orders[d][it_] + 1) * 128)
            qin_ = qinp.get()
            for k_, qi in enumerate(QIs[d]):
                C.dma(qin_.t[:, k_, :, :], sq[qi, :, tok_].rearrange("(ct p) t -> p ct t", p=128), [], [qin_.b])
            return qin_

        def pre(it0, box):
          its = [i_ for i_ in (it0, it0 + 1) if i_ < NC]
          quads = []
          for it_ in its:
            for d in range(2):
                ci = orders[d][it_]
                last_col = last_cols[d]
                tok = slice(ci * 128, (ci + 1) * 128)
                qin = pref.pop((it_, d)) if (it_, d) in pref else load_qin(it_, d)
                cts = []
                for ct in range(2):
                    q = (lambda ct_, qin_: (lambda k_: qin_.t[:, k_, ct_, :]))(ct, qin)
                    eX = ep.get(); eI = eIp.get(); eN = ep.get()
                    C.act(eX.t[:], q(6), AF.Exp, [qin.b], [eX.b])
                    C.act(eI.t[:], q(5), AF.Exp, [qin.b], [eI.b])
                    C.act(eN.t[:], q(5), AF.Exp, [qin.b], [eN.b], scale=-1.0)
                    BR = BRp.get(); AT = b16.get(); KT = b16.get(); AH = b16.get(); KH = b16.get(); vb = b16.get()
                    C.tt("dve", BR.t[:, 0:128], q(0), eX.t[:], ALU.mult, [qin.b, eX.b], [BR.b])
                    C.tt("pool", BR.t[:, 128:256], q(1), eI.t[:], ALU.mult, [qin.b, eI.b], [BR.b])
                    C.tt("dve", AT.t[:], q(4), eN.t[:], ALU.mult, [qin.b, eN.b], [AT.b])
                    C.tt("pool", KT.t[:], q(3), eN.t[:], ALU.mult, [qin.b, eN.b], [KT.b])
                    pc = eI.t[:, last_col:last_col + 1]
                    C.act(AH.t[:], AT.t[:], AF.Copy, [AT.b, eI.b], [AH.b], scale=pc)
                    C.act(KH.t[:], KT.t[:], AF.Copy, [KT.b, eI.b], [KH.b], scale=pc)
                    C.cp("pool", vb.t[:], q(2), [qin.b], [vb.b])
                    pt = tbank()
                    for k_, src in enumerate((AH, KH, vb)):
                        C.tr(pt.t[:, k_ * 128:(k_ + 1) * 128], src.t[:], P.identb.t[:], [src.b, P.identb.b], [pt.b])
                    tk = q4l.get()
                    C.cp("act" if d == 0 else "dve", tk.t[:, 0:3, :].rearrange("p a b -> p (a b)"), pt.t[:, 0:384], [pt.b], [tk.b])
                    cts.append(dict(BR=BR, AT=AT, KT=KT, eI=eI, tk=tk))
                quads.append(dict(d=d, it=it_, cts=cts, tok=tok, last_col=last_col, ev="act" if d == 0 else "dve"))
          if True:
            for itn in (it0 + 2, it0 + 3):
                if itn < NC:
                    for d in range(2):
                        pref[(itn, d)] = load_qin(itn, d)
            yield
            for Q in quads:
                d = Q["d"]
                M1 = m4.get(); M2 = m4.get()
                for (lk, Mq) in (("KT", M1), ("AT", M2)):
                    for hh in range(2):
                        ps = slice(64 * hh, 64 * hh + 64)
                        pg = bank()
                        for ct in range(2):
                            cd = Q["cts"][ct]
                            C.mm(pg.t[:, ct * 256:(ct + 1) * 256], cd[lk].t[ps, :], cd["BR"].t[ps, :], True, True,
                                 [cd[lk].b, cd["BR"].b], [pg.b])
                        C.tt("dve", Mq.t[:, 2 * hh:2 * hh + 2, :], pg.t[:, 0:512].rearrange("p (a b) -> p a b", a=2), gm2[d].t[:],
                             ALU.mult, [pg.b, gm2[d].b], [Mq.b])
                Nq = q4l.get()
                for hh in range(2):
                    ps = slice(64 * hh, 64 * hh + 64)
                    pg = bank()
                    for ct in range(2):
                        cd = Q["cts"][ct]
                        C.mm(pg.t[:, ct * 128:(ct + 1) * 128], cd["BR"].t[ps, 0:128], cd["AT"].t[ps, :], True, True,
                             [cd["AT"].b, cd["BR"].b], [pg.b])
                    C.tt("dve", Nq.t[:, 2 * hh:2 * hh + 2, :], pg.t[:, 0:256].rearrange("p (a b) -> p a b", a=2), nm4[d].t[:, 0:2, :],
                         ALU.mult, [pg.b, nm4[d].b], [Nq.b])
                Q.update(M1=M1, M2=M2, Nq=Nq)
            yield
            for Q in quads:
                Q["Nd"] = masked4(Q["Nq"].t[:], Q["Nq"].b, 0)
                Q["Ndt"] = masked4(Q["M2"].t[:, :, 0:128], Q["M2"].b, 0)
            yield
            for Q in quads:
                Q["Nd2"] = ev4(mm4(Q["Ndt"], Q["Nd"]), Q["ev"])
                Q["Ndt2"] = ev4(mm4(Q["Nd"], Q["Ndt"]), "act" if Q["ev"] == "dve" else "dve")
            yield
            for Q in quads:
                Q["Ndt4"] = ev4(mm4(Q["Nd2"], Q["Ndt2"]), Q["ev"])
                P1 = q4.get()
                C.tt("pool", P1.t[:], Q["Nd"].t[:], id4.t[:], ALU.add, [Q["Nd"].b, id4.b], [P1.b])
                Q["P1"] = P1
            yield
            for Q in quads:
                Q["P2"] = evadd4(mm4(Q["Ndt2"], Q["P1"]), Q["P1"])
            yield
            for Q in quads:
                Q["X"] = evadd4(mm4(Q["Ndt4"], Q["P2"]), Q["P2"])
            yield
            for k in (1, 2, 3):
                for Q in quads:
                    Q["Xt"] = xpose4(Q["X"], Q["ev"])
                    Q["Noff"] = masked4(Q["M2"].t[:, :, 0:128], Q["M2"].b, k)
                yield
                for Q in quads:
                    Q["W"] = ev4(mm4(Q["Noff"], Q["X"]), Q["ev"])
                yield
                for Q in quads:
                    Q["X"] = evadd4(mm4(Q["Xt"], Q["W"]), Q["X"])
                yield
            for Q in quads:
                Q["Xt"] = xpose4(Q["X"], Q["ev"])
                Q["Noff"] = masked4(Q["Nq"].t[:], Q["Nq"].b, 4)
            yield
            for Q in quads:
                Q["W"] = ev4(mm4(Q["Noff"], Q["Xt"]), Q["ev"])
            yield
            for Q in quads:
                pb_ = mm4(Q["X"], Q["W"])
                Tt = q4l.get()
                C.tt("dve", Tt.t[:].rearrange("p a b -> p (a b)"), pb_.t[:, 0:512], Q["Xt"].t[:].rearrange("p a b -> p (a b)"),
                     ALU.add, [pb_.b, Q["Xt"].b], [Tt.b])
                Q["Tt"] = Tt
            yield
            box.append((its, quads))

        def chain(its, allquads):
          for it_ in its:
            quads = [Q for Q in allquads if Q["it"] == it_]
            for Q in quads:
                d = Q["d"]
                Zs = z4.get()
                for hh in range(2):
                    ps = slice(64 * hh, 64 * hh + 64)
                    pz = bank()
                    for ct in range(2):
                        s_ = hh * 2 + ct
                        cd = Q["cts"][ct]
                        o = pz.t[:, ct * 64:(ct + 1) * 64]
                        C.mm(o, cd["BR"].t[ps, 0:128], Hbs[d][ct].t[ps, :], True, False, [cd["BR"].b, Hbs[d][ct].b], [pz.b])
                        C.mm(o, Q["M1"].t[:, s_, 0:128], cd["tk"].t[:, 2, ps], False, True, [Q["M1"].b, cd["tk"].b], [pz.b])
                    C.cp(Q["ev"], Zs.t[:, 2 * hh:2 * hh + 2, :].rearrange("p a b -> p (a b)"), pz.t[:, 0:128], [pz.b], [Zs.b])
                Q["Zs"] = Zs
            yield
            for Q in quads:
                pu = bank()
                for s_ in range(4):
                    C.mm(pu.t[:, s_ * 64:(s_ + 1) * 64], Q["Tt"].t[:, s_, :], Q["Zs"].t[:, s_, :], True, True,
                         [Q["Tt"].b, Q["Zs"].b], [pu.b])
                Us = z4.get()
                C.cp(Q["ev"], Us.t[:].rearrange("p a b -> p (a b)"), pu.t[:, 0:256], [pu.b], [Us.b])
                Q["Us"] = Us
            yield
            for Q in quads:
                d = Q["d"]
                ych = ychp.get()
                yv = ych.t[:].rearrange("p (ct hh v) -> p hh ct v", ct=2, hh=2)
                for hh in range(2):
                    ps = slice(64 * hh, 64 * hh + 64)
                    py = bank()
                    for ct in range(2):
                        s_ = hh * 2 + ct
                        cd = Q["cts"][ct]
                        o = py.t[:, ct * 64:(ct + 1) * 64]
                        C.mm(o, cd["BR"].t[ps, 128:256], Hbs[d][ct].t[ps, :], True, False, [cd["BR"].b, Hbs[d][ct].b], [py.b])
                        C.mm(o, Q["M2"].t[:, s_, 128:256], Q["Us"].t[:, s_, :], False, False, [Q["M2"].b, Q["Us"].b], [py.b])
                        C.mm(o, Q["M1"].t[:, s_, 128:256], cd["tk"].t[:, 2, ps], False, True, [Q["M1"].b, cd["tk"].b], [py.b])
                    C.cp("act", yv[:, hh], py.t[:, 0:128].rearrange("p (a b) -> p a b", a=2), [py.b], [ych.b])
                C.dma(Dr["yd"][d, Q["tok"], :], ych.t[:], [ych.b], [])
                ph = bank()
                for s_ in range(4):
                    hh, ct = s_ // 2, s_ % 2
                    cd = Q["cts"][ct]; ps = slice(64 * hh, 64 * hh + 64)
                    o = ph.t[:, s_ * 64:(s_ + 1) * 64]
                    C.mm(o, cd["tk"].t[:, 0, :], Q["Us"].t[:, s_, :], True, False, [cd["tk"].b, Q["Us"].b], [ph.b])
                    C.mm(o, cd["tk"].t[:, 1, :], cd["tk"].t[:, 2, ps], False, True, [cd["tk"].b], [ph.b])
                for s_ in range(4):
                    hh, ct = s_ // 2, s_ % 2
                    cd = Q["cts"][ct]; ps = slice(64 * hh, 64 * hh + 64)
                    Hh = Hs[d][ct]
                    C.stt("dve", Hh.t[ps, :], Hh.t[ps, :], cd["eI"].t[ps, Q["last_col"]:Q["last_col"] + 1],
                          ph.t[ps, s_ * 64:(s_ + 1) * 64], ALU.mult, ALU.add, [cd["eI"].b, ph.b], [Hh.b])
                for ct in range(2):
                    C.cp("act" if d == 0 else "dve", Hbs[d][ct].t[:], Hs[d][ct].t[:], [Hs[d][ct].b], [Hbs[d][ct].b])
            yield

        box = []
        interleave(pre(0, box))
        cur = box[0]
        for it0 in range(0, NC, 2):
            box = []
            interleave(chain(*cur), pre(it0 + 2, box) if it0 + 2 < NC else None)
            cur = box[0] if box else None
        S.barrier()


def rwkv_readout(P, LP, l):
    C, I, Dr, S = P.C, P.I, P.Dr, P.C.S
    pp = LP.pp1
    sq = Dr["sq"]
    NC = P.TT // 128
    GR = 4
    with ExitStack() as es:
        yp = C.pool_of(es, 4 * GR, [128, 256], F32, "yr")
        sqv = C.pool_of(es, 2 * GR, [128, 256], F32, "ysq")
        st = C.pool_of(es, 8 * GR, [128, 4], F32, "st")
        ynp = C.pool_of(es, 2 * GR, [128, 256], BF16, "yn")
        bgp = C.pool_of(es, 2 * GR, [128, 2, 2, 128], F32, "bg")
        op_ = C.pool_of(es, 4 * GR, [128, 128], F32, "o")
        obp = C.pool_of(es, 4 * GR, [128, 128], BF16, "ob")
        red = lambda o, i_: (lambda e: e.tensor_reduce(out=o, in_=i_, axis=AX.X, op=ALU.add))
        def load_group(c0_):
            G_ = []
            for ci in range(c0_, min(NC, c0_ + GR)):
                tok = slice(ci * 128, (ci + 1) * 128)
                g = dict(tok=tok, y0=yp.get(), y1=yp.get(), bg=bgp.get())
                C.dma(g["y0"].t[:], Dr["yd"][0, tok, :], [], [g["y0"].b])
                C.dma(g["y1"].t[:], Dr["yd"][1, tok, :], [], [g["y1"].b])
                for k_, qi in enumerate((11, 12)):
                    C.dma(g["bg"].t[:, k_, :, :], sq[qi, :, tok].rearrange("(ct p) t -> p ct t", p=128), [], [g["bg"].b])
                G_.append(g)
            return G_

        nxtG = load_group(0)
        for c0_ in range(0, NC, GR):
            G = nxtG
            nxtG = load_group(c0_ + GR) if c0_ + GR < NC else None
            for g in G:
                C.tt("dve", g["y0"].t[:], g["y0"].t[:], g["y1"].t[:], ALU.add, [g["y1"].b], [g["y0"].b])
            for g in G:
                g["ysq"] = sqv.get()
                C.tt("pool", g["ysq"].t[:], g["y0"].t[:], g["y0"].t[:], ALU.mult, [g["y0"].b], [g["ysq"].b])
                g["s1"] = st.get(); g["s2"] = st.get(); g["mu"] = st.get(); g["var"] = st.get()
                S.op("dve", red(g["s1"].t[:], g["y0"].t[:].rearrange("p (h j) -> p h j", j=64)), [g["y0"].b], [g["s1"].b])
            for g in G:
                S.op("dve", red(g["s2"].t[:], g["ysq"].t[:].rearrange("p (h j) -> p h j", j=64)), [g["ysq"].b], [g["s2"].b])
                C.ts("dve", g["mu"].t[:], g["s1"].t[:], 1.0 / 64, None, ALU.mult, None, [g["s1"].b], [g["mu"].b])
            for g in G:
                C.tt("dve", g["var"].t[:], g["mu"].t[:], g["mu"].t[:], ALU.mult, [g["mu"].b], [g["var"].b])
            for g in G:
                C.stt("dve", g["var"].t[:], g["s2"].t[:], 1.0 / 64, g["var"].t[:], ALU.mult, ALU.subtract, [g["s2"].b], [g["var"].b])
            for g in G:
                C.act(g["var"].t[:], g["var"].t[:], AF.Sqrt, [], [g["var"].b], bias=64e-5)
            for g in G:
                C.recip(g["var"].t[:], g["var"].t[:], [], [g["var"].b])
            for g in G:
                g["yn"] = ynp.get()
                for h4 in range(4):
                    hs = slice(h4 * 64, (h4 + 1) * 64)
                    C.ts("dve", g["yn"].t[:, hs], g["y0"].t[:, hs], g["mu"].t[:, h4:h4 + 1], g["var"].t[:, h4:h4 + 1],
                         ALU.subtract, ALU.mult, [g["y0"].b, g["mu"].b, g["var"].b], [g["yn"].b])
            for gi, g in enumerate(G):
                pt = P.PT[gi % 2]
                for ct in range(2):
                    C.tr(pt.t[:, ct * 128:(ct + 1) * 128], g["yn"].t[:, ct * 128:(ct + 1) * 128], P.identb.t[:],
                         [g["yn"].b, P.identb.b], [pt.b])
                g["o"] = []
                for ct in range(2):
                    o = op_.get()
                    C.act(o.t[:], pt.t[:, ct * 128:(ct + 1) * 128], AF.Identity, [pt.b, pp.b], [o.b],
                          scale=pp.t[:, C_GNW + ct:C_GNW + ct + 1], bias=pp.t[:, C_GNB + ct:C_GNB + ct + 1])
                    g["o"].append(o)
            for g in G:
                for ct in range(2):
                    o = g["o"][ct]; ob = obp.get()
                    C.tt("dve", o.t[:], o.t[:], g["bg"].t[:, 0, ct, :], ALU.add, [g["bg"].b], [o.b])
                    C.tt("pool", ob.t[:], o.t[:], g["bg"].t[:, 1, ct, :], ALU.mult, [o.b, g["bg"].b], [ob.b])
                    C.dma(Dr["yT"][ct * 128:(ct + 1) * 128, g["tok"]], ob.t[:], [ob.b], [])
        S.barrier()


def phase_pool(P, LP, l):
    C, I, Dr = P.C, P.I, P.Dr
    with ExitStack() as es:
        pm = C.sb(es, [128, 20, 128], BF16, "poolm")
        C.dma(pm.t[:], I["poolm"].rearrange("g v s t -> s (g v) t"), [], [pm.b])
        pwf = C.sb(es, [128, 2, 128], F32, "pwf")
        C.memset("pool", pwf.t[:], 0.0, [pwf.b])
        for g in range(4):
            ct, gl = g // 2, g % 2
            C.dma(pwf.t[gl * 64:(gl + 1) * 64, ct, gl * 64:(gl + 1) * 64], I["pool_w"][l, g], [], [pwf.b])
        pwb = C.sb(es, [128, 2, 128], BF16, "pwb")
        C.cp("dve", pwb.t[:], pwf.t[:], [pwf.b], [pwb.b])
        uT = [C.sb(es, [128, P.T], BF16, "uTp") for _ in range(2)]
        z = C.sb(es, [128, P.T // 128, 256], BF16, "z")
        yop = C.pool_of(es, 3, [128, 512], BF16, "yo")
        pbi = 0
        for (sn, Ts, c0, m) in streams(P):
            nT = Ts // 128
            for ct in range(2):
                C.dma(uT[ct].t[:, 0:Ts], Dr["uM"][ct * 128:(ct + 1) * 128, c0:c0 + Ts], [], [uT[ct].b])
            for ti in range(nT):
                pb = P.PB[pbi % 6]; pbi += 1
                for ct in range(2):
                    C.mm(pb.t[:, ct * 128:(ct + 1) * 128], uT[ct].t[:, ti * 128:(ti + 1) * 128], pwb.t[:, ct, :],
                         True, True, [uT[ct].b, pwb.b], [pb.b])
                C.evac(z.t[:, ti, :], pb.t[:, 0:256], [pb.b], [Acc(z.b)])
            NB = min(512, Ts)
            for ct in range(2):
                for blk in range(Ts // NB):
                    yo = yop.get()
                    for tj in range(NB // 128):
                        ti = blk * (NB // 128) + tj
                        for gl in range(2):
                            g = 2 * ct + gl
                            pb = P.PB[pbi % 6]; pbi += 1
                            sis = [si for si in (ti - 1, ti, ti + 1) if 0 <= si < nT]
                            for n_, si in enumerate(sis):
                                if si == ti - 1:
                                    v = 0
                                elif si == ti + 1:
                                    v = 2
                                else:
                                    v = 3 if ti == 0 else (4 if ti == nT - 1 else 1)
                                C.mm(pb.t[:, 0:128], z.t[:, si, ct * 128:(ct + 1) * 128], pm.t[:, g * 5 + v, :],
                                     n_ == 0, n_ == len(sis) - 1, [z.b, pm.b], [pb.b])
                            ps = slice(gl * 64, (gl + 1) * 64)
                            C.act(yo.t[ps, tj * 128:(tj + 1) * 128], pb.t[ps, 0:128], AF.Copy, [pb.b, LP.pp1.b], [yo.b],
                                  scale=LP.pp1.t[ps, C_PSC + ct:C_PSC + ct + 1])
                    C.dma(Dr["yT"][256 + ct * 128:256 + (ct + 1) * 128, c0 + blk * NB:c0 + (blk + 1) * NB],
                          yo.t[:, 0:NB], [yo.b], [])
        P.C.S.barrier()


def phase_fourier(P, LP, l):
    C, I, Dr = P.C, P.I, P.Dr
    with ExitStack() as es:
        cc3 = C.sb(es, [128, 384], BF16, "cc3")
        C.dma(cc3.t[:], I["cc3"], [], [cc3.b])
        stg = C.pool_of(es, 2, [128, 1024], F32, "stg")
        fwb = C.sb(es, [128, 2, 256], BF16, "fwb")
        load_cast(C, es, fwb, I["fourier_w"][l], 256, stg)
        Rmax = P.T // 64
        uTp = C.pool_of(es, 2, [128, max(P.T, 2048)], BF16, "uTf")
        Z = C.sb(es, [128, 128, 3, 64], BF16, "Z")
        Pq = C.sb(es, [128, Rmax, 128], BF16, "Pq")
        gm = C.sb(es, [128, Rmax, 64], BF16, "gmf")
        wr = C.sb(es, [128, 2, Rmax], BF16, "wrf")
        fT = [C.sb(es, [128, P.T], BF16, "fT") for _ in range(2)]
        yop = C.pool_of(es, 3, [128, 512], BF16, "yo")
        pbi = 0
        items = [(sn, Ts, c0, ct) for (sn, Ts, c0, m) in streams(P) for ct in range(2)]

        def load_uT(itm):
            sn_, Ts_, c0_, ct_ = itm
            R_ = Ts_ // 64
            Rp_ = max(R_, 32)
            u_ = uTp.get()
            if Rp_ > R_:
                C.memset("pool", u_.t[:, Ts_:Rp_ * 64], 0.0, [u_.b])
            C.dma(u_.t[:, 0:Ts_], Dr["uM"][256 + ct_ * 128:256 + (ct_ + 1) * 128, c0_:c0_ + Ts_], [], [u_.b])
            return u_

        nxt_u = load_uT(items[0])
        ii = 0
        for (sn, Ts, c0, m) in streams(P):
            R = Ts // 64
            C.dma(gm.t[:, 0:R, :], I["gm_x" if sn == "x" else "gm_c"], [], [gm.b])
            C.dma(wr.t[:, :, 0:R], I["wr_x" if sn == "x" else "wr_c"], [], [wr.b])
            for ct in range(2):
                Rp = max(R, 32)
                uT = nxt_u
                ii += 1
                nxt_u = load_uT(items[ii]) if ii < len(items) else None
                uv = uT.t[:, 0:Rp * 64].rearrange("p (t1 t2) -> p t2 t1", t2=64)
                for t2 in range(64):
                    pb = P.PB[pbi % 6]; pbi += 1
                    C.mm(pb.t[0:Rp, 0:384], uv[:, t2, :], cc3.t[:], True, True, [uT.b, cc3.b], [pb.b])
                    C.evac(Z.t[0:Rp, :, :, t2].rearrange("p q r -> p r q"), pb.t[0:Rp, 0:384].rearrange("p (r q) -> p r q", r=3),
                           [pb.b], [Acc(Z.b)])
                QB = min(128, 512 // R)
                for q0 in range(0, 128, QB):
                    pb = P.PB[pbi % 6]; pbi += 1
                    for qi in range(QB):
                        q = q0 + qi
                        o = pb.t[:, qi * R:(qi + 1) * R]
                        C.mm(o, Z.t[0:Rp, q, 0:2, :].rearrange("p r t -> p (r t)"), wr.t[0:Rp, 0, 0:R], True, False, [Z.b, wr.b], [pb.b])
                        C.mm(o, Z.t[0:Rp, q, 1:3, :].rearrange("p r t -> p (r t)"), wr.t[0:Rp, 1, 0:R], False, True, [Z.b, wr.b], [pb.b])
                    C.evac(Pq.t[:, 0:R, q0:q0 + QB], pb.t[:, 0:QB * R].rearrange("p (q k) -> p k q", k=R),
                           [pb.b], [Acc(Pq.b)])
                KB = min(8, R)
                fv = fT[ct].t[:, 0:Ts].rearrange("p (k2 k1) -> p k2 k1", k1=R)
                for k0 in range(0, R, KB):
                    pb = P.PB[pbi % 6]; pbi += 1
                    for ki in range(KB):
                        C.mm(pb.t[:, ki * 64:(ki + 1) * 64], Pq.t[:, k0 + ki, :], gm.t[:, k0 + ki, :], True, True,
                             [Pq.b, gm.b], [pb.b])
                    C.evac(fv[:, :, k0:k0 + KB], pb.t[:, 0:KB * 64].rearrange("p (k1 k2) -> p k2 k1", k2=64),
                           [pb.b], [Acc(fT[ct].b)])
            NB = min(512, Ts)
            for blk in range(Ts // NB):
                for nt in range(2):
                    pb = P.PB[pbi % 6]; pbi += 1
                    for ct in range(2):
                        C.mm(pb.t[:, 0:NB], fwb.t[:, ct, nt * 128:(nt + 1) * 128], fT[ct].t[:, blk * NB:(blk + 1) * NB],
                             ct == 0, ct == 1, [fwb.b, fT[ct].b], [pb.b])
                    yo = yop.get()
                    C.evac(yo.t[:, 0:NB], pb.t[:, 0:NB], [pb.b], [yo.b])
                    C.dma(Dr["yT"][512 + nt * 128:512 + (nt + 1) * 128, c0 + blk * NB:c0 + (blk + 1) * NB],
                          yo.t[:, 0:NB], [yo.b], [])
        P.C.S.barrier()


def phase_conv(P, LP, l):
    C, I, Dr = P.C, P.I, P.Dr
    onesf = P.misc.t[:, 1, :]
    identf = P.misc.t[:, 0, :]
    with ExitStack() as es:
        stg = C.pool_of(es, 2, [128, 1024], F32, "stg")
        pwb = C.sb(es, [128, 2, 256], BF16, "cpw")
        load_cast(C, es, pwb, I["conv_pw"][l], 256, stg)
        dg = C.sb(es, [128, 62, 128], BF16, "dg")
        for jj in range(62):
            C.ts("dve" if jj % 2 else "act", dg.t[:, jj, :], identf, LP.pp2.t[:, jj:jj + 1], None, ALU.mult, None,
                 [P.misc.b, LP.pp2.b], [dg.b]) if jj % 2 else \
                C.act(dg.t[:, jj, :], identf, AF.Copy, [P.misc.b, LP.pp2.b], [dg.b], scale=LP.pp2.t[:, jj:jj + 1])
        hT = [C.sb(es, [128, P.T + 30], BF16, "hT") for _ in range(2)]
        abp = C.pool_of(es, 4, [128, 512], BF16, "ab")
        sgp = C.pool_of(es, 2, [128, 512], F32, "sgc")
        co = [C.pool_of(es, 2, [128, 512], F32, "co") for _ in range(2)]
        sqp = [C.pool_of(es, 2, [128, 512], F32, "sqc") for _ in range(2)]
        mean = C.sb(es, [128, 512], F32, "mean")
        var = C.sb(es, [128, 512], F32, "var")
        dp = C.pool_of(es, 2, [128, 512], F32, "dcv")
        sT = [C.pool_of(es, 2, [128, 512], BF16, "sTc") for _ in range(2)]
        yop = C.pool_of(es, 3, [128, 512], BF16, "yo")
        pbi = 0
        for (sn, Ts, c0, m) in streams(P):
            NB = min(512, Ts)
            for ct in range(2):
                C.memset("pool", hT[ct].t[:, 0:15], 0.0, [hT[ct].b])
                C.memset("pool", hT[ct].t[:, 15 + Ts:30 + Ts], 0.0, [hT[ct].b])
            for blk in range(Ts // NB):
                cs = slice(c0 + blk * NB, c0 + (blk + 1) * NB)
                for ct in range(2):
                    a = abp.get(); b = abp.get()
                    C.dma(a.t[:, 0:NB], Dr["uM"][512 + ct * 128:512 + (ct + 1) * 128, cs], [], [a.b])
                    C.dma(b.t[:, 0:NB], Dr["uM"][768 + ct * 128:768 + (ct + 1) * 128, cs], [], [b.b])
                    sg = sgp.get()
                    C.act(sg.t[:, 0:NB], b.t[:, 0:NB], AF.Sigmoid, [b.b], [sg.b])
                    C.tt("dve", hT[ct].t[:, 15 + blk * NB:15 + (blk + 1) * NB], a.t[:, 0:NB], sg.t[:, 0:NB], ALU.mult,
                         [a.b, sg.b], [hT[ct].b])
            for blk in range(Ts // NB):
                cot, sqt = [], []
                for ct in range(2):
                    pb = P.PB[pbi % 6]; pbi += 1
                    for j in range(31):
                        C.mm(pb.t[:, 0:NB], dg.t[:, j * 2 + ct, :], hT[ct].t[:, blk * NB + j:blk * NB + j + NB],
                             j == 0, j == 30, [dg.b, hT[ct].b], [pb.b])
                    c_ = co[ct].get(); q_ = sqp[ct].get()
                    C.act(c_.t[:, 0:NB], pb.t[:, 0:NB], AF.Identity, [pb.b, LP.pp1.b], [c_.b],
                          bias=LP.pp1.t[:, C_DWB + ct:C_DWB + ct + 1])
                    C.tt("pool", q_.t[:, 0:NB], c_.t[:, 0:NB], c_.t[:, 0:NB], ALU.mult, [c_.b], [q_.b])
                    cot.append(c_); sqt.append(q_)
                pm_ = P.PB[pbi % 6]; pbi += 1
                pq_ = P.PB[pbi % 6]; pbi += 1
                for ct in range(2):
                    C.mm(pm_.t[:, 0:NB], onesf, cot[ct].t[:, 0:NB], ct == 0, ct == 1, [P.misc.b, cot[ct].b], [pm_.b])
                for ct in range(2):
                    C.mm(pq_.t[:, 0:NB], onesf, sqt[ct].t[:, 0:NB], ct == 0, ct == 1, [P.misc.b, sqt[ct].b], [pq_.b])
                C.act(mean.t[:, 0:NB], pm_.t[:, 0:NB], AF.Copy, [pm_.b], [mean.b], scale=1.0 / 256)
                C.tt("dve", var.t[:, 0:NB], mean.t[:, 0:NB], mean.t[:, 0:NB], ALU.mult, [mean.b], [var.b])
                C.stt("dve", var.t[:, 0:NB], pq_.t[:, 0:NB], 1.0 / 256, var.t[:, 0:NB], ALU.mult, ALU.subtract,
                      [pq_.b], [var.b])
                C.act(var.t[:, 0:NB], var.t[:, 0:NB], AF.Sqrt, [], [var.b], bias=1e-5)
                C.recip(var.t[:, 0:NB], var.t[:, 0:NB], [], [var.b])
                sts = []
                for ct in range(2):
                    d_ = dp.get()
                    C.tt("dve", d_.t[:, 0:NB], cot[ct].t[:, 0:NB], mean.t[:, 0:NB], ALU.subtract, [cot[ct].b, mean.b], [d_.b])
                    C.tt("pool", d_.t[:, 0:NB], d_.t[:, 0:NB], var.t[:, 0:NB], ALU.mult, [var.b], [d_.b])
                    s_ = sT[ct].get()
                    C.act(s_.t[:, 0:NB], d_.t[:, 0:NB], AF.Silu, [d_.b, LP.pp1.b], [s_.b],
                          scale=LP.pp1.t[:, C_LNG + ct:C_LNG + ct + 1], bias=LP.pp1.t[:, C_LNB + ct:C_LNB + ct + 1])
                    sts.append(s_)
                for nt in range(2):
                    pb = P.PB[pbi % 6]; pbi += 1
                    for ct in range(2):
                        C.mm(pb.t[:, 0:NB], pwb.t[:, ct, nt * 128:(nt + 1) * 128], sts[ct].t[:, 0:NB], ct == 0, ct == 1,
                             [pwb.b, sts[ct].b], [pb.b])
                    yo = yop.get()
                    C.evac(yo.t[:, 0:NB], pb.t[:, 0:NB], [pb.b], [yo.b])
                    C.dma(Dr["yT"][768 + nt * 128:768 + (nt + 1) * 128, c0 + blk * NB:c0 + (blk + 1) * NB],
                          yo.t[:, 0:NB], [yo.b], [])
        P.C.S.barrier()


def phase_C1(P, LP, l, last=False):
    C, I, Dr = P.C, P.I, P.Dr
    with ExitStack() as es:
        woutb = C.sb(es, [128, 8, D], BF16, "woutb")
        stg = C.pool_of(es, 3, [128, 1024], F32, "stg")
        load_cast(C, es, woutb, I["w_out"][l], D, stg)
        htp = C.pool_of(es, 9, [128, D], F32, "ht")
        ssp = C.pool_of(es, 8, [128, 1], F32, "ss")
        rsp = C.pool_of(es, 8, [128, 1], F32, "rs")
        xsp = C.pool_of(es, 8, [128, D], BF16, "xs")
        xnp = C.pool_of(es, 2, [128, 8, 512], BF16, "xnT")
        ytp = C.pool_of(es, 2, [128, 8, 512], BF16, "ytb")
        tmp = C.pool_of(es, 4, [128, 512], F32, "tmp")
        gt = C.sb(es, [128, D], F32, "gt")
        pbi = 0
        blocks = []
        for (sn, Ts, c0, m) in streams(P):
            if last and sn == "c":
                continue
            NB = min(512, Ts)
            for blk in range(Ts // NB):
                blocks.append((sn, Ts, c0, m, NB, blk))

        def load_blk(bd):
            sn, Ts, c0, m, NB, blk = bd
            hsrc = Dr["hx"] if sn == "x" else (I["ctx"] if l == 0 else Dr["hc"])
            cs = slice(c0 + blk * NB, c0 + blk * NB + NB)
            yt = ytp.get()
            C.dma(yt.t[:, :, 0:NB], Dr["yT"][:, cs].rearrange("(kc p) t -> p kc t", p=128), [], [yt.b])
            hts = []
            for j in range(NB // 128):
                t0 = blk * NB + j * 128
                ht = htp.get()
                C.dma(ht.t[:], hsrc[t0:t0 + 128, :], [], [ht.b])
                hts.append(ht)
            return yt, hts

        cur_m = None
        nxt = load_blk(blocks[0])
        for bi, bd in enumerate(blocks):
            sn, Ts, c0, m, NB, blk = bd
            if m != cur_m:
                C.dma(gt.t[:], Dr["gtb"][m * 2 + 0], [], [gt.b])
                cur_m = m
            hdst = Dr["hx"] if sn == "x" else Dr["hc"]
            cs = slice(c0 + blk * NB, c0 + blk * NB + NB)
            yt, hts = nxt
            nxt = load_blk(blocks[bi + 1]) if bi + 1 < len(blocks) else None
            xnT = xnp.get()
            for j in range(NB // 128):
                ht = hts[j]
                t0 = blk * NB + j * 128
                for nh in range(2):
                    pb = P.PB[pbi % 6]; pbi += 1
                    for kc in range(8):
                        C.mm(pb.t[:], yt.t[:, kc, j * 128:(j + 1) * 128], woutb.t[:, kc, nh * 512:(nh + 1) * 512],
                             kc == 0, kc == 7, [yt.b, woutb.b], [pb.b])
                    tm = tmp.get()
                    C.tt("dve", tm.t[:], pb.t[:], gt.t[:, nh * 512:(nh + 1) * 512], ALU.mult, [pb.b, gt.b], [tm.b])
                    C.tt("pool", ht.t[:, nh * 512:(nh + 1) * 512], ht.t[:, nh * 512:(nh + 1) * 512], tm.t[:],
                         ALU.add, [tm.b], [ht.b])
                C.dma(hdst[t0:t0 + 128, :], ht.t[:], [ht.b], [])
            norm_block(P, C, hts, m, LP.gm2, 24, LP, xnT, ssp, rsp, xsp)
            C.dma(Dr["xn2"][:, cs].rearrange("(kc p) t -> p kc t", p=128), xnT.t[:, :, 0:NB], [xnT.b], [])
        P.C.S.barrier()


def phase_C2(P, LP, l, last):
    C, I, Dr = P.C, P.I, P.Dr
    NB = 512
    with ExitStack() as es:
        w1b = C.sb(es, [128, 8, 2 * DFF], BF16, "w1b")
        w2b = C.sb(es, [128, 22, D], BF16, "w2b")
        with ExitStack() as es_s:
            stg = C.pool_of(es_s, 3, [128, 1024], F32, "stg")
            load_cast(C, es_s, w1b, I["ffn_w_in"][l], 2 * DFF, stg)
            load_cast(C, es_s, w2b, I["ffn_w_out"][l], D, stg)
            P.C.S.barrier()
        htp = C.pool_of(es, 2, [128, D], F32, "ht")
        xnp = C.pool_of(es, 2, [128, 8, 512], BF16, "xn2")
        aT = C.sb(es, [128, 22, 512], BF16, "aT")
        sgp = C.pool_of(es, 2, [128, 512], F32, "sg")
        tmp = C.pool_of(es, 2, [128, 512], F32, "tmp")
        gt = C.sb(es, [128, D], F32, "gt")
        ssp = C.pool_of(es, 2, [128, 1], F32, "ss")
        rsp = C.pool_of(es, 2, [128, 1], F32, "rs")
        if last:
            fgb = C.sb(es, [128, D], F32, "fgb")
            C.dma(fgb.t[:], I["final_g"].partition_broadcast(128), [], [fgb.b])
        pbi = 0
        for (sn, Ts, c0, m) in streams(P):
            if last and sn == "c":
                continue
            C.dma(gt.t[:], Dr["gtb"][m * 2 + 1], [], [gt.b])
            hbuf = Dr["hx"] if sn == "x" else Dr["hc"]
            NB = min(512, Ts)
            def load_xn(blk_):
                cs_ = slice(c0 + blk_ * NB, c0 + blk_ * NB + NB)
                xn_ = xnp.get()
                C.dma(xn_.t[:, :, 0:NB], Dr["xn2"][:, cs_].rearrange("(kc p) t -> p kc t", p=128), [], [xn_.b])
                return xn_

            nxt_xn = load_xn(0)
            for blk in range(Ts // NB):
                cs = slice(c0 + blk * NB, c0 + blk * NB + NB)
                xn = nxt_xn
                for fb in range(22):
                    pg = P.PB[pbi % 6]; pbi += 1
                    pu = P.PB[pbi % 6]; pbi += 1
                    for dc in range(8):
                        C.mm(pg.t[:, 0:NB], w1b.t[:, dc, fb * 128:(fb + 1) * 128], xn.t[:, dc, 0:NB], dc == 0, dc == 7,
                             [w1b.b, xn.b], [pg.b])
                    for dc in range(8):
                        C.mm(pu.t[:, 0:NB], w1b.t[:, dc, DFF + fb * 128:DFF + (fb + 1) * 128], xn.t[:, dc, 0:NB],
                             dc == 0, dc == 7, [w1b.b, xn.b], [pu.b])
                    sg = sgp.get()
                    C.act(sg.t[:, 0:NB], pg.t[:, 0:NB], AF.Silu, [pg.b], [sg.b])
                    C.tt("dve", aT.t[:, fb, 0:NB], sg.t[:, 0:NB], pu.t[:, 0:NB], ALU.mult, [sg.b, pu.b], [aT.b])
                nxt_xn = load_xn(blk + 1) if blk + 1 < Ts // NB else None
                for j in range(NB // 128):
                    t0 = blk * NB + j * 128
                    ht = htp.get()
                    C.dma(ht.t[:], hbuf[t0:t0 + 128, :], [], [ht.b])
                    for nh in range(2):
                        pb = P.PB[pbi % 6]; pbi += 1
                        for fb in range(22):
                            C.mm(pb.t[:], aT.t[:, fb, j * 128:(j + 1) * 128], w2b.t[:, fb, nh * 512:(nh + 1) * 512],
                                 fb == 0, fb == 21, [aT.b, w2b.b], [pb.b])
                        tm = tmp.get()
                        C.tt("dve", tm.t[:], pb.t[:], gt.t[:, nh * 512:(nh + 1) * 512], ALU.mult, [pb.b, gt.b], [tm.b])
                        C.tt("pool", ht.t[:, nh * 512:(nh + 1) * 512], ht.t[:, nh * 512:(nh + 1) * 512], tm.t[:],
                             ALU.add, [tm.b], [ht.b])
                    if last:
                        ss = ssp.get(); rs = rsp.get()
                        tm = tmp.get(); tm2 = tmp.get()
                        C.act(tm.t[:], ht.t[:, 0:512], AF.Square, [ht.b], [tm.b, ss.b], accum=ss.t[:])
                        C.act(tm.t[:], ht.t[:, 512:1024], AF.Square, [ht.b], [tm.b, rs.b], accum=rs.t[:])
                        C.tt("dve", ss.t[:], ss.t[:], rs.t[:], ALU.add, [rs.b], [ss.b])
                        C.act(rs.t[:], ss.t[:], AF.Sqrt, [ss.b], [rs.b], scale=1.0 / D, bias=1e-6)
                        C.recip(rs.t[:], rs.t[:], [rs.b], [rs.b])
                        C.stt("dve", ht.t[:], ht.t[:], rs.t[:], fgb.t[:], ALU.mult, ALU.mult, [rs.b, fgb.b], [ht.b])
                        C.dma(P.out[t0:t0 + 128, :], ht.t[:], [ht.b], [])
                    else:
                        C.dma(hbuf[t0:t0 + 128, :], ht.t[:], [ht.b], [])
        P.C.S.barrier()


def make_in_map(inp, b, T, TC, L):
    f = lambda a: np.ascontiguousarray(np.asarray(a, dtype=np.float32))
    m = {}
    m["x"] = f(inp["x"][b]); m["ctx"] = f(inp["ctx"][b])
    m["cvec"] = f(np.stack([np.asarray(inp["c"][b]), np.asarray(inp["c_ctx"])], axis=0))
    m["w_mod"] = f(inp["w_mod"]); m["b_mod"] = f(inp["b_mod"])
    m["norm1_g"] = f(inp["norm1_g"]); m["norm2_g"] = f(inp["norm2_g"])
    m["w_in"] = f(inp["w_in"]); m["w_out"] = f(inp["w_out"])
    m["mu_prev"] = f(inp["rwkv_mu_prev"]); m["mu_next"] = f(inp["rwkv_mu_next"])
    m["w0"] = f(np.asarray(inp["rwkv_w0"]).reshape(L, 512)); m["w2"] = f(np.asarray(inp["rwkv_w2"]).reshape(L, 128, 256))
    m["a0"] = f(np.asarray(inp["rwkv_a0"]).reshape(L, 512)); m["a2"] = f(np.asarray(inp["rwkv_a2"]).reshape(L, 128, 256))
    m["g2"] = f(inp["rwkv_g2"])
    m["k_k"] = f(inp["rwkv_k_k"]); m["k_a"] = f(inp["rwkv_k_a"])
    m["r_k"] = f(np.asarray(inp["rwkv_r_k"]).reshape(L, 256))
    m["gn_w"] = f(inp["rwkv_gn_w"]); m["gn_b"] = f(inp["rwkv_gn_b"])
    m["pool_scale"] = f(inp["pool_scale"]); m["dw_b"] = f(inp["conv_dw_b"])
    m["ln_g"] = f(inp["conv_ln_g"]); m["ln_b"] = f(inp["conv_ln_b"])
    m["pool_w"] = f(inp["pool_w"]); m["fourier_w"] = f(inp["fourier_w"])
    m["dw_w"] = f(inp["conv_dw_w"]); m["conv_pw"] = f(inp["conv_pw"])
    m["ffn_w_in"] = f(inp["ffn_w_in"]); m["ffn_w_out"] = f(inp["ffn_w_out"])
    m["final_g"] = f(inp["final_norm_g"])
    return m


_CONST_CACHE = {}


def const_map(T, TC):
    key = (T, TC)
    if key not in _CONST_CACHE:
        c = {}
        c["misc"] = _misc_consts()
        c["poolm"] = _bf(_pool_mats())
        cc3, wr_x, gm_x = _fourier_consts(T)
        _, wr_c, gm_c = _fourier_consts(TC)
        c["cc3"] = cc3; c["wr_x"] = wr_x; c["gm_x"] = gm_x; c["wr_c"] = wr_c; c["gm_c"] = gm_c
        pr, pc = _pos_tables(T)
        c["pos_row"] = pr; c["pos_col"] = pc
        sel = np.zeros((2, 2, 128), np.float32)
        sel[0, 0, :] = 1.0
        sel[1, 1, :] = 1.0
        c["sel"] = sel
        _CONST_CACHE[key] = c
    return _CONST_CACHE[key]


def kernel(**inp):
    x = np.asarray(inp["x"])
    B, T, _ = x.shape
    TC = np.asarray(inp["ctx"]).shape[1]
    L = np.asarray(inp["w_in"]).shape[0]
    nc = build_program(T, TC, L)
    cm = const_map(T, TC)
    in_maps = []
    for b in range(B):
        m = make_in_map(inp, b, T, TC, L)
        m.update(cm)
        in_maps.append(m)
    res = run_bass_kernel_spmd(nc, in_maps, core_ids=list(range(B)))
    return np.stack([np.asarray(r["out"], dtype=np.float32) for r in res.results], axis=0)
```

```python
import numpy as np
import ml_dtypes
from contextlib import ExitStack
import concourse.bass as bass
import concourse.mybir as mybir
from concourse.bass_utils import run_bass_kernel_spmd

F32 = mybir.dt.float32
BF16 = mybir.dt.bfloat16
AF = mybir.ActivationFunctionType
ALU = mybir.AluOpType
AX = mybir.AxisListType

D = 1024
NIN = 2176
DFF = 2816
GW = 256
RW = 1152
N_DMA_SEMS = 12
NQ = {"sp": N_DMA_SEMS}


class Buf:
    __slots__ = ("w", "r", "x")

    def __init__(self):
        self.w = {}
        self.r = {}
        self.x = False


class Acc:
    __slots__ = ("b",)

    def __init__(self, b):
        self.b = b


class Sched:
    def __init__(self, nc, comp, dma):
        self.nc = nc
        self.streams = {k: [] for k in ("pe", "act", "dve", "pool", "sp")}
        self.sems = dict(comp)
        self.cnt = {k: 0 for k in comp}
        self.seen = {k: {} for k in self.streams}
        self.dma_sems = dma
        self.dma_cnt = {q: 0 for q in dma}

    def _need(self, eng, deps):
        st = self.streams[eng]
        seen = self.seen[eng]
        for key, val in deps.items():
            if eng == "pe" and key == "pe":
                continue
            if seen.get(key, 0) >= val:
                continue
            seen[key] = val
            st.append(("wait", key, val))

    def _sem_of(self, key):
        if isinstance(key, str):
            return self.sems[key]
        return self.dma_sems[key[0]][key[1]]

    @staticmethod
    def _collect(reads, writes, eng=None):
        deps = {}
        for b in reads:
            for k, v in b.w.items():
                if deps.get(k, 0) < v:
                    deps[k] = v
            if b.x:
                for k, v in b.r.items():
                    if k != eng and deps.get(k, 0) < v:
                        deps[k] = v
        for b in writes:
            if isinstance(b, Acc):
                for k, v in b.b.r.items():
                    if deps.get(k, 0) < v:
                        deps[k] = v
                continue
            for k, v in b.w.items():
                if deps.get(k, 0) < v:
                    deps[k] = v
            for k, v in b.r.items():
                if deps.get(k, 0) < v:
                    deps[k] = v
        return deps

    @staticmethod
    def _record(key, val, reads, writes):
        for b in reads:
            if b.r.get(key, 0) < val:
                b.r[key] = val
        for b in writes:
            if isinstance(b, Acc):
                if b.b.w.get(key, 0) < val:
                    b.b.w[key] = val
                continue
            b.w = {key: val}
            b.r = {}

    def op(self, eng, fn, reads=(), writes=()):
        deps = self._collect(reads, writes, eng)
        self._need(eng, deps)
        self.cnt[eng] += 1
        val = self.cnt[eng]
        self.streams[eng].append(("op", fn, eng))
        self._record(eng, val, reads, writes)

    def dma(self, out, in_, reads=(), writes=(), q="sp"):
        deps = self._collect(reads, writes)
        i = self.dma_cnt[q]
        self.dma_cnt[q] += 1
        slot = i % N_DMA_SEMS
        rnd = i // N_DMA_SEMS
        key = (q, slot)
        if rnd > 0:
            deps[key] = max(deps.get(key, 0), 16 * rnd)
        self._need(q, deps)
        self.streams[q].append(("dma", out, in_, key))
        self._record(key, 16 * (rnd + 1), reads, writes)

    def barrier(self):
        deps = {k: v for k, v in self.cnt.items() if v > 0}
        for q, n in self.dma_cnt.items():
            for slot in range(min(n, N_DMA_SEMS)):
                rounds = (n - 1 - slot) // N_DMA_SEMS + 1
                deps[(q, slot)] = 16 * rounds
        for eng in self.streams:
            d = dict(deps)
            self._need(eng, d)

    def emit(self):
        nc = self.nc
        streams = self.streams
        sem_of = self._sem_of
        sems = self.sems

        def run(name, e):
            for it in streams[name]:
                if it[0] == "wait":
                    e.wait_ge(sem_of(it[1]), it[2])
                elif it[0] == "op":
                    it[1](e).then_inc(sems[it[2]], 1)
                else:
                    e.dma_start(out=it[1], in_=it[2]).then_inc(sem_of(it[3]), 16)

        with nc.Block() as block:
            @block.tensor
            def _(e):
                run("pe", e)

            @block.scalar
            def _(e):
                run("act", e)

            @block.vector
            def _(e):
                run("dve", e)

            @block.gpsimd
            def _(e):
                run("pool", e)

            @block.sync
            def _(e):
                run("sp", e)


class Tl:
    __slots__ = ("t", "b")

    def __init__(self, t):
        self.t = t
        self.b = Buf()


class Ctx:
    def __init__(self, nc, S):
        self.nc = nc
        self.S = S
        self.uid = 0
        self.rr = 0

    def sb(self, es, shape, dt, name="t"):
        self.uid += 1
        return Tl(es.enter_context(self.nc.sbuf_tensor(f"{name}_{self.uid}", list(shape), dt)))

    def pool_of(self, es, n, shape, dt, name="p"):
        return Rot([self.sb(es, shape, dt, name) for _ in range(n)])

    def mm(self, out, lhsT, rhs, start, stop, reads, writes):
        self.S.op("pe", lambda e: e.matmul(out, lhsT=lhsT, rhs=rhs, start=start, stop=stop), reads, writes)

    def tr(self, out, in_, ident, reads, writes):
        self.S.op("pe", lambda e: e.transpose(out=out, in_=in_, identity=ident), reads, writes)

    def act(self, out, in_, func, reads, writes, scale=1.0, bias=0.0, accum=None):
        if accum is None:
            self.S.op("act", lambda e: e.activation(out=out, in_=in_, func=func, scale=scale, bias=bias), reads, writes)
        else:
            self.S.op("act", lambda e: e.activation(out=out, in_=in_, func=func, scale=scale, bias=bias,
                                                    accum_out=accum), reads, writes)

    def tt(self, eng, out, in0, in1, op, reads, writes):
        self.S.op(eng, lambda e: e.tensor_tensor(out=out, in0=in0, in1=in1, op=op), reads, writes)

    def ts(self, eng, out, in0, s1, s2, op0, op1, reads, writes):
        if s2 is None:
            self.S.op(eng, lambda e: e.tensor_scalar(out=out, in0=in0, scalar1=s1, scalar2=None, op0=op0), reads, writes)
        else:
            self.S.op(eng, lambda e: e.tensor_scalar(out=out, in0=in0, scalar1=s1, scalar2=s2, op0=op0, op1=op1),
                      reads, writes)

    def stt(self, eng, out, in0, sc, in1, op0, op1, reads, writes):
        self.S.op(eng, lambda e: e.scalar_tensor_tensor(out=out, in0=in0, scalar=sc, in1=in1, op0=op0, op1=op1),
                  reads, writes)

    def cp(self, eng, out, in_, reads, writes):
        if eng == "act":
            self.S.op("act", lambda e: e.activation(out=out, in_=in_, func=AF.Copy), reads, writes)
        else:
            self.S.op(eng, lambda e: e.tensor_copy(out=out, in_=in_), reads, writes)

    def memset(self, eng, ap, val, writes):
        self.S.op(eng, lambda e: e.memset(ap, val), (), writes)

    def recip(self, out, in_, reads, writes):
        self.S.op("dve", lambda e: e.reciprocal(out=out, in_=in_), reads, writes)

    def dma(self, out, in_, reads, writes, q="sp"):
        self.S.dma(out, in_, reads, writes, q=q)

    def alt(self, engs=("act", "dve")):
        self.rr += 1
        return engs[self.rr % len(engs)]

    def evac(self, out, in_, reads, writes, engs=("act", "dve")):
        self.cp(self.alt(engs), out, in_, reads, writes)


def interleave(*gens):
    gens = [g for g in gens if g is not None]
    while gens:
        for g in list(gens):
            try:
                next(g)
            except StopIteration:
                gens.remove(g)


class Rot:
    def __init__(self, items):
        self.items = items
        self.i = 0

    def get(self):
        it = self.items[self.i % len(self.items)]
        self.i += 1
        return it


def _bf(a):
    return np.ascontiguousarray(a.astype(ml_dtypes.bfloat16))


def _pool_mats():
    Tt = 512
    t = np.arange(Tt)
    out = np.zeros((4, 5, 128, 128), np.float64)
    for gi, w in enumerate((2, 4, 8, 16)):
        hw = w // 2
        lo0, hi0 = np.clip(t - hw, 0, Tt), np.clip(t + hw, 0, Tt)
        lo1, hi1 = np.clip(t - hw + 1, 0, Tt), np.clip(t + hw + 1, 0, Tt)
        cnt = (hi0 - lo0) + (hi1 - lo1)
        M = np.zeros((Tt, Tt))
        for tt in range(Tt):
            M[tt, lo0[tt]:hi0[tt]] += 1.0
            M[tt, lo1[tt]:hi1[tt]] += 1.0
            M[tt] /= cnt[tt]
            M[tt, tt] -= 1.0
        blk = lambda ti, si: M[ti * 128:(ti + 1) * 128, si * 128:(si + 1) * 128].T
        out[gi, 0] = blk(1, 0)
        out[gi, 1] = blk(1, 1)
        out[gi, 2] = blk(1, 2)
        out[gi, 3] = blk(0, 0)
        out[gi, 4] = blk(3, 3)
    return out


def _fourier_consts(T):
    R = T // 64
    c = np.arange(64)
    ang = 2 * np.pi * np.outer(c, c) / 64.0
    nrm = 1.0 / np.sqrt(64.0 * T)
    cc3 = np.zeros((128, 384))
    for g in range(2):
        sl = slice(g * 64, (g + 1) * 64)
        cc3[sl, 0 + g * 64:0 + (g + 1) * 64] = np.cos(ang)
        cc3[sl, 128 + g * 64:128 + (g + 1) * 64] = -np.sin(ang)
        cc3[sl, 256 + g * 64:256 + (g + 1) * 64] = -np.cos(ang)
    t1 = np.arange(R)
    angR = 2 * np.pi * np.outer(t1, t1) / R
    wr = np.zeros((128, 2, R))
    wr[:R, 0] = np.cos(angR)
    wr[:R, 1] = np.sin(angR)
    t2 = np.arange(64)
    k1 = np.arange(R)
    k2 = np.arange(64)
    th = 2 * np.pi * (t2[:, None, None] * k1[None, :, None] / T + t2[:, None, None] * k2[None, None, :] / 64.0)
    gm = np.concatenate([np.cos(th), np.sin(th)], axis=0) * nrm
    return _bf(cc3), _bf(wr), _bf(gm)


def _pos_tables(T):
    quarter = D // 4
    omega = 1.0 / (10000.0 ** (np.arange(quarter, dtype=np.float32) / np.float32(quarter)))
    omega = omega.astype(np.float32)

    def enc(p):
        ang = (p[:, None].astype(np.float32) * omega[None, :]).astype(np.float32)
        return np.concatenate([np.sin(ang), np.cos(ang)], axis=-1).astype(np.float32)

    rows = T // 64
    return enc(np.arange(rows, dtype=np.float32)), enc(np.arange(64, dtype=np.float32))


def _misc_consts():
    i = np.arange(128)
    su = (i[:, None] < i[None, :]).astype(np.float32)
    iu = (i[:, None] <= i[None, :]).astype(np.float32)
    sl = su.T.copy()
    il = iu.T.copy()
    ident = np.eye(128, dtype=np.float32)
    ones = np.ones((128, 128), np.float32)
    blk = np.zeros((128, 128), np.float32)
    blk[:64, :64] = 1.0
    blk[64:, 64:] = 1.0
    mats = [ident, ones, blk, su, iu, sl, il]
    mats.append(((i[:, None] // 8) == (i[None, :] // 8)).astype(np.float32))
    for b in (8, 16, 32, 64):
        same2 = (i[:, None] // (2 * b)) == (i[None, :] // (2 * b))
        diffb = (i[:, None] // b) != (i[None, :] // b)
        mats.append((same2 & diffb).astype(np.float32))
    return np.stack(mats, axis=0)


class Prog:
    pass


def build_program(T, TC, L, dbg=()):
    nc = bass.Bass("TRN2", target_bir_lowering=False)
    P = Prog()
    P.T, P.TC, P.L, P.TT = T, TC, L, T + TC
    P.dbg = dbg
    TT = P.TT
    R = T // 64
    RC = TC // 64

    def din(name, shape, dt=F32):
        return nc.dram_tensor(name, list(shape), dt, kind="ExternalInput").ap()

    def dscr(name, shape, dt=F32):
        kind = "ExternalOutput" if name in dbg else ("ExternalInput" if name + "_in" in dbg else "Internal")
        return nc.dram_tensor(name, list(shape), dt, kind=kind).ap()

    I = {}
    I["x"] = din("x", [T, D]); I["ctx"] = din("ctx", [TC, D]); I["cvec"] = din("cvec", [2, D])
    I["w_mod"] = din("w_mod", [L, D, 6 * D]); I["b_mod"] = din("b_mod", [L, 6 * D])
    I["norm1_g"] = din("norm1_g", [L, D]); I["norm2_g"] = din("norm2_g", [L, D])
    I["w_in"] = din("w_in", [L, D, NIN]); I["w_out"] = din("w_out", [L, D, D])
    I["mu_prev"] = din("mu_prev", [L, RW]); I["mu_next"] = din("mu_next", [L, RW])
    I["w0"] = din("w0", [L, 512]); I["w2"] = din("w2", [L, 128, 256])
    I["a0"] = din("a0", [L, 512]); I["a2"] = din("a2", [L, 128, 256])
    I["g2"] = din("g2", [L, 128, 256])
    for nm in ("k_k", "k_a", "r_k", "gn_w", "gn_b", "pool_scale", "dw_b", "ln_g", "ln_b"):
        I[nm] = din(nm, [L, 256])
    I["pool_w"] = din("pool_w", [L, 4, 64, 64]); I["fourier_w"] = din("fourier_w", [L, 256, 256])
    I["dw_w"] = din("dw_w", [L, 31, 256]); I["conv_pw"] = din("conv_pw", [L, 256, 256])
    I["ffn_w_in"] = din("ffn_w_in", [L, D, 2 * DFF]); I["ffn_w_out"] = din("ffn_w_out", [L, DFF, D])
    I["final_g"] = din("final_g", [D])
    I["misc"] = din("misc", [12, 128, 128]); I["poolm"] = din("poolm", [4, 5, 128, 128], BF16)
    I["cc3"] = din("cc3", [128, 384], BF16)
    I["wr_x"] = din("wr_x", [128, 2, R], BF16); I["gm_x"] = din("gm_x", [128, R, 64], BF16)
    I["wr_c"] = din("wr_c", [128, 2, RC], BF16); I["gm_c"] = din("gm_c", [128, RC, 64], BF16)
    I["pos_row"] = din("pos_row", [R, 512]); I["pos_col"] = din("pos_col", [64, 512])
    I["sel"] = din("sel", [2, 2, 128])
    out = nc.dram_tensor("out", [T, D], F32, kind="ExternalOutput").ap()
    P.I, P.out = I, out

    Dr = {}
    Dr["hx"] = dscr("hx", [T, D]); Dr["hc"] = dscr("hc", [TC, D])
    Dr["uR"] = dscr("uR", [RW, TT]); Dr["uM"] = dscr("uM", [1024, TT], BF16)
    Dr["yT"] = dscr("yT", [1024, TT], BF16); Dr["xn2"] = dscr("xn2", [1024, TT], BF16)
    Dr["sq"] = dscr("sq", [13, 256, TT]); Dr["yd"] = dscr("yd", [2, TT, 256])
    Dr["gtb"] = dscr("gtb", [4, 128, 1024])
    if "scan1" in dbg:
        Dr["dumpf"] = nc.dram_tensor("dumpf", [3, 128, 256], F32, kind="ExternalOutput").ap()
        Dr["dumpb"] = nc.dram_tensor("dumpb", [5, 128, 256], BF16, kind="ExternalOutput").ap()
    P.Dr = Dr

    with ExitStack() as es0:
        comp = {k: es0.enter_context(nc.semaphore(f"s_{k}")) for k in ("pe", "act", "dve", "pool")}
        dma = {"sp": [es0.enter_context(nc.semaphore(f"d_sp{i}")) for i in range(N_DMA_SEMS)],
               "act": [es0.enter_context(nc.semaphore(f"d_act{i}")) for i in range(N_DMA_SEMS)]}
        S = Sched(nc, comp, dma)
        C = Ctx(nc, S)
        P.C = C
        P.PB = [Tl(es0.enter_context(nc.psum_tensor(f"pb{i}", [128, 512], F32))) for i in range(6)]
        P.PT = [Tl(es0.enter_context(nc.psum_tensor(f"pt{i}", [128, 1024], BF16))) for i in range(2)]
        for t_ in P.PB + P.PT:
            t_.b.x = True
        P.misc = C.sb(es0, [128, 12, 128], F32, "misc")
        C.dma(P.misc.t[:], I["misc"].rearrange("k p n -> p k n"), [], [P.misc.b])
        P.identb = C.sb(es0, [128, 128], BF16, "identb")
        C.cp("dve", P.identb.t[:], P.misc.t[:, 0, :], [P.misc.b], [P.identb.b])
        P.maskb = C.sb(es0, [128, 4, 128], BF16, "maskb")
        C.cp("dve", P.maskb.t[:], P.misc.t[:, 3:7, :], [P.misc.b], [P.maskb.b])
        P.onesb = C.sb(es0, [128, 128], BF16, "onesb")
        C.cp("dve", P.onesb.t[:], P.misc.t[:, 1, :], [P.misc.b], [P.onesb.b])

        for l in range(L):
            with ExitStack() as esl:
                last = (l == L - 1)
                import os as _os
                stop = _os.environ.get("KSTOP", "")
                LP = phase_params(P, esl, l)
                S.barrier()
                if stop == "params":
                    break
                phase_A(P, LP, l)
                S.barrier()
                if stop == "A":
                    break
                rwkv_prep(P, LP, l)
                if stop == "prep":
                    break
                rwkv_scan_q(P, LP, l)
                if stop == "scan0":
                    break
                rwkv_readout(P, LP, l)
                if stop == "rwkv":
                    break
                phase_pool(P, LP, l)
                if stop == "pool":
                    break
                phase_fourier(P, LP, l)
                if stop == "fourier":
                    break
                phase_conv(P, LP, l)
                if stop == "conv":
                    break
                phase_C1(P, LP, l, last)
                if stop == "C1":
                    break
                phase_C2(P, LP, l, last)
        S.barrier()
        S.emit()
    return nc


def phase_params(P, esl, l):
    C, I, S = P.C, P.I, P.C.S
    LP = Prog()
    LP.pp1 = C.sb(esl, [128, 128], F32, "pp1")
    LP.pp2 = C.sb(esl, [128, 64], F32, "pp2")
    LP.modT = C.sb(esl, [128, 48, 2], F32, "modT")
    LP.gm1 = C.sb(esl, [128, 8, 2], F32, "gm1")
    LP.gm2 = C.sb(esl, [128, 8, 2], F32, "gm2")
    identf = P.misc.t[:, 0, :]
    with ExitStack() as es:
        rs1 = C.sb(es, [128, 128], F32, "rs1")
        rs2 = C.sb(es, [128, 128], F32, "rs2")
        rs3 = C.sb(es, [128, 128], F32, "rs3")
        for r in (rs1, rs2, rs3):
            C.memset("pool", r.t[:], 0.0, [r.b])
        row = [0]

        def put(dst, vec_ap, n):
            C.dma(dst.t[row[0]:row[0] + n, :], vec_ap.rearrange("(n p) -> n p", p=128), [], [dst.b])
            row[0] += n

        put(rs1, I["b_mod"][l], 48)
        put(rs1, I["norm1_g"][l], 8); put(rs1, I["norm2_g"][l], 8)
        put(rs1, I["mu_prev"][l], 9); put(rs1, I["mu_next"][l], 9)
        put(rs1, I["w0"][l], 4); put(rs1, I["a0"][l], 4)
        for nm in ("k_k", "k_a", "r_k", "gn_w", "gn_b", "pool_scale", "dw_b", "ln_g", "ln_b"):
            put(rs1, I[nm][l], 2)
        put(rs1, I["final_g"], 8)
        assert row[0] == 116
        C.dma(rs2.t[0:62, :], I["dw_w"][l].rearrange("j (n p) -> (j n) p", p=128), [], [rs2.b])
        C.dma(rs3.t[0:16, :], I["cvec"].rearrange("m (n p) -> (m n) p", p=128), [], [rs3.b])
        pb = P.PB[0]
        C.tr(pb.t[:, 0:128], rs1.t[:], identf, [rs1.b, P.misc.b], [pb.b])
        C.cp("dve", LP.pp1.t[:], pb.t[:, 0:128], [pb.b], [LP.pp1.b])
        pb = P.PB[1]
        C.tr(pb.t[:, 0:128], rs2.t[:], identf, [rs2.b, P.misc.b], [pb.b])
        C.cp("dve", LP.pp2.t[:], pb.t[:, 0:64], [pb.b], [LP.pp2.b])
        pb = P.PB[2]
        C.tr(pb.t[:, 0:128], rs3.t[:], identf, [rs3.b, P.misc.b], [pb.b])
        scT = C.sb(es, [128, 16], F32, "scT")
        C.act(scT.t[:], pb.t[:, 0:16], AF.Silu, [pb.b], [scT.b])
        scv = scT.t[:].rearrange("p (m dc) -> p dc m", m=2)
        sel = C.sb(es, [2, 2, 128], F32, "sel")
        C.dma(sel.t[:], I["sel"], [], [sel.b])
        wmp = C.pool_of(es, 4, [128, 8, 512], F32, "wm")
        grow = C.sb(es, [2, 512], F32, "grow")
        gts = C.pool_of(es, 2, [128, 512], F32, "gts")
        bmb = C.sb(es, [128, 512], F32, "bmb")
        pfm = P.PB[3]
        for cg in range(12):
            wm = wmp.get()
            C.dma(wm.t[:], I["w_mod"][l][:, cg * 512:(cg + 1) * 512].rearrange("(dc p) n -> p dc n", p=128),
                  [], [wm.b], q="sp" if cg % 2 == 0 else "act")
            for j in range(4):
                nb = cg * 4 + j
                for dc in range(8):
                    C.mm(pfm.t[:, nb * 2:nb * 2 + 2], wm.t[:, dc, j * 128:(j + 1) * 128], scv[:, dc, :],
                         dc == 0, dc == 7, [wm.b, scT.b], [pfm.b])
            if cg in (4, 5, 10, 11):
                g = 0 if cg < 6 else 1
                half = cg % 2
                prow = P.PB[4]
                for dc in range(8):
                    C.mm(prow.t[0:2, :], scv[:, dc, :], wm.t[:, dc, :], dc == 0, dc == 7, [wm.b, scT.b], [prow.b])
                C.cp("dve", grow.t[:], prow.t[0:2, :], [prow.b], [grow.b])
                C.dma(bmb.t[:], I["b_mod"][l][cg * 512:(cg + 1) * 512].partition_broadcast(128), [], [bmb.b])
                for m in range(2):
                    pbc = P.PB[5]
                    C.mm(pbc.t[:], sel.t[:, m, :], grow.t[:], True, True, [sel.b, grow.b], [pbc.b])
                    gt = gts.get()
                    C.tt("dve", gt.t[:], pbc.t[:], bmb.t[:], ALU.add, [pbc.b, bmb.b], [gt.b])
                    C.dma(P.Dr["gtb"][m * 2 + g, :, half * 512:(half + 1) * 512], gt.t[:], [gt.b], [])
        pv = pfm.t[:, 0:96].rearrange("p (nb m) -> p nb m", m=2)
        for m in range(2):
            C.tt("dve", LP.modT.t[:, :, m], pv[:, :, m], LP.pp1.t[:, 0:48], ALU.add, [pfm.b, LP.pp1.b], [LP.modT.b])
        for m in range(2):
            C.stt("dve", LP.gm1.t[:, :, m], LP.modT.t[:, 8:16, m], 1.0, LP.pp1.t[:, 48:56], ALU.add, ALU.mult,
                  [LP.modT.b, LP.pp1.b], [LP.gm1.b])
            C.stt("dve", LP.gm2.t[:, :, m], LP.modT.t[:, 32:40, m], 1.0, LP.pp1.t[:, 56:64], ALU.add, ALU.mult,
                  [LP.modT.b, LP.pp1.b], [LP.gm2.b])
        S.barrier()
    return LP


C_MUP, C_MUN, C_W0, C_A0 = 64, 73, 82, 86
C_KK, C_KA, C_RK, C_GNW, C_GNB, C_PSC, C_DWB, C_LNG, C_LNB, C_FG = 90, 92, 94, 96, 98, 100, 102, 104, 106, 108


def load_cast(C, es, dst, src_ap, ncols, stg_pool, engs=("pool", "dve", "act")):
    nk = src_ap.shape[0] // 128
    CH = 1024
    for k in range(nk):
        for c0 in range(0, ncols, CH):
            cw = min(CH, ncols - c0)
            st = stg_pool.get()
            C.dq = getattr(C, "dq", 0) + 1
            C.dma(st.t[:, 0:cw], src_ap[k * 128:(k + 1) * 128, c0:c0 + cw], [], [st.b], q="sp" if C.dq % 2 else "act")
            C.cp(C.alt(engs), dst.t[:, k, c0:c0 + cw], st.t[:, 0:cw], [st.b], [Acc(dst.b)])


def norm_block_g(P, C, hts, m, gm, sh_col0, LP, xnT, ss_p, rs_p, xs_p):
    n = len(hts)
    sss = [ss_p.get() for _ in range(n)]; rss = [rs_p.get() for _ in range(n)]; xss = [xs_p.get() for _ in range(n)]
    for j in range(n):
        C.act(xss[j].t[:], hts[j].t[:], AF.Square, [hts[j].b], [xss[j].b, sss[j].b], accum=sss[j].t[:])
    yield
    for j in range(n):
        C.act(rss[j].t[:], sss[j].t[:], AF.Sqrt, [sss[j].b], [rss[j].b], scale=1.0 / D, bias=1e-6)
    for j in range(n):
        C.recip(rss[j].t[:], rss[j].t[:], [rss[j].b], [rss[j].b])
    yield
    for j in range(n):
        C.act(xss[j].t[:], hts[j].t[:], AF.Copy, [hts[j].b, rss[j].b], [xss[j].b], scale=rss[j].t[:])
    yield
    for j in range(n):
        pt = P.PT[j % 2]
        yield
        for dc in range(8):
            C.tr(pt.t[:, dc * 128:(dc + 1) * 128], xss[j].t[:, dc * 128:(dc + 1) * 128], P.identb.t[:],
                 [xss[j].b, P.identb.b], [pt.b])
        for dc in range(8):
            o = xnT.t[:, dc, j * 128:(j + 1) * 128]
            i_ = pt.t[:, dc * 128:(dc + 1) * 128]
            if j % 2 == 0:
                C.act(o, i_, AF.Identity, [pt.b, gm.b, LP.modT.b], [Acc(xnT.b)], scale=gm.t[:, dc, m:m + 1],
                      bias=LP.modT.t[:, sh_col0 + dc, m:m + 1])
            else:
                C.ts("dve", o, i_, gm.t[:, dc, m:m + 1], LP.modT.t[:, sh_col0 + dc, m:m + 1], ALU.mult, ALU.add,
                     [pt.b, gm.b, LP.modT.b], [Acc(xnT.b)])


def norm_block(*a):
    for _ in norm_block_g(*a):
        pass


def streams(P):
    return [("c", P.TC, 0, 1), ("x", P.T, P.TC, 0)]


def phase_A(P, LP, l):
    C, I, Dr = P.C, P.I, P.Dr
    with ExitStack() as es:
        winb = C.sb(es, [128, 8, NIN], BF16, "winb")
        stg = C.pool_of(es, 3, [128, 1024], F32, "stg")
        load_cast(C, es, winb, I["w_in"][l], NIN, stg)
        htp = C.pool_of(es, 8, [128, D], F32, "ht")
        ssp = C.pool_of(es, 8, [128, 1], F32, "ss")
        rsp = C.pool_of(es, 8, [128, 1], F32, "rs")
        xsp = C.pool_of(es, 8, [128, D], BF16, "xs")
        xnp = C.pool_of(es, 2, [128, 8, 512], BF16, "xnT")
        osf = C.pool_of(es, 4, [128, 512], F32, "osf")
        osb = C.pool_of(es, 4, [128, 512], BF16, "osb")
        posc = C.sb(es, [128, 512], F32, "posc")
        posr = C.pool_of(es, 4, [128, 512], F32, "posr")
        if l == 0:
            for hh in range(2):
                C.dma(posc.t[hh * 64:(hh + 1) * 64, :], I["pos_col"], [], [posc.b])
        st = {"pbi": 0}
        blocks = []
        for (sn, Ts, c0, m) in streams(P):
            NB = min(512, Ts)
            for blk in range(Ts // NB):
                blocks.append((sn, Ts, c0, m, NB, blk))

        def do_norm(bd, out_box):
            sn, Ts, c0, m, NB, blk = bd
            src = (I["x"] if sn == "x" else I["ctx"]) if l == 0 else (Dr["hx"] if sn == "x" else Dr["hc"])
            xnT = xnp.get()
            hts = []
            for j in range(NB // 128):
                t0 = blk * NB + j * 128
                ht = htp.get()
                C.dma(ht.t[:], src[t0:t0 + 128, :], [], [ht.b])
                if l == 0 and sn == "x":
                    pr = posr.get()
                    for hh in range(2):
                        rr = t0 // 64 + hh
                        C.dma(pr.t[hh * 64:(hh + 1) * 64, :], I["pos_row"][rr, :].partition_broadcast(64), [], [pr.b])
                    C.tt("pool", ht.t[:, 0:512], ht.t[:, 0:512], pr.t[:], ALU.add, [pr.b], [ht.b])
                    C.tt("pool", ht.t[:, 512:1024], ht.t[:, 512:1024], posc.t[:], ALU.add, [posc.b], [ht.b])
                    C.dma(Dr["hx"][t0:t0 + 128, :], ht.t[:], [ht.b], [])
                hts.append(ht)
            out_box.append(xnT)
            yield
            yield from norm_block_g(P, C, hts, m, LP.gm1, 0, LP, xnT, ssp, rsp, xsp)

        def do_mm(bd, xnT):
            sn, Ts, c0, m, NB, blk = bd
            for nb in range(17):
                pb = P.PB[st["pbi"] % 6]; st["pbi"] += 1
                for dc in range(8):
                    C.mm(pb.t[:, 0:NB], winb.t[:, dc, nb * 128:(nb + 1) * 128], xnT.t[:, dc, 0:NB],
                         dc == 0, dc == 7, [winb.b, xnT.b], [pb.b])
                cs = slice(c0 + blk * NB, c0 + blk * NB + NB)
                if nb < 9:
                    o = osf.get()
                    C.evac(o.t[:, 0:NB], pb.t[:, 0:NB], [pb.b], [o.b])
                    C.dma(Dr["uR"][nb * 128:(nb + 1) * 128, cs], o.t[:, 0:NB], [o.b], [])
                else:
                    o = osb.get()
                    C.evac(o.t[:, 0:NB], pb.t[:, 0:NB], [pb.b], [o.b])
                    C.dma(Dr["uM"][(nb - 9) * 128:(nb - 8) * 128, cs], o.t[:, 0:NB], [o.b], [])
                yield

        box = []
        interleave(do_norm(blocks[0], box))
        cur = box[0]
        for i, bd in enumerate(blocks):
            box = []
            interleave(do_mm(bd, cur), do_norm(blocks[i + 1], box) if i + 1 < len(blocks) else None)
            cur = box[0] if box else None
        P.C.S.barrier()


def phase_rwkv(P, LP, l):
    rwkv_prep(P, LP, l)
    if "stop_prep" in P.dbg:
        return
    rwkv_scan_q(P, LP, l)
    rwkv_readout(P, LP, l)


def rwkv_prep(P, LP, l):
    C, I, Dr, S = P.C, P.I, P.Dr, P.C.S
    pp = LP.pp1
    blkf = P.misc.t[:, 2, :]
    onesf = P.misc.t[:, 1, :]
    sq = Dr["sq"]
    with ExitStack() as es:
        stg = C.pool_of(es, 2, [128, 1024], F32, "stg")
        w2b = C.sb(es, [128, 1, 256], BF16, "w2b"); load_cast(C, es, w2b, I["w2"][l], 256, stg)
        a2b = C.sb(es, [128, 1, 256], BF16, "a2b"); load_cast(C, es, a2b, I["a2"][l], 256, stg)
        g2b = C.sb(es, [128, 1, 256], BF16, "g2b"); load_cast(C, es, g2b, I["g2"][l], 256, stg)
        c0t = C.sb(es, [128, 9], F32, "c0t")
        C.tt("dve", c0t.t[:], pp.t[:, C_MUP:C_MUP + 9], pp.t[:, C_MUN:C_MUN + 9], ALU.add, [pp.b], [c0t.b])
        C.ts("dve", c0t.t[:], c0t.t[:], -1.0, 1.0, ALU.mult, ALU.add, [], [c0t.b])
        omka = C.sb(es, [128, 2], F32, "omka")
        C.ts("dve", omka.t[:], pp.t[:, C_KA:C_KA + 2], -1.0, 1.0, ALU.mult, ALU.add, [pp.b], [omka.b])
        dgs = C.sb(es, [128, 27, 128], F32, "dgs")
        identf_ = P.misc.t[:, 0, :]
        for n in range(9):
            for k_, col in enumerate((pp.t[:, C_MUP + n:C_MUP + n + 1], c0t.t[:, n:n + 1], pp.t[:, C_MUN + n:C_MUN + n + 1])):
                C.act(dgs.t[:, n * 3 + k_, :], identf_, AF.Copy, [P.misc.b, pp.b, c0t.b], [Acc(dgs.b)], scale=col)
        uhp = C.pool_of(es, 3, [128, 9, 514], F32, "uh")
        sxp = C.pool_of(es, 2, [128, 9, 512], F32, "sx")
        wk = C.pool_of(es, 34, [128, 512], F32, "wk")
        wkb = C.pool_of(es, 6, [128, 512], BF16, "wkb")
        pbs = {"i": 0}
        blocks = []
        for (sn, Ts, c0, m) in streams(P):
            NB = min(512, Ts)
            for blk in range(Ts // NB):
                blocks.append((sn, Ts, c0, m, NB, blk))

        def load_uh(bd):
            sn, Ts, c0, m, NB, blk = bd
            t0 = blk * NB
            uh = uhp.get()
            lo = 0 if blk == 0 else -1
            hi = 0 if blk == Ts // NB - 1 else 1
            if lo == 0:
                C.memset("pool", uh.t[:, :, 0:1], 0.0, [uh.b])
            if hi == 0:
                C.memset("pool", uh.t[:, :, NB + 1:NB + 2], 0.0, [uh.b])
            C.dma(uh.t[:, :, 1 + lo:NB + 1 + hi],
                  Dr["uR"][:, c0 + t0 + lo:c0 + t0 + NB + hi].rearrange("(n p) t -> p n t", p=128), [], [uh.b])
            return uh

        def tshift(bd, uh, sx):
            NB_ = bd[4]
            for n in range(9):
                pb = P.PB[pbs["i"] % 6]; pbs["i"] += 1
                for k_ in range(3):
                    C.mm(pb.t[:, 0:NB_], dgs.t[:, n * 3 + k_, :], uh.t[:, n, k_:k_ + NB_], k_ == 0, k_ == 2,
                         [dgs.b, uh.b], [pb.b])
                C.cp("act", sx.t[:, n, 0:NB_], pb.t[:, 0:NB_], [pb.b], [Acc(sx.b)])
                yield

        uh0 = load_uh(blocks[0])
        nxt_uh = load_uh(blocks[1]) if len(blocks) > 1 else None
        nxt_sx = sxp.get()
        interleave(tshift(blocks[0], uh0, nxt_sx))
        for bi, bd in enumerate(blocks):
            if True:
                sn, Ts, c0, m, NB, blk = bd
                t0 = blk * NB
                cs = slice(c0 + t0, c0 + t0 + NB)
                sx = nxt_sx
                if bi + 1 < len(blocks):
                    uh_n = nxt_uh
                    nxt_uh = load_uh(blocks[bi + 2]) if bi + 2 < len(blocks) else None
                    nxt_sx = sxp.get()
                    ts_gen = tshift(blocks[bi + 1], uh_n, nxt_sx)
                else:
                    ts_gen = None

                def store(qi, ct, tl):
                    C.dma(sq[qi, ct * 128:(ct + 1) * 128, cs], tl.t[:, 0:NB], [tl.b], [])

                tw = wkb.get(); alb = wkb.get(); sgl = wkb.get()
                C.act(tw.t[:, 0:NB], sx.t[:, 7, 0:NB], AF.Tanh, [sx.b], [tw.b])
                C.cp("pool", alb.t[:, 0:NB], sx.t[:, 8, 0:NB], [sx.b], [alb.b])
                C.act(sgl.t[:, 0:NB], sx.t[:, 0, 0:NB], AF.Sigmoid, [sx.b], [sgl.b])
                def ct_body(ct, sx=sx, tw=tw, alb=alb, sgl=sgl, cs=cs, NB=NB, store=store):
                    for qi, n in ((1, 1 + ct), (2, 5 + ct)):
                        C.dma(sq[qi, ct * 128:(ct + 1) * 128, cs], sx.t[:, n, 0:NB], [sx.b], [])
                    pb = P.PB[pbs["i"] % 6]; pbs["i"] += 1
                    C.mm(pb.t[:, 0:NB], g2b.t[:, 0, ct * 128:(ct + 1) * 128], sgl.t[:, 0:NB], True, True, [g2b.b, sgl.b], [pb.b])
                    gt_ = wk.get()
                    C.cp("act", gt_.t[:, 0:NB], pb.t[:, 0:NB], [pb.b], [gt_.b])
                    store(12, ct, gt_)
                    yield
                    kc = wk.get(); sk = wk.get(); rn = wk.get(); kk = wk.get()
                    C.act(kc.t[:, 0:NB], sx.t[:, 3 + ct, 0:NB], AF.Copy, [sx.b, pp.b], [kc.b], scale=pp.t[:, C_KK + ct:C_KK + ct + 1])
                    C.tt("pool", sk.t[:, 0:NB], kc.t[:, 0:NB], kc.t[:, 0:NB], ALU.mult, [kc.b], [sk.b])
                    pb = P.PB[pbs["i"] % 6]; pbs["i"] += 1
                    C.mm(pb.t[:, 0:NB], blkf, sk.t[:, 0:NB], True, True, [P.misc.b, sk.b], [pb.b])
                    C.act(rn.t[:, 0:NB], pb.t[:, 0:NB], AF.Sqrt, [pb.b], [rn.b])
                    C.ts("dve", rn.t[:, 0:NB], rn.t[:, 0:NB], 1e-12, None, ALU.max, None, [], [rn.b])
                    C.recip(rn.t[:, 0:NB], rn.t[:, 0:NB], [], [rn.b])
                    C.tt("dve", kk.t[:, 0:NB], kc.t[:, 0:NB], rn.t[:, 0:NB], ALU.mult, [kc.b, rn.b], [kk.b])
                    store(0, ct, kk)
                    yield
                    kds = []
                    for d in range(2):
                        ps = slice(64 * d, 64 * d + 64)
                        pw = P.PB[pbs["i"] % 6]; pbs["i"] += 1
                        C.mm(pw.t[:, 0:NB], w2b.t[ps, 0, ct * 128:(ct + 1) * 128], tw.t[ps, 0:NB], True, True, [w2b.b, tw.b], [pw.b])
                        pa = P.PB[pbs["i"] % 6]; pbs["i"] += 1
                        C.mm(pa.t[:, 0:NB], a2b.t[ps, 0, ct * 128:(ct + 1) * 128], alb.t[ps, 0:NB], True, True, [a2b.b, alb.b], [pa.b])
                        lw = wk.get(); ad = wk.get()
                        C.act(lw.t[:, 0:NB], pw.t[:, 0:NB], AF.Sigmoid, [pw.b, pp.b], [lw.b],
                              bias=pp.t[:, C_W0 + d * 2 + ct:C_W0 + d * 2 + ct + 1])
                        C.ts("dve", lw.t[:, 0:NB], lw.t[:, 0:NB], -0.6065306597126334, None, ALU.mult, None, [], [lw.b])
                        C.act(ad.t[:, 0:NB], pa.t[:, 0:NB], AF.Sigmoid, [pa.b, pp.b], [ad.b],
                              bias=pp.t[:, C_A0 + d * 2 + ct:C_A0 + d * 2 + ct + 1])
                        tmp = wk.get(); kd = wk.get(); al = wk.get()
                        C.ts("dve", tmp.t[:, 0:NB], ad.t[:, 0:NB], pp.t[:, C_KA + ct:C_KA + ct + 1], omka.t[:, ct:ct + 1],
                             ALU.mult, ALU.add, [ad.b, pp.b, omka.b], [tmp.b])
                        C.tt("pool", kd.t[:, 0:NB], sx.t[:, 3 + ct, 0:NB], tmp.t[:, 0:NB], ALU.mult, [sx.b, tmp.b], [kd.b])
                        C.stt("dve", al.t[:, 0:NB], ad.t[:, 0:NB], -1.0, kk.t[:, 0:NB], ALU.mult, ALU.mult, [ad.b, kk.b], [al.b])
                        store(3 + d, ct, kd); store(5 + d, ct, al)
                        yield
                        kds.append(kd)
                        cI = wk.get(); cX = wk.get()
                        for ch in range(NB // 128):
                            sl_ = slice(ch * 128, (ch + 1) * 128)
                            if d == 0:
                                S.op("dve", (lambda o, a, b: (lambda e: e.tensor_tensor_scan(out=o, data0=a, data1=b, initial=0.0,
                                     op0=ALU.mult, op1=ALU.add)))(cI.t[:, sl_], onesf, lw.t[:, sl_]), [lw.b, P.misc.b], [cI.b])
                                C.tt("pool", cX.t[:, sl_], cI.t[:, sl_], lw.t[:, sl_], ALU.subtract, [cI.b, lw.b], [cX.b])
                            else:
                                S.op("dve", (lambda o, a, b: (lambda e: e.tensor_tensor_scan(out=o, data0=a, data1=b, initial=0.0,
                                     op0=ALU.mult, op1=ALU.add)))(cI.t[:, sl_], onesf, lw.t[:, sl_]), [lw.b, P.misc.b], [cI.b])
                                C.ts("dve", cX.t[:, sl_], cI.t[:, sl_], cI.t[:, ch * 128 + 127:ch * 128 + 128], -1.0,
                                     ALU.subtract, ALU.mult, [cI.b], [cX.b])
                                C.tt("dve", cI.t[:, sl_], cX.t[:, sl_], lw.t[:, sl_], ALU.add, [cX.b, lw.b], [cI.b])
                        store(7 + 2 * d, ct, cI); store(8 + 2 * d, ct, cX)
                        yield
                    t1 = wk.get(); t3 = wk.get(); bo = wk.get()
                    C.tt("pool", t1.t[:, 0:NB], kds[0].t[:, 0:NB], kds[1].t[:, 0:NB], ALU.add, [kds[0].b, kds[1].b], [t1.b])
                    C.tt("pool", t1.t[:, 0:NB], t1.t[:, 0:NB], sx.t[:, 1 + ct, 0:NB], ALU.mult, [sx.b], [t1.b])
                    C.act(t3.t[:, 0:NB], t1.t[:, 0:NB], AF.Copy, [t1.b, pp.b], [t3.b], scale=pp.t[:, C_RK + ct:C_RK + ct + 1])
                    pb = P.PB[pbs["i"] % 6]; pbs["i"] += 1
                    C.mm(pb.t[:, 0:NB], blkf, t3.t[:, 0:NB], True, True, [P.misc.b, t3.b], [pb.b])
                    C.tt("dve", bo.t[:, 0:NB], pb.t[:, 0:NB], sx.t[:, 5 + ct, 0:NB], ALU.mult, [pb.b, sx.b], [bo.b])
                    store(11, ct, bo)
                    yield
                interleave(ct_body(0), ct_body(1), ts_gen)
        S.barrier()


import os as _os0
INV_BF16 = _os0.environ.get("INV_BF16", "1") == "1"


def rwkv_scan(P, LP, l):
    C, I, Dr, S = P.C, P.I, P.Dr, P.C.S
    sq = Dr["sq"]
    NC = P.TT // 128
    NCc = P.TC // 128
    orders = [list(range(NC)), list(range(NCc - 1, -1, -1)) + list(range(NC - 1, NCc - 1, -1))]
    gmasks = [(P.misc.t[:, 3:5, :]).rearrange("p a b -> p (a b)"), (P.misc.t[:, 5:7, :]).rearrange("p a b -> p (a b)")]
    nmasks = [P.misc.t[:, 5, :], P.misc.t[:, 3, :]]
    last_cols = [127, 0]
    identf = P.misc.t[:, 0, :]
    IDT = BF16 if INV_BF16 else F32
    ident_i = P.identb.t[:] if INV_BF16 else identf
    ident_b = P.identb.b if INV_BF16 else P.misc.b
    QIs = [(0, 1, 2, 3 + d, 5 + d, 7 + 2 * d, 8 + 2 * d) for d in range(2)]
    mk = lambda k: P.misc.t[:, 7 + k, :]
    with ExitStack() as es:
        Hs = [[C.sb(es, [128, 64], F32, "H") for _ in range(2)] for _ in range(2)]
        Hbs = [[C.sb(es, [128, 64], BF16, "Hb") for _ in range(2)] for _ in range(2)]
        for d in range(2):
            for ct in range(2):
                C.memset("pool", Hs[d][ct].t[:], 0.0, [Hs[d][ct].b])
                C.memset("pool", Hbs[d][ct].t[:], 0.0, [Hbs[d][ct].b])
        qinp = C.pool_of(es, 4, [128, 7, 2, 128], F32, "qin")
        ep = C.pool_of(es, 24, [128, 128], F32, "e")
        BRp = C.pool_of(es, 12, [128, 256], BF16, "BR")
        b16 = C.pool_of(es, 72, [128, 128], BF16, "b16")
        m256 = C.pool_of(es, 48, [128, 256], BF16, "m256")
        f128 = C.pool_of(es, 192 if INV_BF16 else 110, [128, 128], IDT, "f128")
        z64 = C.pool_of(es, 32, [128, 64], BF16, "z64")
        z64f = C.pool_of(es, 24, [128, 64], IDT, "z64f")
        ychp = C.pool_of(es, 4, [128, 256], F32, "ych")
        st = {"pbi": 0, "pti": 0}

        def bank():
            pb_ = P.PB[st["pbi"] % 6]; st["pbi"] += 1
            return pb_

        def tbank():
            if INV_BF16:
                pb_ = P.PT[st["pti"] % 2]; st["pti"] += 1
                return pb_
            return bank()

        def mmf(lhsT, rhs):
            pb_ = bank()
            C.mm(pb_.t[:, 0:128], lhsT.t[:], rhs.t[:], True, True, [lhsT.b, rhs.b], [pb_.b])
            return pb_

        def evc(pb_):
            o_ = f128.get()
            C.cp(C.alt(("act", "act", "dve")), o_.t[:], pb_.t[:, 0:128], [pb_.b], [o_.b])
            return o_

        def evadd(pb_, addt):
            o_ = f128.get()
            C.tt("dve", o_.t[:], pb_.t[:, 0:128], addt.t[:], ALU.add, [pb_.b, addt.b], [o_.b])
            return o_

        def masked(src, k):
            o_ = f128.get()
            C.tt("pool", o_.t[:], src.t[:], mk(k), ALU.mult, [src.b, P.misc.b], [o_.b])
            return o_

        def xpose(X):
            pb_ = tbank()
            C.tr(pb_.t[:, 0:128], X.t[:], ident_i, [X.b, ident_b], [pb_.b])
            return evc(pb_)

        for it_ in range(NC):
          heads = []
          ychs = []
          toksl = []
          for d in range(2):
            ci = orders[d][it_]
            gmask, nmask, last_col = gmasks[d], nmasks[d], last_cols[d]
            H, Hb = Hs[d], Hbs[d]
            tok = slice(ci * 128, (ci + 1) * 128)
            toksl.append(tok)
            qin = qinp.get()
            for k_, qi in enumerate(QIs[d]):
                C.dma(qin.t[:, k_, :, :], sq[qi, :, tok].rearrange("(ct p) t -> p ct t", p=128), [], [qin.b])
            ych = ychp.get()
            ychs.append(ych)
            cts = []
            for ct in range(2):
                q = (lambda ct_: (lambda k_: qin.t[:, k_, ct_, :]))(ct)
                eX = ep.get(); eI = ep.get(); eN = ep.get()
                C.act(eX.t[:], q(6), AF.Exp, [qin.b], [eX.b])
                C.act(eI.t[:], q(5), AF.Exp, [qin.b], [eI.b])
                C.act(eN.t[:], q(5), AF.Exp, [qin.b], [eN.b], scale=-1.0)
                BR = BRp.get(); AT = b16.get(); KT = b16.get(); AH = b16.get(); KH = b16.get(); vb = b16.get()
                C.tt("dve", BR.t[:, 0:128], q(0), eX.t[:], ALU.mult, [qin.b, eX.b], [BR.b])
                C.tt("pool", BR.t[:, 128:256], q(1), eI.t[:], ALU.mult, [qin.b, eI.b], [BR.b])
                C.tt("dve", AT.t[:], q(4), eN.t[:], ALU.mult, [qin.b, eN.b], [AT.b])
                C.tt("pool", KT.t[:], q(3), eN.t[:], ALU.mult, [qin.b, eN.b], [KT.b])
                pc = eI.t[:, last_col:last_col + 1]
                C.act(AH.t[:], AT.t[:], AF.Copy, [AT.b, eI.b], [AH.b], scale=pc)
                C.act(KH.t[:], KT.t[:], AF.Copy, [KT.b, eI.b], [KH.b], scale=pc)
                C.cp("pool", vb.t[:], q(2), [qin.b], [vb.b])
                pt = P.PT[st["pti"] % 2]; st["pti"] += 1
                for k_, src in enumerate((AH, KH, vb)):
                    C.tr(pt.t[:, k_ * 128:(k_ + 1) * 128], src.t[:], P.identb.t[:], [src.b, P.identb.b], [pt.b])
                toks = []
                for k_ in range(3):
                    tk = b16.get()
                    C.cp("act" if ct == 0 else "dve", tk.t[:], pt.t[:, k_ * 128:(k_ + 1) * 128], [pt.b], [tk.b])
                    toks.append(tk)
                cts.append(dict(BR=BR, AT=AT, KT=KT, eI=eI, AHt=toks[0], KHt=toks[1], Vt=toks[2]))
            for ct in range(2):
                for hh in range(2):
                    hd = dict(cts[ct]); hd["ct"] = ct; hd["ps"] = slice(64 * hh, 64 * hh + 64); hd["h4"] = ct * 2 + hh
                    hd.update(gmask=gmask, nmask=nmask, last_col=last_col, H=H, Hb=Hb, ych=ych)
                    heads.append(hd)
          if True:
            for hd in heads:
                ps, BR, AT, KT = hd["ps"], hd["BR"], hd["AT"], hd["KT"]
                gmask, nmask = hd["gmask"], hd["nmask"]
                pg1 = bank()
                C.mm(pg1.t[:, 0:256], KT.t[ps, :], BR.t[ps, :], True, True, [KT.b, BR.b], [pg1.b])
                M1 = m256.get()
                C.tt("dve", M1.t[:], pg1.t[:, 0:256], gmask, ALU.mult, [pg1.b, P.misc.b], [M1.b])
                pg2 = bank()
                C.mm(pg2.t[:, 0:256], AT.t[ps, :], BR.t[ps, :], True, True, [AT.b, BR.b], [pg2.b])
                M2 = m256.get()
                C.tt("dve", M2.t[:], pg2.t[:, 0:256], gmask, ALU.mult, [pg2.b, P.misc.b], [M2.b])
                Ntf = f128.get()
                C.tt("dve", Ntf.t[:], pg2.t[:, 0:128], gmask[:, 0:128], ALU.mult, [pg2.b, P.misc.b], [Ntf.b])
                pg3 = bank()
                C.mm(pg3.t[:, 0:128], BR.t[ps, 0:128], AT.t[ps, :], True, True, [AT.b, BR.b], [pg3.b])
                Nf = f128.get()
                C.tt("dve", Nf.t[:], pg3.t[:, 0:128], nmask, ALU.mult, [pg3.b, P.misc.b], [Nf.b])
                hd.update(M1=M1, M2=M2, Nf=Nf, Ntf=Ntf)
            for hd in heads:
                hd["Nd"] = masked(hd["Nf"], 0); hd["Ndt"] = masked(hd["Ntf"], 0)
            for hd in heads:
                hd["p1"] = mmf(hd["Ndt"], hd["Nd"]); hd["p2"] = mmf(hd["Nd"], hd["Ndt"])
                hd["Nd2"] = evc(hd["p1"]); hd["Ndt2"] = evc(hd["p2"])
            for hd in heads:
                hd["p2"] = mmf(hd["Nd2"], hd["Ndt2"])
                hd["Ndt4"] = evc(hd["p2"])
                P1 = f128.get()
                C.tt("pool", P1.t[:], hd["Nd"].t[:], ident_i, ALU.add, [hd["Nd"].b, ident_b], [P1.b])
                hd["P1"] = P1
            for hd in heads:
                hd["P2"] = evadd(mmf(hd["Ndt2"], hd["P1"]), hd["P1"])
            for hd in heads:
                hd["X"] = evadd(mmf(hd["Ndt4"], hd["P2"]), hd["P2"])
            for k in (1, 2, 3):
                for hd in heads:
                    hd["Xt"] = xpose(hd["X"])
                    hd["Noff"] = masked(hd["Ntf"], k)
                for hd in heads:
                    hd["W"] = evc(mmf(hd["Noff"], hd["X"]))
                for hd in heads:
                    hd["X"] = evadd(mmf(hd["Xt"], hd["W"]), hd["X"])
            for hd in heads:
                hd["Xt"] = xpose(hd["X"])
                hd["Noff"] = masked(hd["Nf"], 4)
            for hd in heads:
                hd["W"] = evc(mmf(hd["Noff"], hd["Xt"]))
            for hd in heads:
                hd["Tt"] = evadd(mmf(hd["X"], hd["W"]), hd["Xt"])
            for hd in heads:
                ps, ct = hd["ps"], hd["ct"]
                Hb = hd["Hb"]
                vh = hd["Vt"].t[:, ps]
                pz = bank()
                C.mm(pz.t[:, 0:64], hd["BR"].t[ps, 0:128], Hb[ct].t[ps, :], True, False, [hd["BR"].b, Hb[ct].b], [pz.b])
                C.mm(pz.t[:, 0:64], hd["M1"].t[:, 0:128], vh, False, True, [hd["M1"].b, hd["Vt"].b], [pz.b])
                Zs = z64f.get()
                C.cp("act", Zs.t[:], pz.t[:, 0:64], [pz.b], [Zs.b])
                hd["Zs"] = Zs
            for hd in heads:
                pu = bank()
                C.mm(pu.t[:, 0:64], hd["Tt"].t[:], hd["Zs"].t[:], True, True, [hd["Tt"].b, hd["Zs"].b], [pu.b])
                Us = z64.get()
                C.cp("dve", Us.t[:], pu.t[:, 0:64], [pu.b], [Us.b])
                hd["Us"] = Us
            for hd in heads:
                ps, ct, h4 = hd["ps"], hd["ct"], hd["h4"]
                H, Hb, ych, last_col = hd["H"], hd["Hb"], hd["ych"], hd["last_col"]
                vh = hd["Vt"].t[:, ps]
                py = bank()
                C.mm(py.t[:, 0:64], hd["BR"].t[ps, 128:256], Hb[ct].t[ps, :], True, False, [hd["BR"].b, Hb[ct].b], [py.b])
                C.mm(py.t[:, 0:64], hd["M2"].t[:, 128:256], hd["Us"].t[:], False, False, [hd["M2"].b, hd["Us"].b], [py.b])
                C.mm(py.t[:, 0:64], hd["M1"].t[:, 128:256], vh, False, True, [hd["M1"].b, hd["Vt"].b], [py.b])
                C.cp("act", ych.t[:, h4 * 64:(h4 + 1) * 64], py.t[:, 0:64], [py.b], [ych.b])
                ph = bank()
                C.mm(ph.t[:, 0:64], hd["AHt"].t[:], hd["Us"].t[:], True, False, [hd["AHt"].b, hd["Us"].b], [ph.b])
                C.mm(ph.t[:, 0:64], hd["KHt"].t[:], vh, False, True, [hd["KHt"].b, hd["Vt"].b], [ph.b])
                C.stt("dve", H[ct].t[ps, :], H[ct].t[ps, :], hd["eI"].t[ps, last_col:last_col + 1], ph.t[ps, 0:64],
                      ALU.mult, ALU.add, [hd["eI"].b, ph.b], [H[ct].b])
            for d in range(2):
                for ct in range(2):
                    C.cp("act" if d == 0 else "dve", Hbs[d][ct].t[:], Hs[d][ct].t[:], [Hs[d][ct].b], [Hbs[d][ct].b])
                C.dma(Dr["yd"][d, toksl[d], :], ychs[d].t[:], [ychs[d].b], [])
        S.barrier()


def rwkv_scan_q(P, LP, l):
    C, I, Dr, S = P.C, P.I, P.Dr, P.C.S
    sq = Dr["sq"]
    NC = P.TT // 128
    NCc = P.TC // 128
    orders = [list(range(NC)), list(range(NCc - 1, -1, -1)) + list(range(NC - 1, NCc - 1, -1))]
    last_cols = [127, 0]
    QIs = [(0, 1, 2, 3 + d, 5 + d, 7 + 2 * d, 8 + 2 * d) for d in range(2)]
    with ExitStack() as es:
        gm2 = [C.sb(es, [128, 2, 256], F32, "gm2") for _ in range(2)]
        nm4 = [C.sb(es, [128, 4, 128], F32, "nm4") for _ in range(2)]
        mk4 = [C.sb(es, [128, 4, 128], F32, "mk4") for _ in range(5)]
        id4 = C.sb(es, [128, 4, 128], BF16, "id4")
        for d in range(2):
            src = P.misc.t[:, 3:5, :] if d == 0 else P.misc.t[:, 5:7, :]
            for r in range(2):
                C.cp("pool", gm2[d].t[:, r, :].rearrange("p (a b) -> p a b", a=2), src, [P.misc.b], [gm2[d].b])
            for r in range(4):
                C.cp("pool", nm4[d].t[:, r, :], P.misc.t[:, 5 if d == 0 else 3, :], [P.misc.b], [nm4[d].b])
        for k in range(5):
            for r in range(4):
                C.cp("pool", mk4[k].t[:, r, :], P.misc.t[:, 7 + k, :], [P.misc.b], [mk4[k].b])
        for r in range(4):
            C.cp("pool", id4.t[:, r, :], P.misc.t[:, 0, :], [P.misc.b], [id4.b])
        Hs = [[C.sb(es, [128, 64], F32, "H") for _ in range(2)] for _ in range(2)]
        Hbs = [[C.sb(es, [128, 64], BF16, "Hb") for _ in range(2)] for _ in range(2)]
        for d in range(2):
            for ct in range(2):
                C.memset("pool", Hs[d][ct].t[:], 0.0, [Hs[d][ct].b])
                C.memset("pool", Hbs[d][ct].t[:], 0.0, [Hbs[d][ct].b])
        qinp = C.pool_of(es, 4, [128, 7, 2, 128], F32, "qin")
        eIp = C.pool_of(es, 18, [128, 128], F32, "eI")
        ep = C.pool_of(es, 8, [128, 128], F32, "e")
        BRp = C.pool_of(es, 18, [128, 256], BF16, "BR")
        b16 = C.pool_of(es, 48, [128, 128], BF16, "b16")
        m4 = C.pool_of(es, 18, [128, 4, 256], BF16, "m4")
        q4l = C.pool_of(es, 34, [128, 4, 128], BF16, "q4l")
        q4 = C.pool_of(es, 30, [128, 4, 128], BF16, "q4")
        z4 = C.pool_of(es, 12, [128, 4, 64], BF16, "z4")
        ychp = C.pool_of(es, 4, [128, 256], F32, "ych")
        st = {"pbi": 0, "pti": 0}

        def bank():
            pb_ = P.PB[st["pbi"] % 6]; st["pbi"] += 1
            return pb_

        def tbank():
            pb_ = P.PT[st["pti"] % 2]; st["pti"] += 1
            return pb_

        def mm4(A, B, bsl=None):
            pb_ = bank()
            for h in range(4):
                C.mm(pb_.t[:, h * 128:(h + 1) * 128], A.t[:, h, :] if bsl is None else A.t[:, h, bsl], B.t[:, h, :],
                     True, True, [A.b, B.b], [pb_.b])
            return pb_

        def ev4(pb_, eng):
            o_ = q4.get()
            C.cp(C.alt(("act", "act", "dve")), o_.t[:].rearrange("p a b -> p (a b)"), pb_.t[:, 0:512], [pb_.b], [o_.b])
            return o_

        def evadd4(pb_, addt):
            o_ = q4.get()
            C.tt("dve", o_.t[:].rearrange("p a b -> p (a b)"), pb_.t[:, 0:512], addt.t[:].rearrange("p a b -> p (a b)"),
                 ALU.add, [pb_.b, addt.b], [o_.b])
            return o_

        def masked4(src_ap, src_b, k):
            o_ = q4.get()
            C.tt("pool", o_.t[:], src_ap, mk4[k].t[:], ALU.mult, [src_b, mk4[k].b], [o_.b])
            return o_

        def xpose4(X, eng):
            pb_ = tbank()
            for h in range(4):
                C.tr(pb_.t[:, h * 128:(h + 1) * 128], X.t[:, h, :], P.identb.t[:], [X.b, P.identb.b], [pb_.b])
            o_ = q4.get()
            C.cp(eng, o_.t[:].rearrange("p a b -> p (a b)"), pb_.t[:, 0:512], [pb_.b], [o_.b])
            return o_

        pref = {}

        def load_qin(it_, d):
            tok_ = slice(orders[d][it_] * 128, (orders[d][it_] + 1) * 128)
            qin_ = qinp.get()
            for k_, qi in enumerate(QIs[d]):
                C.dma(qin_.t[:, k_, :, :], sq[qi, :, tok_].rearrange("(ct p) t -> p ct t", p=128), [], [qin_.b])
            return qin_

        def pre(it0, box):
          its = [i_ for i_ in (it0, it0 + 1) if i_ < NC]
          quads = []
          for it_ in its:
            for d in range(2):
                ci = orders[d][it_]
                last_col = last_cols[d]
                tok = slice(ci * 128, (ci + 1) * 128)
                qin = pref.pop((it_, d)) if (it_, d) in pref else load_qin(it_, d)
                cts = []
                for ct in range(2):
                    q = (lambda ct_, qin_: (lambda k_: qin_.t[:, k_, ct_, :]))(ct, qin)
                    eX = ep.get(); eI = eIp.get(); eN = ep.get()
                    C.act(eX.t[:], q(6), AF.Exp, [qin.b], [eX.b])
                    C.act(eI.t[:], q(5), AF.Exp, [qin.b], [eI.b])
                    C.act(eN.t[:], q(5), AF.Exp, [qin.b], [eN.b], scale=-1.0)
                    BR = BRp.get(); AT = b16.get(); KT = b16.get(); AH = b16.get(); KH = b16.get(); vb = b16.get()
                    C.tt("dve", BR.t[:, 0:128], q(0), eX.t[:], ALU.mult, [qin.b, eX.b], [BR.b])
                    C.tt("pool", BR.t[:, 128:256], q(1), eI.t[:], ALU.mult, [qin.b, eI.b], [BR.b])
                    C.tt("dve", AT.t[:], q(4), eN.t[:], ALU.mult, [qin.b, eN.b], [AT.b])
                    C.tt("pool", KT.t[:], q(3), eN.t[:], ALU.mult, [qin.b, eN.b], [KT.b])
                    pc = eI.t[:, last_col:last_col + 1]
                    C.act(AH.t[:], AT.t[:], AF.Copy, [AT.b, eI.b], [AH.b], scale=pc)
                    C.act(KH.t[:], KT.t[:], AF.Copy, [KT.b, eI.b], [KH.b], scale=pc)
                    C.cp("pool", vb.t[:], q(2), [qin.b], [vb.b])
                    pt = tbank()
                    for k_, src in enumerate((AH, KH, vb)):
                        C.tr(pt.t[:, k_ * 128:(k_ + 1) * 128], src.t[:], P.identb.t[:], [src.b, P.identb.b], [pt.b])
                    tk = q4l.get()
                    C.cp("act" if d == 0 else "dve", tk.t[:, 0:3, :].rearrange("p a b -> p (a b)"), pt.t[:, 0:384], [pt.b], [tk.b])
                    cts.append(dict(BR=BR, AT=AT, KT=KT, eI=eI, tk=tk))
                quads.append(dict(d=d, it=it_, cts=cts, tok=tok, last_col=last_col, ev="act" if d == 0 else "dve"))
          if True:
            for itn in (it0 + 2, it0 + 3):
                if itn < NC:
                    for d in range(2):
                        pref[(itn, d)] = load_qin(itn, d)
            yield
            for Q in quads:
                d = Q["d"]
                M1 = m4.get(); M2 = m4.get()
                for (lk, Mq) in (("KT", M1), ("AT", M2)):
                    for hh in range(2):
                        ps = slice(64 * hh, 64 * hh + 64)
                        pg = bank()
                        for ct in range(2):
                            cd = Q["cts"][ct]
                            C.mm(pg.t[:, ct * 256:(ct + 1) * 256], cd[lk].t[ps, :], cd["BR"].t[ps, :], True, True,
                                 [cd[lk].b, cd["BR"].b], [pg.b])
                        C.tt("dve", Mq.t[:, 2 * hh:2 * hh + 2, :], pg.t[:, 0:512].rearrange("p (a b) -> p a b", a=2), gm2[d].t[:],
                             ALU.mult, [pg.b, gm2[d].b], [Mq.b])
                Nq = q4l.get()
                for hh in range(2):
                    ps = slice(64 * hh, 64 * hh + 64)
                    pg = bank()
                    for ct in range(2):
                        cd = Q["cts"][ct]
                        C.mm(pg.t[:, ct * 128:(ct + 1) * 128], cd["BR"].t[ps, 0:128], cd["AT"].t[ps, :], True, True,
                             [cd["AT"].b, cd["BR"].b], [pg.b])
                    C.tt("dve", Nq.t[:, 2 * hh:2 * hh + 2, :], pg.t[:, 0:256].rearrange("p (a b) -> p a b", a=2), nm4[d].t[:, 0:2, :],
                         ALU.mult, [pg.b, nm4[d].b], [Nq.b])
                Q.update(M1=M1, M2=M2, Nq=Nq)
            yield
            for Q in quads:
                Q["Nd"] = masked4(Q["Nq"].t[:], Q["Nq"].b, 0)
                Q["Ndt"] = masked4(Q["M2"].t[:, :, 0:128], Q["M2"].b, 0)
            yield
            for Q in quads:
                Q["Nd2"] = ev4(mm4(Q["Ndt"], Q["Nd"]), Q["ev"])
                Q["Ndt2"] = ev4(mm4(Q["Nd"], Q["Ndt"]), "act" if Q["ev"] == "dve" else "dve")
            yield
            for Q in quads:
                Q["Ndt4"] = ev4(mm4(Q["Nd2"], Q["Ndt2"]), Q["ev"])
                P1 = q4.get()
                C.tt("pool", P1.t[:], Q["Nd"].t[:], id4.t[:], ALU.add, [Q["Nd"].b, id4.b], [P1.b])
                Q["P1"] = P1
            yield
            for Q in quads:
                Q["P2"] = evadd4(mm4(Q["Ndt2"], Q["P1"]), Q["P1"])
            yield
            for Q in quads:
                Q["X"] = evadd4(mm4(Q["Ndt4"], Q["P2"]), Q["P2"])
            yield
            for k in (1, 2, 3):
                for Q in quads:
                    Q["Xt"] = xpose4(Q["X"], Q["ev"])
                    Q["Noff"] = masked4(Q["M2"].t[:, :, 0:128], Q["M2"].b, k)
                yield
                for Q in quads:
                    Q["W"] = ev4(mm4(Q["Noff"], Q["X"]), Q["ev"])
                yield
                for Q in quads:
                    Q["X"] = evadd4(mm4(Q["Xt"], Q["W"]), Q["X"])
                yield
            for Q in quads:
                Q["Xt"] = xpose4(Q["X"], Q["ev"])
                Q["Noff"] = masked4(Q["Nq"].t[:], Q["Nq"].b, 4)
            yield
            for Q in quads:
                Q["W"] = ev4(mm4(Q["Noff"], Q["Xt"]), Q["ev"])
            yield
            for Q in quads:
                pb_ = mm4(Q["X"], Q["W"])
                Tt = q4l.get()
                C.tt("dve", Tt.t[:].rearrange("p a b -> p (a b)"), pb_.t[:, 0:512], Q["Xt"].t[:].rearrange("p a b -> p (a b)"),
                     ALU.add, [pb_.b, Q["Xt"].b], [Tt.b])
                Q["Tt"] = Tt
            yield
            box.append((its, quads))

        def chain(its, allquads):
          for it_ in its:
            quads = [Q for Q in allquads if Q["it"] == it_]
            for Q in quads:
                d = Q["d"]
                Zs = z4.get()
                for hh in range(2):
                    ps = slice(64 * hh, 64 * hh + 64)
                    pz = bank()
                    for ct in range(2):
                        s_ = hh * 2 + ct
                        cd = Q["cts"][ct]
                        o = pz.t[:, ct * 64:(ct + 1) * 64]
                        C.mm(o, cd["BR"].t[ps, 0:128], Hbs[d][ct].t[ps, :], True, False, [cd["BR"].b, Hbs[d][ct].b], [pz.b])
                        C.mm(o, Q["M1"].t[:, s_, 0:128], cd["tk"].t[:, 2, ps], False, True, [Q["M1"].b, cd["tk"].b], [pz.b])
                    C.cp(Q["ev"], Zs.t[:, 2 * hh:2 * hh + 2, :].rearrange("p a b -> p (a b)"), pz.t[:, 0:128], [pz.b], [Zs.b])
                Q["Zs"] = Zs
            yield
            for Q in quads:
                pu = bank()
                for s_ in range(4):
                    C.mm(pu.t[:, s_ * 64:(s_ + 1) * 64], Q["Tt"].t[:, s_, :], Q["Zs"].t[:, s_, :], True, True,
                         [Q["Tt"].b, Q["Zs"].b], [pu.b])
                Us = z4.get()
                C.cp(Q["ev"], Us.t[:].rearrange("p a b -> p (a b)"), pu.t[:, 0:256], [pu.b], [Us.b])
                Q["Us"] = Us
            yield
            for Q in quads:
                d = Q["d"]
                ych = ychp.get()
                yv = ych.t[:].rearrange("p (ct hh v) -> p hh ct v", ct=2, hh=2)
                for hh in range(2):
                    ps = slice(64 * hh, 64 * hh + 64)
                    py = bank()
                    for ct in range(2):
                        s_ = hh * 2 + ct
                        cd = Q["cts"][ct]
                        o = py.t[:, ct * 64:(ct + 1) * 64]
                        C.mm(o, cd["BR"].t[ps, 128:256], Hbs[d][ct].t[ps, :], True, False, [cd["BR"].b, Hbs[d][ct].b], [py.b])
                        C.mm(o, Q["M2"].t[:, s_, 128:256], Q["Us"].t[:, s_, :], False, False, [Q["M2"].b, Q["Us"].b], [py.b])
                        C.mm(o, Q["M1"].t[:, s_, 128:256], cd["tk"].t[:, 2, ps], False, True, [Q["M1"].b, cd["tk"].b], [py.b])
                    C.cp("act", yv[:, hh], py.t[:, 0:128].rearrange("p (a b) -> p a b", a=2), [py.b], [ych.b])
                C.dma(Dr["yd"][d, Q["tok"], :], ych.t[:], [ych.b], [])
                ph = bank()
                for s_ in range(4):
                    hh, ct = s_ // 2, s_ % 2
                    cd = Q["cts"][ct]; ps = slice(64 * hh, 64 * hh + 64)
                    o = ph.t[:, s_ * 64:(s_ + 1) * 64]
                    C.mm(o, cd["tk"].t[:, 0, :], Q["Us"].t[:, s_, :], True, False, [cd["tk"].b, Q["Us"].b], [ph.b])
                    C.mm(o, cd["tk"].t[:, 1, :], cd["tk"].t[:, 2, ps], False, True, [cd["tk"].b], [ph.b])
                for s_ in range(4):
                    hh, ct = s_ // 2, s_ % 2
                    cd = Q["cts"][ct]; ps = slice(64 * hh, 64 * hh + 64)
                    Hh = Hs[d][ct]
                    C.stt("dve", Hh.t[ps, :], Hh.t[ps, :], cd["eI"].t[ps, Q["last_col"]:Q["last_col"] + 1],
                          ph.t[ps, s_ * 64:(s_ + 1) * 64], ALU.mult, ALU.add, [cd["eI"].b, ph.b], [Hh.b])
                for ct in range(2):
                    C.cp("act" if d == 0 else "dve", Hbs[d][ct].t[:], Hs[d][ct].t[:], [Hs[d][ct].b], [Hbs[d][ct].b])
            yield

        box = []
        interleave(pre(0, box))
        cur = box[0]
        for it0 in range(0, NC, 2):
            box = []
            interleave(chain(*cur), pre(it0 + 2, box) if it0 + 2 < NC else None)
            cur = box[0] if box else None
        S.barrier()


def rwkv_readout(P, LP, l):
    C, I, Dr, S = P.C, P.I, P.Dr, P.C.S
    pp = LP.pp1
    sq = Dr["sq"]
    NC = P.TT // 128
    GR = 4
    with ExitStack() as es:
        yp = C.pool_of(es, 4 * GR, [128, 256], F32, "yr")
        sqv = C.pool_of(es, 2 * GR, [128, 256], F32, "ysq")
        st = C.pool_of(es, 8 * GR, [128, 4], F32, "st")
        ynp = C.pool_of(es, 2 * GR, [128, 256], BF16, "yn")
        bgp = C.pool_of(es, 2 * GR, [128, 2, 2, 128], F32, "bg")
        op_ = C.pool_of(es, 4 * GR, [128, 128], F32, "o")
        obp = C.pool_of(es, 4 * GR, [128, 128], BF16, "ob")
        red = lambda o, i_: (lambda e: e.tensor_reduce(out=o, in_=i_, axis=AX.X, op=ALU.add))
        def load_group(c0_):
            G_ = []
            for ci in range(c0_, min(NC, c0_ + GR)):
                tok = slice(ci * 128, (ci + 1) * 128)
                g = dict(tok=tok, y0=yp.get(), y1=yp.get(), bg=bgp.get())
                C.dma(g["y0"].t[:], Dr["yd"][0, tok, :], [], [g["y0"].b])
                C.dma(g["y1"].t[:], Dr["yd"][1, tok, :], [], [g["y1"].b])
                for k_, qi in enumerate((11, 12)):
                    C.dma(g["bg"].t[:, k_, :, :], sq[qi, :, tok].rearrange("(ct p) t -> p ct t", p=128), [], [g["bg"].b])
                G_.append(g)
            return G_

        nxtG = load_group(0)
        for c0_ in range(0, NC, GR):
            G = nxtG
            nxtG = load_group(c0_ + GR) if c0_ + GR < NC else None
            for g in G:
                C.tt("dve", g["y0"].t[:], g["y0"].t[:], g["y1"].t[:], ALU.add, [g["y1"].b], [g["y0"].b])
            for g in G:
                g["ysq"] = sqv.get()
                C.tt("pool", g["ysq"].t[:], g["y0"].t[:], g["y0"].t[:], ALU.mult, [g["y0"].b], [g["ysq"].b])
                g["s1"] = st.get(); g["s2"] = st.get(); g["mu"] = st.get(); g["var"] = st.get()
                S.op("dve", red(g["s1"].t[:], g["y0"].t[:].rearrange("p (h j) -> p h j", j=64)), [g["y0"].b], [g["s1"].b])
            for g in G:
                S.op("dve", red(g["s2"].t[:], g["ysq"].t[:].rearrange("p (h j) -> p h j", j=64)), [g["ysq"].b], [g["s2"].b])
                C.ts("dve", g["mu"].t[:], g["s1"].t[:], 1.0 / 64, None, ALU.mult, None, [g["s1"].b], [g["mu"].b])
            for g in G:
                C.tt("dve", g["var"].t[:], g["mu"].t[:], g["mu"].t[:], ALU.mult, [g["mu"].b], [g["var"].b])
            for g in G:
                C.stt("dve", g["var"].t[:], g["s2"].t[:], 1.0 / 64, g["var"].t[:], ALU.mult, ALU.subtract, [g["s2"].b], [g["var"].b])
            for g in G:
                C.act(g["var"].t[:], g["var"].t[:], AF.Sqrt, [], [g["var"].b], bias=64e-5)
            for g in G:
                C.recip(g["var"].t[:], g["var"].t[:], [], [g["var"].b])
            for g in G:
                g["yn"] = ynp.get()
                for h4 in range(4):
                    hs = slice(h4 * 64, (h4 + 1) * 64)
                    C.ts("dve", g["yn"].t[:, hs], g["y0"].t[:, hs], g["mu"].t[:, h4:h4 + 1], g["var"].t[:, h4:h4 + 1],
                         ALU.subtract, ALU.mult, [g["y0"].b, g["mu"].b, g["var"].b], [g["yn"].b])
            for gi, g in enumerate(G):
                pt = P.PT[gi % 2]
                for ct in range(2):
                    C.tr(pt.t[:, ct * 128:(ct + 1) * 128], g["yn"].t[:, ct * 128:(ct + 1) * 128], P.identb.t[:],
                         [g["yn"].b, P.identb.b], [pt.b])
                g["o"] = []
                for ct in range(2):
                    o = op_.get()
                    C.act(o.t[:], pt.t[:, ct * 128:(ct + 1) * 128], AF.Identity, [pt.b, pp.b], [o.b],
                          scale=pp.t[:, C_GNW + ct:C_GNW + ct + 1], bias=pp.t[:, C_GNB + ct:C_GNB + ct + 1])
                    g["o"].append(o)
            for g in G:
                for ct in range(2):
                    o = g["o"][ct]; ob = obp.get()
                    C.tt("dve", o.t[:], o.t[:], g["bg"].t[:, 0, ct, :], ALU.add, [g["bg"].b], [o.b])
                    C.tt("pool", ob.t[:], o.t[:], g["bg"].t[:, 1, ct, :], ALU.mult, [o.b, g["bg"].b], [ob.b])
                    C.dma(Dr["yT"][ct * 128:(ct + 1) * 128, g["tok"]], ob.t[:], [ob.b], [])
        S.barrier()


def phase_pool(P, LP, l):
    C, I, Dr = P.C, P.I, P.Dr
    with ExitStack() as es:
        pm = C.sb(es, [128, 20, 128], BF16, "poolm")
        C.dma(pm.t[:], I["poolm"].rearrange("g v s t -> s (g v) t"), [], [pm.b])
        pwf = C.sb(es, [128, 2, 128], F32, "pwf")
        C.memset("pool", pwf.t[:], 0.0, [pwf.b])
        for g in range(4):
            ct, gl = g // 2, g % 2
            C.dma(pwf.t[gl * 64:(gl + 1) * 64, ct, gl * 64:(gl + 1) * 64], I["pool_w"][l, g], [], [pwf.b])
        pwb = C.sb(es, [128, 2, 128], BF16, "pwb")
        C.cp("dve", pwb.t[:], pwf.t[:], [pwf.b], [pwb.b])
        uT = [C.sb(es, [128, P.T], BF16, "uTp") for _ in range(2)]
        z = C.sb(es, [128, P.T // 128, 256], BF16, "z")
        yop = C.pool_of(es, 3, [128, 512], BF16, "yo")
        pbi = 0
        for (sn, Ts, c0, m) in streams(P):
            nT = Ts // 128
            for ct in range(2):
                C.dma(uT[ct].t[:, 0:Ts], Dr["uM"][ct * 128:(ct + 1) * 128, c0:c0 + Ts], [], [uT[ct].b])
            for ti in range(nT):
                pb = P.PB[pbi % 6]; pbi += 1
                for ct in range(2):
                    C.mm(pb.t[:, ct * 128:(ct + 1) * 128], uT[ct].t[:, ti * 128:(ti + 1) * 128], pwb.t[:, ct, :],
                         True, True, [uT[ct].b, pwb.b], [pb.b])
                C.evac(z.t[:, ti, :], pb.t[:, 0:256], [pb.b], [Acc(z.b)])
            NB = min(512, Ts)
            for ct in range(2):
                for blk in range(Ts // NB):
                    yo = yop.get()
                    for tj in range(NB // 128):
                        ti = blk * (NB // 128) + tj
                        for gl in range(2):
                            g = 2 * ct + gl
                            pb = P.PB[pbi % 6]; pbi += 1
                            sis = [si for si in (ti - 1, ti, ti + 1) if 0 <= si < nT]
                            for n_, si in enumerate(sis):
                                if si == ti - 1:
                                    v = 0
                                elif si == ti + 1:
                                    v = 2
                                else:
                                    v = 3 if ti == 0 else (4 if ti == nT - 1 else 1)
                                C.mm(pb.t[:, 0:128], z.t[:, si, ct * 128:(ct + 1) * 128], pm.t[:, g * 5 + v, :],
                                     n_ == 0, n_ == len(sis) - 1, [z.b, pm.b], [pb.b])
                            ps = slice(gl * 64, (gl + 1) * 64)
                            C.act(yo.t[ps, tj * 128:(tj + 1) * 128], pb.t[ps, 0:128], AF.Copy, [pb.b, LP.pp1.b], [yo.b],
                                  scale=LP.pp1.t[ps, C_PSC + ct:C_PSC + ct + 1])
                    C.dma(Dr["yT"][256 + ct * 128:256 + (ct + 1) * 128, c0 + blk * NB:c0 + (blk + 1) * NB],
                          yo.t[:, 0:NB], [yo.b], [])
        P.C.S.barrier()


def phase_fourier(P, LP, l):
    C, I, Dr = P.C, P.I, P.Dr
    with ExitStack() as es:
        cc3 = C.sb(es, [128, 384], BF16, "cc3")
        C.dma(cc3.t[:], I["cc3"], [], [cc3.b])
        stg = C.pool_of(es, 2, [128, 1024], F32, "stg")
        fwb = C.sb(es, [128, 2, 256], BF16, "fwb")
        load_cast(C, es, fwb, I["fourier_w"][l], 256, stg)
        Rmax = P.T // 64
        uTp = C.pool_of(es, 2, [128, max(P.T, 2048)], BF16, "uTf")
        Z = C.sb(es, [128, 128, 3, 64], BF16, "Z")
        Pq = C.sb(es, [128, Rmax, 128], BF16, "Pq")
        gm = C.sb(es, [128, Rmax, 64], BF16, "gmf")
        wr = C.sb(es, [128, 2, Rmax], BF16, "wrf")
        fT = [C.sb(es, [128, P.T], BF16, "fT") for _ in range(2)]
        yop = C.pool_of(es, 3, [128, 512], BF16, "yo")
        pbi = 0
        items = [(sn, Ts, c0, ct) for (sn, Ts, c0, m) in streams(P) for ct in range(2)]

        def load_uT(itm):
            sn_, Ts_, c0_, ct_ = itm
            R_ = Ts_ // 64
            Rp_ = max(R_, 32)
            u_ = uTp.get()
            if Rp_ > R_:
                C.memset("pool", u_.t[:, Ts_:Rp_ * 64], 0.0, [u_.b])
            C.dma(u_.t[:, 0:Ts_], Dr["uM"][256 + ct_ * 128:256 + (ct_ + 1) * 128, c0_:c0_ + Ts_], [], [u_.b])
            return u_

        nxt_u = load_uT(items[0])
        ii = 0
        for (sn, Ts, c0, m) in streams(P):
            R = Ts // 64
            C.dma(gm.t[:, 0:R, :], I["gm_x" if sn == "x" else "gm_c"], [], [gm.b])
            C.dma(wr.t[:, :, 0:R], I["wr_x" if sn == "x" else "wr_c"], [], [wr.b])
            for ct in range(2):
                Rp = max(R, 32)
                uT = nxt_u
                ii += 1
                nxt_u = load_uT(items[ii]) if ii < len(items) else None
                uv = uT.t[:, 0:Rp * 64].rearrange("p (t1 t2) -> p t2 t1", t2=64)
                for t2 in range(64):
                    pb = P.PB[pbi % 6]; pbi += 1
                    C.mm(pb.t[0:Rp, 0:384], uv[:, t2, :], cc3.t[:], True, True, [uT.b, cc3.b], [pb.b])
                    C.evac(Z.t[0:Rp, :, :, t2].rearrange("p q r -> p r q"), pb.t[0:Rp, 0:384].rearrange("p (r q) -> p r q", r=3),
                           [pb.b], [Acc(Z.b)])
                QB = min(128, 512 // R)
                for q0 in range(0, 128, QB):
                    pb = P.PB[pbi % 6]; pbi += 1
                    for qi in range(QB):
                        q = q0 + qi
                        o = pb.t[:, qi * R:(qi + 1) * R]
                        C.mm(o, Z.t[0:Rp, q, 0:2, :].rearrange("p r t -> p (r t)"), wr.t[0:Rp, 0, 0:R], True, False, [Z.b, wr.b], [pb.b])
                        C.mm(o, Z.t[0:Rp, q, 1:3, :].rearrange("p r t -> p (r t)"), wr.t[0:Rp, 1, 0:R], False, True, [Z.b, wr.b], [pb.b])
                    C.evac(Pq.t[:, 0:R, q0:q0 + QB], pb.t[:, 0:QB * R].rearrange("p (q k) -> p k q", k=R),
                           [pb.b], [Acc(Pq.b)])
                KB = min(8, R)
                fv = fT[ct].t[:, 0:Ts].rearrange("p (k2 k1) -> p k2 k1", k1=R)
                for k0 in range(0, R, KB):
                    pb = P.PB[pbi % 6]; pbi += 1
                    for ki in range(KB):
                        C.mm(pb.t[:, ki * 64:(ki + 1) * 64], Pq.t[:, k0 + ki, :], gm.t[:, k0 + ki, :], True, True,
                             [Pq.b, gm.b], [pb.b])
                    C.evac(fv[:, :, k0:k0 + KB], pb.t[:, 0:KB * 64].rearrange("p (k1 k2) -> p k2 k1", k2=64),
                           [pb.b], [Acc(fT[ct].b)])
            NB = min(512, Ts)
            for blk in range(Ts // NB):
                for nt in range(2):
                    pb = P.PB[pbi % 6]; pbi += 1
                    for ct in range(2):
                        C.mm(pb.t[:, 0:NB], fwb.t[:, ct, nt * 128:(nt + 1) * 128], fT[ct].t[:, blk * NB:(blk + 1) * NB],
                             ct == 0, ct == 1, [fwb.b, fT[ct].b], [pb.b])
                    yo = yop.get()
                    C.evac(yo.t[:, 0:NB], pb.t[:, 0:NB], [pb.b], [yo.b])
                    C.dma(Dr["yT"][512 + nt * 128:512 + (nt + 1) * 128, c0 + blk * NB:c0 + (blk + 1) * NB],
                          yo.t[:, 0:NB], [yo.b], [])
        P.C.S.barrier()


def phase_conv(P, LP, l):
    C, I, Dr = P.C, P.I, P.Dr
    onesf = P.misc.t[:, 1, :]
    identf = P.misc.t[:, 0, :]
    with ExitStack() as es:
        stg = C.pool_of(es, 2, [128, 1024], F32, "stg")
        pwb = C.sb(es, [128, 2, 256], BF16, "cpw")
        load_cast(C, es, pwb, I["conv_pw"][l], 256, stg)
        dg = C.sb(es, [128, 62, 128], BF16, "dg")
        for jj in range(62):
            C.ts("dve" if jj % 2 else "act", dg.t[:, jj, :], identf, LP.pp2.t[:, jj:jj + 1], None, ALU.mult, None,
                 [P.misc.b, LP.pp2.b], [dg.b]) if jj % 2 else \
                C.act(dg.t[:, jj, :], identf, AF.Copy, [P.misc.b, LP.pp2.b], [dg.b], scale=LP.pp2.t[:, jj:jj + 1])
        hT = [C.sb(es, [128, P.T + 30], BF16, "hT") for _ in range(2)]
        abp = C.pool_of(es, 4, [128, 512], BF16, "ab")
        sgp = C.pool_of(es, 2, [128, 512], F32, "sgc")
        co = [C.pool_of(es, 3, [128, 512], F32, "co") for _ in range(2)]
        sqp = [C.pool_of(es, 3, [128, 512], F32, "sqc") for _ in range(2)]
        meanp = C.pool_of(es, 2, [128, 512], F32, "mean")
        varp = C.pool_of(es, 2, [128, 512], F32, "var")
        dp = C.pool_of(es, 2, [128, 512], F32, "dcv")
        sT = [C.pool_of(es, 2, [128, 512], BF16, "sTc") for _ in range(2)]
        yop = C.pool_of(es, 3, [128, 512], BF16, "yo")
        pbi = 0
        for (sn, Ts, c0, m) in streams(P):
            NB = min(512, Ts)
            for ct in range(2):
                C.memset("pool", hT[ct].t[:, 0:15], 0.0, [hT[ct].b])
                C.memset("pool", hT[ct].t[:, 15 + Ts:30 + Ts], 0.0, [hT[ct].b])
            for blk in range(Ts // NB):
                cs = slice(c0 + blk * NB, c0 + (blk + 1) * NB)
                for ct in range(2):
                    a = abp.get(); b = abp.get()
                    C.dma(a.t[:, 0:NB], Dr["uM"][512 + ct * 128:512 + (ct + 1) * 128, cs], [], [a.b])
                    C.dma(b.t[:, 0:NB], Dr["uM"][768 + ct * 128:768 + (ct + 1) * 128, cs], [], [b.b])
                    sg = sgp.get()
                    C.act(sg.t[:, 0:NB], b.t[:, 0:NB], AF.Sigmoid, [b.b], [sg.b])
                    C.tt("dve", hT[ct].t[:, 15 + blk * NB:15 + (blk + 1) * NB], a.t[:, 0:NB], sg.t[:, 0:NB], ALU.mult,
                         [a.b, sg.b], [hT[ct].b])
            pbs = {"i": pbi}

            def bank_():
                pb_ = P.PB[pbs["i"] % 6]; pbs["i"] += 1
                return pb_

            def partA(blk, box):
                cot, sqt = [], []
                for ct in range(2):
                    pb = bank_()
                    for j in range(31):
                        C.mm(pb.t[:, 0:NB], dg.t[:, j * 2 + ct, :], hT[ct].t[:, blk * NB + j:blk * NB + j + NB],
                             j == 0, j == 30, [dg.b, hT[ct].b], [pb.b])
                        if j % 8 == 7:
                            yield
                    c_ = co[ct].get(); q_ = sqp[ct].get()
                    C.act(c_.t[:, 0:NB], pb.t[:, 0:NB], AF.Identity, [pb.b, LP.pp1.b], [c_.b],
                          bias=LP.pp1.t[:, C_DWB + ct:C_DWB + ct + 1])
                    C.tt("pool", q_.t[:, 0:NB], c_.t[:, 0:NB], c_.t[:, 0:NB], ALU.mult, [c_.b], [q_.b])
                    cot.append(c_); sqt.append(q_)
                    yield
                box.append((cot, sqt))

            def partB(blk, cot, sqt):
                pm_ = bank_()
                pq_ = bank_()
                for ct in range(2):
                    C.mm(pm_.t[:, 0:NB], onesf, cot[ct].t[:, 0:NB], ct == 0, ct == 1, [P.misc.b, cot[ct].b], [pm_.b])
                for ct in range(2):
                    C.mm(pq_.t[:, 0:NB], onesf, sqt[ct].t[:, 0:NB], ct == 0, ct == 1, [P.misc.b, sqt[ct].b], [pq_.b])
                mean = meanp.get(); var = varp.get()
                C.act(mean.t[:, 0:NB], pm_.t[:, 0:NB], AF.Copy, [pm_.b], [mean.b], scale=1.0 / 256)
                yield
                C.tt("dve", var.t[:, 0:NB], mean.t[:, 0:NB], mean.t[:, 0:NB], ALU.mult, [mean.b], [var.b])
                C.stt("dve", var.t[:, 0:NB], pq_.t[:, 0:NB], 1.0 / 256, var.t[:, 0:NB], ALU.mult, ALU.subtract,
                      [pq_.b], [var.b])
                yield
                C.act(var.t[:, 0:NB], var.t[:, 0:NB], AF.Sqrt, [], [var.b], bias=1e-5)
                yield
                C.recip(var.t[:, 0:NB], var.t[:, 0:NB], [], [var.b])
                yield
                sts = []
                for ct in range(2):
                    d_ = dp.get()
                    C.tt("dve", d_.t[:, 0:NB], cot[ct].t[:, 0:NB], mean.t[:, 0:NB], ALU.subtract, [cot[ct].b, mean.b], [d_.b])
                    C.tt("pool", d_.t[:, 0:NB], d_.t[:, 0:NB], var.t[:, 0:NB], ALU.mult, [var.b], [d_.b])
                    s_ = sT[ct].get()
                    C.act(s_.t[:, 0:NB], d_.t[:, 0:NB], AF.Silu, [d_.b, LP.pp1.b], [s_.b],
                          scale=LP.pp1.t[:, C_LNG + ct:C_LNG + ct + 1], bias=LP.pp1.t[:, C_LNB + ct:C_LNB + ct + 1])
                    sts.append(s_)
                    yield
                for nt in range(2):
                    pb = bank_()
                    for ct in range(2):
                        C.mm(pb.t[:, 0:NB], pwb.t[:, ct, nt * 128:(nt + 1) * 128], sts[ct].t[:, 0:NB], ct == 0, ct == 1,
                             [pwb.b, sts[ct].b], [pb.b])
                    yo = yop.get()
                    C.evac(yo.t[:, 0:NB], pb.t[:, 0:NB], [pb.b], [yo.b])
                    C.dma(Dr["yT"][768 + nt * 128:768 + (nt + 1) * 128, c0 + blk * NB:c0 + (blk + 1) * NB],
                          yo.t[:, 0:NB], [yo.b], [])
                    yield

            nblk = Ts // NB
            box = []
            interleave(partA(0, box))
            cur = box[0]
            for blk in range(nblk):
                box = []
                interleave(partB(blk, *cur), partA(blk + 1, box) if blk + 1 < nblk else None)
                cur = box[0] if box else None
            pbi = pbs["i"]
        P.C.S.barrier()


def phase_C1(P, LP, l, last=False):
    C, I, Dr = P.C, P.I, P.Dr
    with ExitStack() as es:
        woutb = C.sb(es, [128, 8, D], BF16, "woutb")
        stg = C.pool_of(es, 3, [128, 1024], F32, "stg")
        load_cast(C, es, woutb, I["w_out"][l], D, stg)
        htp = C.pool_of(es, 9, [128, D], F32, "ht")
        ssp = C.pool_of(es, 8, [128, 1], F32, "ss")
        rsp = C.pool_of(es, 8, [128, 1], F32, "rs")
        xsp = C.pool_of(es, 8, [128, D], BF16, "xs")
        xnp = C.pool_of(es, 2, [128, 8, 512], BF16, "xnT")
        ytp = C.pool_of(es, 2, [128, 8, 512], BF16, "ytb")
        tmp = C.pool_of(es, 4, [128, 512], F32, "tmp")
        gt = C.sb(es, [128, D], F32, "gt")
        pbi = 0
        blocks = []
        for (sn, Ts, c0, m) in streams(P):
            if last and sn == "c":
                continue
            NB = min(512, Ts)
            for blk in range(Ts // NB):
                blocks.append((sn, Ts, c0, m, NB, blk))

        def load_blk(bd):
            sn, Ts, c0, m, NB, blk = bd
            hsrc = Dr["hx"] if sn == "x" else (I["ctx"] if l == 0 else Dr["hc"])
            cs = slice(c0 + blk * NB, c0 + blk * NB + NB)
            yt = ytp.get()
            C.dma(yt.t[:, :, 0:NB], Dr["yT"][:, cs].rearrange("(kc p) t -> p kc t", p=128), [], [yt.b])
            hts = []
            for j in range(NB // 128):
                t0 = blk * NB + j * 128
                ht = htp.get()
                C.dma(ht.t[:], hsrc[t0:t0 + 128, :], [], [ht.b])
                hts.append(ht)
            return yt, hts

        cur_m = None
        nxt = load_blk(blocks[0])
        for bi, bd in enumerate(blocks):
            sn, Ts, c0, m, NB, blk = bd
            if m != cur_m:
                C.dma(gt.t[:], Dr["gtb"][m * 2 + 0], [], [gt.b])
                cur_m = m
            hdst = Dr["hx"] if sn == "x" else Dr["hc"]
            cs = slice(c0 + blk * NB, c0 + blk * NB + NB)
            yt, hts = nxt
            nxt = load_blk(blocks[bi + 1]) if bi + 1 < len(blocks) else None
            xnT = xnp.get()
            for j in range(NB // 128):
                ht = hts[j]
                t0 = blk * NB + j * 128
                for nh in range(2):
                    pb = P.PB[pbi % 6]; pbi += 1
                    for kc in range(8):
                        C.mm(pb.t[:], yt.t[:, kc, j * 128:(j + 1) * 128], woutb.t[:, kc, nh * 512:(nh + 1) * 512],
                             kc == 0, kc == 7, [yt.b, woutb.b], [pb.b])
                    tm = tmp.get()
                    C.tt("dve", tm.t[:], pb.t[:], gt.t[:, nh * 512:(nh + 1) * 512], ALU.mult, [pb.b, gt.b], [tm.b])
                    C.tt("pool", ht.t[:, nh * 512:(nh + 1) * 512], ht.t[:, nh * 512:(nh + 1) * 512], tm.t[:],
                         ALU.add, [tm.b], [ht.b])
                C.dma(hdst[t0:t0 + 128, :], ht.t[:], [ht.b], [])
            norm_block(P, C, hts, m, LP.gm2, 24, LP, xnT, ssp, rsp, xsp)
            C.dma(Dr["xn2"][:, cs].rearrange("(kc p) t -> p kc t", p=128), xnT.t[:, :, 0:NB], [xnT.b], [])
        P.C.S.barrier()


def phase_C2(P, LP, l, last):
    C, I, Dr = P.C, P.I, P.Dr
    NB = 512
    with ExitStack() as es:
        w1b = C.sb(es, [128, 8, 2 * DFF], BF16, "w1b")
        w2b = C.sb(es, [128, 22, D], BF16, "w2b")
        with ExitStack() as es_s:
            stg = C.pool_of(es_s, 3, [128, 1024], F32, "stg")
            load_cast(C, es_s, w1b, I["ffn_w_in"][l], 2 * DFF, stg)
            load_cast(C, es_s, w2b, I["ffn_w_out"][l], D, stg)
            P.C.S.barrier()
        htp = C.pool_of(es, 2, [128, D], F32, "ht")
        xnp = C.pool_of(es, 2, [128, 8, 512], BF16, "xn2")
        aT = C.sb(es, [128, 22, 512], BF16, "aT")
        sgp = C.pool_of(es, 2, [128, 512], F32, "sg")
        tmp = C.pool_of(es, 2, [128, 512], F32, "tmp")
        gt = C.sb(es, [128, D], F32, "gt")
        ssp = C.pool_of(es, 2, [128, 1], F32, "ss")
        rsp = C.pool_of(es, 2, [128, 1], F32, "rs")
        if last:
            fgb = C.sb(es, [128, D], F32, "fgb")
            C.dma(fgb.t[:], I["final_g"].partition_broadcast(128), [], [fgb.b])
        pbi = 0
        for (sn, Ts, c0, m) in streams(P):
            if last and sn == "c":
                continue
            C.dma(gt.t[:], Dr["gtb"][m * 2 + 1], [], [gt.b])
            hbuf = Dr["hx"] if sn == "x" else Dr["hc"]
            NB = min(512, Ts)
            def load_xn(blk_):
                cs_ = slice(c0 + blk_ * NB, c0 + blk_ * NB + NB)
                xn_ = xnp.get()
                C.dma(xn_.t[:, :, 0:NB], Dr["xn2"][:, cs_].rearrange("(kc p) t -> p kc t", p=128), [], [xn_.b])
                return xn_

            nxt_xn = load_xn(0)
            for blk in range(Ts // NB):
                cs = slice(c0 + blk * NB, c0 + blk * NB + NB)
                xn = nxt_xn
                for fb in range(22):
                    pg = P.PB[pbi % 6]; pbi += 1
                    pu = P.PB[pbi % 6]; pbi += 1
                    for dc in range(8):
                        C.mm(pg.t[:, 0:NB], w1b.t[:, dc, fb * 128:(fb + 1) * 128], xn.t[:, dc, 0:NB], dc == 0, dc == 7,
                             [w1b.b, xn.b], [pg.b])
                    for dc in range(8):
                        C.mm(pu.t[:, 0:NB], w1b.t[:, dc, DFF + fb * 128:DFF + (fb + 1) * 128], xn.t[:, dc, 0:NB],
                             dc == 0, dc == 7, [w1b.b, xn.b], [pu.b])
                    sg = sgp.get()
                    C.act(sg.t[:, 0:NB], pg.t[:, 0:NB], AF.Silu, [pg.b], [sg.b])
                    C.tt("dve", aT.t[:, fb, 0:NB], sg.t[:, 0:NB], pu.t[:, 0:NB], ALU.mult, [sg.b, pu.b], [aT.b])
                nxt_xn = load_xn(blk + 1) if blk + 1 < Ts // NB else None
                for j in range(NB // 128):
                    t0 = blk * NB + j * 128
                    ht = htp.get()
                    C.dma(ht.t[:], hbuf[t0:t0 + 128, :], [], [ht.b])
                    for nh in range(2):
                        pb = P.PB[pbi % 6]; pbi += 1
                        for fb in range(22):
                            C.mm(pb.t[:], aT.t[:, fb, j * 128:(j + 1) * 128], w2b.t[:, fb, nh * 512:(nh + 1) * 512],
                                 fb == 0, fb == 21, [aT.b, w2b.b], [pb.b])
                        tm = tmp.get()
                        C.tt("dve", tm.t[:], pb.t[:], gt.t[:, nh * 512:(nh + 1) * 512], ALU.mult, [pb.b, gt.b], [tm.b])
                        C.tt("pool", ht.t[:, nh * 512:(nh + 1) * 512], ht.t[:, nh * 512:(nh + 1) * 512], tm.t[:],
                             ALU.add, [tm.b], [ht.b])
                    if last:
                        ss = ssp.get(); rs = rsp.get()
                        tm = tmp.get(); tm2 = tmp.get()
                        C.act(tm.t[:], ht.t[:, 0:512], AF.Square, [ht.b], [tm.b, ss.b], accum=ss.t[:])
                        C.act(tm.t[:], ht.t[:, 512:1024], AF.Square, [ht.b], [tm.b, rs.b], accum=rs.t[:])
                        C.tt("dve", ss.t[:], ss.t[:], rs.t[:], ALU.add, [rs.b], [ss.b])
                        C.act(rs.t[:], ss.t[:], AF.Sqrt, [ss.b], [rs.b], scale=1.0 / D, bias=1e-6)
                        C.recip(rs.t[:], rs.t[:], [rs.b], [rs.b])
                        C.stt("dve", ht.t[:], ht.t[:], rs.t[:], fgb.t[:], ALU.mult, ALU.mult, [rs.b, fgb.b], [ht.b])
                        C.dma(P.out[t0:t0 + 128, :], ht.t[:], [ht.b], [])
                    else:
                        C.dma(hbuf[t0:t0 + 128, :], ht.t[:], [ht.b], [])
        P.C.S.barrier()


def make_in_map(inp, b, T, TC, L):
    f = lambda a: np.ascontiguousarray(np.asarray(a, dtype=np.float32))
    m = {}
    m["x"] = f(inp["x"][b]); m["ctx"] = f(inp["ctx"][b])
    m["cvec"] = f(np.stack([np.asarray(inp["c"][b]), np.asarray(inp["c_ctx"])], axis=0))
    m["w_mod"] = f(inp["w_mod"]); m["b_mod"] = f(inp["b_mod"])
    m["norm1_g"] = f(inp["norm1_g"]); m["norm2_g"] = f(inp["norm2_g"])
    m["w_in"] = f(inp["w_in"]); m["w_out"] = f(inp["w_out"])
    m["mu_prev"] = f(inp["rwkv_mu_prev"]); m["mu_next"] = f(inp["rwkv_mu_next"])
    m["w0"] = f(np.asarray(inp["rwkv_w0"]).reshape(L, 512)); m["w2"] = f(np.asarray(inp["rwkv_w2"]).reshape(L, 128, 256))
    m["a0"] = f(np.asarray(inp["rwkv_a0"]).reshape(L, 512)); m["a2"] = f(np.asarray(inp["rwkv_a2"]).reshape(L, 128, 256))
    m["g2"] = f(inp["rwkv_g2"])
    m["k_k"] = f(inp["rwkv_k_k"]); m["k_a"] = f(inp["rwkv_k_a"])
    m["r_k"] = f(np.asarray(inp["rwkv_r_k"]).reshape(L, 256))
    m["gn_w"] = f(inp["rwkv_gn_w"]); m["gn_b"] = f(inp["rwkv_gn_b"])
    m["pool_scale"] = f(inp["pool_scale"]); m["dw_b"] = f(inp["conv_dw_b"])
    m["ln_g"] = f(inp["conv_ln_g"]); m["ln_b"] = f(inp["conv_ln_b"])
    m["pool_w"] = f(inp["pool_w"]); m["fourier_w"] = f(inp["fourier_w"])
    m["dw_w"] = f(inp["conv_dw_w"]); m["conv_pw"] = f(inp["conv_pw"])
    m["ffn_w_in"] = f(inp["ffn_w_in"]); m["ffn_w_out"] = f(inp["ffn_w_out"])
    m["final_g"] = f(inp["final_norm_g"])
    return m


_CONST_CACHE = {}


def const_map(T, TC):
    key = (T, TC)
    if key not in _CONST_CACHE:
        c = {}
        c["misc"] = _misc_consts()
        c["poolm"] = _bf(_pool_mats())
        cc3, wr_x, gm_x = _fourier_consts(T)
        _, wr_c, gm_c = _fourier_consts(TC)
        c["cc3"] = cc3; c["wr_x"] = wr_x; c["gm_x"] = gm_x; c["wr_c"] = wr_c; c["gm_c"] = gm_c
        pr, pc = _pos_tables(T)
        c["pos_row"] = pr; c["pos_col"] = pc
        sel = np.zeros((2, 2, 128), np.float32)
        sel[0, 0, :] = 1.0
        sel[1, 1, :] = 1.0
        c["sel"] = sel
        _CONST_CACHE[key] = c
    return _CONST_CACHE[key]


def kernel(**inp):
    x = np.asarray(inp["x"])
    B, T, _ = x.shape
    TC = np.asarray(inp["ctx"]).shape[1]
    L = np.asarray(inp["w_in"]).shape[0]
    nc = build_program(T, TC, L)
    cm = const_map(T, TC)
    in_maps = []
    for b in range(B):
        m = make_in_map(inp, b, T, TC, L)
        m.update(cm)
        in_maps.append(m)
    res = run_bass_kernel_spmd(nc, in_maps, core_ids=list(range(B)))
    return np.stack([np.asarray(r["out"], dtype=np.float32) for r in res.results], axis=0)
```

```python
import numpy as np
import ml_dtypes
from contextlib import ExitStack
import concourse.bass as bass
import concourse.mybir as mybir
from concourse.bass_utils import run_bass_kernel_spmd

F32 = mybir.dt.float32
BF16 = mybir.dt.bfloat16
AF = mybir.ActivationFunctionType
ALU = mybir.AluOpType
AX = mybir.AxisListType

D = 1024
NIN = 2176
DFF = 2816
GW = 256
RW = 1152
N_DMA_SEMS = 12
NQ = {"sp": N_DMA_SEMS}


class Buf:
    __slots__ = ("w", "r", "x")

    def __init__(self):
        self.w = {}
        self.r = {}
        self.x = False


class Acc:
    __slots__ = ("b",)

    def __init__(self, b):
        self.b = b


class Sched:
    def __init__(self, nc, comp, dma):
        self.nc = nc
        self.streams = {k: [] for k in ("pe", "act", "dve", "pool", "sp")}
        self.sems = dict(comp)
        self.cnt = {k: 0 for k in comp}
        self.seen = {k: {} for k in self.streams}
        self.dma_sems = dma
        self.dma_cnt = {q: 0 for q in dma}

    def _need(self, eng, deps):
        st = self.streams[eng]
        seen = self.seen[eng]
        for key, val in deps.items():
            if eng == "pe" and key == "pe":
                continue
            if seen.get(key, 0) >= val:
                continue
            seen[key] = val
            st.append(("wait", key, val))

    def _sem_of(self, key):
        if isinstance(key, str):
            return self.sems[key]
        return self.dma_sems[key[0]][key[1]]

    @staticmethod
    def _collect(reads, writes, eng=None):
        deps = {}
        for b in reads:
            for k, v in b.w.items():
                if deps.get(k, 0) < v:
                    deps[k] = v
            if b.x:
                for k, v in b.r.items():
                    if k != eng and deps.get(k, 0) < v:
                        deps[k] = v
        for b in writes:
            if isinstance(b, Acc):
                for k, v in b.b.r.items():
                    if deps.get(k, 0) < v:
                        deps[k] = v
                continue
            for k, v in b.w.items():
                if deps.get(k, 0) < v:
                    deps[k] = v
            for k, v in b.r.items():
                if deps.get(k, 0) < v:
                    deps[k] = v
        return deps

    @staticmethod
    def _record(key, val, reads, writes):
        for b in reads:
            if b.r.get(key, 0) < val:
                b.r[key] = val
        for b in writes:
            if isinstance(b, Acc):
                if b.b.w.get(key, 0) < val:
                    b.b.w[key] = val
                continue
            b.w = {key: val}
            b.r = {}

    def op(self, eng, fn, reads=(), writes=()):
        deps = self._collect(reads, writes, eng)
        self._need(eng, deps)
        self.cnt[eng] += 1
        val = self.cnt[eng]
        self.streams[eng].append(("op", fn, eng))
        self._record(eng, val, reads, writes)

    def dma(self, out, in_, reads=(), writes=(), q="sp"):
        deps = self._collect(reads, writes)
        i = self.dma_cnt[q]
        self.dma_cnt[q] += 1
        slot = i % N_DMA_SEMS
        rnd = i // N_DMA_SEMS
        key = (q, slot)
        if rnd > 0:
            deps[key] = max(deps.get(key, 0), 16 * rnd)
        self._need(q, deps)
        self.streams[q].append(("dma", out, in_, key))
        self._record(key, 16 * (rnd + 1), reads, writes)

    def barrier(self):
        deps = {k: v for k, v in self.cnt.items() if v > 0}
        for q, n in self.dma_cnt.items():
            for slot in range(min(n, N_DMA_SEMS)):
                rounds = (n - 1 - slot) // N_DMA_SEMS + 1
                deps[(q, slot)] = 16 * rounds
        for eng in self.streams:
            d = dict(deps)
            self._need(eng, d)

    def emit(self):
        nc = self.nc
        streams = self.streams
        sem_of = self._sem_of
        sems = self.sems

        def run(name, e):
            for it in streams[name]:
                if it[0] == "wait":
                    e.wait_ge(sem_of(it[1]), it[2])
                elif it[0] == "op":
                    it[1](e).then_inc(sems[it[2]], 1)
                else:
                    e.dma_start(out=it[1], in_=it[2]).then_inc(sem_of(it[3]), 16)

        with nc.Block() as block:
            @block.tensor
            def _(e):
                run("pe", e)

            @block.scalar
            def _(e):
                run("act", e)

            @block.vector
            def _(e):
                run("dve", e)

            @block.gpsimd
            def _(e):
                run("pool", e)

            @block.sync
            def _(e):
                run("sp", e)


class Tl:
    __slots__ = ("t", "b")

    def __init__(self, t):
        self.t = t
        self.b = Buf()


class Ctx:
    def __init__(self, nc, S):
        self.nc = nc
        self.S = S
        self.uid = 0
        self.rr = 0

    def sb(self, es, shape, dt, name="t"):
        self.uid += 1
        return Tl(es.enter_context(self.nc.sbuf_tensor(f"{name}_{self.uid}", list(shape), dt)))

    def pool_of(self, es, n, shape, dt, name="p"):
        return Rot([self.sb(es, shape, dt, name) for _ in range(n)])

    def mm(self, out, lhsT, rhs, start, stop, reads, writes):
        self.S.op("pe", lambda e: e.matmul(out, lhsT=lhsT, rhs=rhs, start=start, stop=stop), reads, writes)

    def tr(self, out, in_, ident, reads, writes):
        self.S.op("pe", lambda e: e.transpose(out=out, in_=in_, identity=ident), reads, writes)

    def act(self, out, in_, func, reads, writes, scale=1.0, bias=0.0, accum=None):
        if accum is None:
            self.S.op("act", lambda e: e.activation(out=out, in_=in_, func=func, scale=scale, bias=bias), reads, writes)
        else:
            self.S.op("act", lambda e: e.activation(out=out, in_=in_, func=func, scale=scale, bias=bias,
                                                    accum_out=accum), reads, writes)

    def tt(self, eng, out, in0, in1, op, reads, writes):
        self.S.op(eng, lambda e: e.tensor_tensor(out=out, in0=in0, in1=in1, op=op), reads, writes)

    def ts(self, eng, out, in0, s1, s2, op0, op1, reads, writes):
        if s2 is None:
            self.S.op(eng, lambda e: e.tensor_scalar(out=out, in0=in0, scalar1=s1, scalar2=None, op0=op0), reads, writes)
        else:
            self.S.op(eng, lambda e: e.tensor_scalar(out=out, in0=in0, scalar1=s1, scalar2=s2, op0=op0, op1=op1),
                      reads, writes)

    def stt(self, eng, out, in0, sc, in1, op0, op1, reads, writes):
        self.S.op(eng, lambda e: e.scalar_tensor_tensor(out=out, in0=in0, scalar=sc, in1=in1, op0=op0, op1=op1),
                  reads, writes)

    def cp(self, eng, out, in_, reads, writes):
        if eng == "act":
            self.S.op("act", lambda e: e.activation(out=out, in_=in_, func=AF.Copy), reads, writes)
        else:
            self.S.op(eng, lambda e: e.tensor_copy(out=out, in_=in_), reads, writes)

    def memset(self, eng, ap, val, writes):
        self.S.op(eng, lambda e: e.memset(ap, val), (), writes)

    def recip(self, out, in_, reads, writes):
        self.S.op("dve", lambda e: e.reciprocal(out=out, in_=in_), reads, writes)

    def dma(self, out, in_, reads, writes, q="sp"):
        self.S.dma(out, in_, reads, writes, q=q)

    def alt(self, engs=("act", "dve")):
        self.rr += 1
        return engs[self.rr % len(engs)]

    def evac(self, out, in_, reads, writes, engs=("act", "dve")):
        self.cp(self.alt(engs), out, in_, reads, writes)


def interleave(*gens):
    gens = [g for g in gens if g is not None]
    while gens:
        for g in list(gens):
            try:
                next(g)
            except StopIteration:
                gens.remove(g)


class Rot:
    def __init__(self, items):
        self.items = items
        self.i = 0

    def get(self):
        it = self.items[self.i % len(self.items)]
        self.i += 1
        return it


def _bf(a):
    return np.ascontiguousarray(a.astype(ml_dtypes.bfloat16))


def _pool_mats():
    Tt = 512
    t = np.arange(Tt)
    out = np.zeros((4, 5, 128, 128), np.float64)
    for gi, w in enumerate((2, 4, 8, 16)):
        hw = w // 2
        lo0, hi0 = np.clip(t - hw, 0, Tt), np.clip(t + hw, 0, Tt)
        lo1, hi1 = np.clip(t - hw + 1, 0, Tt), np.clip(t + hw + 1, 0, Tt)
        cnt = (hi0 - lo0) + (hi1 - lo1)
        M = np.zeros((Tt, Tt))
        for tt in range(Tt):
            M[tt, lo0[tt]:hi0[tt]] += 1.0
            M[tt, lo1[tt]:hi1[tt]] += 1.0
            M[tt] /= cnt[tt]
            M[tt, tt] -= 1.0
        blk = lambda ti, si: M[ti * 128:(ti + 1) * 128, si * 128:(si + 1) * 128].T
        out[gi, 0] = blk(1, 0)
        out[gi, 1] = blk(1, 1)
        out[gi, 2] = blk(1, 2)
        out[gi, 3] = blk(0, 0)
        out[gi, 4] = blk(3, 3)
    return out


def _fourier_consts(T):
    R = T // 64
    c = np.arange(64)
    ang = 2 * np.pi * np.outer(c, c) / 64.0
    nrm = 1.0 / np.sqrt(64.0 * T)
    cc3 = np.zeros((128, 384))
    for g in range(2):
        sl = slice(g * 64, (g + 1) * 64)
        cc3[sl, 0 + g * 64:0 + (g + 1) * 64] = np.cos(ang)
        cc3[sl, 128 + g * 64:128 + (g + 1) * 64] = -np.sin(ang)
        cc3[sl, 256 + g * 64:256 + (g + 1) * 64] = -np.cos(ang)
    t1 = np.arange(R)
    angR = 2 * np.pi * np.outer(t1, t1) / R
    wr = np.zeros((128, 2, R))
    wr[:R, 0] = np.cos(angR)
    wr[:R, 1] = np.sin(angR)
    t2 = np.arange(64)
    k1 = np.arange(R)
    k2 = np.arange(64)
    th = 2 * np.pi * (t2[:, None, None] * k1[None, :, None] / T + t2[:, None, None] * k2[None, None, :] / 64.0)
    gm = np.concatenate([np.cos(th), np.sin(th)], axis=0) * nrm
    return _bf(cc3), _bf(wr), _bf(gm)


def _pos_tables(T):
    quarter = D // 4
    omega = 1.0 / (10000.0 ** (np.arange(quarter, dtype=np.float32) / np.float32(quarter)))
    omega = omega.astype(np.float32)

    def enc(p):
        ang = (p[:, None].astype(np.float32) * omega[None, :]).astype(np.float32)
        return np.concatenate([np.sin(ang), np.cos(ang)], axis=-1).astype(np.float32)

    rows = T // 64
    return enc(np.arange(rows, dtype=np.float32)), enc(np.arange(64, dtype=np.float32))


def _misc_consts():
    i = np.arange(128)
    su = (i[:, None] < i[None, :]).astype(np.float32)
    iu = (i[:, None] <= i[None, :]).astype(np.float32)
    sl = su.T.copy()
    il = iu.T.copy()
    ident = np.eye(128, dtype=np.float32)
    ones = np.ones((128, 128), np.float32)
    blk = np.zeros((128, 128), np.float32)
    blk[:64, :64] = 1.0
    blk[64:, 64:] = 1.0
    mats = [ident, ones, blk, su, iu, sl, il]
    mats.append(((i[:, None] // 8) == (i[None, :] // 8)).astype(np.float32))
    for b in (8, 16, 32, 64):
        same2 = (i[:, None] // (2 * b)) == (i[None, :] // (2 * b))
        diffb = (i[:, None] // b) != (i[None, :] // b)
        mats.append((same2 & diffb).astype(np.float32))
    return np.stack(mats, axis=0)


class Prog:
    pass


def build_program(T, TC, L, dbg=()):
    nc = bass.Bass("TRN2", target_bir_lowering=False)
    P = Prog()
    P.T, P.TC, P.L, P.TT = T, TC, L, T + TC
    P.dbg = dbg
    TT = P.TT
    R = T // 64
    RC = TC // 64

    def din(name, shape, dt=F32):
        return nc.dram_tensor(name, list(shape), dt, kind="ExternalInput").ap()

    def dscr(name, shape, dt=F32):
        kind = "ExternalOutput" if name in dbg else ("ExternalInput" if name + "_in" in dbg else "Internal")
        return nc.dram_tensor(name, list(shape), dt, kind=kind).ap()

    I = {}
    I["x"] = din("x", [T, D]); I["ctx"] = din("ctx", [TC, D]); I["cvec"] = din("cvec", [2, D])
    I["w_mod"] = din("w_mod", [L, D, 6 * D]); I["b_mod"] = din("b_mod", [L, 6 * D])
    I["norm1_g"] = din("norm1_g", [L, D]); I["norm2_g"] = din("norm2_g", [L, D])
    I["w_in"] = din("w_in", [L, D, NIN]); I["w_out"] = din("w_out", [L, D, D])
    I["mu_prev"] = din("mu_prev", [L, RW]); I["mu_next"] = din("mu_next", [L, RW])
    I["w0"] = din("w0", [L, 512]); I["w2"] = din("w2", [L, 128, 256])
    I["a0"] = din("a0", [L, 512]); I["a2"] = din("a2", [L, 128, 256])
    I["g2"] = din("g2", [L, 128, 256])
    for nm in ("k_k", "k_a", "r_k", "gn_w", "gn_b", "pool_scale", "dw_b", "ln_g", "ln_b"):
        I[nm] = din(nm, [L, 256])
    I["pool_w"] = din("pool_w", [L, 4, 64, 64]); I["fourier_w"] = din("fourier_w", [L, 256, 256])
    I["dw_w"] = din("dw_w", [L, 31, 256]); I["conv_pw"] = din("conv_pw", [L, 256, 256])
    I["ffn_w_in"] = din("ffn_w_in", [L, D, 2 * DFF]); I["ffn_w_out"] = din("ffn_w_out", [L, DFF, D])
    I["final_g"] = din("final_g", [D])
    I["misc"] = din("misc", [12, 128, 128]); I["poolm"] = din("poolm", [4, 5, 128, 128], BF16)
    I["cc3"] = din("cc3", [128, 384], BF16)
    I["wr_x"] = din("wr_x", [128, 2, R], BF16); I["gm_x"] = din("gm_x", [128, R, 64], BF16)
    I["wr_c"] = din("wr_c", [128, 2, RC], BF16); I["gm_c"] = din("gm_c", [128, RC, 64], BF16)
    I["pos_row"] = din("pos_row", [R, 512]); I["pos_col"] = din("pos_col", [64, 512])
    I["sel"] = din("sel", [2, 2, 128])
    out = nc.dram_tensor("out", [T, D], F32, kind="ExternalOutput").ap()
    P.I, P.out = I, out

    Dr = {}
    Dr["hx"] = dscr("hx", [T, D]); Dr["hc"] = dscr("hc", [TC, D])
    Dr["uR"] = dscr("uR", [RW, TT]); Dr["uM"] = dscr("uM", [1024, TT], BF16)
    Dr["yT"] = dscr("yT", [1024, TT], BF16); Dr["xn2"] = dscr("xn2", [1024, TT], BF16)
    Dr["sq"] = dscr("sq", [13, 256, TT]); Dr["yd"] = dscr("yd", [2, TT, 256])
    Dr["gtb"] = dscr("gtb", [4, 128, 1024])
    if "scan1" in dbg:
        Dr["dumpf"] = nc.dram_tensor("dumpf", [3, 128, 256], F32, kind="ExternalOutput").ap()
        Dr["dumpb"] = nc.dram_tensor("dumpb", [5, 128, 256], BF16, kind="ExternalOutput").ap()
    P.Dr = Dr

    with ExitStack() as es0:
        comp = {k: es0.enter_context(nc.semaphore(f"s_{k}")) for k in ("pe", "act", "dve", "pool")}
        dma = {"sp": [es0.enter_context(nc.semaphore(f"d_sp{i}")) for i in range(N_DMA_SEMS)],
               "act": [es0.enter_context(nc.semaphore(f"d_act{i}")) for i in range(N_DMA_SEMS)]}
        S = Sched(nc, comp, dma)
        C = Ctx(nc, S)
        P.C = C
        P.PB = [Tl(es0.enter_context(nc.psum_tensor(f"pb{i}", [128, 512], F32))) for i in range(6)]
        P.PT = [Tl(es0.enter_context(nc.psum_tensor(f"pt{i}", [128, 1024], BF16))) for i in range(2)]
        for t_ in P.PB + P.PT:
            t_.b.x = True
        P.misc = C.sb(es0, [128, 12, 128], F32, "misc")
        C.dma(P.misc.t[:], I["misc"].rearrange("k p n -> p k n"), [], [P.misc.b])
        P.identb = C.sb(es0, [128, 128], BF16, "identb")
        C.cp("dve", P.identb.t[:], P.misc.t[:, 0, :], [P.misc.b], [P.identb.b])
        P.maskb = C.sb(es0, [128, 4, 128], BF16, "maskb")
        C.cp("dve", P.maskb.t[:], P.misc.t[:, 3:7, :], [P.misc.b], [P.maskb.b])
        P.onesb = C.sb(es0, [128, 128], BF16, "onesb")
        C.cp("dve", P.onesb.t[:], P.misc.t[:, 1, :], [P.misc.b], [P.onesb.b])

        for l in range(L):
            with ExitStack() as esl:
                last = (l == L - 1)
                import os as _os
                stop = _os.environ.get("KSTOP", "")
                LP = phase_params(P, esl, l)
                S.barrier()
                if stop == "params":
                    break
                phase_A(P, LP, l)
                S.barrier()
                if stop == "A":
                    break
                rwkv_prep(P, LP, l)
                if stop == "prep":
                    break
                rwkv_scan_q(P, LP, l)
                if stop == "scan0":
                    break
                rwkv_readout(P, LP, l)
                if stop == "rwkv":
                    break
                phase_pool(P, LP, l)
                if stop == "pool":
                    break
                phase_fourier(P, LP, l)
                if stop == "fourier":
                    break
                phase_conv(P, LP, l)
                if stop == "conv":
                    break
                phase_C1(P, LP, l, last)
                if stop == "C1":
                    break
                phase_C2(P, LP, l, last)
        S.barrier()
        S.emit()
    return nc


def phase_params(P, esl, l):
    C, I, S = P.C, P.I, P.C.S
    LP = Prog()
    LP.pp1 = C.sb(esl, [128, 128], F32, "pp1")
    LP.pp2 = C.sb(esl, [128, 64], F32, "pp2")
    LP.modT = C.sb(esl, [128, 48, 2], F32, "modT")
    LP.gm1 = C.sb(esl, [128, 8, 2], F32, "gm1")
    LP.gm2 = C.sb(esl, [128, 8, 2], F32, "gm2")
    identf = P.misc.t[:, 0, :]
    with ExitStack() as es:
        rs1 = C.sb(es, [128, 128], F32, "rs1")
        rs2 = C.sb(es, [128, 128], F32, "rs2")
        rs3 = C.sb(es, [128, 128], F32, "rs3")
        for r in (rs1, rs2, rs3):
            C.memset("pool", r.t[:], 0.0, [r.b])
        row = [0]

        def put(dst, vec_ap, n):
            C.dma(dst.t[row[0]:row[0] + n, :], vec_ap.rearrange("(n p) -> n p", p=128), [], [dst.b])
            row[0] += n

        put(rs1, I["b_mod"][l], 48)
        put(rs1, I["norm1_g"][l], 8); put(rs1, I["norm2_g"][l], 8)
        put(rs1, I["mu_prev"][l], 9); put(rs1, I["mu_next"][l], 9)
        put(rs1, I["w0"][l], 4); put(rs1, I["a0"][l], 4)
        for nm in ("k_k", "k_a", "r_k", "gn_w", "gn_b", "pool_scale", "dw_b", "ln_g", "ln_b"):
            put(rs1, I[nm][l], 2)
        put(rs1, I["final_g"], 8)
        assert row[0] == 116
        C.dma(rs2.t[0:62, :], I["dw_w"][l].rearrange("j (n p) -> (j n) p", p=128), [], [rs2.b])
        C.dma(rs3.t[0:16, :], I["cvec"].rearrange("m (n p) -> (m n) p", p=128), [], [rs3.b])
        pb = P.PB[0]
        C.tr(pb.t[:, 0:128], rs1.t[:], identf, [rs1.b, P.misc.b], [pb.b])
        C.cp("dve", LP.pp1.t[:], pb.t[:, 0:128], [pb.b], [LP.pp1.b])
        pb = P.PB[1]
        C.tr(pb.t[:, 0:128], rs2.t[:], identf, [rs2.b, P.misc.b], [pb.b])
        C.cp("dve", LP.pp2.t[:], pb.t[:, 0:64], [pb.b], [LP.pp2.b])
        pb = P.PB[2]
        C.tr(pb.t[:, 0:128], rs3.t[:], identf, [rs3.b, P.misc.b], [pb.b])
        scT = C.sb(es, [128, 16], F32, "scT")
        C.act(scT.t[:], pb.t[:, 0:16], AF.Silu, [pb.b], [scT.b])
        scv = scT.t[:].rearrange("p (m dc) -> p dc m", m=2)
        sel = C.sb(es, [2, 2, 128], F32, "sel")
        C.dma(sel.t[:], I["sel"], [], [sel.b])
        wmp = C.pool_of(es, 4, [128, 8, 512], F32, "wm")
        grow = C.sb(es, [2, 512], F32, "grow")
        gts = C.pool_of(es, 2, [128, 512], F32, "gts")
        bmb = C.sb(es, [128, 512], F32, "bmb")
        pfm = P.PB[3]
        for cg in range(12):
            wm = wmp.get()
            C.dma(wm.t[:], I["w_mod"][l][:, cg * 512:(cg + 1) * 512].rearrange("(dc p) n -> p dc n", p=128),
                  [], [wm.b], q="sp" if cg % 2 == 0 else "act")
            for j in range(4):
                nb = cg * 4 + j
                for dc in range(8):
                    C.mm(pfm.t[:, nb * 2:nb * 2 + 2], wm.t[:, dc, j * 128:(j + 1) * 128], scv[:, dc, :],
                         dc == 0, dc == 7, [wm.b, scT.b], [pfm.b])
            if cg in (4, 5, 10, 11):
                g = 0 if cg < 6 else 1
                half = cg % 2
                prow = P.PB[4]
                for dc in range(8):
                    C.mm(prow.t[0:2, :], scv[:, dc, :], wm.t[:, dc, :], dc == 0, dc == 7, [wm.b, scT.b], [prow.b])
                C.cp("dve", grow.t[:], prow.t[0:2, :], [prow.b], [grow.b])
                C.dma(bmb.t[:], I["b_mod"][l][cg * 512:(cg + 1) * 512].partition_broadcast(128), [], [bmb.b])
                for m in range(2):
                    pbc = P.PB[5]
                    C.mm(pbc.t[:], sel.t[:, m, :], grow.t[:], True, True, [sel.b, grow.b], [pbc.b])
                    gt = gts.get()
                    C.tt("dve", gt.t[:], pbc.t[:], bmb.t[:], ALU.add, [pbc.b, bmb.b], [gt.b])
                    C.dma(P.Dr["gtb"][m * 2 + g, :, half * 512:(half + 1) * 512], gt.t[:], [gt.b], [])
        pv = pfm.t[:, 0:96].rearrange("p (nb m) -> p nb m", m=2)
        for m in range(2):
            C.tt("dve", LP.modT.t[:, :, m], pv[:, :, m], LP.pp1.t[:, 0:48], ALU.add, [pfm.b, LP.pp1.b], [LP.modT.b])
        for m in range(2):
            C.stt("dve", LP.gm1.t[:, :, m], LP.modT.t[:, 8:16, m], 1.0, LP.pp1.t[:, 48:56], ALU.add, ALU.mult,
                  [LP.modT.b, LP.pp1.b], [LP.gm1.b])
            C.stt("dve", LP.gm2.t[:, :, m], LP.modT.t[:, 32:40, m], 1.0, LP.pp1.t[:, 56:64], ALU.add, ALU.mult,
                  [LP.modT.b, LP.pp1.b], [LP.gm2.b])
        S.barrier()
    return LP


C_MUP, C_MUN, C_W0, C_A0 = 64, 73, 82, 86
C_KK, C_KA, C_RK, C_GNW, C_GNB, C_PSC, C_DWB, C_LNG, C_LNB, C_FG = 90, 92, 94, 96, 98, 100, 102, 104, 106, 108


def load_cast(C, es, dst, src_ap, ncols, stg_pool, engs=("pool", "dve", "act")):
    nk = src_ap.shape[0] // 128
    CH = 1024
    for k in range(nk):
        for c0 in range(0, ncols, CH):
            cw = min(CH, ncols - c0)
            st = stg_pool.get()
            C.dq = getattr(C, "dq", 0) + 1
            C.dma(st.t[:, 0:cw], src_ap[k * 128:(k + 1) * 128, c0:c0 + cw], [], [st.b], q="sp" if C.dq % 2 else "act")
            C.cp(C.alt(engs), dst.t[:, k, c0:c0 + cw], st.t[:, 0:cw], [st.b], [Acc(dst.b)])


def norm_block_g(P, C, hts, m, gm, sh_col0, LP, xnT, ss_p, rs_p, xs_p):
    n = len(hts)
    sss = [ss_p.get() for _ in range(n)]; rss = [rs_p.get() for _ in range(n)]; xss = [xs_p.get() for _ in range(n)]
    for j in range(n):
        C.act(xss[j].t[:], hts[j].t[:], AF.Square, [hts[j].b], [xss[j].b, sss[j].b], accum=sss[j].t[:])
    yield
    for j in range(n):
        C.act(rss[j].t[:], sss[j].t[:], AF.Sqrt, [sss[j].b], [rss[j].b], scale=1.0 / D, bias=1e-6)
    for j in range(n):
        C.recip(rss[j].t[:], rss[j].t[:], [rss[j].b], [rss[j].b])
    yield
    for j in range(n):
        C.act(xss[j].t[:], hts[j].t[:], AF.Copy, [hts[j].b, rss[j].b], [xss[j].b], scale=rss[j].t[:])
    yield
    for j in range(n):
        pt = P.PT[j % 2]
        yield
        for dc in range(8):
            C.tr(pt.t[:, dc * 128:(dc + 1) * 128], xss[j].t[:, dc * 128:(dc + 1) * 128], P.identb.t[:],
                 [xss[j].b, P.identb.b], [pt.b])
        for dc in range(8):
            o = xnT.t[:, dc, j * 128:(j + 1) * 128]
            i_ = pt.t[:, dc * 128:(dc + 1) * 128]
            if j % 2 == 0:
                C.act(o, i_, AF.Identity, [pt.b, gm.b, LP.modT.b], [Acc(xnT.b)], scale=gm.t[:, dc, m:m + 1],
                      bias=LP.modT.t[:, sh_col0 + dc, m:m + 1])
            else:
                C.ts("dve", o, i_, gm.t[:, dc, m:m + 1], LP.modT.t[:, sh_col0 + dc, m:m + 1], ALU.mult, ALU.add,
                     [pt.b, gm.b, LP.modT.b], [Acc(xnT.b)])


def norm_block(*a):
    for _ in norm_block_g(*a):
        pass


def streams(P):
    return [("c", P.TC, 0, 1), ("x", P.T, P.TC, 0)]


def phase_A(P, LP, l):
    C, I, Dr = P.C, P.I, P.Dr
    with ExitStack() as es:
        winb = C.sb(es, [128, 8, NIN], BF16, "winb")
        stg = C.pool_of(es, 3, [128, 1024], F32, "stg")
        load_cast(C, es, winb, I["w_in"][l], NIN, stg)
        htp = C.pool_of(es, 8, [128, D], F32, "ht")
        ssp = C.pool_of(es, 8, [128, 1], F32, "ss")
        rsp = C.pool_of(es, 8, [128, 1], F32, "rs")
        xsp = C.pool_of(es, 8, [128, D], BF16, "xs")
        xnp = C.pool_of(es, 2, [128, 8, 512], BF16, "xnT")
        osf = C.pool_of(es, 4, [128, 512], F32, "osf")
        osb = C.pool_of(es, 4, [128, 512], BF16, "osb")
        posc = C.sb(es, [128, 512], F32, "posc")
        posr = C.pool_of(es, 4, [128, 512], F32, "posr")
        if l == 0:
            for hh in range(2):
                C.dma(posc.t[hh * 64:(hh + 1) * 64, :], I["pos_col"], [], [posc.b])
        st = {"pbi": 0}
        blocks = []
        for (sn, Ts, c0, m) in streams(P):
            NB = min(512, Ts)
            for blk in range(Ts // NB):
                blocks.append((sn, Ts, c0, m, NB, blk))

        def do_norm(bd, out_box):
            sn, Ts, c0, m, NB, blk = bd
            src = (I["x"] if sn == "x" else I["ctx"]) if l == 0 else (Dr["hx"] if sn == "x" else Dr["hc"])
            xnT = xnp.get()
            hts = []
            for j in range(NB // 128):
                t0 = blk * NB + j * 128
                ht = htp.get()
                C.dma(ht.t[:], src[t0:t0 + 128, :], [], [ht.b])
                if l == 0 and sn == "x":
                    pr = posr.get()
                    for hh in range(2):
                        rr = t0 // 64 + hh
                        C.dma(pr.t[hh * 64:(hh + 1) * 64, :], I["pos_row"][rr, :].partition_broadcast(64), [], [pr.b])
                    C.tt("pool", ht.t[:, 0:512], ht.t[:, 0:512], pr.t[:], ALU.add, [pr.b], [ht.b])
                    C.tt("pool", ht.t[:, 512:1024], ht.t[:, 512:1024], posc.t[:], ALU.add, [posc.b], [ht.b])
                    C.dma(Dr["hx"][t0:t0 + 128, :], ht.t[:], [ht.b], [])
                hts.append(ht)
            out_box.append(xnT)
            yield
            yield from norm_block_g(P, C, hts, m, LP.gm1, 0, LP, xnT, ssp, rsp, xsp)

        def do_mm(bd, xnT):
            sn, Ts, c0, m, NB, blk = bd
            for nb in range(17):
                pb = P.PB[st["pbi"] % 6]; st["pbi"] += 1
                for dc in range(8):
                    C.mm(pb.t[:, 0:NB], winb.t[:, dc, nb * 128:(nb + 1) * 128], xnT.t[:, dc, 0:NB],
                         dc == 0, dc == 7, [winb.b, xnT.b], [pb.b])
                cs = slice(c0 + blk * NB, c0 + blk * NB + NB)
                if nb < 9:
                    o = osf.get()
                    C.evac(o.t[:, 0:NB], pb.t[:, 0:NB], [pb.b], [o.b])
                    C.dma(Dr["uR"][nb * 128:(nb + 1) * 128, cs], o.t[:, 0:NB], [o.b], [])
                else:
                    o = osb.get()
                    C.evac(o.t[:, 0:NB], pb.t[:, 0:NB], [pb.b], [o.b])
                    C.dma(Dr["uM"][(nb - 9) * 128:(nb - 8) * 128, cs], o.t[:, 0:NB], [o.b], [])
                yield

        box = []
        interleave(do_norm(blocks[0], box))
        cur = box[0]
        for i, bd in enumerate(blocks):
            box = []
            interleave(do_mm(bd, cur), do_norm(blocks[i + 1], box) if i + 1 < len(blocks) else None)
            cur = box[0] if box else None
        P.C.S.barrier()


def phase_rwkv(P, LP, l):
    rwkv_prep(P, LP, l)
    if "stop_prep" in P.dbg:
        return
    rwkv_scan_q(P, LP, l)
    rwkv_readout(P, LP, l)


def rwkv_prep(P, LP, l):
    C, I, Dr, S = P.C, P.I, P.Dr, P.C.S
    pp = LP.pp1
    blkf = P.misc.t[:, 2, :]
    onesf = P.misc.t[:, 1, :]
    sq = Dr["sq"]
    with ExitStack() as es:
        stg = C.pool_of(es, 2, [128, 1024], F32, "stg")
        w2b = C.sb(es, [128, 1, 256], BF16, "w2b"); load_cast(C, es, w2b, I["w2"][l], 256, stg)
        a2b = C.sb(es, [128, 1, 256], BF16, "a2b"); load_cast(C, es, a2b, I["a2"][l], 256, stg)
        g2b = C.sb(es, [128, 1, 256], BF16, "g2b"); load_cast(C, es, g2b, I["g2"][l], 256, stg)
        c0t = C.sb(es, [128, 9], F32, "c0t")
        C.tt("dve", c0t.t[:], pp.t[:, C_MUP:C_MUP + 9], pp.t[:, C_MUN:C_MUN + 9], ALU.add, [pp.b], [c0t.b])
        C.ts("dve", c0t.t[:], c0t.t[:], -1.0, 1.0, ALU.mult, ALU.add, [], [c0t.b])
        omka = C.sb(es, [128, 2], F32, "omka")
        C.ts("dve", omka.t[:], pp.t[:, C_KA:C_KA + 2], -1.0, 1.0, ALU.mult, ALU.add, [pp.b], [omka.b])
        dgs = C.sb(es, [128, 27, 128], F32, "dgs")
        identf_ = P.misc.t[:, 0, :]
        for n in range(9):
            for k_, col in enumerate((pp.t[:, C_MUP + n:C_MUP + n + 1], c0t.t[:, n:n + 1], pp.t[:, C_MUN + n:C_MUN + n + 1])):
                C.act(dgs.t[:, n * 3 + k_, :], identf_, AF.Copy, [P.misc.b, pp.b, c0t.b], [Acc(dgs.b)], scale=col)
        uhp = C.pool_of(es, 3, [128, 9, 514], F32, "uh")
        sxp = C.pool_of(es, 2, [128, 9, 512], F32, "sx")
        wk = C.pool_of(es, 34, [128, 512], F32, "wk")
        wkb = C.pool_of(es, 6, [128, 512], BF16, "wkb")
        pbs = {"i": 0}
        blocks = []
        for (sn, Ts, c0, m) in streams(P):
            NB = min(512, Ts)
            for blk in range(Ts // NB):
                blocks.append((sn, Ts, c0, m, NB, blk))

        def load_uh(bd):
            sn, Ts, c0, m, NB, blk = bd
            t0 = blk * NB
            uh = uhp.get()
            lo = 0 if blk == 0 else -1
            hi = 0 if blk == Ts // NB - 1 else 1
            if lo == 0:
                C.memset("pool", uh.t[:, :, 0:1], 0.0, [uh.b])
            if hi == 0:
                C.memset("pool", uh.t[:, :, NB + 1:NB + 2], 0.0, [uh.b])
            C.dma(uh.t[:, :, 1 + lo:NB + 1 + hi],
                  Dr["uR"][:, c0 + t0 + lo:c0 + t0 + NB + hi].rearrange("(n p) t -> p n t", p=128), [], [uh.b])
            return uh

        def tshift(bd, uh, sx):
            NB_ = bd[4]
            for n in range(9):
                pb = P.PB[pbs["i"] % 6]; pbs["i"] += 1
                for k_ in range(3):
                    C.mm(pb.t[:, 0:NB_], dgs.t[:, n * 3 + k_, :], uh.t[:, n, k_:k_ + NB_], k_ == 0, k_ == 2,
                         [dgs.b, uh.b], [pb.b])
                C.cp("act", sx.t[:, n, 0:NB_], pb.t[:, 0:NB_], [pb.b], [Acc(sx.b)])
                yield

        uh0 = load_uh(blocks[0])
        nxt_uh = load_uh(blocks[1]) if len(blocks) > 1 else None
        nxt_sx = sxp.get()
        interleave(tshift(blocks[0], uh0, nxt_sx))
        for bi, bd in enumerate(blocks):
            if True:
                sn, Ts, c0, m, NB, blk = bd
                t0 = blk * NB
                cs = slice(c0 + t0, c0 + t0 + NB)
                sx = nxt_sx
                if bi + 1 < len(blocks):
                    uh_n = nxt_uh
                    nxt_uh = load_uh(blocks[bi + 2]) if bi + 2 < len(blocks) else None
                    nxt_sx = sxp.get()
                    ts_gen = tshift(blocks[bi + 1], uh_n, nxt_sx)
                else:
                    ts_gen = None

                def store(qi, ct, tl):
                    C.dma(sq[qi, ct * 128:(ct + 1) * 128, cs], tl.t[:, 0:NB], [tl.b], [])

                tw = wkb.get(); alb = wkb.get(); sgl = wkb.get()
                C.act(tw.t[:, 0:NB], sx.t[:, 7, 0:NB], AF.Tanh, [sx.b], [tw.b])
                C.cp("pool", alb.t[:, 0:NB], sx.t[:, 8, 0:NB], [sx.b], [alb.b])
                C.act(sgl.t[:, 0:NB], sx.t[:, 0, 0:NB], AF.Sigmoid, [sx.b], [sgl.b])
                def ct_body(ct, sx=sx, tw=tw, alb=alb, sgl=sgl, cs=cs, NB=NB, store=store):
                    for qi, n in ((1, 1 + ct), (2, 5 + ct)):
                        C.dma(sq[qi, ct * 128:(ct + 1) * 128, cs], sx.t[:, n, 0:NB], [sx.b], [])
                    pb = P.PB[pbs["i"] % 6]; pbs["i"] += 1
                    C.mm(pb.t[:, 0:NB], g2b.t[:, 0, ct * 128:(ct + 1) * 128], sgl.t[:, 0:NB], True, True, [g2b.b, sgl.b], [pb.b])
                    gt_ = wk.get()
                    C.cp("act", gt_.t[:, 0:NB], pb.t[:, 0:NB], [pb.b], [gt_.b])
                    store(12, ct, gt_)
                    yield
                    kc = wk.get(); sk = wk.get(); rn = wk.get(); kk = wk.get()
                    C.act(kc.t[:, 0:NB], sx.t[:, 3 + ct, 0:NB], AF.Copy, [sx.b, pp.b], [kc.b], scale=pp.t[:, C_KK + ct:C_KK + ct + 1])
                    C.tt("pool", sk.t[:, 0:NB], kc.t[:, 0:NB], kc.t[:, 0:NB], ALU.mult, [kc.b], [sk.b])
                    pb = P.PB[pbs["i"] % 6]; pbs["i"] += 1
                    C.mm(pb.t[:, 0:NB], blkf, sk.t[:, 0:NB], True, True, [P.misc.b, sk.b], [pb.b])
                    C.act(rn.t[:, 0:NB], pb.t[:, 0:NB], AF.Sqrt, [pb.b], [rn.b])
                    C.ts("dve", rn.t[:, 0:NB], rn.t[:, 0:NB], 1e-12, None, ALU.max, None, [], [rn.b])
                    C.recip(rn.t[:, 0:NB], rn.t[:, 0:NB], [], [rn.b])
                    C.tt("dve", kk.t[:, 0:NB], kc.t[:, 0:NB], rn.t[:, 0:NB], ALU.mult, [kc.b, rn.b], [kk.b])
                    store(0, ct, kk)
                    yield
                    kds = []
                    for d in range(2):
                        ps = slice(64 * d, 64 * d + 64)
                        pw = P.PB[pbs["i"] % 6]; pbs["i"] += 1
                        C.mm(pw.t[:, 0:NB], w2b.t[ps, 0, ct * 128:(ct + 1) * 128], tw.t[ps, 0:NB], True, True, [w2b.b, tw.b], [pw.b])
                        pa = P.PB[pbs["i"] % 6]; pbs["i"] += 1
                        C.mm(pa.t[:, 0:NB], a2b.t[ps, 0, ct * 128:(ct + 1) * 128], alb.t[ps, 0:NB], True, True, [a2b.b, alb.b], [pa.b])
                        lw = wk.get(); ad = wk.get()
                        C.act(lw.t[:, 0:NB], pw.t[:, 0:NB], AF.Sigmoid, [pw.b, pp.b], [lw.b],
                              bias=pp.t[:, C_W0 + d * 2 + ct:C_W0 + d * 2 + ct + 1])
                        C.ts("dve", lw.t[:, 0:NB], lw.t[:, 0:NB], -0.6065306597126334, None, ALU.mult, None, [], [lw.b])
                        C.act(ad.t[:, 0:NB], pa.t[:, 0:NB], AF.Sigmoid, [pa.b, pp.b], [ad.b],
                              bias=pp.t[:, C_A0 + d * 2 + ct:C_A0 + d * 2 + ct + 1])
                        tmp = wk.get(); kd = wk.get(); al = wk.get()
                        C.ts("dve", tmp.t[:, 0:NB], ad.t[:, 0:NB], pp.t[:, C_KA + ct:C_KA + ct + 1], omka.t[:, ct:ct + 1],
                             ALU.mult, ALU.add, [ad.b, pp.b, omka.b], [tmp.b])
                        C.tt("pool", kd.t[:, 0:NB], sx.t[:, 3 + ct, 0:NB], tmp.t[:, 0:NB], ALU.mult, [sx.b, tmp.b], [kd.b])
                        C.stt("dve", al.t[:, 0:NB], ad.t[:, 0:NB], -1.0, kk.t[:, 0:NB], ALU.mult, ALU.mult, [ad.b, kk.b], [al.b])
                        store(3 + d, ct, kd); store(5 + d, ct, al)
                        yield
                        kds.append(kd)
                        cI = wk.get(); cX = wk.get()
                        for ch in range(NB // 128):
                            sl_ = slice(ch * 128, (ch + 1) * 128)
                            if d == 0:
                                S.op("dve", (lambda o, a, b: (lambda e: e.tensor_tensor_scan(out=o, data0=a, data1=b, initial=0.0,
                                     op0=ALU.mult, op1=ALU.add)))(cI.t[:, sl_], onesf, lw.t[:, sl_]), [lw.b, P.misc.b], [cI.b])
                                C.tt("pool", cX.t[:, sl_], cI.t[:, sl_], lw.t[:, sl_], ALU.subtract, [cI.b, lw.b], [cX.b])
                            else:
                                S.op("dve", (lambda o, a, b: (lambda e: e.tensor_tensor_scan(out=o, data0=a, data1=b, initial=0.0,
                                     op0=ALU.mult, op1=ALU.add)))(cI.t[:, sl_], onesf, lw.t[:, sl_]), [lw.b, P.misc.b], [cI.b])
                                C.ts("dve", cX.t[:, sl_], cI.t[:, sl_], cI.t[:, ch * 128 + 127:ch * 128 + 128], -1.0,
                                     ALU.subtract, ALU.mult, [cI.b], [cX.b])
                                C.tt("dve", cI.t[:, sl_], cX.t[:, sl_], lw.t[:, sl_], ALU.add, [cX.b, lw.b], [cI.b])
                        store(7 + 2 * d, ct, cI); store(8 + 2 * d, ct, cX)
                        yield
                    t1 = wk.get(); t3 = wk.get(); bo = wk.get()
                    C.tt("pool", t1.t[:, 0:NB], kds[0].t[:, 0:NB], kds[1].t[:, 0:NB], ALU.add, [kds[0].b, kds[1].b], [t1.b])
                    C.tt("pool", t1.t[:, 0:NB], t1.t[:, 0:NB], sx.t[:, 1 + ct, 0:NB], ALU.mult, [sx.b], [t1.b])
                    C.act(t3.t[:, 0:NB], t1.t[:, 0:NB], AF.Copy, [t1.b, pp.b], [t3.b], scale=pp.t[:, C_RK + ct:C_RK + ct + 1])
                    pb = P.PB[pbs["i"] % 6]; pbs["i"] += 1
                    C.mm(pb.t[:, 0:NB], blkf, t3.t[:, 0:NB], True, True, [P.misc.b, t3.b], [pb.b])
                    C.tt("dve", bo.t[:, 0:NB], pb.t[:, 0:NB], sx.t[:, 5 + ct, 0:NB], ALU.mult, [pb.b, sx.b], [bo.b])
                    store(11, ct, bo)
                    yield
                interleave(ct_body(0), ct_body(1), ts_gen)
        S.barrier()


import os as _os0
INV_BF16 = _os0.environ.get("INV_BF16", "1") == "1"


def rwkv_scan(P, LP, l):
    C, I, Dr, S = P.C, P.I, P.Dr, P.C.S
    sq = Dr["sq"]
    NC = P.TT // 128
    NCc = P.TC // 128
    orders = [list(range(NC)), list(range(NCc - 1, -1, -1)) + list(range(NC - 1, NCc - 1, -1))]
    gmasks = [(P.misc.t[:, 3:5, :]).rearrange("p a b -> p (a b)"), (P.misc.t[:, 5:7, :]).rearrange("p a b -> p (a b)")]
    nmasks = [P.misc.t[:, 5, :], P.misc.t[:, 3, :]]
    last_cols = [127, 0]
    identf = P.misc.t[:, 0, :]
    IDT = BF16 if INV_BF16 else F32
    ident_i = P.identb.t[:] if INV_BF16 else identf
    ident_b = P.identb.b if INV_BF16 else P.misc.b
    QIs = [(0, 1, 2, 3 + d, 5 + d, 7 + 2 * d, 8 + 2 * d) for d in range(2)]
    mk = lambda k: P.misc.t[:, 7 + k, :]
    with ExitStack() as es:
        Hs = [[C.sb(es, [128, 64], F32, "H") for _ in range(2)] for _ in range(2)]
        Hbs = [[C.sb(es, [128, 64], BF16, "Hb") for _ in range(2)] for _ in range(2)]
        for d in range(2):
            for ct in range(2):
                C.memset("pool", Hs[d][ct].t[:], 0.0, [Hs[d][ct].b])
                C.memset("pool", Hbs[d][ct].t[:], 0.0, [Hbs[d][ct].b])
        qinp = C.pool_of(es, 4, [128, 7, 2, 128], F32, "qin")
        ep = C.pool_of(es, 24, [128, 128], F32, "e")
        BRp = C.pool_of(es, 12, [128, 256], BF16, "BR")
        b16 = C.pool_of(es, 72, [128, 128], BF16, "b16")
        m256 = C.pool_of(es, 48, [128, 256], BF16, "m256")
        f128 = C.pool_of(es, 192 if INV_BF16 else 110, [128, 128], IDT, "f128")
        z64 = C.pool_of(es, 32, [128, 64], BF16, "z64")
        z64f = C.pool_of(es, 24, [128, 64], IDT, "z64f")
        ychp = C.pool_of(es, 4, [128, 256], F32, "ych")
        st = {"pbi": 0, "pti": 0}

        def bank():
            pb_ = P.PB[st["pbi"] % 6]; st["pbi"] += 1
            return pb_

        def tbank():
            if INV_BF16:
                pb_ = P.PT[st["pti"] % 2]; st["pti"] += 1
                return pb_
            return bank()

        def mmf(lhsT, rhs):
            pb_ = bank()
            C.mm(pb_.t[:, 0:128], lhsT.t[:], rhs.t[:], True, True, [lhsT.b, rhs.b], [pb_.b])
            return pb_

        def evc(pb_):
            o_ = f128.get()
            C.cp(C.alt(("act", "act", "dve")), o_.t[:], pb_.t[:, 0:128], [pb_.b], [o_.b])
            return o_

        def evadd(pb_, addt):
            o_ = f128.get()
            C.tt("dve", o_.t[:], pb_.t[:, 0:128], addt.t[:], ALU.add, [pb_.b, addt.b], [o_.b])
            return o_

        def masked(src, k):
            o_ = f128.get()
            C.tt("pool", o_.t[:], src.t[:], mk(k), ALU.mult, [src.b, P.misc.b], [o_.b])
            return o_

        def xpose(X):
            pb_ = tbank()
            C.tr(pb_.t[:, 0:128], X.t[:], ident_i, [X.b, ident_b], [pb_.b])
            return evc(pb_)

        for it_ in range(NC):
          heads = []
          ychs = []
          toksl = []
          for d in range(2):
            ci = orders[d][it_]
            gmask, nmask, last_col = gmasks[d], nmasks[d], last_cols[d]
            H, Hb = Hs[d], Hbs[d]
            tok = slice(ci * 128, (ci + 1) * 128)
            toksl.append(tok)
            qin = qinp.get()
            for k_, qi in enumerate(QIs[d]):
                C.dma(qin.t[:, k_, :, :], sq[qi, :, tok].rearrange("(ct p) t -> p ct t", p=128), [], [qin.b])
            ych = ychp.get()
            ychs.append(ych)
            cts = []
            for ct in range(2):
                q = (lambda ct_: (lambda k_: qin.t[:, k_, ct_, :]))(ct)
                eX = ep.get(); eI = ep.get(); eN = ep.get()
                C.act(eX.t[:], q(6), AF.Exp, [qin.b], [eX.b])
                C.act(eI.t[:], q(5), AF.Exp, [qin.b], [eI.b])
                C.act(eN.t[:], q(5), AF.Exp, [qin.b], [eN.b], scale=-1.0)
                BR = BRp.get(); AT = b16.get(); KT = b16.get(); AH = b16.get(); KH = b16.get(); vb = b16.get()
                C.tt("dve", BR.t[:, 0:128], q(0), eX.t[:], ALU.mult, [qin.b, eX.b], [BR.b])
                C.tt("pool", BR.t[:, 128:256], q(1), eI.t[:], ALU.mult, [qin.b, eI.b], [BR.b])
                C.tt("dve", AT.t[:], q(4), eN.t[:], ALU.mult, [qin.b, eN.b], [AT.b])
                C.tt("pool", KT.t[:], q(3), eN.t[:], ALU.mult, [qin.b, eN.b], [KT.b])
                pc = eI.t[:, last_col:last_col + 1]
                C.act(AH.t[:], AT.t[:], AF.Copy, [AT.b, eI.b], [AH.b], scale=pc)
                C.act(KH.t[:], KT.t[:], AF.Copy, [KT.b, eI.b], [KH.b], scale=pc)
                C.cp("pool", vb.t[:], q(2), [qin.b], [vb.b])
                pt = P.PT[st["pti"] % 2]; st["pti"] += 1
                for k_, src in enumerate((AH, KH, vb)):
                    C.tr(pt.t[:, k_ * 128:(k_ + 1) * 128], src.t[:], P.identb.t[:], [src.b, P.identb.b], [pt.b])
                toks = []
                for k_ in range(3):
                    tk = b16.get()
                    C.cp("act" if ct == 0 else "dve", tk.t[:], pt.t[:, k_ * 128:(k_ + 1) * 128], [pt.b], [tk.b])
                    toks.append(tk)
                cts.append(dict(BR=BR, AT=AT, KT=KT, eI=eI, AHt=toks[0], KHt=toks[1], Vt=toks[2]))
            for ct in range(2):
                for hh in range(2):
                    hd = dict(cts[ct]); hd["ct"] = ct; hd["ps"] = slice(64 * hh, 64 * hh + 64); hd["h4"] = ct * 2 + hh
                    hd.update(gmask=gmask, nmask=nmask, last_col=last_col, H=H, Hb=Hb, ych=ych)
                    heads.append(hd)
          if True:
            for hd in heads:
                ps, BR, AT, KT = hd["ps"], hd["BR"], hd["AT"], hd["KT"]
                gmask, nmask = hd["gmask"], hd["nmask"]
                pg1 = bank()
                C.mm(pg1.t[:, 0:256], KT.t[ps, :], BR.t[ps, :], True, True, [KT.b, BR.b], [pg1.b])
                M1 = m256.get()
                C.tt("dve", M1.t[:], pg1.t[:, 0:256], gmask, ALU.mult, [pg1.b, P.misc.b], [M1.b])
                pg2 = bank()
                C.mm(pg2.t[:, 0:256], AT.t[ps, :], BR.t[ps, :], True, True, [AT.b, BR.b], [pg2.b])
                M2 = m256.get()
                C.tt("dve", M2.t[:], pg2.t[:, 0:256], gmask, ALU.mult, [pg2.b, P.misc.b], [M2.b])
                Ntf = f128.get()
                C.tt("dve", Ntf.t[:], pg2.t[:, 0:128], gmask[:, 0:128], ALU.mult, [pg2.b, P.misc.b], [Ntf.b])
                pg3 = bank()
                C.mm(pg3.t[:, 0:128], BR.t[ps, 0:128], AT.t[ps, :], True, True, [AT.b, BR.b], [pg3.b])
                Nf = f128.get()
                C.tt("dve", Nf.t[:], pg3.t[:, 0:128], nmask, ALU.mult, [pg3.b, P.misc.b], [Nf.b])
                hd.update(M1=M1, M2=M2, Nf=Nf, Ntf=Ntf)
            for hd in heads:
                hd["Nd"] = masked(hd["Nf"], 0); hd["Ndt"] = masked(hd["Ntf"], 0)
            for hd in heads:
                hd["p1"] = mmf(hd["Ndt"], hd["Nd"]); hd["p2"] = mmf(hd["Nd"], hd["Ndt"])
                hd["Nd2"] = evc(hd["p1"]); hd["Ndt2"] = evc(hd["p2"])
            for hd in heads:
                hd["p2"] = mmf(hd["Nd2"], hd["Ndt2"])
                hd["Ndt4"] = evc(hd["p2"])
                P1 = f128.get()
                C.tt("pool", P1.t[:], hd["Nd"].t[:], ident_i, ALU.add, [hd["Nd"].b, ident_b], [P1.b])
                hd["P1"] = P1
            for hd in heads:
                hd["P2"] = evadd(mmf(hd["Ndt2"], hd["P1"]), hd["P1"])
            for hd in heads:
                hd["X"] = evadd(mmf(hd["Ndt4"], hd["P2"]), hd["P2"])
            for k in (1, 2, 3):
                for hd in heads:
                    hd["Xt"] = xpose(hd["X"])
                    hd["Noff"] = masked(hd["Ntf"], k)
                for hd in heads:
                    hd["W"] = evc(mmf(hd["Noff"], hd["X"]))
                for hd in heads:
                    hd["X"] = evadd(mmf(hd["Xt"], hd["W"]), hd["X"])
            for hd in heads:
                hd["Xt"] = xpose(hd["X"])
                hd["Noff"] = masked(hd["Nf"], 4)
            for hd in heads:
                hd["W"] = evc(mmf(hd["Noff"], hd["Xt"]))
            for hd in heads:
                hd["Tt"] = evadd(mmf(hd["X"], hd["W"]), hd["Xt"])
            for hd in heads:
                ps, ct = hd["ps"], hd["ct"]
                Hb = hd["Hb"]
                vh = hd["Vt"].t[:, ps]
                pz = bank()
                C.mm(pz.t[:, 0:64], hd["BR"].t[ps, 0:128], Hb[ct].t[ps, :], True, False, [hd["BR"].b, Hb[ct].b], [pz.b])
                C.mm(pz.t[:, 0:64], hd["M1"].t[:, 0:128], vh, False, True, [hd["M1"].b, hd["Vt"].b], [pz.b])
                Zs = z64f.get()
                C.cp("act", Zs.t[:], pz.t[:, 0:64], [pz.b], [Zs.b])
                hd["Zs"] = Zs
            for hd in heads:
                pu = bank()
                C.mm(pu.t[:, 0:64], hd["Tt"].t[:], hd["Zs"].t[:], True, True, [hd["Tt"].b, hd["Zs"].b], [pu.b])
                Us = z64.get()
                C.cp("dve", Us.t[:], pu.t[:, 0:64], [pu.b], [Us.b])
                hd["Us"] = Us
            for hd in heads:
                ps, ct, h4 = hd["ps"], hd["ct"], hd["h4"]
                H, Hb, ych, last_col = hd["H"], hd["Hb"], hd["ych"], hd["last_col"]
                vh = hd["Vt"].t[:, ps]
                py = bank()
                C.mm(py.t[:, 0:64], hd["BR"].t[ps, 128:256], Hb[ct].t[ps, :], True, False, [hd["BR"].b, Hb[ct].b], [py.b])
                C.mm(py.t[:, 0:64], hd["M2"].t[:, 128:256], hd["Us"].t[:], False, False, [hd["M2"].b, hd["Us"].b], [py.b])
                C.mm(py.t[:, 0:64], hd["M1"].t[:, 128:256], vh, False, True, [hd["M1"].b, hd["Vt"].b], [py.b])
                C.cp("act", ych.t[:, h4 * 64:(h4 + 1) * 64], py.t[:, 0:64], [py.b], [ych.b])
                ph = bank()
                C.mm(ph.t[:, 0:64], hd["AHt"].t[:], hd["Us"].t[:], True, False, [hd["AHt"].b, hd["Us"].b], [ph.b])
                C.mm(ph.t[:, 0:64], hd["KHt"].t[:], vh, False, True, [hd["KHt"].b, hd["Vt"].b], [ph.b])
                C.stt("dve", H[ct].t[ps, :], H[ct].t[ps, :], hd["eI"].t[ps, last_col:last_col + 1], ph.t[ps, 0:64],
                      ALU.mult, ALU.add, [hd["eI"].b, ph.b], [H[ct].b])
            for d in range(2):
                for ct in range(2):
                    C.cp("act" if d == 0 else "dve", Hbs[d][ct].t[:], Hs[d][ct].t[:], [Hs[d][ct].b], [Hbs[d][ct].b])
                C.dma(Dr["yd"][d, toksl[d], :], ychs[d].t[:], [ychs[d].b], [])
        S.barrier()


def rwkv_scan_q(P, LP, l):
    C, I, Dr, S = P.C, P.I, P.Dr, P.C.S
    sq = Dr["sq"]
    NC = P.TT // 128
    NCc = P.TC // 128
    orders = [list(range(NC)), list(range(NCc - 1, -1, -1)) + list(range(NC - 1, NCc - 1, -1))]
    last_cols = [127, 0]
    QIs = [(0, 1, 2, 3 + d, 5 + d, 7 + 2 * d, 8 + 2 * d) for d in range(2)]
    with ExitStack() as es:
        gm2 = [C.sb(es, [128, 2, 256], F32, "gm2") for _ in range(2)]
        nm4 = [C.sb(es, [128, 4, 128], F32, "nm4") for _ in range(2)]
        mk4 = [C.sb(es, [128, 4, 128], F32, "mk4") for _ in range(5)]
        id4 = C.sb(es, [128, 4, 128], BF16, "id4")
        for d in range(2):
            src = P.misc.t[:, 3:5, :] if d == 0 else P.misc.t[:, 5:7, :]
            for r in range(2):
                C.cp("pool", gm2[d].t[:, r, :].rearrange("p (a b) -> p a b", a=2), src, [P.misc.b], [gm2[d].b])
            for r in range(4):
                C.cp("pool", nm4[d].t[:, r, :], P.misc.t[:, 5 if d == 0 else 3, :], [P.misc.b], [nm4[d].b])
        for k in range(5):
            for r in range(4):
                C.cp("pool", mk4[k].t[:, r, :], P.misc.t[:, 7 + k, :], [P.misc.b], [mk4[k].b])
        for r in range(4):
            C.cp("pool", id4.t[:, r, :], P.misc.t[:, 0, :], [P.misc.b], [id4.b])
        Hs = [[C.sb(es, [128, 64], F32, "H") for _ in range(2)] for _ in range(2)]
        Hbs = [[C.sb(es, [128, 64], BF16, "Hb") for _ in range(2)] for _ in range(2)]
        for d in range(2):
            for ct in range(2):
                C.memset("pool", Hs[d][ct].t[:], 0.0, [Hs[d][ct].b])
                C.memset("pool", Hbs[d][ct].t[:], 0.0, [Hbs[d][ct].b])
        qinp = C.pool_of(es, 4, [128, 7, 2, 128], F32, "qin")
        eIp = C.pool_of(es, 18, [128, 128], F32, "eI")
        ep = C.pool_of(es, 8, [128, 128], F32, "e")
        BRp = C.pool_of(es, 18, [128, 256], BF16, "BR")
        b16 = C.pool_of(es, 48, [128, 128], BF16, "b16")
        m4 = C.pool_of(es, 18, [128, 4, 256], BF16, "m4")
        q4l = C.pool_of(es, 34, [128, 4, 128], BF16, "q4l")
        q4 = C.pool_of(es, 30, [128, 4, 128], BF16, "q4")
        z4 = C.pool_of(es, 12, [128, 4, 64], BF16, "z4")
        ychp = C.pool_of(es, 4, [128, 256], F32, "ych")
        st = {"pbi": 0, "pti": 0}

        def bank():
            pb_ = P.PB[st["pbi"] % 6]; st["pbi"] += 1
            return pb_

        def tbank():
            pb_ = P.PT[st["pti"] % 2]; st["pti"] += 1
            return pb_

        def mm4(A, B, bsl=None):
            pb_ = bank()
            for h in range(4):
                C.mm(pb_.t[:, h * 128:(h + 1) * 128], A.t[:, h, :] if bsl is None else A.t[:, h, bsl], B.t[:, h, :],
                     True, True, [A.b, B.b], [pb_.b])
            return pb_

        def ev4(pb_, eng):
            o_ = q4.get()
            C.cp(C.alt(("act", "act", "dve")), o_.t[:].rearrange("p a b -> p (a b)"), pb_.t[:, 0:512], [pb_.b], [o_.b])
            return o_

        def evadd4(pb_, addt):
            o_ = q4.get()
            C.tt("dve", o_.t[:].rearrange("p a b -> p (a b)"), pb_.t[:, 0:512], addt.t[:].rearrange("p a b -> p (a b)"),
                 ALU.add, [pb_.b, addt.b], [o_.b])
            return o_

        def masked4(src_ap, src_b, k):
            o_ = q4.get()
            C.tt("pool", o_.t[:], src_ap, mk4[k].t[:], ALU.mult, [src_b, mk4[k].b], [o_.b])
            return o_

        def xpose4(X, eng):
            pb_ = tbank()
            for h in range(4):
                C.tr(pb_.t[:, h * 128:(h + 1) * 128], X.t[:, h, :], P.identb.t[:], [X.b, P.identb.b], [pb_.b])
            o_ = q4.get()
            C.cp(eng, o_.t[:].rearrange("p a b -> p (a b)"), pb_.t[:, 0:512], [pb_.b], [o_.b])
            return o_

        pref = {}

        def load_qin(it_, d):
            tok_ = slice(orders[d][it_] * 128, (orders[d][it_] + 1) * 128)
            qin_ = qinp.get()
            for k_, qi in enumerate(QIs[d]):
                C.dma(qin_.t[:, k_, :, :], sq[qi, :, tok_].rearrange("(ct p) t -> p ct t", p=128), [], [qin_.b])
            return qin_

        def pre(it0, box):
          its = [i_ for i_ in (it0, it0 + 1) if i_ < NC]
          quads = []
          for it_ in its:
            for d in range(2):
                ci = orders[d][it_]
                last_col = last_cols[d]
                tok = slice(ci * 128, (ci + 1) * 128)
                qin = pref.pop((it_, d)) if (it_, d) in pref else load_qin(it_, d)
                cts = []
                for ct in range(2):
                    q = (lambda ct_, qin_: (lambda k_: qin_.t[:, k_, ct_, :]))(ct, qin)
                    eX = ep.get(); eI = eIp.get(); eN = ep.get()
                    C.act(eX.t[:], q(6), AF.Exp, [qin.b], [eX.b])
                    C.act(eI.t[:], q(5), AF.Exp, [qin.b], [eI.b])
                    C.act(eN.t[:], q(5), AF.Exp, [qin.b], [eN.b], scale=-1.0)
                    BR = BRp.get(); AT = b16.get(); KT = b16.get(); AH = b16.get(); KH = b16.get(); vb = b16.get()
                    C.tt("dve", BR.t[:, 0:128], q(0), eX.t[:], ALU.mult, [qin.b, eX.b], [BR.b])
                    C.tt("pool", BR.t[:, 128:256], q(1), eI.t[:], ALU.mult, [qin.b, eI.b], [BR.b])
                    C.tt("dve", AT.t[:], q(4), eN.t[:], ALU.mult, [qin.b, eN.b], [AT.b])
                    C.tt("pool", KT.t[:], q(3), eN.t[:], ALU.mult, [qin.b, eN.b], [KT.b])
                    pc = eI.t[:, last_col:last_col + 1]
                    C.act(AH.t[:], AT.t[:], AF.Copy, [AT.b, eI.b], [AH.b], scale=pc)
                    C.act(KH.t[:], KT.t[:], AF.Copy, [KT.b, eI.b], [KH.b], scale=pc)
                    C.cp("pool", vb.t[:], q(2), [qin.b], [vb.b])
                    pt = tbank()
                    for k_, src in enumerate((AH, KH, vb)):
                        C.tr(pt.t[:, k_ * 128:(k_ + 1) * 128], src.t[:], P.identb.t[:], [src.b, P.identb.b], [pt.b])
                    tk = q4l.get()
                    C.cp("act" if d == 0 else "dve", tk.t[:, 0:3, :].rearrange("p a b -> p (a b)"), pt.t[:, 0:384], [pt.b], [tk.b])
                    cts.append(dict(BR=BR, AT=AT, KT=KT, eI=eI, tk=tk))
                quads.append(dict(d=d, it=it_, cts=cts, tok=tok, last_col=last_col, ev="act" if d == 0 else "dve"))
          if True:
            for itn in (it0 + 2, it0 + 3):
                if itn < NC:
                    for d in range(2):
                        pref[(itn, d)] = load_qin(itn, d)
            yield
            for Q in quads:
                d = Q["d"]
                M1 = m4.get(); M2 = m4.get()
                for (lk, Mq) in (("KT", M1), ("AT", M2)):
                    for hh in range(2):
                        ps = slice(64 * hh, 64 * hh + 64)
                        pg = bank()
                        for ct in range(2):
                            cd = Q["cts"][ct]
                            C.mm(pg.t[:, ct * 256:(ct + 1) * 256], cd[lk].t[ps, :], cd["BR"].t[ps, :], True, True,
                                 [cd[lk].b, cd["BR"].b], [pg.b])
                        C.tt("dve", Mq.t[:, 2 * hh:2 * hh + 2, :], pg.t[:, 0:512].rearrange("p (a b) -> p a b", a=2), gm2[d].t[:],
                             ALU.mult, [pg.b, gm2[d].b], [Mq.b])
                Nq = q4l.get()
                for hh in range(2):
                    ps = slice(64 * hh, 64 * hh + 64)
                    pg = bank()
                    for ct in range(2):
                        cd = Q["cts"][ct]
                        C.mm(pg.t[:, ct * 128:(ct + 1) * 128], cd["BR"].t[ps, 0:128], cd["AT"].t[ps, :], True, True,
                             [cd["AT"].b, cd["BR"].b], [pg.b])
                    C.tt("dve", Nq.t[:, 2 * hh:2 * hh + 2, :], pg.t[:, 0:256].rearrange("p (a b) -> p a b", a=2), nm4[d].t[:, 0:2, :],
                         ALU.mult, [pg.b, nm4[d].b], [Nq.b])
                Q.update(M1=M1, M2=M2, Nq=Nq)
            yield
            for Q in quads:
                Q["Nd"] = masked4(Q["Nq"].t[:], Q["Nq"].b, 0)
                Q["Ndt"] = masked4(Q["M2"].t[:, :, 0:128], Q["M2"].b, 0)
            yield
            for Q in quads:
                Q["Nd2"] = ev4(mm4(Q["Ndt"], Q["Nd"]), Q["ev"])
                Q["Ndt2"] = ev4(mm4(Q["Nd"], Q["Ndt"]), "act" if Q["ev"] == "dve" else "dve")
            yield
            for Q in quads:
                Q["Ndt4"] = ev4(mm4(Q["Nd2"], Q["Ndt2"]), Q["ev"])
                P1 = q4.get()
                C.tt("pool", P1.t[:], Q["Nd"].t[:], id4.t[:], ALU.add, [Q["Nd"].b, id4.b], [P1.b])
                Q["P1"] = P1
            yield
            for Q in quads:
                Q["P2"] = evadd4(mm4(Q["Ndt2"], Q["P1"]), Q["P1"])
            yield
            for Q in quads:
                Q["X"] = evadd4(mm4(Q["Ndt4"], Q["P2"]), Q["P2"])
            yield
            for k in (1, 2, 3):
                for Q in quads:
                    Q["Xt"] = xpose4(Q["X"], Q["ev"])
                    Q["Noff"] = masked4(Q["M2"].t[:, :, 0:128], Q["M2"].b, k)
                yield
                for Q in quads:
                    Q["W"] = ev4(mm4(Q["Noff"], Q["X"]), Q["ev"])
                yield
                for Q in quads:
                    Q["X"] = evadd4(mm4(Q["Xt"], Q["W"]), Q["X"])
                yield
            for Q in quads:
                Q["Xt"] = xpose4(Q["X"], Q["ev"])
                Q["Noff"] = masked4(Q["Nq"].t[:], Q["Nq"].b, 4)
            yield
            for Q in quads:
                Q["W"] = ev4(mm4(Q["Noff"], Q["Xt"]), Q["ev"])
            yield
            for Q in quads:
                pb_ = mm4(Q["X"], Q["W"])
                Tt = q4l.get()
                C.tt("dve", Tt.t[:].rearrange("p a b -> p (a b)"), pb_.t[:, 0:512], Q["Xt"].t[:].rearrange("p a b -> p (a b)"),
                     ALU.add, [pb_.b, Q["Xt"].b], [Tt.b])
                Q["Tt"] = Tt
            yield
            box.append((its, quads))

        def chain(its, allquads):
          for it_ in its:
            quads = [Q for Q in allquads if Q["it"] == it_]
            for Q in quads:
                d = Q["d"]
                Zs = z4.get()
                for hh in range(2):
                    ps = slice(64 * hh, 64 * hh + 64)
                    pz = bank()
                    for ct in range(2):
                        s_ = hh * 2 + ct
                        cd = Q["cts"][ct]
                        o = pz.t[:, ct * 64:(ct + 1) * 64]
                        C.mm(o, cd["BR"].t[ps, 0:128], Hbs[d][ct].t[ps, :], True, False, [cd["BR"].b, Hbs[d][ct].b], [pz.b])
                        C.mm(o, Q["M1"].t[:, s_, 0:128], cd["tk"].t[:, 2, ps], False, True, [Q["M1"].b, cd["tk"].b], [pz.b])
                    C.cp(Q["ev"], Zs.t[:, 2 * hh:2 * hh + 2, :].rearrange("p a b -> p (a b)"), pz.t[:, 0:128], [pz.b], [Zs.b])
                Q["Zs"] = Zs
            yield
            for Q in quads:
                pu = bank()
                for s_ in range(4):
                    C.mm(pu.t[:, s_ * 64:(s_ + 1) * 64], Q["Tt"].t[:, s_, :], Q["Zs"].t[:, s_, :], True, True,
                         [Q["Tt"].b, Q["Zs"].b], [pu.b])
                Us = z4.get()
                C.cp(Q["ev"], Us.t[:].rearrange("p a b -> p (a b)"), pu.t[:, 0:256], [pu.b], [Us.b])
                Q["Us"] = Us
            yield
            for Q in quads:
                d = Q["d"]
                ych = ychp.get()
                yv = ych.t[:].rearrange("p (ct hh v) -> p hh ct v", ct=2, hh=2)
                for hh in range(2):
                    ps = slice(64 * hh, 64 * hh + 64)
                    py = bank()
                    for ct in range(2):
                        s_ = hh * 2 + ct
                        cd = Q["cts"][ct]
                        o = py.t[:, ct * 64:(ct + 1) * 64]
                        C.mm(o, cd["BR"].t[ps, 128:256], Hbs[d][ct].t[ps, :], True, False, [cd["BR"].b, Hbs[d][ct].b], [py.b])
                        C.mm(o, Q["M2"].t[:, s_, 128:256], Q["Us"].t[:, s_, :], False, False, [Q["M2"].b, Q["Us"].b], [py.b])
                        C.mm(o, Q["M1"].t[:, s_, 128:256], cd["tk"].t[:, 2, ps], False, True, [Q["M1"].b, cd["tk"].b], [py.b])
                    C.cp("act", yv[:, hh], py.t[:, 0:128].rearrange("p (a b) -> p a b", a=2), [py.b], [ych.b])
                C.dma(Dr["yd"][d, Q["tok"], :], ych.t[:], [ych.b], [])
                ph = bank()
                for s_ in range(4):
                    hh, ct = s_ // 2, s_ % 2
                    cd = Q["cts"][ct]; ps = slice(64 * hh, 64 * hh + 64)
                    o = ph.t[:, s_ * 64:(s_ + 1) * 64]
                    C.mm(o, cd["tk"].t[:, 0, :], Q["Us"].t[:, s_, :], True, False, [cd["tk"].b, Q["Us"].b], [ph.b])
                    C.mm(o, cd["tk"].t[:, 1, :], cd["tk"].t[:, 2, ps], False, True, [cd["tk"].b], [ph.b])
                for s_ in range(4):
                    hh, ct = s_ // 2, s_ % 2
                    cd = Q["cts"][ct]; ps = slice(64 * hh, 64 * hh + 64)
                    Hh = Hs[d][ct]
                    C.stt("dve", Hh.t[ps, :], Hh.t[ps, :], cd["eI"].t[ps, Q["last_col"]:Q["last_col"] + 1],
                          ph.t[ps, s_ * 64:(s_ + 1) * 64], ALU.mult, ALU.add, [cd["eI"].b, ph.b], [Hh.b])
                for ct in range(2):
                    C.cp("act" if d == 0 else "dve", Hbs[d][ct].t[:], Hs[d][ct].t[:], [Hs[d][ct].b], [Hbs[d][ct].b])
            yield

        box = []
        interleave(pre(0, box))
        cur = box[0]
        for it0 in range(0, NC, 2):
            box = []
            interleave(chain(*cur), pre(it0 + 2, box) if it0 + 2 < NC else None)
            cur = box[0] if box else None
        S.barrier()


def rwkv_readout(P, LP, l):
    C, I, Dr, S = P.C, P.I, P.Dr, P.C.S
    pp = LP.pp1
    sq = Dr["sq"]
    NC = P.TT // 128
    GR = 4
    with ExitStack() as es:
        yp = C.pool_of(es, 4 * GR, [128, 256], F32, "yr")
        sqv = C.pool_of(es, 2 * GR, [128, 256], F32, "ysq")
        st = C.pool_of(es, 8 * GR, [128, 4], F32, "st")
        ynp = C.pool_of(es, 2 * GR, [128, 256], BF16, "yn")
        bgp = C.pool_of(es, 2 * GR, [128, 2, 2, 128], F32, "bg")
        op_ = C.pool_of(es, 4 * GR, [128, 128], F32, "o")
        obp = C.pool_of(es, 4 * GR, [128, 128], BF16, "ob")
        red = lambda o, i_: (lambda e: e.tensor_reduce(out=o, in_=i_, axis=AX.X, op=ALU.add))
        def load_group(c0_):
            G_ = []
            for ci in range(c0_, min(NC, c0_ + GR)):
                tok = slice(ci * 128, (ci + 1) * 128)
                g = dict(tok=tok, y0=yp.get(), y1=yp.get(), bg=bgp.get())
                C.dma(g["y0"].t[:], Dr["yd"][0, tok, :], [], [g["y0"].b])
                C.dma(g["y1"].t[:], Dr["yd"][1, tok, :], [], [g["y1"].b])
                for k_, qi in enumerate((11, 12)):
                    C.dma(g["bg"].t[:, k_, :, :], sq[qi, :, tok].rearrange("(ct p) t -> p ct t", p=128), [], [g["bg"].b])
                G_.append(g)
            return G_

        nxtG = load_group(0)
        for c0_ in range(0, NC, GR):
            G = nxtG
            nxtG = load_group(c0_ + GR) if c0_ + GR < NC else None
            for g in G:
                C.tt("dve", g["y0"].t[:], g["y0"].t[:], g["y1"].t[:], ALU.add, [g["y1"].b], [g["y0"].b])
            for g in G:
                g["ysq"] = sqv.get()
                C.tt("pool", g["ysq"].t[:], g["y0"].t[:], g["y0"].t[:], ALU.mult, [g["y0"].b], [g["ysq"].b])
                g["s1"] = st.get(); g["s2"] = st.get(); g["mu"] = st.get(); g["var"] = st.get()
                S.op("dve", red(g["s1"].t[:], g["y0"].t[:].rearrange("p (h j) -> p h j", j=64)), [g["y0"].b], [g["s1"].b])
            for g in G:
                S.op("dve", red(g["s2"].t[:], g["ysq"].t[:].rearrange("p (h j) -> p h j", j=64)), [g["ysq"].b], [g["s2"].b])
                C.ts("dve", g["mu"].t[:], g["s1"].t[:], 1.0 / 64, None, ALU.mult, None, [g["s1"].b], [g["mu"].b])
            for g in G:
                C.tt("dve", g["var"].t[:], g["mu"].t[:], g["mu"].t[:], ALU.mult, [g["mu"].b], [g["var"].b])
            for g in G:
                C.stt("dve", g["var"].t[:], g["s2"].t[:], 1.0 / 64, g["var"].t[:], ALU.mult, ALU.subtract, [g["s2"].b], [g["var"].b])
            for g in G:
                C.ts("dve", g["var"].t[:], g["var"].t[:], 0.0, None, ALU.max, None, [], [g["var"].b])
            for g in G:
                C.act(g["var"].t[:], g["var"].t[:], AF.Sqrt, [], [g["var"].b], bias=64e-5)
            for g in G:
                C.recip(g["var"].t[:], g["var"].t[:], [], [g["var"].b])
            for g in G:
                g["yn"] = ynp.get()
                for h4 in range(4):
                    hs = slice(h4 * 64, (h4 + 1) * 64)
                    C.ts("dve", g["yn"].t[:, hs], g["y0"].t[:, hs], g["mu"].t[:, h4:h4 + 1], g["var"].t[:, h4:h4 + 1],
                         ALU.subtract, ALU.mult, [g["y0"].b, g["mu"].b, g["var"].b], [g["yn"].b])
            for gi, g in enumerate(G):
                pt = P.PT[gi % 2]
                for ct in range(2):
                    C.tr(pt.t[:, ct * 128:(ct + 1) * 128], g["yn"].t[:, ct * 128:(ct + 1) * 128], P.identb.t[:],
                         [g["yn"].b, P.identb.b], [pt.b])
                g["o"] = []
                for ct in range(2):
                    o = op_.get()
                    C.act(o.t[:], pt.t[:, ct * 128:(ct + 1) * 128], AF.Identity, [pt.b, pp.b], [o.b],
                          scale=pp.t[:, C_GNW + ct:C_GNW + ct + 1], bias=pp.t[:, C_GNB + ct:C_GNB + ct + 1])
                    g["o"].append(o)
            for g in G:
                for ct in range(2):
                    o = g["o"][ct]; ob = obp.get()
                    C.tt("dve", o.t[:], o.t[:], g["bg"].t[:, 0, ct, :], ALU.add, [g["bg"].b], [o.b])
                    C.tt("pool", ob.t[:], o.t[:], g["bg"].t[:, 1, ct, :], ALU.mult, [o.b, g["bg"].b], [ob.b])
                    C.dma(Dr["yT"][ct * 128:(ct + 1) * 128, g["tok"]], ob.t[:], [ob.b], [])
        S.barrier()


def phase_pool(P, LP, l):
    C, I, Dr = P.C, P.I, P.Dr
    with ExitStack() as es:
        pm = C.sb(es, [128, 20, 128], BF16, "poolm")
        C.dma(pm.t[:], I["poolm"].rearrange("g v s t -> s (g v) t"), [], [pm.b])
        pwf = C.sb(es, [128, 2, 128], F32, "pwf")
        C.memset("pool", pwf.t[:], 0.0, [pwf.b])
        for g in range(4):
            ct, gl = g // 2, g % 2
            C.dma(pwf.t[gl * 64:(gl + 1) * 64, ct, gl * 64:(gl + 1) * 64], I["pool_w"][l, g], [], [pwf.b])
        pwb = C.sb(es, [128, 2, 128], BF16, "pwb")
        C.cp("dve", pwb.t[:], pwf.t[:], [pwf.b], [pwb.b])
        uT = [C.sb(es, [128, P.T], BF16, "uTp") for _ in range(2)]
        z = C.sb(es, [128, P.T // 128, 256], BF16, "z")
        yop = C.pool_of(es, 3, [128, 512], BF16, "yo")
        pbi = 0
        for (sn, Ts, c0, m) in streams(P):
            nT = Ts // 128
            for ct in range(2):
                C.dma(uT[ct].t[:, 0:Ts], Dr["uM"][ct * 128:(ct + 1) * 128, c0:c0 + Ts], [], [uT[ct].b])
            for ti in range(nT):
                pb = P.PB[pbi % 6]; pbi += 1
                for ct in range(2):
                    C.mm(pb.t[:, ct * 128:(ct + 1) * 128], uT[ct].t[:, ti * 128:(ti + 1) * 128], pwb.t[:, ct, :],
                         True, True, [uT[ct].b, pwb.b], [pb.b])
                C.evac(z.t[:, ti, :], pb.t[:, 0:256], [pb.b], [Acc(z.b)])
            NB = min(512, Ts)
            for ct in range(2):
                for blk in range(Ts // NB):
                    yo = yop.get()
                    for tj in range(NB // 128):
                        ti = blk * (NB // 128) + tj
                        for gl in range(2):
                            g = 2 * ct + gl
                            pb = P.PB[pbi % 6]; pbi += 1
                            sis = [si for si in (ti - 1, ti, ti + 1) if 0 <= si < nT]
                            for n_, si in enumerate(sis):
                                if si == ti - 1:
                                    v = 0
                                elif si == ti + 1:
                                    v = 2
                                else:
                                    v = 3 if ti == 0 else (4 if ti == nT - 1 else 1)
                                C.mm(pb.t[:, 0:128], z.t[:, si, ct * 128:(ct + 1) * 128], pm.t[:, g * 5 + v, :],
                                     n_ == 0, n_ == len(sis) - 1, [z.b, pm.b], [pb.b])
                            ps = slice(gl * 64, (gl + 1) * 64)
                            C.act(yo.t[ps, tj * 128:(tj + 1) * 128], pb.t[ps, 0:128], AF.Copy, [pb.b, LP.pp1.b], [yo.b],
                                  scale=LP.pp1.t[ps, C_PSC + ct:C_PSC + ct + 1])
                    C.dma(Dr["yT"][256 + ct * 128:256 + (ct + 1) * 128, c0 + blk * NB:c0 + (blk + 1) * NB],
                          yo.t[:, 0:NB], [yo.b], [])
        P.C.S.barrier()


def phase_fourier(P, LP, l):
    C, I, Dr = P.C, P.I, P.Dr
    with ExitStack() as es:
        cc3 = C.sb(es, [128, 384], BF16, "cc3")
        C.dma(cc3.t[:], I["cc3"], [], [cc3.b])
        stg = C.pool_of(es, 2, [128, 1024], F32, "stg")
        fwb = C.sb(es, [128, 2, 256], BF16, "fwb")
        load_cast(C, es, fwb, I["fourier_w"][l], 256, stg)
        Rmax = P.T // 64
        uTp = C.pool_of(es, 2, [128, max(P.T, 2048)], BF16, "uTf")
        Z = C.sb(es, [128, 128, 3, 64], BF16, "Z")
        Pq = C.sb(es, [128, Rmax, 128], BF16, "Pq")
        gm = C.sb(es, [128, Rmax, 64], BF16, "gmf")
        wr = C.sb(es, [128, 2, Rmax], BF16, "wrf")
        fT = [C.sb(es, [128, P.T], BF16, "fT") for _ in range(2)]
        yop = C.pool_of(es, 3, [128, 512], BF16, "yo")
        pbi = 0
        items = [(sn, Ts, c0, ct) for (sn, Ts, c0, m) in streams(P) for ct in range(2)]

        def load_uT(itm):
            sn_, Ts_, c0_, ct_ = itm
            R_ = Ts_ // 64
            Rp_ = max(R_, 32)
            u_ = uTp.get()
            if Rp_ > R_:
                C.memset("pool", u_.t[:, Ts_:Rp_ * 64], 0.0, [u_.b])
            C.dma(u_.t[:, 0:Ts_], Dr["uM"][256 + ct_ * 128:256 + (ct_ + 1) * 128, c0_:c0_ + Ts_], [], [u_.b])
            return u_

        nxt_u = load_uT(items[0])
        ii = 0
        for (sn, Ts, c0, m) in streams(P):
            R = Ts // 64
            C.dma(gm.t[:, 0:R, :], I["gm_x" if sn == "x" else "gm_c"], [], [gm.b])
            C.dma(wr.t[:, :, 0:R], I["wr_x" if sn == "x" else "wr_c"], [], [wr.b])
            for ct in range(2):
                Rp = max(R, 32)
                uT = nxt_u
                ii += 1
                nxt_u = load_uT(items[ii]) if ii < len(items) else None
                uv = uT.t[:, 0:Rp * 64].rearrange("p (t1 t2) -> p t2 t1", t2=64)
                for t2 in range(64):
                    pb = P.PB[pbi % 6]; pbi += 1
                    C.mm(pb.t[0:Rp, 0:384], uv[:, t2, :], cc3.t[:], True, True, [uT.b, cc3.b], [pb.b])
                    C.evac(Z.t[0:Rp, :, :, t2].rearrange("p q r -> p r q"), pb.t[0:Rp, 0:384].rearrange("p (r q) -> p r q", r=3),
                           [pb.b], [Acc(Z.b)])
                QB = min(128, 512 // R)
                for q0 in range(0, 128, QB):
                    pb = P.PB[pbi % 6]; pbi += 1
                    for qi in range(QB):
                        q = q0 + qi
                        o = pb.t[:, qi * R:(qi + 1) * R]
                        C.mm(o, Z.t[0:Rp, q, 0:2, :].rearrange("p r t -> p (r t)"), wr.t[0:Rp, 0, 0:R], True, False, [Z.b, wr.b], [pb.b])
                        C.mm(o, Z.t[0:Rp, q, 1:3, :].rearrange("p r t -> p (r t)"), wr.t[0:Rp, 1, 0:R], False, True, [Z.b, wr.b], [pb.b])
                    C.evac(Pq.t[:, 0:R, q0:q0 + QB], pb.t[:, 0:QB * R].rearrange("p (q k) -> p k q", k=R),
                           [pb.b], [Acc(Pq.b)])
                KB = min(8, R)
                fv = fT[ct].t[:, 0:Ts].rearrange("p (k2 k1) -> p k2 k1", k1=R)
                for k0 in range(0, R, KB):
                    pb = P.PB[pbi % 6]; pbi += 1
                    for ki in range(KB):
                        C.mm(pb.t[:, ki * 64:(ki + 1) * 64], Pq.t[:, k0 + ki, :], gm.t[:, k0 + ki, :], True, True,
                             [Pq.b, gm.b], [pb.b])
                    C.evac(fv[:, :, k0:k0 + KB], pb.t[:, 0:KB * 64].rearrange("p (k1 k2) -> p k2 k1", k2=64),
                           [pb.b], [Acc(fT[ct].b)])
            NB = min(512, Ts)
            for blk in range(Ts // NB):
                for nt in range(2):
                    pb = P.PB[pbi % 6]; pbi += 1
                    for ct in range(2):
                        C.mm(pb.t[:, 0:NB], fwb.t[:, ct, nt * 128:(nt + 1) * 128], fT[ct].t[:, blk * NB:(blk + 1) * NB],
                             ct == 0, ct == 1, [fwb.b, fT[ct].b], [pb.b])
                    yo = yop.get()
                    C.evac(yo.t[:, 0:NB], pb.t[:, 0:NB], [pb.b], [yo.b])
                    C.dma(Dr["yT"][512 + nt * 128:512 + (nt + 1) * 128, c0 + blk * NB:c0 + (blk + 1) * NB],
                          yo.t[:, 0:NB], [yo.b], [])
        P.C.S.barrier()


def phase_conv(P, LP, l):
    C, I, Dr = P.C, P.I, P.Dr
    onesf = P.misc.t[:, 1, :]
    identf = P.misc.t[:, 0, :]
    with ExitStack() as es:
        stg = C.pool_of(es, 2, [128, 1024], F32, "stg")
        pwb = C.sb(es, [128, 2, 256], BF16, "cpw")
        load_cast(C, es, pwb, I["conv_pw"][l], 256, stg)
        dg = C.sb(es, [128, 62, 128], BF16, "dg")
        for jj in range(62):
            C.ts("dve" if jj % 2 else "act", dg.t[:, jj, :], identf, LP.pp2.t[:, jj:jj + 1], None, ALU.mult, None,
                 [P.misc.b, LP.pp2.b], [dg.b]) if jj % 2 else \
                C.act(dg.t[:, jj, :], identf, AF.Copy, [P.misc.b, LP.pp2.b], [dg.b], scale=LP.pp2.t[:, jj:jj + 1])
        hT = [C.sb(es, [128, P.T + 30], BF16, "hT") for _ in range(2)]
        abp = C.pool_of(es, 4, [128, 512], BF16, "ab")
        sgp = C.pool_of(es, 2, [128, 512], F32, "sgc")
        co = [C.pool_of(es, 3, [128, 512], F32, "co") for _ in range(2)]
        sqp = [C.pool_of(es, 3, [128, 512], F32, "sqc") for _ in range(2)]
        meanp = C.pool_of(es, 2, [128, 512], F32, "mean")
        varp = C.pool_of(es, 2, [128, 512], F32, "var")
        dp = C.pool_of(es, 2, [128, 512], F32, "dcv")
        sT = [C.pool_of(es, 2, [128, 512], BF16, "sTc") for _ in range(2)]
        yop = C.pool_of(es, 3, [128, 512], BF16, "yo")
        pbi = 0
        for (sn, Ts, c0, m) in streams(P):
            NB = min(512, Ts)
            for ct in range(2):
                C.memset("pool", hT[ct].t[:, 0:15], 0.0, [hT[ct].b])
                C.memset("pool", hT[ct].t[:, 15 + Ts:30 + Ts], 0.0, [hT[ct].b])
            for blk in range(Ts // NB):
                cs = slice(c0 + blk * NB, c0 + (blk + 1) * NB)
                for ct in range(2):
                    a = abp.get(); b = abp.get()
                    C.dma(a.t[:, 0:NB], Dr["uM"][512 + ct * 128:512 + (ct + 1) * 128, cs], [], [a.b])
                    C.dma(b.t[:, 0:NB], Dr["uM"][768 + ct * 128:768 + (ct + 1) * 128, cs], [], [b.b])
                    sg = sgp.get()
                    C.act(sg.t[:, 0:NB], b.t[:, 0:NB], AF.Sigmoid, [b.b], [sg.b])
                    C.tt("dve", hT[ct].t[:, 15 + blk * NB:15 + (blk + 1) * NB], a.t[:, 0:NB], sg.t[:, 0:NB], ALU.mult,
                         [a.b, sg.b], [hT[ct].b])
            pbs = {"i": pbi}

            def bank_():
                pb_ = P.PB[pbs["i"] % 6]; pbs["i"] += 1
                return pb_

            def partA(blk, box):
                cot, sqt = [], []
                for ct in range(2):
                    pb = bank_()
                    for j in range(31):
                        C.mm(pb.t[:, 0:NB], dg.t[:, j * 2 + ct, :], hT[ct].t[:, blk * NB + j:blk * NB + j + NB],
                             j == 0, j == 30, [dg.b, hT[ct].b], [pb.b])
                        if j % 8 == 7:
                            yield
                    c_ = co[ct].get(); q_ = sqp[ct].get()
                    C.act(c_.t[:, 0:NB], pb.t[:, 0:NB], AF.Identity, [pb.b, LP.pp1.b], [c_.b],
                          bias=LP.pp1.t[:, C_DWB + ct:C_DWB + ct + 1])
                    C.tt("pool", q_.t[:, 0:NB], c_.t[:, 0:NB], c_.t[:, 0:NB], ALU.mult, [c_.b], [q_.b])
                    cot.append(c_); sqt.append(q_)
                    yield
                box.append((cot, sqt))

            def partB(blk, cot, sqt):
                pm_ = bank_()
                pq_ = bank_()
                for ct in range(2):
                    C.mm(pm_.t[:, 0:NB], onesf, cot[ct].t[:, 0:NB], ct == 0, ct == 1, [P.misc.b, cot[ct].b], [pm_.b])
                for ct in range(2):
                    C.mm(pq_.t[:, 0:NB], onesf, sqt[ct].t[:, 0:NB], ct == 0, ct == 1, [P.misc.b, sqt[ct].b], [pq_.b])
                mean = meanp.get(); var = varp.get()
                C.act(mean.t[:, 0:NB], pm_.t[:, 0:NB], AF.Copy, [pm_.b], [mean.b], scale=1.0 / 256)
                yield
                C.tt("dve", var.t[:, 0:NB], mean.t[:, 0:NB], mean.t[:, 0:NB], ALU.mult, [mean.b], [var.b])
                C.stt("dve", var.t[:, 0:NB], pq_.t[:, 0:NB], 1.0 / 256, var.t[:, 0:NB], ALU.mult, ALU.subtract,
                      [pq_.b], [var.b])
                C.ts("dve", var.t[:, 0:NB], var.t[:, 0:NB], 0.0, None, ALU.max, None, [], [var.b])
                yield
                C.act(var.t[:, 0:NB], var.t[:, 0:NB], AF.Sqrt, [], [var.b], bias=1e-5)
                yield
                C.recip(var.t[:, 0:NB], var.t[:, 0:NB], [], [var.b])
                yield
                sts = []
                for ct in range(2):
                    d_ = dp.get()
                    C.tt("dve", d_.t[:, 0:NB], cot[ct].t[:, 0:NB], mean.t[:, 0:NB], ALU.subtract, [cot[ct].b, mean.b], [d_.b])
                    C.tt("pool", d_.t[:, 0:NB], d_.t[:, 0:NB], var.t[:, 0:NB], ALU.mult, [var.b], [d_.b])
                    s_ = sT[ct].get()
                    C.act(s_.t[:, 0:NB], d_.t[:, 0:NB], AF.Silu, [d_.b, LP.pp1.b], [s_.b],
                          scale=LP.pp1.t[:, C_LNG + ct:C_LNG + ct + 1], bias=LP.pp1.t[:, C_LNB + ct:C_LNB + ct + 1])
                    sts.append(s_)
                    yield
                for nt in range(2):
                    pb = bank_()
                    for ct in range(2):
                        C.mm(pb.t[:, 0:NB], pwb.t[:, ct, nt * 128:(nt + 1) * 128], sts[ct].t[:, 0:NB], ct == 0, ct == 1,
                             [pwb.b, sts[ct].b], [pb.b])
                    yo = yop.get()
                    C.evac(yo.t[:, 0:NB], pb.t[:, 0:NB], [pb.b], [yo.b])
                    C.dma(Dr["yT"][768 + nt * 128:768 + (nt + 1) * 128, c0 + blk * NB:c0 + (blk + 1) * NB],
                          yo.t[:, 0:NB], [yo.b], [])
                    yield

            nblk = Ts // NB
            box = []
            interleave(partA(0, box))
            cur = box[0]
            for blk in range(nblk):
                box = []
                interleave(partB(blk, *cur), partA(blk + 1, box) if blk + 1 < nblk else None)
                cur = box[0] if box else None
            pbi = pbs["i"]
        P.C.S.barrier()


def phase_C1(P, LP, l, last=False):
    C, I, Dr = P.C, P.I, P.Dr
    with ExitStack() as es:
        woutb = C.sb(es, [128, 8, D], BF16, "woutb")
        stg = C.pool_of(es, 3, [128, 1024], F32, "stg")
        load_cast(C, es, woutb, I["w_out"][l], D, stg)
        htp = C.pool_of(es, 9, [128, D], F32, "ht")
        ssp = C.pool_of(es, 8, [128, 1], F32, "ss")
        rsp = C.pool_of(es, 8, [128, 1], F32, "rs")
        xsp = C.pool_of(es, 8, [128, D], BF16, "xs")
        xnp = C.pool_of(es, 2, [128, 8, 512], BF16, "xnT")
        ytp = C.pool_of(es, 2, [128, 8, 512], BF16, "ytb")
        tmp = C.pool_of(es, 4, [128, 512], F32, "tmp")
        gt = C.sb(es, [128, D], F32, "gt")
        pbi = 0
        blocks = []
        for (sn, Ts, c0, m) in streams(P):
            if last and sn == "c":
                continue
            NB = min(512, Ts)
            for blk in range(Ts // NB):
                blocks.append((sn, Ts, c0, m, NB, blk))

        def load_blk(bd):
            sn, Ts, c0, m, NB, blk = bd
            hsrc = Dr["hx"] if sn == "x" else (I["ctx"] if l == 0 else Dr["hc"])
            cs = slice(c0 + blk * NB, c0 + blk * NB + NB)
            yt = ytp.get()
            C.dma(yt.t[:, :, 0:NB], Dr["yT"][:, cs].rearrange("(kc p) t -> p kc t", p=128), [], [yt.b])
            hts = []
            for j in range(NB // 128):
                t0 = blk * NB + j * 128
                ht = htp.get()
                C.dma(ht.t[:], hsrc[t0:t0 + 128, :], [], [ht.b])
                hts.append(ht)
            return yt, hts

        cur_m = None
        nxt = load_blk(blocks[0])
        for bi, bd in enumerate(blocks):
            sn, Ts, c0, m, NB, blk = bd
            if m != cur_m:
                C.dma(gt.t[:], Dr["gtb"][m * 2 + 0], [], [gt.b])
                cur_m = m
            hdst = Dr["hx"] if sn == "x" else Dr["hc"]
            cs = slice(c0 + blk * NB, c0 + blk * NB + NB)
            yt, hts = nxt
            nxt = load_blk(blocks[bi + 1]) if bi + 1 < len(blocks) else None
            xnT = xnp.get()
            for j in range(NB // 128):
                ht = hts[j]
                t0 = blk * NB + j * 128
                for nh in range(2):
                    pb = P.PB[pbi % 6]; pbi += 1
                    for kc in range(8):
                        C.mm(pb.t[:], yt.t[:, kc, j * 128:(j + 1) * 128], woutb.t[:, kc, nh * 512:(nh + 1) * 512],
                             kc == 0, kc == 7, [yt.b, woutb.b], [pb.b])
                    tm = tmp.get()
                    C.tt("dve", tm.t[:], pb.t[:], gt.t[:, nh * 512:(nh + 1) * 512], ALU.mult, [pb.b, gt.b], [tm.b])
                    C.tt("pool", ht.t[:, nh * 512:(nh + 1) * 512], ht.t[:, nh * 512:(nh + 1) * 512], tm.t[:],
                         ALU.add, [tm.b], [ht.b])
                C.dma(hdst[t0:t0 + 128, :], ht.t[:], [ht.b], [])
            norm_block(P, C, hts, m, LP.gm2, 24, LP, xnT, ssp, rsp, xsp)
            C.dma(Dr["xn2"][:, cs].rearrange("(kc p) t -> p kc t", p=128), xnT.t[:, :, 0:NB], [xnT.b], [])
        P.C.S.barrier()


def phase_C2(P, LP, l, last):
    C, I, Dr = P.C, P.I, P.Dr
    NB = 512
    with ExitStack() as es:
        w1b = C.sb(es, [128, 8, 2 * DFF], BF16, "w1b")
        w2b = C.sb(es, [128, 22, D], BF16, "w2b")
        with ExitStack() as es_s:
            stg = C.pool_of(es_s, 3, [128, 1024], F32, "stg")
            load_cast(C, es_s, w1b, I["ffn_w_in"][l], 2 * DFF, stg)
            load_cast(C, es_s, w2b, I["ffn_w_out"][l], D, stg)
            P.C.S.barrier()
        htp = C.pool_of(es, 2, [128, D], F32, "ht")
        xnp = C.pool_of(es, 2, [128, 8, 512], BF16, "xn2")
        aT = C.sb(es, [128, 22, 512], BF16, "aT")
        sgp = C.pool_of(es, 2, [128, 512], F32, "sg")
        tmp = C.pool_of(es, 2, [128, 512], F32, "tmp")
        gt = C.sb(es, [128, D], F32, "gt")
        ssp = C.pool_of(es, 2, [128, 1], F32, "ss")
        rsp = C.pool_of(es, 2, [128, 1], F32, "rs")
        if last:
            fgb = C.sb(es, [128, D], F32, "fgb")
            C.dma(fgb.t[:], I["final_g"].partition_broadcast(128), [], [fgb.b])
        pbi = 0
        for (sn, Ts, c0, m) in streams(P):
            if last and sn == "c":
                continue
            C.dma(gt.t[:], Dr["gtb"][m * 2 + 1], [], [gt.b])
            hbuf = Dr["hx"] if sn == "x" else Dr["hc"]
            NB = min(512, Ts)
            def load_xn(blk_):
                cs_ = slice(c0 + blk_ * NB, c0 + blk_ * NB + NB)
                xn_ = xnp.get()
                C.dma(xn_.t[:, :, 0:NB], Dr["xn2"][:, cs_].rearrange("(kc p) t -> p kc t", p=128), [], [xn_.b])
                return xn_

            nxt_xn = load_xn(0)
            for blk in range(Ts // NB):
                cs = slice(c0 + blk * NB, c0 + blk * NB + NB)
                xn = nxt_xn
                for fb in range(22):
                    pg = P.PB[pbi % 6]; pbi += 1
                    pu = P.PB[pbi % 6]; pbi += 1
                    for dc in range(8):
                        C.mm(pg.t[:, 0:NB], w1b.t[:, dc, fb * 128:(fb + 1) * 128], xn.t[:, dc, 0:NB], dc == 0, dc == 7,
                             [w1b.b, xn.b], [pg.b])
                    for dc in range(8):
                        C.mm(pu.t[:, 0:NB], w1b.t[:, dc, DFF + fb * 128:DFF + (fb + 1) * 128], xn.t[:, dc, 0:NB],
                             dc == 0, dc == 7, [w1b.b, xn.b], [pu.b])
                    sg = sgp.get()
                    C.act(sg.t[:, 0:NB], pg.t[:, 0:NB], AF.Silu, [pg.b], [sg.b])
                    C.tt("dve", aT.t[:, fb, 0:NB], sg.t[:, 0:NB], pu.t[:, 0:NB], ALU.mult, [sg.b, pu.b], [aT.b])
                nxt_xn = load_xn(blk + 1) if blk + 1 < Ts // NB else None
                for j in range(NB // 128):
                    t0 = blk * NB + j * 128
                    ht = htp.get()
                    C.dma(ht.t[:], hbuf[t0:t0 + 128, :], [], [ht.b])
                    for nh in range(2):
                        pb = P.PB[pbi % 6]; pbi += 1
                        for fb in range(22):
                            C.mm(pb.t[:], aT.t[:, fb, j * 128:(j + 1) * 128], w2b.t[:, fb, nh * 512:(nh + 1) * 512],
                                 fb == 0, fb == 21, [aT.b, w2b.b], [pb.b])
                        tm = tmp.get()
                        C.tt("dve", tm.t[:], pb.t[:], gt.t[:, nh * 512:(nh + 1) * 512], ALU.mult, [pb.b, gt.b], [tm.b])
                        C.tt("pool", ht.t[:, nh * 512:(nh + 1) * 512], ht.t[:, nh * 512:(nh + 1) * 512], tm.t[:],
                             ALU.add, [tm.b], [ht.b])
                    if last:
                        ss = ssp.get(); rs = rsp.get()
                        tm = tmp.get(); tm2 = tmp.get()
                        C.act(tm.t[:], ht.t[:, 0:512], AF.Square, [ht.b], [tm.b, ss.b], accum=ss.t[:])
                        C.act(tm.t[:], ht.t[:, 512:1024], AF.Square, [ht.b], [tm.b, rs.b], accum=rs.t[:])
                        C.tt("dve", ss.t[:], ss.t[:], rs.t[:], ALU.add, [rs.b], [ss.b])
                        C.act(rs.t[:], ss.t[:], AF.Sqrt, [ss.b], [rs.b], scale=1.0 / D, bias=1e-6)
                        C.recip(rs.t[:], rs.t[:], [rs.b], [rs.b])
                        C.stt("dve", ht.t[:], ht.t[:], rs.t[:], fgb.t[:], ALU.mult, ALU.mult, [rs.b, fgb.b], [ht.b])
                        C.dma(P.out[t0:t0 + 128, :], ht.t[:], [ht.b], [])
                    else:
                        C.dma(hbuf[t0:t0 + 128, :], ht.t[:], [ht.b], [])
        P.C.S.barrier()


def make_in_map(inp, b, T, TC, L):
    f = lambda a: np.ascontiguousarray(np.asarray(a, dtype=np.float32))
    m = {}
    m["x"] = f(inp["x"][b]); m["ctx"] = f(inp["ctx"][b])
    m["cvec"] = f(np.stack([np.asarray(inp["c"][b]), np.asarray(inp["c_ctx"])], axis=0))
    m["w_mod"] = f(inp["w_mod"]); m["b_mod"] = f(inp["b_mod"])
    m["norm1_g"] = f(inp["norm1_g"]); m["norm2_g"] = f(inp["norm2_g"])
    m["w_in"] = f(inp["w_in"]); m["w_out"] = f(inp["w_out"])
    m["mu_prev"] = f(inp["rwkv_mu_prev"]); m["mu_next"] = f(inp["rwkv_mu_next"])
    m["w0"] = f(np.asarray(inp["rwkv_w0"]).reshape(L, 512)); m["w2"] = f(np.asarray(inp["rwkv_w2"]).reshape(L, 128, 256))
    m["a0"] = f(np.asarray(inp["rwkv_a0"]).reshape(L, 512)); m["a2"] = f(np.asarray(inp["rwkv_a2"]).reshape(L, 128, 256))
    m["g2"] = f(inp["rwkv_g2"])
    m["k_k"] = f(inp["rwkv_k_k"]); m["k_a"] = f(inp["rwkv_k_a"])
    m["r_k"] = f(np.asarray(inp["rwkv_r_k"]).reshape(L, 256))
    m["gn_w"] = f(inp["rwkv_gn_w"]); m["gn_b"] = f(inp["rwkv_gn_b"])
    m["pool_scale"] = f(inp["pool_scale"]); m["dw_b"] = f(inp["conv_dw_b"])
    m["ln_g"] = f(inp["conv_ln_g"]); m["ln_b"] = f(inp["conv_ln_b"])
    m["pool_w"] = f(inp["pool_w"]); m["fourier_w"] = f(inp["fourier_w"])
    m["dw_w"] = f(inp["conv_dw_w"]); m["conv_pw"] = f(inp["conv_pw"])
    m["ffn_w_in"] = f(inp["ffn_w_in"]); m["ffn_w_out"] = f(inp["ffn_w_out"])
    m["final_g"] = f(inp["final_norm_g"])
    return m


_CONST_CACHE = {}


def const_map(T, TC):
    key = (T, TC)
    if key not in _CONST_CACHE:
        c = {}
        c["misc"] = _misc_consts()
        c["poolm"] = _bf(_pool_mats())
        cc3, wr_x, gm_x = _fourier_consts(T)
        _, wr_c, gm_c = _fourier_consts(TC)
        c["cc3"] = cc3; c["wr_x"] = wr_x; c["gm_x"] = gm_x; c["wr_c"] = wr_c; c["gm_c"] = gm_c
        pr, pc = _pos_tables(T)
        c["pos_row"] = pr; c["pos_col"] = pc
        sel = np.zeros((2, 2, 128), np.float32)
        sel[0, 0, :] = 1.0
        sel[1, 1, :] = 1.0
        c["sel"] = sel
        _CONST_CACHE[key] = c
    return _CONST_CACHE[key]


def kernel(**inp):
    x = np.asarray(inp["x"])
    B, T, _ = x.shape
    TC = np.asarray(inp["ctx"]).shape[1]
    L = np.asarray(inp["w_in"]).shape[0]
    nc = build_program(T, TC, L)
    cm = const_map(T, TC)
    in_maps = []
    for b in range(B):
        m = make_in_map(inp, b, T, TC, L)
        m.update(cm)
        in_maps.append(m)
    res = run_bass_kernel_spmd(nc, in_maps, core_ids=list(range(B)))
    return np.stack([np.asarray(r["out"], dtype=np.float32) for r in res.results], axis=0)
```
